# Optimizing a Trainium2 kernel written in Bass

```python
import jax, jax.numpy as jnp
from jax import lax
import numpy as np

D_MODEL = 1024
BATCH = 16
SEQ = 2048
DEPTH = 1

N_META = 16
BLOCK = 128
WINDOW = 128
PREFIX = BLOCK
N_PAD = PREFIX - N_META
HEAD_DIM = 64
A_HEADS = D_MODEL // 128
A_KV_HEADS = A_HEADS // 4
A_GROUP = A_HEADS // A_KV_HEADS
B_HEADS = D_MODEL // 128
A_WIDTH = A_HEADS * HEAD_DIM
A_KV_WIDTH = A_KV_HEADS * HEAD_DIM
B_WIDTH = B_HEADS * HEAD_DIM
W_IN_COLS = A_WIDTH + 2 * A_KV_WIDTH + 3 * B_WIDTH + B_HEADS + 2 * D_MODEL
D_FF = ((8 * D_MODEL // 3 + 127) // 128) * 128
EPS = 1e-6
NEG = -1e30

kernel_name = "hybrid_swa_sink_fox_macaron_block"


def rms_norm(x, g):
    xf = x.astype(jnp.float32)
    y = xf * lax.rsqrt(jnp.mean(xf * xf, axis=-1, keepdims=True) + EPS)
    return (y * g.astype(jnp.float32)).astype(x.dtype)


def swiglu(x, w_in, w_out):
    gu = x @ w_in
    g, u = jnp.split(gu, 2, axis=-1)
    return (jax.nn.silu(g) * u) @ w_out


def alibi_slopes(n_heads):
    return jnp.exp2(-8.0 * jnp.arange(1, n_heads + 1, dtype=jnp.float32) / n_heads)


def sliding_window_sink_attention(q, k, v, sinks):
    b, l, _, dh = q.shape
    nb = l // BLOCK
    qb = q.reshape(b, nb, BLOCK, A_KV_HEADS, A_GROUP, dh)
    kb = k.reshape(b, nb, BLOCK, A_KV_HEADS, dh)
    vb = v.reshape(b, nb, BLOCK, A_KV_HEADS, dh)
    pad_blk = ((0, 0), (1, 0), (0, 0), (0, 0), (0, 0))
    band_k = jnp.concatenate([jnp.pad(kb[:, :-1], pad_blk), kb], axis=2)
    band_v = jnp.concatenate([jnp.pad(vb[:, :-1], pad_blk), vb], axis=2)
    meta_k = jnp.broadcast_to(k[:, None, N_PAD:PREFIX], (b, nb, N_META, A_KV_HEADS, dh))
    meta_v = jnp.broadcast_to(v[:, None, N_PAD:PREFIX], (b, nb, N_META, A_KV_HEADS, dh))
    keys = jnp.concatenate([meta_k, band_k], axis=2)
    vals = jnp.concatenate([meta_v, band_v], axis=2)

    q_pos = jnp.arange(l).reshape(nb, BLOCK)
    band_pos = (jnp.arange(nb)[:, None] - 1) * BLOCK + jnp.arange(2 * BLOCK)[None, :]
    meta_pos = jnp.broadcast_to(N_PAD + jnp.arange(N_META)[None, :], (nb, N_META))
    k_pos = jnp.concatenate([meta_pos, band_pos], axis=1)
    is_band = jnp.concatenate([jnp.zeros((N_META,), bool), jnp.ones((2 * BLOCK,), bool)])
    dist = q_pos[:, :, None] - k_pos[:, None, :]
    band_ok = (dist < WINDOW) & (k_pos[:, None, :] >= PREFIX)
    allowed = (dist >= 0) & jnp.where(is_band[None, None, :], band_ok, True)

    slopes = alibi_slopes(A_HEADS).reshape(A_KV_HEADS, A_GROUP)
    s = jnp.einsum('bnqkgd,bnskd->bnkgqs', qb, keys).astype(jnp.float32) * (dh ** -0.5)
    s = s - slopes[None, None, :, :, None, None] * dist.astype(jnp.float32)[None, :, None, None, :, :]
    s = jnp.where(allowed[None, :, None, None, :, :], s, NEG)
    sink = jnp.broadcast_to(
        sinks.astype(jnp.float32).reshape(A_KV_HEADS, A_GROUP)[None, None, :, :, None, None],
        s.shape[:-1] + (1,))
    p = jax.nn.softmax(jnp.concatenate([s, sink], axis=-1), axis=-1)[..., :-1]
    o = jnp.einsum('bnkgqs,bnskd->bnqkgd', p.astype(v.dtype), vals)
    return o.reshape(b, l, A_HEADS * dh)


def forgetting_attention(q, k, v, log_f):
    b, l, h, dh = q.shape
    nb = l // BLOCK
    c = jnp.cumsum(log_f, axis=1).transpose(0, 2, 1)
    outs = []
    for i in range(nb):
        q_lo, k_hi = i * BLOCK, (i + 1) * BLOCK
        s = jnp.einsum('bqhd,bshd->bhqs', q[:, q_lo:k_hi], k[:, :k_hi]).astype(jnp.float32) * (dh ** -0.5)
        s = s + c[:, :, q_lo:k_hi, None] - c[:, :, None, :k_hi]
        q_pos = q_lo + jnp.arange(BLOCK)
        k_pos = jnp.arange(k_hi)
        allowed = (k_pos[None, :] <= q_pos[:, None]) & (k_pos[None, :] >= N_PAD)
        s = jnp.where(allowed[None, None], s, NEG)
        p = jax.nn.softmax(s, axis=-1)
        outs.append(jnp.einsum('bhqs,bshd->bqhd', p.astype(v.dtype), v[:, :k_hi]))
    return jnp.concatenate(outs, axis=1).reshape(b, l, h * dh)


def setup_inputs(seed: int = 0) -> dict:
    key = jax.random.key(seed)
    ks = jax.random.split(key, 20)
    f32 = jnp.float32
    n = lambda k, shape, scale: jax.random.normal(k, shape, f32) * scale
    return {
        "x": n(ks[0], (BATCH, SEQ, D_MODEL), 1.0),
        "meta_tokens": n(ks[1], (N_META, D_MODEL), 1.0),
        "ffn1_norm": 1.0 + n(ks[2], (DEPTH, D_MODEL), 0.02),
        "ffn1_w_in": n(ks[3], (DEPTH, D_MODEL, 2 * D_FF), D_MODEL ** -0.5),
        "ffn1_w_out": n(ks[4], (DEPTH, D_FF, D_MODEL), D_FF ** -0.5),
        "mix_norm": 1.0 + n(ks[5], (DEPTH, D_MODEL), 0.02),
        "w_in": n(ks[6], (DEPTH, D_MODEL, W_IN_COLS), D_MODEL ** -0.5),
        "b_forget": 2.0 + n(ks[7], (DEPTH, B_HEADS), 0.1),
        "attn_sinks": n(ks[8], (DEPTH, A_HEADS), 0.5),
        "w_branch_a": n(ks[9], (DEPTH, A_WIDTH, D_MODEL), A_WIDTH ** -0.5),
        "w_branch_b": n(ks[10], (DEPTH, B_WIDTH, D_MODEL), B_WIDTH ** -0.5),
        "w_out": n(ks[11], (DEPTH, D_MODEL, D_MODEL), D_MODEL ** -0.5),
        "ffn2_norm": 1.0 + n(ks[12], (DEPTH, D_MODEL), 0.02),
        "ffn2_w_in": n(ks[13], (DEPTH, D_MODEL, 2 * D_FF), D_MODEL ** -0.5),
        "ffn2_w_out": n(ks[14], (DEPTH, D_FF, D_MODEL), D_FF ** -0.5),
        "final_norm": 1.0 + n(ks[15], (D_MODEL,), 0.02),
    }


def reference(x, meta_tokens, ffn1_norm, ffn1_w_in, ffn1_w_out, mix_norm, w_in, b_forget,
              attn_sinks, w_branch_a, w_branch_b, w_out, ffn2_norm, ffn2_w_in, ffn2_w_out,
              final_norm):
    b = x.shape[0]
    pads = jnp.zeros((b, N_PAD, D_MODEL), x.dtype)
    meta = jnp.broadcast_to(meta_tokens.astype(x.dtype)[None], (b, N_META, D_MODEL))
    h = jnp.concatenate([pads, meta, x], axis=1)
    l = h.shape[1]

    sizes = [A_WIDTH, A_KV_WIDTH, A_KV_WIDTH, B_WIDTH, B_WIDTH, B_WIDTH, B_HEADS, D_MODEL, D_MODEL]
    offsets = []
    acc = 0
    for sz in sizes[:-1]:
        acc += sz
        offsets.append(acc)

    for i in range(DEPTH):
        h = h + 0.5 * swiglu(rms_norm(h, ffn1_norm[i]), ffn1_w_in[i], ffn1_w_out[i])

        u = rms_norm(h, mix_norm[i])
        proj = u @ w_in[i]
        qa, ka, va, qb, kb, vb, f_logit, g_a, g_b = jnp.split(proj, offsets, axis=-1)
        qa = qa.reshape(b, l, A_HEADS, HEAD_DIM)
        ka = ka.reshape(b, l, A_KV_HEADS, HEAD_DIM)
        va = va.reshape(b, l, A_KV_HEADS, HEAD_DIM)
        qb = qb.reshape(b, l, B_HEADS, HEAD_DIM)
        kb = kb.reshape(b, l, B_HEADS, HEAD_DIM)
        vb = vb.reshape(b, l, B_HEADS, HEAD_DIM)
        log_f = jax.nn.log_sigmoid((f_logit + b_forget[i]).astype(jnp.float32))

        y_a = sliding_window_sink_attention(qa, ka, va, attn_sinks[i]) @ w_branch_a[i]
        y_b = forgetting_attention(qb, kb, vb, log_f) @ w_branch_b[i]
        mixed = jax.nn.sigmoid(g_a) * y_a + jax.nn.sigmoid(g_b) * y_b
        h = h + mixed @ w_out[i]

        h = h + 0.5 * swiglu(rms_norm(h, ffn2_norm[i]), ffn2_w_in[i], ffn2_w_out[i])

    return rms_norm(h, final_norm)[:, PREFIX:]
```

```python
import contextlib
import numpy as np
import ml_dtypes
import concourse.bass as bass
import concourse.mybir as mybir
from concourse.bass_utils import run_bass_kernel_spmd

F32 = mybir.dt.float32
BF16 = mybir.dt.bfloat16
AF = mybir.ActivationFunctionType
ALU = mybir.AluOpType

D = 1024
SEQ = 2048
NBLK = 16
NMETA = 16
TCOLS = SEQ + NMETA
DFF = 2816
NCH = 22
PASSES = [(0, 6), (6, 6), (12, 5), (17, 5)]
EPS = 1e-6
SLOPES = [2.0 ** (-(h + 1)) for h in range(8)]
ENGS = ("pe", "act", "dve", "pool", "sp")
ARENA_BYTES = 98304
DBG_SWA = 9

CF_TRI, CF_ONES, CF_IOTA, CF_BCUR, CF_BPREV, CF_BMETA, CF_N = 0, 128, 256, 384, 392, 400, 528
CB_ID, CB_MCUR, CB_MPREV, CB_OLO, CB_OHI, CB_N = 0, 128, 640, 1152, 1280, 1408


def make_consts():
    cf = np.zeros((128, CF_N), np.float32)
    p = np.arange(128)
    cf[:, CF_TRI:CF_TRI + 128] = (p[:, None] <= p[None, :]).astype(np.float32)
    cf[:, CF_ONES:CF_ONES + 128] = 1.0
    cf[:, CF_IOTA:CF_IOTA + 128] = p[None, :].astype(np.float32)
    sl = np.array(SLOPES, np.float32)
    cf[:, CF_BCUR:CF_BCUR + 8] = p[:, None] * sl[None, :]
    cf[:, CF_BPREV:CF_BPREV + 8] = (p[:, None] - 128.0) * sl[None, :]
    bm = np.zeros((128, 16, 8), np.float32)
    for i in range(16):
        bm[:, i, :] = (p[:, None] - 16.0 - 128.0 * i) * sl[None, :]
    cf[:, CF_BMETA:CF_BMETA + 128] = bm.reshape(128, 128)
    cb = np.zeros((128, CB_N), np.float32)
    cb[:, CB_ID:CB_ID + 128] = np.eye(128)
    mc = (p[:, None] <= p[None, :]).astype(np.float32)
    mp = (p[:, None] > p[None, :]).astype(np.float32)
    cb[:, CB_MCUR:CB_MCUR + 512] = np.tile(mc, (1, 4))
    cb[:, CB_MPREV:CB_MPREV + 512] = np.tile(mp, (1, 4))
    cb[:, CB_OLO:CB_OLO + 64] = 1.0
    cb[:, CB_OHI + 64:CB_OHI + 128] = 1.0
    return cf, cb.astype(ml_dtypes.bfloat16)


class Op:
    __slots__ = ("idx", "eng", "fn", "reads", "writes", "dma", "deps", "signal", "semval", "semname", "attach")

    def __init__(self, idx, eng, fn, reads, writes, dma):
        self.idx, self.eng, self.fn, self.reads, self.writes, self.dma = idx, eng, fn, reads, writes, dma
        self.deps = set()
        self.signal = False
        self.semval = 0
        self.semname = None
        self.attach = False


class Prog:
    def __init__(self, nc):
        self.nc = nc
        self.ops = []
        self.last_writer = {}
        self.readers = {}
        self.last_on = {}

    def add(self, eng, fn, reads=(), writes=(), dma=None, full=False, attach=False):
        if dma is not None:
            writes = tuple(writes) + (("__slot", dma),)
        op = Op(len(self.ops), eng, fn, tuple(reads), tuple(writes), dma)
        op.attach = attach
        deps = set()
        for r in op.reads:
            w = self.last_writer.get(r)
            if w is not None:
                deps.add(w)
        for r in op.writes:
            w = self.last_writer.get(r)
            if w is not None:
                deps.add(w)
            deps.update(self.readers.get(r, ()))
        for d in deps:
            dop = self.ops[d]
            if dop.dma is None and dop.eng == eng and dma is None:
                if eng == "pe":
                    continue
                if not full and not any(self.last_writer.get(r) == d for r in op.reads):
                    continue
            op.deps.add(d)
        for r in op.reads:
            self.readers.setdefault(r, []).append(op.idx)
        for r in op.writes:
            self.last_writer[r] = op.idx
            self.readers[r] = []
        self.ops.append(op)
        if dma is None:
            self.last_on[eng] = op.idx
        return op

    def barrier(self):
        lasts = dict(self.last_on)
        dmas = [idx for (k, idx) in self.last_writer.items() if isinstance(k, tuple) and k[0] == "__slot"]
        for e in ENGS:
            op = Op(len(self.ops), e, None, (), (), None)
            for e2, idx in lasts.items():
                if e2 != e:
                    op.deps.add(idx)
            op.deps.update(dmas)
            self.ops.append(op)

    def emit(self):
        nc, ops = self.nc, self.ops
        for op in ops:
            best = {}
            keep = set()
            for d in op.deps:
                dop = ops[d]
                if dop.dma is not None:
                    keep.add(d)
                elif d > best.get(dop.eng, -1):
                    best[dop.eng] = d
            keep.update(best.values())
            op.deps = keep
            for d in keep:
                ops[d].signal = True
        cnt = {}
        for op in ops:
            if op.dma is not None:
                op.signal = True
                op.semname = "d_" + op.dma
                cnt[op.semname] = cnt.get(op.semname, 0) + 16
                op.semval = cnt[op.semname]
            elif op.signal:
                op.semname = "e_" + op.eng
                cnt[op.semname] = cnt.get(op.semname, 0) + 1
                op.semval = cnt[op.semname]
        semnames = sorted(cnt)
        with contextlib.ExitStack() as es:
            sems = {n: es.enter_context(nc.semaphore(n)) for n in semnames}
            block = es.enter_context(nc.Block())
            by_eng = {e: [op for op in ops if op.eng == e] for e in ENGS}
            final = [(n, cnt[n]) for n in semnames if n.startswith("d_")]

            def run(engname, eng):
                known = {}
                for op in by_eng[engname]:
                    waits = {}
                    for d in op.deps:
                        dop = ops[d]
                        if dop.semval > waits.get(dop.semname, 0):
                            waits[dop.semname] = dop.semval
                    need = [(n, v) for n, v in sorted(waits.items()) if known.get(n, 0) < v]
                    emb = None
                    if op.attach and need:
                        emb = need.pop()
                    for n, v in need:
                        eng.wait_ge(sems[n], v)
                        known[n] = v
                    if op.fn is None:
                        continue
                    if op.attach:
                        if emb is not None:
                            known[emb[0]] = emb[1]
                        ins = op.fn(eng, None if emb is None else (sems[emb[0]], emb[1]))
                    else:
                        ins = op.fn(eng)
                    if op.signal:
                        ins.then_inc(sems[op.semname], 16 if op.dma is not None else 1)
                if engname == "sp":
                    for n, v in final:
                        eng.wait_ge(sems[n], v)

            @block.tensor
            def _(e):
                run("pe", e)

            @block.scalar
            def _(e):
                run("act", e)

            @block.vector
            def _(e):
                run("dve", e)

            @block.gpsimd
            def _(e):
                run("pool", e)

            @block.sync
            def _(e):
                run("sp", e)


def build(stage=3):
    nc = bass.Bass("TRN2", target_bir_lowering=False)
    dt_in = lambda n, s, d=F32: nc.dram_tensor(n, s, d, kind="ExternalInput").ap()
    x_d = dt_in("x", [2, SEQ, D])
    meta_d = dt_in("meta", [NMETA, D])
    n1_d, n2_d, n3_d, n4_d = (dt_in(n, [1, D]) for n in ("n1", "n2", "n3", "n4"))
    w1i_d, w1o_d = dt_in("w1i", [D, 2 * DFF]), dt_in("w1o", [DFF, D])
    w2i_d, w2o_d = dt_in("w2i", [D, 2 * DFF]), dt_in("w2o", [DFF, D])
    wi_d = dt_in("wi", [D, 4360])
    bf_d, sk_d = dt_in("bfg", [1, 8]), dt_in("snk", [1, 8])
    wa_d, wb_d, wo_d = dt_in("wa", [512, D]), dt_in("wb", [512, D]), dt_in("wo", [D, D])
    cf_d, cb_d = dt_in("cf", [128, CF_N]), dt_in("cb", [128, CB_N], BF16)
    out_d = nc.dram_tensor("out", [2, SEQ, D], F32, kind="ExternalOutput").ap()

    es = contextlib.ExitStack()
    sb = lambda n, s, d: es.enter_context(nc.sbuf_tensor(n, s, d))
    h = sb("h", [128, NBLK, D], F32)
    uT = sb("uT", [128, 8, TCOLS], BF16)
    cf = sb("cf_s", [128, CF_N], F32)
    cb = sb("cb_s", [128, CB_N], BF16)
    gb = sb("gb", [128, D], F32)
    ss = sb("ss", [128, 17], F32)
    rs = sb("rs", [128, 17], F32)
    epsc = sb("epsc", [128, 1], F32)
    bfg = sb("bfg_s", [128, 8], F32)
    snk = sb("snk_s", [128, 8], F32)
    sinktab = sb("sinktab", [128, 512], F32)
    arena = sb("arena", [128, ARENA_BYTES // 2], BF16)
    ps = [es.enter_context(nc.psum_tensor(f"ps{i}", [128, 512], F32)) for i in range(8)]
    pt = [ps[6 + i][:, 0:256].bitcast(BF16) for i in range(2)]

    def AR(off, shape, dtype=BF16):
        n = int(np.prod(shape[1:]))
        nb = n * (4 if dtype == F32 else 2)
        assert off % 4 == 0 and off + nb <= ARENA_BYTES, (off, nb)
        v = arena[:, off // 2:(off + nb) // 2]
        if dtype == F32:
            v = v.bitcast(F32)
        if len(shape) == 2:
            return v
        names = " ".join(f"a{i}" for i in range(len(shape) - 1))
        kw = {f"a{i}": shape[i + 1] for i in range(len(shape) - 2)}
        return v.rearrange(f"p ({names}) -> p {names}", **kw)

    P = Prog(nc)
    rot = {}

    def nxt(name, n):
        rot[name] = (rot.get(name, -1) + 1) % n
        return rot[name]

    P.add("sp", lambda e: e.dma_start(out=cf[:], in_=cf_d), writes=["cf"], dma="cf")
    P.add("sp", lambda e: e.dma_start(out=cb[:], in_=cb_d), writes=["cb"], dma="cb")
    P.add("sp", lambda e: e.dma_start(out=bfg[:], in_=bf_d.broadcast_to([128, 8])), writes=["bfg"], dma="bfg")
    P.add("sp", lambda e: e.dma_start(out=snk[:], in_=sk_d.broadcast_to([128, 8])), writes=["snk"], dma="snk")
    P.add("pool", lambda e: e.memset(epsc[:], EPS), writes=["epsc"])
    onec = sb("onec", [128, 1], F32)
    P.add("pool", lambda e: e.memset(onec[:], 1.0), writes=["onec"])
    ALLSS = ["ss"] + [("ss", b) for b in ["m"] + list(range(NBLK))]
    for g in range(4):
        for hh in range(2):
            hd = 4 * hh + g
            P.add("act", lambda e, g=g, hh=hh, hd=hd: e.activation(
                out=sinktab[64 * hh:64 * hh + 64, 128 * g:128 * g + 128],
                in_=cf[64 * hh:64 * hh + 64, CF_IOTA:CF_IOTA + 128], func=AF.Exp,
                bias=snk[64 * hh:64 * hh + 64, hd:hd + 1], scale=SLOPES[hd]),
                reads=["cf", "snk"], writes=["sinktab"])

    def tile_cols(t):
        if t == "m":
            return 0, NMETA, ["m"]
        return NMETA + 512 * t, 512, [4 * t + j for j in range(4)]

    def blk_cols(b):
        if b == "m":
            return 0, NMETA
        return NMETA + 128 * b, 128

    def cast_dma(dst, src, res, slot):
        P.add("pool", lambda e: e.dma_start(out=dst, in_=src), writes=[res], dma=slot)

    def mm_group(out, pairs, reads, writes):
        def fn(e):
            n = len(pairs)
            ins = None
            for i, (l, r) in enumerate(pairs):
                ins = e.matmul(out, lhsT=l, rhs=r, start=(i == 0), stop=(i == n - 1))
            return ins
        P.add("pe", fn, reads=reads, writes=writes)

    evac_flip = [0]

    def evac(out, in_, reads, writes, eng=None):
        if eng is None:
            evac_flip[0] ^= 1
            eng = "dve" if evac_flip[0] else "act"
        if eng == "act":
            P.add("act", lambda e: e.copy(out=out, in_=in_), reads=reads, writes=writes)
        else:
            P.add(eng, lambda e: e.tensor_copy(out=out, in_=in_), reads=reads, writes=writes)

    def norm_to_uT(gain_d, blocks, hm, hooked=False):
        ut = [AR(OFF_UT + 2048 * i, [128, D]) for i in range(2)]
        P.add("sp", lambda e: e.dma_start(out=gb[:], in_=gain_d.broadcast_to([128, D])), writes=["gb"], dma="gb")
        P.add("pool", lambda e: e.memset(ss[:], 0.0), writes=ALLSS)
        jmap = {}

        def stage_a(b):
            c0, nr = blk_cols(b)
            src = hm[0:nr, :] if b == "m" else h[:, b, :]
            col = 16 if b == "m" else b
            j = nxt("ut", 2)
            jmap[b] = j
            P.add("act", lambda e: e.activation(
                out=ut[j][0:nr, :], in_=src, func=AF.Square, accum_out=ss[0:nr, col:col + 1]),
                reads=[("h", b), ("ss", b)], writes=[("ss", b), ("ut", j)], full=True)
            P.add("act", lambda e: e.activation(
                out=rs[0:nr, col:col + 1], in_=ss[0:nr, col:col + 1], func=AF.Sqrt, bias=epsc[0:nr, 0:1], scale=1.0 / D),
                reads=[("ss", b), "epsc"], writes=[("rs", b)])
            P.add("dve", lambda e: e.reciprocal(out=rs[0:nr, col:col + 1], in_=rs[0:nr, col:col + 1]),
                  reads=[("rs", b)], writes=[("rs", b)])
            P.add("dve", lambda e: e.scalar_tensor_tensor(
                out=ut[j][0:nr, :], in0=src, scalar=rs[0:nr, col:col + 1], in1=gb[0:nr, :],
                op0=ALU.mult, op1=ALU.mult), reads=[("h", b), ("rs", b), "gb"], writes=[("ut", j)])

        def stage_b(b):
            c0, nr = blk_cols(b)
            j = jmap[b]
            for half in range(2):
                def fn(e, half=half):
                    ins = None
                    for c in range(4):
                        cc = 4 * half + c
                        ins = e.transpose(pt[half][:, 128 * c:128 * c + nr], ut[j][0:nr, 128 * cc:128 * cc + 128],
                                          cb[0:nr, CB_ID:CB_ID + nr])
                    return ins
                P.add("pe", fn, reads=[("ut", j), "cb"], writes=[("ps", 6 + half)])
                src_v = pt[half][:, :].rearrange("p (c n) -> p c n", c=4)[:, :, 0:nr]
                evac(uT[:, 4 * half:4 * half + 4, c0:c0 + nr], src_v, [("ps", 6 + half)], [("uT", b, half)],
                     eng=("act" if half == 0 else "dve"))
        if hooked:
            pend = []

            def hook(b):
                stage_a(b)
                if pend:
                    stage_b(pend.pop())
                pend.append(b)

            def flush():
                while pend:
                    stage_b(pend.pop())
            return hook, flush
        stage_a(blocks[0])
        for i, b in enumerate(blocks):
            if i + 1 < len(blocks):
                stage_a(blocks[i + 1])
            stage_b(b)

    def uT_res(blks):
        return [("uT", b, hf) for b in blks for hf in range(2)]

    def ffn(w_in_d, w_out_d, gain_d, with_meta, hm, tagp, do_norm=True, tail=None):
        blocks = (["m"] if with_meta else []) + list(range(NBLK))
        if do_norm:
            norm_to_uT(gain_d, blocks, hm)
        tiles = (["m"] if with_meta else []) + [0, 1, 2, 3]
        act = AR(OFF_ACT, [128, 6, TCOLS])
        wis = [AR(OFF_WI + 8192 * i, [128, 2, 8, 256]) for i in range(3)]
        wos = [AR(OFF_WO + 12288 * i, [128, 6, D]) for i in range(2)]
        tmp = [AR(OFF_TMP + 2048 * i, [128, 512], F32) for i in range(2)]
        for (m0, nch) in PASSES:
            wslot = nxt("wo", 2)
            wo_t = wos[wslot]
            ml = 0
            while ml < nch:
                ns = min(2, nch - ml)
                s = nxt("wi", 3)
                wi_t = wis[s]
                c0 = (m0 + ml) * 128
                cast_dma(wi_t[:, 0, :, 0:ns * 128], w_in_d[:, c0:c0 + ns * 128].rearrange("(k p) n -> p k n", p=128),
                         ("wi", s, 0), f"wi{s}g")
                cast_dma(wi_t[:, 1, :, 0:ns * 128],
                         w_in_d[:, DFF + c0:DFF + c0 + ns * 128].rearrange("(k p) n -> p k n", p=128),
                         ("wi", s, 1), f"wi{s}u")
                if ml == 0:
                    cast_dma(wo_t[:, 0:nch, :], w_out_d[m0 * 128:(m0 + nch) * 128, :].rearrange("(k p) n -> p k n", p=128),
                             ("wo", wslot), f"wo{wslot}")
                for q in range(ns):
                    mloc = ml + q
                    for t in tiles:
                        tc0, tn, tb = tile_cols(t)
                        a = nxt("psA", 2)
                        pA, pB = ps[2 * a], ps[2 * a + 1]
                        for which, pp in ((0, pA), (1, pB)):
                            mm_group(pp[:, 0:tn],
                                     [(wi_t[:, which, k, q * 128:(q + 1) * 128], uT[:, k, tc0:tc0 + tn]) for k in range(8)],
                                     reads=[("wi", s, which)] + uT_res(tb), writes=[("ps", 2 * a + which)])
                        j = nxt("tmp", 2)
                        P.add("act", lambda e, j=j, pA=pA, tn=tn: e.activation(out=tmp[j][:, 0:tn], in_=pA[:, 0:tn], func=AF.Silu),
                              reads=[("ps", 2 * a)], writes=[("tmp", j)])
                        P.add("dve", lambda e, j=j, pB=pB, tn=tn, mloc=mloc, tc0=tc0: e.tensor_tensor(
                            out=act[:, mloc, tc0:tc0 + tn], in0=tmp[j][:, 0:tn], in1=pB[:, 0:tn], op=ALU.mult),
                            reads=[("tmp", j), ("ps", 2 * a + 1)], writes=[("act", mloc, t)])
                ml += ns
            hook = flush = None
            if tail is not None and (m0, nch) == PASSES[-1]:
                hook, flush = tail()
            for b in blocks:
                bc0, nr = blk_cols(b)
                t = "m" if b == "m" else b // 4
                for half in range(2):
                    o = 4 + nxt("psO", 2)
                    mm_group(ps[o][0:nr, :],
                             [(act[:, k, bc0:bc0 + nr], wo_t[:, k, 512 * half:512 * half + 512]) for k in range(nch)],
                             reads=[("act", k, t) for k in range(nch)] + [("wo", wslot)], writes=[("ps", o)])
                    dst = hm[0:nr, 512 * half:512 * half + 512] if b == "m" else h[:, b, 512 * half:512 * half + 512]
                    P.add("dve", lambda e, dst=dst, o=o, nr=nr: e.scalar_tensor_tensor(
                        out=dst, in0=ps[o][0:nr, :], scalar=0.5, in1=dst, op0=ALU.mult, op1=ALU.add),
                        reads=[("ps", o), ("h", b)], writes=[("h", b)])
                if hook is not None:
                    hook(b)
            if flush is not None:
                flush()

    OFF_HM = 0
    OFF_UT = 4096
    OFF_ACT = 8192
    OFF_WI = OFF_ACT + 24768
    OFF_WO = OFF_WI + 24576
    OFF_TMP = OFF_WO + 24576
    OFF_OST = OFF_TMP + 4096
    assert OFF_OST + 8192 <= ARENA_BYTES
    OFF_OA = 8192
    OFF_OB = OFF_OA + 16384
    OFF_LF = OFF_OB + 16384
    OFF_BIASF = OFF_LF + 3264
    OFF_PT = OFF_BIASF + 2176
    OFF_DEN = OFF_PT + 4096
    OFF_X = OFF_DEN + 2048
    assert OFF_X + 45632 <= ARENA_BYTES, OFF_X
    OFF_PW = OFF_OB + 16384
    OFF_MIX = OFF_PW + 24576
    assert OFF_MIX + 32768 <= ARENA_BYTES
    OFF_PTMP = 0

    def mixer(hm, mstop=9, post_tail=None):
        ablocks = ["m"] + list(range(NBLK))
        NPT = 4
        PT = [AR(OFF_PT + 1024 * i, [128, 512]) for i in range(NPT)]
        den = [AR(OFF_DEN, [128, 512], F32)]
        OaT = AR(OFF_OA, [128, 4, SEQ])
        ObT = AR(OFF_OB, [128, 4, SEQ])
        lfv = [AR(OFF_LF + 544 * i, [128, 136], F32) for i in range(6)]
        xb, lt, tots, offv, cum = lfv[0], lfv[1], lfv[2], lfv[3], lfv[4]
        biasF = AR(OFF_BIASF, [128, 17, 4, 8], F32)
        wf = lfv[5][:, :].bitcast(BF16)[:, 0:64].rearrange("p (k n) -> p k n", k=8)
        cast_dma(wf, wi_d[:, 2304:2312].rearrange("(k p) n -> p k n", p=128), "wf", "wf")

        def f_fn(e):
            ins = None
            for bi, b in enumerate(ablocks):
                c0, nr = blk_cols(b)
                for k in range(8):
                    ins = e.matmul(ps[0][0:nr, 8 * bi:8 * bi + 8], lhsT=uT[:, k, c0:c0 + nr], rhs=wf[:, k, :],
                                   start=(k == 0), stop=(k == 7))
            return ins
        P.add("pe", f_fn, reads=["wf"] + uT_res(ablocks), writes=[("ps", 0)])
        P.add("pool", lambda e: e.memset(lt[:], 0.0), writes=["lt"])
        for (r0, r1, c0, c1) in ((0, 16, 0, 8), (0, 128, 8, 136)):
            bb = bfg[r0:r1, :] if c1 == 8 else bfg[r0:r1, :].unsqueeze(1).to_broadcast([r1 - r0, 16, 8])
            i0 = ps[0][r0:r1, c0:c1] if c1 == 8 else ps[0][r0:r1, c0:c1].rearrange("p (b h) -> p b h", h=8)
            o0 = xb[r0:r1, c0:c1] if c1 == 8 else xb[r0:r1, c0:c1].rearrange("p (b h) -> p b h", h=8)
            P.add("dve", lambda e, bb=bb, i0=i0, o0=o0: e.tensor_tensor(out=o0, in0=i0, in1=bb, op=ALU.add),
                  reads=[("ps", 0), "bfg"], writes=["xb"])
            P.add("act", lambda e, r0=r0, r1=r1, c0=c0, c1=c1: e.activation(
                out=xb[r0:r1, c0:c1], in_=xb[r0:r1, c0:c1], func=AF.Exp, scale=-1.0), reads=["xb"], writes=["xb"])
            P.add("act", lambda e, r0=r0, r1=r1, c0=c0, c1=c1: e.activation(
                out=lt[r0:r1, c0:c1], in_=xb[r0:r1, c0:c1], func=AF.Ln, bias=onec[r0:r1, 0:1]), reads=["xb", "lt", "onec"], writes=["lt"])
        P.add("pe", lambda e: e.matmul(ps[1][:, 0:136], lhsT=cf[:, CF_ONES:CF_ONES + 128], rhs=lt[:, :], start=True, stop=True),
              reads=["lt", "cf"], writes=[("ps", 1)])
        P.add("pe", lambda e: e.matmul(ps[2][:, 0:136], lhsT=cf[:, CF_TRI:CF_TRI + 128], rhs=lt[:, :], start=True, stop=True),
              reads=["lt", "cf"], writes=[("ps", 2)])
        P.add("dve", lambda e: e.tensor_copy(out=tots[:, :], in_=ps[1][:, 0:136]), reads=[("ps", 1)], writes=["tots"])
        P.add("pool", lambda e: e.memset(offv[:, :], 0.0), writes=["offv"])
        for b in range(1, 17):
            P.add("dve", lambda e, b=b: e.tensor_tensor(out=offv[:, 8 * b:8 * b + 8], in0=offv[:, 8 * b - 8:8 * b],
                                                        in1=tots[:, 8 * b - 8:8 * b], op=ALU.add),
                  reads=["offv", "tots"], writes=["offv"])
        P.add("dve", lambda e: e.tensor_tensor(out=cum[:, :], in0=ps[2][:, 0:136], in1=offv[:, :], op=ALU.add),
              reads=[("ps", 2), "offv"], writes=["cum"])
        cum3 = cum[:, :].rearrange("p (b h) -> p b h", h=8)
        for j in range(4):
            ref = 4 * j + 3
            for hd in range(8):
                P.add("dve", lambda e, j=j, hd=hd, ref=ref: e.tensor_scalar(
                    out=biasF[:, :, j, hd], in0=cum3[:, :, hd], scalar1=offv[:, 8 * ref + hd:8 * ref + hd + 1],
                    scalar2=None, op0=ALU.subtract), reads=["cum", "offv"], writes=["biasF"])
        if mstop <= 1:
            return

        wqa = AR(OFF_X, [128, 8, 4, 2, 64])
        wka = AR(OFF_X + 8192, [128, 8, 128])
        wva = AR(OFF_X + 10240, [128, 8, 128])
        QaT = AR(OFF_X + 12288, [128, 4, SEQ])
        Ka2 = [AR(OFF_X + 28672 + 4128 * i, [128, TCOLS]) for i in range(2)]
        Va = [AR(OFF_X + 36928 + 4352 * i, [128, 17, 128]) for i in range(2)]
        P.add("pool", lambda e: e.memset(Ka2[0][64:128, :], 0.0), writes=[("KaT", b) for b in ablocks])
        P.add("pool", lambda e: e.memset(Ka2[1][0:64, :], 0.0), writes=[("KaT", b) for b in ablocks])
        for hh in range(2):
            for g in range(4):
                cq = 256 * hh + 64 * g
                cast_dma(wqa[:, :, g, hh, :], wi_d[:, cq:cq + 64].rearrange("(k p) d -> p k d", p=128),
                         ("wqa", hh, g), f"wqa{hh}{g}")
        cast_dma(wka, wi_d[:, 512:640].rearrange("(k p) n -> p k n", p=128), "wka", "wka")
        cast_dma(wva, wi_d[:, 640:768].rearrange("(k p) n -> p k n", p=128), "wva", "wva")
        for i in range(2):
            P.add("pool", lambda e, i=i: e.memset(Va[i][:, :, :], 0.0), writes=[("Va", i)])
        for g in range(4 if DBG_SWA >= 0.2 else 0):
            for t in range(4):
                tc0, tn, tb = tile_cols(t)
                a = nxt("psQ", 2)
                mm_group(ps[a][:, :], [(wqa[:, k, g].rearrange("p a d -> p (a d)"), uT[:, k, tc0:tc0 + tn]) for k in range(8)],
                         reads=[("wqa", 0, g), ("wqa", 1, g)] + uT_res(tb), writes=[("ps", a)])
                evac(QaT[:, g, 512 * t:512 * t + 512], ps[a][:, :], [("ps", a)], [("QaT", 4 * t + j) for j in range(4)])
        for t in (["m", 0, 1, 2, 3] if DBG_SWA >= 0.3 else []):
            tc0, tn, tb = tile_cols(t)
            a = nxt("psQ", 2)
            mm_group(ps[a][:, 0:tn], [(wka[:, k, :], uT[:, k, tc0:tc0 + tn]) for k in range(8)],
                     reads=["wka"] + uT_res(tb), writes=[("ps", a)])
            evac(Ka2[0][0:64, tc0:tc0 + tn], ps[a][0:64, 0:tn], [("ps", a)], [("KaT", b) for b in tb], eng="dve")
            evac(Ka2[1][64:128, tc0:tc0 + tn], ps[a][64:128, 0:tn], [("ps", a)], [("KaT", b) for b in tb], eng="dve")
        for bi, b in enumerate(ablocks if DBG_SWA >= 0.4 else []):
            if DBG_SWA == 0.45 and b == "m":
                continue
            c0, nr = blk_cols(b)
            a = nxt("psQ", 2)
            mm_group(ps[a][0:nr, 0:128], [(uT[:, k, c0:c0 + nr], wva[:, k, :]) for k in range(8)],
                     reads=["wva"] + uT_res([b]), writes=[("ps", a)])
            evac(Va[0][0:nr, bi, 0:64], ps[a][0:nr, 0:64], [("ps", a)], [("Va", 0)], eng="dve")
            evac(Va[1][0:nr, bi, 64:128], ps[a][0:nr, 64:128], [("ps", a)], [("Va", 1)], eng="dve")
        SB = [0, 1, 2, 5]
        LOOK = 3
        ODS = [(3, 4), (6, 7)]
        steps = []
        for i in range(NBLK):
            roles = [("meta", 0, 16, 0)] + ([("prev", i, 128, NMETA + 128 * (i - 1))] if i >= 1 else []) + \
                    [("cur", i + 1, 128, NMETA + 128 * i)]
            n_i = 2 * len(roles)
            cnt = 0
            for kvh in range(2):
                for (role, kb, nk, kc0) in roles:
                    steps.append((i, kvh, role, kb, nk, kc0, cnt == 0, cnt == n_i - 1))
                    cnt += 1

        def swa_qk(n):
            i, kvh, role, kb, nk, kc0, st, last = steps[n]
            sidx = SB[n % 4]
            pS = ps[sidx]
            base = 64 * kvh
            kres = ("KaT", "m") if role == "meta" else ("KaT", kb - 1)
            def qk(e, w):
                ins = e.matmul(pS[0:nk, :].rearrange("p (g q) -> p g q", g=4), lhsT=Ka2[kvh][:, kc0:kc0 + nk],
                               rhs=QaT[:, :, 128 * i:128 * i + 128], start=True, stop=True)
                if w is not None:
                    ins._wait_ge(*w)
                return ins
            P.add("pe", qk, reads=[kres, ("QaT", i)], writes=[("ps", sidx)], attach=True)

        def swa_rest(n):
            i, kvh, role, kb, nk, kc0, st, last = steps[n]
            sidx = SB[n % 4]
            pS = ps[sidx]
            pj = n % NPT
            iO, iD = ODS[i % 2]
            pO, pD = ps[iO], ps[iD]
            for g in range(4):
                hd = 4 * kvh + g
                if role == "meta":
                    bcol = cf[0:nk, CF_BMETA + 8 * i + hd:CF_BMETA + 8 * i + hd + 1]
                elif role == "prev":
                    bcol = cf[0:nk, CF_BPREV + hd:CF_BPREV + hd + 1]
                else:
                    bcol = cf[0:nk, CF_BCUR + hd:CF_BCUR + hd + 1]
                P.add("act", lambda e, g=g, bcol=bcol: e.activation(
                    out=PT[pj][0:nk, 128 * g:128 * g + 128], in_=pS[0:nk, 128 * g:128 * g + 128],
                    func=AF.Exp, bias=bcol, scale=0.125), reads=[("ps", sidx), "cf"], writes=[("PT", pj)])
            if role != "meta":
                mo = CB_MPREV if role == "prev" else CB_MCUR
                P.add("dve", lambda e: e.tensor_tensor(
                    out=PT[pj][:, :], in0=PT[pj][:, :], in1=cb[:, mo:mo + 512], op=ALU.mult),
                    reads=[("PT", pj), "cb"], writes=[("PT", pj)])
            oo = CB_OLO if kvh == 0 else CB_OHI

            def pv(e, w):
                i0 = e.matmul(pO[:, :], lhsT=Va[kvh][0:nk, kb, :], rhs=PT[pj][0:nk, :], start=st, stop=last)
                if w is not None:
                    i0._wait_ge(*w)
                return e.matmul(pD[:, :], lhsT=cb[0:nk, oo:oo + 128], rhs=PT[pj][0:nk, :], start=st, stop=last)
            P.add("pe", pv, reads=[("PT", pj), ("Va", kvh), "cb"], writes=[("ps", iO), ("ps", iD)], attach=True)
            if defer:
                defer.pop(0)()
            if last:
                dj = 0
                defer.append(lambda: P.add("dve", lambda e: e.tensor_tensor(out=den[dj][:, :], in0=pD[:, :], in1=sinktab[:, :], op=ALU.add),
                                           reads=[("ps", iD), "sinktab"], writes=[("den", dj)]))
                defer.append(lambda: P.add("dve", lambda e: e.reciprocal(out=den[dj][:, :], in_=den[dj][:, :]),
                                           reads=[("den", dj)], writes=[("den", dj)]))
                defer.append(lambda: P.add("dve", lambda e: e.tensor_tensor(
                    out=OaT[:, :, 128 * i:128 * i + 128], in0=pO[:, :].rearrange("p (g q) -> p g q", g=4),
                    in1=den[dj][:, :].rearrange("p (g q) -> p g q", g=4), op=ALU.mult),
                    reads=[("ps", iO), ("den", dj)], writes=[("OaT", i // 4)]))
        defer = []
        for n in range(len(steps) + LOOK):
            if n < len(steps):
                swa_qk(n)
            if n - LOOK >= 0:
                swa_rest(n - LOOK)
        while defer:
            defer.pop(0)()
        P.barrier()
        if mstop <= 2:
            return

        wqb = AR(OFF_X, [128, 8, 512])
        wkb = AR(OFF_X + 8192, [128, 8, 512])
        wvb = AR(OFF_X + 16384, [128, 8, 512])
        QbT = AR(OFF_X + 24576, [128, SEQ])
        Kb2 = [AR(OFF_X + 28672 + 4128 * i, [128, TCOLS]) for i in range(2)]
        Vb = [AR(OFF_X + 36928 + 4352 * i, [128, 17, 128]) for i in range(2)]
        P.add("pool", lambda e: e.memset(Kb2[0][64:128, :], 0.0), writes=[("KbT", b) for b in ablocks])
        P.add("pool", lambda e: e.memset(Kb2[1][0:64, :], 0.0), writes=[("KbT", b) for b in ablocks])
        for c in range(4):
            for (wt, col0, nm) in ((wqb, 768, "wqb"), (wkb, 1280, "wkb"), (wvb, 1792, "wvb")):
                cast_dma(wt[:, :, 128 * c:128 * c + 128],
                         wi_d[:, col0 + 128 * c:col0 + 128 * c + 128].rearrange("(k p) n -> p k n", p=128), (nm, c), nm)
        for i in range(2):
            P.add("pool", lambda e, i=i: e.memset(Vb[i][:, :, :], 0.0), writes=[("Vb", i, bi) for bi in range(17)])
        for c in range(4):
            for t in range(4):
                tc0, tn, tb = tile_cols(t)
                a = nxt("psQ", 2)
                mm_group(ps[a][:, :], [(wqb[:, k, 128 * c:128 * c + 128], uT[:, k, tc0:tc0 + tn]) for k in range(8)],
                         reads=[("wqb", c)] + uT_res(tb), writes=[("ps", a)])
                evac(QbT[:, 512 * t:512 * t + 512], ps[a][:, :], [("ps", a)], [("QbT", t)], eng="dve")
            for t in ["m", 0, 1, 2, 3]:
                tc0, tn, tb = tile_cols(t)
                a = nxt("psQ", 2)
                mm_group(ps[a][:, 0:tn], [(wkb[:, k, 128 * c:128 * c + 128], uT[:, k, tc0:tc0 + tn]) for k in range(8)],
                         reads=[("wkb", c)] + uT_res(tb), writes=[("ps", a)])
                evac(Kb2[0][0:64, tc0:tc0 + tn], ps[a][0:64, 0:tn], [("ps", a)], [("KbT", b) for b in tb], eng="dve")
                evac(Kb2[1][64:128, tc0:tc0 + tn], ps[a][64:128, 0:tn], [("ps", a)], [("KbT", b) for b in tb], eng="dve")
            for bi, b in enumerate(ablocks):
                c0, nr = blk_cols(b)
                a = nxt("psQ", 2)
                mm_group(ps[a][0:nr, 0:128], [(uT[:, k, c0:c0 + nr], wvb[:, k, 128 * c:128 * c + 128]) for k in range(8)],
                         reads=[("wvb", c)] + uT_res([b]), writes=[("ps", a)])
                evac(Vb[0][0:nr, bi, 0:64], ps[a][0:nr, 0:64], [("ps", a)], [("Vb", 0, bi)], eng="dve")
                evac(Vb[1][0:nr, bi, 64:128], ps[a][0:nr, 64:128], [("ps", a)], [("Vb", 1, bi)], eng="dve")
            fsteps = []
            for j in range(4):
                kbs = [0] + [1 + r for r in range(4 * j + 4)]
                n_j = 2 * len(kbs)
                cnt = 0
                for kb in kbs:
                    for hh in range(2):
                        fsteps.append((j, kb, hh, cnt == 0, cnt == n_j - 1))
                        cnt += 1

            def fparams(n):
                j, kb, hh, st, last = fsteps[n]
                if kb == 0:
                    nk, kc0, c0, kres = 16, 0, 0, ("KbT", "m")
                else:
                    r = kb - 1
                    nk, kc0, kres = 128, NMETA + 128 * r, ("KbT", r)
                    c0 = 128 * (r - 4 * j) if r >= 4 * j else 0
                diag = kb >= 1 and (kb - 1) >= 4 * j
                return j, kb, hh, st, last, nk, kc0, c0, kres, diag

            def fox_qk(n, c=c):
                j, kb, hh, st, last, nk, kc0, c0, kres, diag = fparams(n)
                sidx = SB[n % 4]
                pS = ps[sidx]
                base = 64 * hh
                def qk(e, w):
                    ins = e.matmul(pS[0:nk, c0:512], lhsT=Kb2[hh][:, kc0:kc0 + nk],
                                   rhs=QbT[:, 512 * j + c0:512 * j + 512], start=True, stop=True)
                    if w is not None:
                        ins._wait_ge(*w)
                    return ins
                P.add("pe", qk, reads=[kres, ("QbT", j)], writes=[("ps", sidx)], attach=True)

            def fox_rest(n, c=c):
                j, kb, hh, st, last, nk, kc0, c0, kres, diag = fparams(n)
                sidx = SB[n % 4]
                pS = ps[sidx]
                pj = n % NPT
                iO, iD = ODS[(4 * c + j) % 2]
                pO, pD = ps[iO], ps[iD]
                hd = 2 * c + hh
                P.add("act", lambda e: e.activation(
                    out=PT[pj][0:nk, c0:512], in_=pS[0:nk, c0:512], func=AF.Exp,
                    bias=biasF[0:nk, kb, j, hd:hd + 1], scale=0.125),
                    reads=[("ps", sidx), "biasF"], writes=[("PT", pj)])
                if diag:
                    P.add("dve", lambda e: e.tensor_tensor(
                        out=PT[pj][:, c0:c0 + 128], in0=PT[pj][:, c0:c0 + 128],
                        in1=cb[:, CB_MCUR:CB_MCUR + 128], op=ALU.mult),
                        reads=[("PT", pj), "cb"], writes=[("PT", pj)])
                oo = CB_OLO if hh == 0 else CB_OHI

                def pv(e, w):
                    i0 = e.matmul(pO[:, c0:512], lhsT=Vb[hh][0:nk, kb, :], rhs=PT[pj][0:nk, c0:512], start=st, stop=last)
                    if w is not None:
                        i0._wait_ge(*w)
                    return e.matmul(pD[:, c0:512], lhsT=cb[0:nk, oo:oo + 128], rhs=PT[pj][0:nk, c0:512], start=st, stop=last)
                P.add("pe", pv, reads=[("PT", pj), ("Vb", hh, kb), "cb"], writes=[("ps", iO), ("ps", iD)], attach=True)
                if defer:
                    defer.pop(0)()
                if last:
                    dj = 0
                    defer.append(lambda: P.add("dve", lambda e: e.reciprocal(out=den[dj][:, :], in_=pD[:, :]),
                                               reads=[("ps", iD)], writes=[("den", dj)]))
                    defer.append(lambda: P.add("dve", lambda e: e.tensor_tensor(
                        out=ObT[:, c, 512 * j:512 * j + 512], in0=pO[:, :], in1=den[dj][:, :], op=ALU.mult),
                        reads=[("ps", iO), ("den", dj)], writes=[("ObT", j)]))
            for n in range(len(fsteps) + LOOK):
                if n < len(fsteps):
                    fox_qk(n)
                if n - LOOK >= 0:
                    fox_rest(n - LOOK)
            while defer:
                defer.pop(0)()
        P.barrier()
        if mstop <= 3:
            return

        mixT = AR(OFF_MIX, [128, 8, SEQ])
        wo_t = AR(OFF_PW, [128, 8, D])
        ovl = [[("pw", 0, "a", 0), ("pw", 0, "a", 1), ("pw", 0, "b"), ("pw", 0, "ga")],
               [("pw", 0, "gb"), ("pw", 1, "a", 0), ("pw", 1, "a", 1), ("pw", 1, "b")]]

        def issue_wo(hf):
            P.add("pool", lambda e: e.dma_start(
                out=wo_t[:, 4 * hf:4 * hf + 4, :], in_=wo_d[512 * hf:512 * hf + 512, :].rearrange("(k p) n -> p k n", p=128)),
                writes=[("wout", hf)] + ovl[hf], dma=f"wout{hf}")
        ptmp = [AR(OFF_PTMP + 2048 * i, [128, 512], F32) for i in range(2)]
        for st in range(4):
            s = nxt("pw", 2)
            o = OFF_PW + 12288 * s
            wa_t = AR(o, [128, 4, 256])
            wb_t = AR(o + 2048, [128, 4, 256])
            wga_t = AR(o + 4096, [128, 8, 256])
            wgb_t = AR(o + 8192, [128, 8, 256])
            c0 = 256 * st
            for hh in range(2):
                cast_dma(wa_t[64 * hh:64 * hh + 64, :, :],
                         wa_d[256 * hh:256 * hh + 256, c0:c0 + 256].rearrange("(g d) n -> d g n", d=64),
                         ("pw", s, "a", hh), f"pw{s}a{hh}")
            cast_dma(wb_t, wb_d[:, c0:c0 + 256].rearrange("(c p) n -> p c n", p=128), ("pw", s, "b"), f"pw{s}b")
            cast_dma(wga_t, wi_d[:, 2312 + c0:2312 + c0 + 256].rearrange("(k p) n -> p k n", p=128), ("pw", s, "ga"), f"pw{s}ga")
            cast_dma(wgb_t, wi_d[:, 3336 + c0:3336 + c0 + 256].rearrange("(k p) n -> p k n", p=128), ("pw", s, "gb"), f"pw{s}gb")
            if st == 3:
                issue_wo(0)
            for q in range(2):
                m = 2 * st + q
                for t in range(4):
                    tc0, tn, tb = tile_cols(t)
                    ia, ib, iga, igb = (nxt("psP", 8) for _ in range(4))
                    mm_group(ps[ia][:, :], [(wa_t[:, k, 128 * q:128 * q + 128], OaT[:, k, 512 * t:512 * t + 512]) for k in range(4)],
                             reads=[("pw", s, "a", 0), ("pw", s, "a", 1), ("OaT", t)], writes=[("ps", ia)])
                    mm_group(ps[ib][:, :], [(wb_t[:, k, 128 * q:128 * q + 128], ObT[:, k, 512 * t:512 * t + 512]) for k in range(4)],
                             reads=[("pw", s, "b"), ("ObT", t)], writes=[("ps", ib)])
                    mm_group(ps[iga][:, :], [(wga_t[:, k, 128 * q:128 * q + 128], uT[:, k, tc0:tc0 + 512]) for k in range(8)],
                             reads=[("pw", s, "ga")] + uT_res(tb), writes=[("ps", iga)])
                    mm_group(ps[igb][:, :], [(wgb_t[:, k, 128 * q:128 * q + 128], uT[:, k, tc0:tc0 + 512]) for k in range(8)],
                             reads=[("pw", s, "gb")] + uT_res(tb), writes=[("ps", igb)])
                    ja, jb = 0, 1
                    P.add("act", lambda e, ja=ja, iga=iga: e.activation(out=ptmp[ja][:, :], in_=ps[iga][:, :], func=AF.Sigmoid),
                          reads=[("ps", iga)], writes=[("ptmp", ja)])
                    P.add("act", lambda e, jb=jb, igb=igb: e.activation(out=ptmp[jb][:, :], in_=ps[igb][:, :], func=AF.Sigmoid),
                          reads=[("ps", igb)], writes=[("ptmp", jb)])
                    P.add("dve", lambda e, ja=ja, ia=ia: e.tensor_tensor(out=ptmp[ja][:, :], in0=ptmp[ja][:, :], in1=ps[ia][:, :], op=ALU.mult),
                          reads=[("ptmp", ja), ("ps", ia)], writes=[("ptmp", ja)])
                    P.add("dve", lambda e, jb=jb, ib=ib: e.tensor_tensor(out=ptmp[jb][:, :], in0=ptmp[jb][:, :], in1=ps[ib][:, :], op=ALU.mult),
                          reads=[("ptmp", jb), ("ps", ib)], writes=[("ptmp", jb)])
                    P.add("pool", lambda e, ja=ja, jb=jb, m=m, t=t: e.tensor_tensor(
                        out=mixT[:, m, 512 * t:512 * t + 512], in0=ptmp[ja][:, :], in1=ptmp[jb][:, :], op=ALU.add),
                        reads=[("ptmp", ja), ("ptmp", jb)], writes=[("mixT", t)])
        issue_wo(1)
        hook = flush = None
        if post_tail is not None:
            hook, flush = post_tail()
        for b in range(NBLK):
            for half in range(2):
                o = nxt("psP", 8)
                mm_group(ps[o][:, :], [(mixT[:, k, 128 * b:128 * b + 128], wo_t[:, k, 512 * half:512 * half + 512]) for k in range(8)],
                         reads=[("mixT", b // 4), ("wout", 0), ("wout", 1)], writes=[("ps", o)])
                dst = h[:, b, 512 * half:512 * half + 512]
                P.add("dve", lambda e, dst=dst, o=o: e.tensor_tensor(out=dst, in0=ps[o][:, :], in1=dst, op=ALU.add),
                      reads=[("ps", o), ("h", b)], writes=[("h", b)])
            if hook is not None:
                hook(b)
        if flush is not None:
            flush()

    def final_out(s, raw, hooked=False, after_block=None):
        ost = [AR(OFF_OST + 4096 * i, [128, D], F32) for i in range(2)]
        if raw:
            for b in range(NBLK):
                P.add("sp", lambda e, b=b: e.dma_start(out=out_d[s, 128 * b:128 * b + 128, :], in_=h[:, b, :]),
                      reads=[("h", b)], dma=f"out{b % 2}")
            return
        P.add("sp", lambda e: e.dma_start(out=gb[:], in_=n4_d.broadcast_to([128, D])), writes=["gb"], dma="gb")
        P.add("pool", lambda e: e.memset(ss[:], 0.0), writes=ALLSS)

        def hook(b):
            j = nxt("ost", 2)
            P.add("act", lambda e, b=b, j=j: e.activation(out=ost[j][:, :], in_=h[:, b, :], func=AF.Square,
                                                          accum_out=ss[:, b:b + 1]),
                  reads=[("h", b), ("ss", b), ("ost", j)], writes=[("ss", b), ("ost", j)], full=True)
            P.add("act", lambda e, b=b: e.activation(out=rs[:, b:b + 1], in_=ss[:, b:b + 1], func=AF.Sqrt,
                                                     bias=epsc[:, 0:1], scale=1.0 / D),
                  reads=[("ss", b), "epsc"], writes=[("rs", b)])
            P.add("dve", lambda e, b=b: e.reciprocal(out=rs[:, b:b + 1], in_=rs[:, b:b + 1]), reads=[("rs", b)], writes=[("rs", b)])
            P.add("dve", lambda e, b=b, j=j: e.scalar_tensor_tensor(
                out=ost[j][:, :], in0=h[:, b, :], scalar=rs[:, b:b + 1], in1=gb[:, :], op0=ALU.mult, op1=ALU.mult),
                reads=[("h", b), ("rs", b), "gb"], writes=[("ost", j)])
            P.add("sp", lambda e, b=b, j=j: e.dma_start(out=out_d[s, 128 * b:128 * b + 128, :], in_=ost[j][:, :]),
                  reads=[("ost", j)], dma=f"out{j}")
        if hooked:
            return hook, (lambda: None)
        for b in range(NBLK):
            hook(b)
            if after_block is not None:
                after_block(b)

    def load_x(s, b):
        P.add("sp", lambda e: e.dma_start(out=h[:, b, :], in_=x_d[s, 128 * b:128 * b + 128, :]),
              writes=[("h", b)], dma=f"x{b % 4}")

    for s in range(2):
        hm = AR(OFF_HM, [128, D], F32)
        if s == 0 or stage != 3:
            for b in range(NBLK):
                load_x(s, b)
        P.add("sp", lambda e: e.dma_start(out=hm[0:NMETA, :], in_=meta_d), writes=[("h", "m")], dma="xm")
        if stage == 3:
            ffn(w1i_d, w1o_d, n1_d, True, hm, "f1",
                tail=lambda: norm_to_uT(n2_d, ["m"] + list(range(NBLK)), hm, hooked=True))
            P.barrier()
            mixer(hm, post_tail=lambda: norm_to_uT(n3_d, list(range(NBLK)), hm, hooked=True))
            P.barrier()
            ffn(w2i_d, w2o_d, n3_d, False, hm, "f2", do_norm=False)
            final_out(s, raw=False, after_block=((lambda b: load_x(1, b)) if s == 0 else None))
        else:
            ffn(w1i_d, w1o_d, n1_d, True, hm, "f1")
            if stage >= 2:
                norm_to_uT(n2_d, ["m"] + list(range(NBLK)), hm)
                P.barrier()
                mixer(hm, mstop=(stage - 10 if stage >= 10 else 9))
            P.barrier()
            final_out(s, raw=True)
    P.emit()
    es.close()
    return nc


_CACHE = {}


def kernel(x, meta_tokens, ffn1_norm, ffn1_w_in, ffn1_w_out, mix_norm, w_in, b_forget, attn_sinks,
           w_branch_a, w_branch_b, w_out, ffn2_norm, ffn2_w_in, ffn2_w_out, final_norm, _stage=3, _cores=8):
    f = lambda a: np.ascontiguousarray(np.asarray(a, dtype=np.float32))
    x = f(x)
    cf, cb = make_consts()
    shared = {
        "meta": f(meta_tokens), "n1": f(ffn1_norm).reshape(1, D), "n2": f(mix_norm).reshape(1, D),
        "n3": f(ffn2_norm).reshape(1, D), "n4": f(final_norm).reshape(1, D),
        "w1i": f(ffn1_w_in)[0], "w1o": f(ffn1_w_out)[0], "w2i": f(ffn2_w_in)[0], "w2o": f(ffn2_w_out)[0],
        "wi": f(w_in)[0], "bfg": f(b_forget).reshape(1, 8), "snk": f(attn_sinks).reshape(1, 8),
        "wa": f(w_branch_a)[0], "wb": f(w_branch_b)[0], "wo": f(w_out)[0], "cf": cf, "cb": cb,
    }
    if _stage not in _CACHE:
        _CACHE[_stage] = build(_stage)
    nc = _CACHE[_stage]
    in_maps = [dict(shared, x=x[2 * c:2 * c + 2]) for c in range(_cores)]
    res = run_bass_kernel_spmd(nc, in_maps, core_ids=list(range(_cores)))
    return np.concatenate([r["out"] for r in res.results], axis=0)
```

```python
import contextlib
import numpy as np
import ml_dtypes
import concourse.bass as bass
import concourse.mybir as mybir
from concourse.bass_utils import run_bass_kernel_spmd

F32 = mybir.dt.float32
BF16 = mybir.dt.bfloat16
AF = mybir.ActivationFunctionType
ALU = mybir.AluOpType

D = 1024
SEQ = 2048
NBLK = 16
NMETA = 16
TCOLS = SEQ + NMETA
DFF = 2816
NCH = 22
PASSES = [(0, 6), (6, 6), (12, 5), (17, 5)]
EPS = 1e-6
SLOPES = [2.0 ** (-(h + 1)) for h in range(8)]
ENGS = ("pe", "act", "dve", "pool", "sp")
ARENA_BYTES = 98304
DBG_SWA = 9

CF_TRI, CF_ONES, CF_IOTA, CF_BCUR, CF_BPREV, CF_BMETA, CF_N = 0, 128, 256, 384, 392, 400, 528
CB_ID, CB_MCUR, CB_MPREV, CB_OLO, CB_OHI, CB_N = 0, 128, 640, 1152, 1280, 1408


def make_consts():
    cf = np.zeros((128, CF_N), np.float32)
    p = np.arange(128)
    cf[:, CF_TRI:CF_TRI + 128] = (p[:, None] <= p[None, :]).astype(np.float32)
    cf[:, CF_ONES:CF_ONES + 128] = 1.0
    cf[:, CF_IOTA:CF_IOTA + 128] = p[None, :].astype(np.float32)
    sl = np.array(SLOPES, np.float32)
    cf[:, CF_BCUR:CF_BCUR + 8] = p[:, None] * sl[None, :]
    cf[:, CF_BPREV:CF_BPREV + 8] = (p[:, None] - 128.0) * sl[None, :]
    bm = np.zeros((128, 16, 8), np.float32)
    for i in range(16):
        bm[:, i, :] = (p[:, None] - 16.0 - 128.0 * i) * sl[None, :]
    cf[:, CF_BMETA:CF_BMETA + 128] = bm.reshape(128, 128)
    cb = np.zeros((128, CB_N), np.float32)
    cb[:, CB_ID:CB_ID + 128] = np.eye(128)
    mc = (p[:, None] <= p[None, :]).astype(np.float32)
    mp = (p[:, None] > p[None, :]).astype(np.float32)
    cb[:, CB_MCUR:CB_MCUR + 512] = np.tile(mc, (1, 4))
    cb[:, CB_MPREV:CB_MPREV + 512] = np.tile(mp, (1, 4))
    cb[:, CB_OLO:CB_OLO + 64] = 1.0
    cb[:, CB_OHI + 64:CB_OHI + 128] = 1.0
    return cf, cb.astype(ml_dtypes.bfloat16)


class Op:
    __slots__ = ("idx", "eng", "fn", "reads", "writes", "dma", "deps", "signal", "semval", "semname", "attach")

    def __init__(self, idx, eng, fn, reads, writes, dma):
        self.idx, self.eng, self.fn, self.reads, self.writes, self.dma = idx, eng, fn, reads, writes, dma
        self.deps = set()
        self.signal = False
        self.semval = 0
        self.semname = None
        self.attach = False


class Prog:
    def __init__(self, nc):
        self.nc = nc
        self.ops = []
        self.last_writer = {}
        self.readers = {}
        self.last_on = {}

    def add(self, eng, fn, reads=(), writes=(), dma=None, full=False, attach=False):
        if dma is not None:
            writes = tuple(writes) + (("__slot", dma),)
        op = Op(len(self.ops), eng, fn, tuple(reads), tuple(writes), dma)
        op.attach = attach
        deps = set()
        for r in op.reads:
            w = self.last_writer.get(r)
            if w is not None:
                deps.add(w)
        for r in op.writes:
            w = self.last_writer.get(r)
            if w is not None:
                deps.add(w)
            deps.update(self.readers.get(r, ()))
        for d in deps:
            dop = self.ops[d]
            if dop.dma is None and dop.eng == eng and dma is None:
                if eng == "pe":
                    continue
                if not full and not any(self.last_writer.get(r) == d for r in op.reads):
                    continue
            op.deps.add(d)
        for r in op.reads:
            self.readers.setdefault(r, []).append(op.idx)
        for r in op.writes:
            self.last_writer[r] = op.idx
            self.readers[r] = []
        self.ops.append(op)
        if dma is None:
            self.last_on[eng] = op.idx
        return op

    def barrier(self):
        lasts = dict(self.last_on)
        dmas = [idx for (k, idx) in self.last_writer.items() if isinstance(k, tuple) and k[0] == "__slot"]
        for e in ENGS:
            op = Op(len(self.ops), e, None, (), (), None)
            for e2, idx in lasts.items():
                if e2 != e:
                    op.deps.add(idx)
            op.deps.update(dmas)
            self.ops.append(op)

    def emit(self):
        nc, ops = self.nc, self.ops
        for op in ops:
            best = {}
            keep = set()
            for d in op.deps:
                dop = ops[d]
                if dop.dma is not None:
                    keep.add(d)
                elif d > best.get(dop.eng, -1):
                    best[dop.eng] = d
            keep.update(best.values())
            op.deps = keep
            for d in keep:
                ops[d].signal = True
        cnt = {}
        for op in ops:
            if op.dma is not None:
                op.signal = True
                op.semname = "d_" + op.dma
                cnt[op.semname] = cnt.get(op.semname, 0) + 16
                op.semval = cnt[op.semname]
            elif op.signal:
                op.semname = "e_" + op.eng
                cnt[op.semname] = cnt.get(op.semname, 0) + 1
                op.semval = cnt[op.semname]
        semnames = sorted(cnt)
        with contextlib.ExitStack() as es:
            sems = {n: es.enter_context(nc.semaphore(n)) for n in semnames}
            block = es.enter_context(nc.Block())
            by_eng = {e: [op for op in ops if op.eng == e] for e in ENGS}
            final = [(n, cnt[n]) for n in semnames if n.startswith("d_")]

            def run(engname, eng):
                known = {}
                for op in by_eng[engname]:
                    waits = {}
                    for d in op.deps:
                        dop = ops[d]
                        if dop.semval > waits.get(dop.semname, 0):
                            waits[dop.semname] = dop.semval
                    need = [(n, v) for n, v in sorted(waits.items()) if known.get(n, 0) < v]
                    emb = None
                    if op.attach and need:
                        emb = need.pop()
                    for n, v in need:
                        eng.wait_ge(sems[n], v)
                        known[n] = v
                    if op.fn is None:
                        continue
                    if op.attach:
                        if emb is not None:
                            known[emb[0]] = emb[1]
                        ins = op.fn(eng, None if emb is None else (sems[emb[0]], emb[1]))
                    else:
                        ins = op.fn(eng)
                    if op.signal:
                        ins.then_inc(sems[op.semname], 16 if op.dma is not None else 1)
                if engname == "sp":
                    for n, v in final:
                        eng.wait_ge(sems[n], v)

            @block.tensor
            def _(e):
                run("pe", e)

            @block.scalar
            def _(e):
                run("act", e)

            @block.vector
            def _(e):
                run("dve", e)

            @block.gpsimd
            def _(e):
                run("pool", e)

            @block.sync
            def _(e):
                run("sp", e)


def build(stage=3):
    nc = bass.Bass("TRN2", target_bir_lowering=False)
    dt_in = lambda n, s, d=F32: nc.dram_tensor(n, s, d, kind="ExternalInput").ap()
    x_d = dt_in("x", [2, SEQ, D])
    meta_d = dt_in("meta", [NMETA, D])
    n1_d, n2_d, n3_d, n4_d = (dt_in(n, [1, D]) for n in ("n1", "n2", "n3", "n4"))
    w1i_d, w1o_d = dt_in("w1i", [D, 2 * DFF]), dt_in("w1o", [DFF, D])
    w2i_d, w2o_d = dt_in("w2i", [D, 2 * DFF]), dt_in("w2o", [DFF, D])
    wi_d = dt_in("wi", [D, 4360])
    bf_d, sk_d = dt_in("bfg", [1, 8]), dt_in("snk", [1, 8])
    wa_d, wb_d, wo_d = dt_in("wa", [512, D]), dt_in("wb", [512, D]), dt_in("wo", [D, D])
    cf_d, cb_d = dt_in("cf", [128, CF_N]), dt_in("cb", [128, CB_N], BF16)
    out_d = nc.dram_tensor("out", [2, SEQ, D], F32, kind="ExternalOutput").ap()

    es = contextlib.ExitStack()
    sb = lambda n, s, d: es.enter_context(nc.sbuf_tensor(n, s, d))
    h = sb("h", [128, NBLK, D], F32)
    uT = sb("uT", [128, 8, TCOLS], BF16)
    cf = sb("cf_s", [128, CF_N], F32)
    cb = sb("cb_s", [128, CB_N], BF16)
    gb = sb("gb", [128, D], F32)
    ss = sb("ss", [128, 17], F32)
    rs = sb("rs", [128, 17], F32)
    epsc = sb("epsc", [128, 1], F32)
    bfg = sb("bfg_s", [128, 8], F32)
    snk = sb("snk_s", [128, 8], F32)
    sinktab = sb("sinktab", [128, 512], F32)
    arena = sb("arena", [128, ARENA_BYTES // 2], BF16)
    ps = [es.enter_context(nc.psum_tensor(f"ps{i}", [128, 512], F32)) for i in range(8)]
    pt = [ps[6 + i][:, 0:256].bitcast(BF16) for i in range(2)]

    def AR(off, shape, dtype=BF16):
        n = int(np.prod(shape[1:]))
        nb = n * (4 if dtype == F32 else 2)
        assert off % 4 == 0 and off + nb <= ARENA_BYTES, (off, nb)
        v = arena[:, off // 2:(off + nb) // 2]
        if dtype == F32:
            v = v.bitcast(F32)
        if len(shape) == 2:
            return v
        names = " ".join(f"a{i}" for i in range(len(shape) - 1))
        kw = {f"a{i}": shape[i + 1] for i in range(len(shape) - 2)}
        return v.rearrange(f"p ({names}) -> p {names}", **kw)

    P = Prog(nc)
    rot = {}

    def nxt(name, n):
        rot[name] = (rot.get(name, -1) + 1) % n
        return rot[name]

    P.add("sp", lambda e: e.dma_start(out=cf[:], in_=cf_d), writes=["cf"], dma="cf")
    P.add("sp", lambda e: e.dma_start(out=cb[:], in_=cb_d), writes=["cb"], dma="cb")
    P.add("sp", lambda e: e.dma_start(out=bfg[:], in_=bf_d.broadcast_to([128, 8])), writes=["bfg"], dma="bfg")
    P.add("sp", lambda e: e.dma_start(out=snk[:], in_=sk_d.broadcast_to([128, 8])), writes=["snk"], dma="snk")
    P.add("pool", lambda e: e.memset(epsc[:], EPS), writes=["epsc"])
    onec = sb("onec", [128, 1], F32)
    P.add("pool", lambda e: e.memset(onec[:], 1.0), writes=["onec"])
    ALLSS = ["ss"] + [("ss", b) for b in ["m"] + list(range(NBLK))]
    for g in range(4):
        for hh in range(2):
            hd = 4 * hh + g
            P.add("act", lambda e, g=g, hh=hh, hd=hd: e.activation(
                out=sinktab[64 * hh:64 * hh + 64, 128 * g:128 * g + 128],
                in_=cf[64 * hh:64 * hh + 64, CF_IOTA:CF_IOTA + 128], func=AF.Exp,
                bias=snk[64 * hh:64 * hh + 64, hd:hd + 1], scale=SLOPES[hd]),
                reads=["cf", "snk"], writes=["sinktab"])

    def tile_cols(t):
        if t == "m":
            return 0, NMETA, ["m"]
        return NMETA + 512 * t, 512, [4 * t + j for j in range(4)]

    def blk_cols(b):
        if b == "m":
            return 0, NMETA
        return NMETA + 128 * b, 128

    def cast_dma(dst, src, res, slot):
        P.add("pool", lambda e: e.dma_start(out=dst, in_=src), writes=[res], dma=slot)

    def mm_group(out, pairs, reads, writes):
        def fn(e):
            n = len(pairs)
            ins = None
            for i, (l, r) in enumerate(pairs):
                ins = e.matmul(out, lhsT=l, rhs=r, start=(i == 0), stop=(i == n - 1))
            return ins
        P.add("pe", fn, reads=reads, writes=writes)

    evac_flip = [0]

    def evac(out, in_, reads, writes, eng=None):
        if eng is None:
            evac_flip[0] ^= 1
            eng = "dve" if evac_flip[0] else "act"
        if eng == "act":
            P.add("act", lambda e: e.copy(out=out, in_=in_), reads=reads, writes=writes)
        else:
            P.add(eng, lambda e: e.tensor_copy(out=out, in_=in_), reads=reads, writes=writes)

    def norm_to_uT(gain_d, blocks, hm, hooked=False, junk=None, junkres="junk"):
        ut = [AR(OFF_UT + 2048 * i, [128, D]) for i in range(2)]
        P.add("sp", lambda e: e.dma_start(out=gb[:], in_=gain_d.broadcast_to([128, D])), writes=["gb"], dma="gb")
        P.add("pool", lambda e: e.memset(ss[:], 0.0), writes=ALLSS)
        jmap = {}

        def geom(b):
            c0, nr = blk_cols(b)
            src = hm[0:nr, :] if b == "m" else h[:, b, :]
            col = 16 if b == "m" else b
            return c0, nr, src, col

        def a1(b, j=None):
            c0, nr, src, col = geom(b)
            if junk is not None:
                out, ores = junk[0:nr, :], junkres
            else:
                out, ores = ut[j][0:nr, :], ("ut", j)
            P.add("act", lambda e: e.activation(out=out, in_=src, func=AF.Square, accum_out=ss[0:nr, col:col + 1]),
                  reads=[("h", b), ("ss", b)], writes=[("ss", b), ores], full=True)
            P.add("act", lambda e: e.activation(
                out=rs[0:nr, col:col + 1], in_=ss[0:nr, col:col + 1], func=AF.Sqrt, bias=epsc[0:nr, 0:1], scale=1.0 / D),
                reads=[("ss", b), "epsc"], writes=[("rs", b)])

        def a2(b, j):
            c0, nr, src, col = geom(b)
            jmap[b] = j
            P.add("dve", lambda e: e.reciprocal(out=rs[0:nr, col:col + 1], in_=rs[0:nr, col:col + 1]),
                  reads=[("rs", b)], writes=[("rs", b)])
            P.add("dve", lambda e: e.scalar_tensor_tensor(
                out=ut[j][0:nr, :], in0=src, scalar=rs[0:nr, col:col + 1], in1=gb[0:nr, :],
                op0=ALU.mult, op1=ALU.mult), reads=[("h", b), ("rs", b), "gb"], writes=[("ut", j)])

        def stage_a(b):
            j = nxt("ut", 2)
            a1(b, j)
            a2(b, j)

        def stage_b(b):
            c0, nr = blk_cols(b)
            j = jmap[b]
            for half in range(2):
                def fn(e, half=half):
                    ins = None
                    for c in range(4):
                        cc = 4 * half + c
                        ins = e.transpose(pt[half][:, 128 * c:128 * c + nr], ut[j][0:nr, 128 * cc:128 * cc + 128],
                                          cb[0:nr, CB_ID:CB_ID + nr])
                    return ins
                P.add("pe", fn, reads=[("ut", j), "cb"], writes=[("ps", 6 + half)])
                src_v = pt[half][:, :].rearrange("p (c n) -> p c n", c=4)[:, :, 0:nr]
                evac(uT[:, 4 * half:4 * half + 4, c0:c0 + nr], src_v, [("ps", 6 + half)], [("uT", b, half)],
                     eng=("act" if half == 0 else "dve"))
        if hooked:
            assert junk is not None
            seq = []

            def hook(b):
                seq.append(b)
                n = len(seq)
                if n >= 3:
                    stage_b(seq[n - 3])
                if n >= 2:
                    a2(seq[n - 2], nxt("ut", 2))
                a1(b)

            def flush():
                n = len(seq)
                if n >= 2:
                    stage_b(seq[n - 2])
                a2(seq[n - 1], nxt("ut", 2))
                stage_b(seq[n - 1])
            return hook, flush
        stage_a(blocks[0])
        for i, b in enumerate(blocks):
            if i + 1 < len(blocks):
                stage_a(blocks[i + 1])
            stage_b(b)

    def uT_res(blks):
        return [("uT", b, hf) for b in blks for hf in range(2)]

    def ffn(w_in_d, w_out_d, gain_d, with_meta, hm, tagp, do_norm=True, tail=None):
        blocks = (["m"] if with_meta else []) + list(range(NBLK))
        if do_norm:
            norm_to_uT(gain_d, blocks, hm)
        tiles = (["m"] if with_meta else []) + [0, 1, 2, 3]
        act = AR(OFF_ACT, [128, 6, TCOLS])
        wis = [AR(OFF_WI + 8192 * i, [128, 2, 8, 256]) for i in range(3)]
        wos = [AR(OFF_WO + 12288 * i, [128, 6, D]) for i in range(2)]
        tmp = [AR(OFF_TMP + 2048 * i, [128, 512], F32) for i in range(2)]
        for (m0, nch) in PASSES:
            wslot = nxt("wo", 2)
            wo_t = wos[wslot]
            ml = 0
            while ml < nch:
                ns = min(2, nch - ml)
                s = nxt("wi", 3)
                wi_t = wis[s]
                c0 = (m0 + ml) * 128
                cast_dma(wi_t[:, 0, :, 0:ns * 128], w_in_d[:, c0:c0 + ns * 128].rearrange("(k p) n -> p k n", p=128),
                         ("wi", s, 0), f"wi{s}g")
                cast_dma(wi_t[:, 1, :, 0:ns * 128],
                         w_in_d[:, DFF + c0:DFF + c0 + ns * 128].rearrange("(k p) n -> p k n", p=128),
                         ("wi", s, 1), f"wi{s}u")
                if ml == 0:
                    cast_dma(wo_t[:, 0:nch, :], w_out_d[m0 * 128:(m0 + nch) * 128, :].rearrange("(k p) n -> p k n", p=128),
                             ("wo", wslot), f"wo{wslot}")
                for q in range(ns):
                    mloc = ml + q
                    for t in tiles:
                        tc0, tn, tb = tile_cols(t)
                        a = nxt("psA", 2)
                        pA, pB = ps[2 * a], ps[2 * a + 1]
                        for which, pp in ((0, pA), (1, pB)):
                            mm_group(pp[:, 0:tn],
                                     [(wi_t[:, which, k, q * 128:(q + 1) * 128], uT[:, k, tc0:tc0 + tn]) for k in range(8)],
                                     reads=[("wi", s, which)] + uT_res(tb), writes=[("ps", 2 * a + which)])
                        j = nxt("tmp", 2)
                        P.add("act", lambda e, j=j, pA=pA, tn=tn: e.activation(out=tmp[j][:, 0:tn], in_=pA[:, 0:tn], func=AF.Silu),
                              reads=[("ps", 2 * a)], writes=[("tmp", j)])
                        P.add("dve", lambda e, j=j, pB=pB, tn=tn, mloc=mloc, tc0=tc0: e.tensor_tensor(
                            out=act[:, mloc, tc0:tc0 + tn], in0=tmp[j][:, 0:tn], in1=pB[:, 0:tn], op=ALU.mult),
                            reads=[("tmp", j), ("ps", 2 * a + 1)], writes=[("act", mloc, t)])
                ml += ns
            hook = flush = None
            if tail is not None and (m0, nch) == PASSES[-1]:
                hook, flush = tail()
            for b in blocks:
                bc0, nr = blk_cols(b)
                t = "m" if b == "m" else b // 4
                for half in range(2):
                    o = 4 + nxt("psO", 2)
                    mm_group(ps[o][0:nr, :],
                             [(act[:, k, bc0:bc0 + nr], wo_t[:, k, 512 * half:512 * half + 512]) for k in range(nch)],
                             reads=[("act", k, t) for k in range(nch)] + [("wo", wslot)], writes=[("ps", o)])
                    dst = hm[0:nr, 512 * half:512 * half + 512] if b == "m" else h[:, b, 512 * half:512 * half + 512]
                    P.add("dve", lambda e, dst=dst, o=o, nr=nr: e.scalar_tensor_tensor(
                        out=dst, in0=ps[o][0:nr, :], scalar=0.5, in1=dst, op0=ALU.mult, op1=ALU.add),
                        reads=[("ps", o), ("h", b)], writes=[("h", b)])
                if hook is not None:
                    hook(b)
            if flush is not None:
                flush()

    OFF_HM = 0
    OFF_UT = 4096
    OFF_ACT = 8192
    OFF_WI = OFF_ACT + 24768
    OFF_WO = OFF_WI + 24576
    OFF_TMP = OFF_WO + 24576
    OFF_OST = OFF_TMP + 4096
    assert OFF_OST + 8192 <= ARENA_BYTES
    OFF_OA = 8192
    OFF_OB = OFF_OA + 16384
    OFF_LF = OFF_OB + 16384
    OFF_BIASF = OFF_LF + 3264
    OFF_PT = OFF_BIASF + 2176
    OFF_DEN = OFF_PT + 4096
    OFF_X = OFF_DEN + 2048
    assert OFF_X + 45632 <= ARENA_BYTES, OFF_X
    OFF_PW = OFF_OB + 16384
    OFF_MIX = OFF_PW + 24576
    assert OFF_MIX + 32768 <= ARENA_BYTES
    OFF_PTMP = 0

    def mixer(hm, mstop=9, post_tail=None):
        ablocks = ["m"] + list(range(NBLK))
        NPT = 4
        PT = [AR(OFF_PT + 1024 * i, [128, 512]) for i in range(NPT)]
        den = [AR(OFF_DEN, [128, 512], F32)]
        OaT = AR(OFF_OA, [128, 4, SEQ])
        ObT = AR(OFF_OB, [128, 4, SEQ])
        lfv = [AR(OFF_LF + 544 * i, [128, 136], F32) for i in range(6)]
        xb, lt, tots, offv, cum = lfv[0], lfv[1], lfv[2], lfv[3], lfv[4]
        biasF = AR(OFF_BIASF, [128, 17, 4, 8], F32)
        wf = lfv[5][:, :].bitcast(BF16)[:, 0:64].rearrange("p (k n) -> p k n", k=8)
        cast_dma(wf, wi_d[:, 2304:2312].rearrange("(k p) n -> p k n", p=128), "wf", "wf")

        def f_fn(e):
            ins = None
            for bi, b in enumerate(ablocks):
                c0, nr = blk_cols(b)
                for k in range(8):
                    ins = e.matmul(ps[0][0:nr, 8 * bi:8 * bi + 8], lhsT=uT[:, k, c0:c0 + nr], rhs=wf[:, k, :],
                                   start=(k == 0), stop=(k == 7))
            return ins
        P.add("pe", f_fn, reads=["wf"] + uT_res(ablocks), writes=[("ps", 0)])
        P.add("pool", lambda e: e.memset(lt[:], 0.0), writes=["lt"])
        for (r0, r1, c0, c1) in ((0, 16, 0, 8), (0, 128, 8, 136)):
            bb = bfg[r0:r1, :] if c1 == 8 else bfg[r0:r1, :].unsqueeze(1).to_broadcast([r1 - r0, 16, 8])
            i0 = ps[0][r0:r1, c0:c1] if c1 == 8 else ps[0][r0:r1, c0:c1].rearrange("p (b h) -> p b h", h=8)
            o0 = xb[r0:r1, c0:c1] if c1 == 8 else xb[r0:r1, c0:c1].rearrange("p (b h) -> p b h", h=8)
            P.add("dve", lambda e, bb=bb, i0=i0, o0=o0: e.tensor_tensor(out=o0, in0=i0, in1=bb, op=ALU.add),
                  reads=[("ps", 0), "bfg"], writes=["xb"])
            P.add("act", lambda e, r0=r0, r1=r1, c0=c0, c1=c1: e.activation(
                out=xb[r0:r1, c0:c1], in_=xb[r0:r1, c0:c1], func=AF.Exp, scale=-1.0), reads=["xb"], writes=["xb"])
            P.add("act", lambda e, r0=r0, r1=r1, c0=c0, c1=c1: e.activation(
                out=lt[r0:r1, c0:c1], in_=xb[r0:r1, c0:c1], func=AF.Ln, bias=onec[r0:r1, 0:1]), reads=["xb", "lt", "onec"], writes=["lt"])
        P.add("pe", lambda e: e.matmul(ps[1][:, 0:136], lhsT=cf[:, CF_ONES:CF_ONES + 128], rhs=lt[:, :], start=True, stop=True),
              reads=["lt", "cf"], writes=[("ps", 1)])
        P.add("pe", lambda e: e.matmul(ps[2][:, 0:136], lhsT=cf[:, CF_TRI:CF_TRI + 128], rhs=lt[:, :], start=True, stop=True),
              reads=["lt", "cf"], writes=[("ps", 2)])
        P.add("dve", lambda e: e.tensor_copy(out=tots[:, :], in_=ps[1][:, 0:136]), reads=[("ps", 1)], writes=["tots"])
        P.add("pool", lambda e: e.memset(offv[:, :], 0.0), writes=["offv"])
        for b in range(1, 17):
            P.add("dve", lambda e, b=b: e.tensor_tensor(out=offv[:, 8 * b:8 * b + 8], in0=offv[:, 8 * b - 8:8 * b],
                                                        in1=tots[:, 8 * b - 8:8 * b], op=ALU.add),
                  reads=["offv", "tots"], writes=["offv"])
        P.add("dve", lambda e: e.tensor_tensor(out=cum[:, :], in0=ps[2][:, 0:136], in1=offv[:, :], op=ALU.add),
              reads=[("ps", 2), "offv"], writes=["cum"])
        cum3 = cum[:, :].rearrange("p (b h) -> p b h", h=8)
        for j in range(4):
            ref = 4 * j + 3
            for hd in range(8):
                P.add("dve", lambda e, j=j, hd=hd, ref=ref: e.tensor_scalar(
                    out=biasF[:, :, j, hd], in0=cum3[:, :, hd], scalar1=offv[:, 8 * ref + hd:8 * ref + hd + 1],
                    scalar2=None, op0=ALU.subtract), reads=["cum", "offv"], writes=["biasF"])
        if mstop <= 1:
            return

        wqa = AR(OFF_X, [128, 8, 4, 2, 64])
        wka = AR(OFF_X + 8192, [128, 8, 128])
        wva = AR(OFF_X + 10240, [128, 8, 128])
        QaT = AR(OFF_X + 12288, [128, 4, SEQ])
        Ka2 = [AR(OFF_X + 28672 + 4128 * i, [128, TCOLS]) for i in range(2)]
        Va = [AR(OFF_X + 36928 + 4352 * i, [128, 17, 128]) for i in range(2)]
        P.add("pool", lambda e: e.memset(Ka2[0][64:128, :], 0.0), writes=[("KaT", b) for b in ablocks])
        P.add("pool", lambda e: e.memset(Ka2[1][0:64, :], 0.0), writes=[("KaT", b) for b in ablocks])
        for hh in range(2):
            for g in range(4):
                cq = 256 * hh + 64 * g
                cast_dma(wqa[:, :, g, hh, :], wi_d[:, cq:cq + 64].rearrange("(k p) d -> p k d", p=128),
                         ("wqa", hh, g), f"wqa{hh}{g}")
        cast_dma(wka, wi_d[:, 512:640].rearrange("(k p) n -> p k n", p=128), "wka", "wka")
        cast_dma(wva, wi_d[:, 640:768].rearrange("(k p) n -> p k n", p=128), "wva", "wva")
        for i in range(2):
            P.add("pool", lambda e, i=i: e.memset(Va[i][:, :, :], 0.0), writes=[("Va", i)])
        for g in range(4 if DBG_SWA >= 0.2 else 0):
            for t in range(4):
                tc0, tn, tb = tile_cols(t)
                a = nxt("psQ", 2)
                mm_group(ps[a][:, :], [(wqa[:, k, g].rearrange("p a d -> p (a d)"), uT[:, k, tc0:tc0 + tn]) for k in range(8)],
                         reads=[("wqa", 0, g), ("wqa", 1, g)] + uT_res(tb), writes=[("ps", a)])
                evac(QaT[:, g, 512 * t:512 * t + 512], ps[a][:, :], [("ps", a)], [("QaT", 4 * t + j) for j in range(4)])
        for t in (["m", 0, 1, 2, 3] if DBG_SWA >= 0.3 else []):
            tc0, tn, tb = tile_cols(t)
            a = nxt("psQ", 2)
            mm_group(ps[a][:, 0:tn], [(wka[:, k, :], uT[:, k, tc0:tc0 + tn]) for k in range(8)],
                     reads=["wka"] + uT_res(tb), writes=[("ps", a)])
            evac(Ka2[0][0:64, tc0:tc0 + tn], ps[a][0:64, 0:tn], [("ps", a)], [("KaT", b) for b in tb], eng="dve")
            evac(Ka2[1][64:128, tc0:tc0 + tn], ps[a][64:128, 0:tn], [("ps", a)], [("KaT", b) for b in tb], eng="dve")
        for bi, b in enumerate(ablocks if DBG_SWA >= 0.4 else []):
            if DBG_SWA == 0.45 and b == "m":
                continue
            c0, nr = blk_cols(b)
            a = nxt("psQ", 2)
            mm_group(ps[a][0:nr, 0:128], [(uT[:, k, c0:c0 + nr], wva[:, k, :]) for k in range(8)],
                     reads=["wva"] + uT_res([b]), writes=[("ps", a)])
            evac(Va[0][0:nr, bi, 0:64], ps[a][0:nr, 0:64], [("ps", a)], [("Va", 0)], eng="dve")
            evac(Va[1][0:nr, bi, 64:128], ps[a][0:nr, 64:128], [("ps", a)], [("Va", 1)], eng="dve")
        SB = [0, 1, 2, 5]
        LOOK = 3
        ODS = [(3, 4), (6, 7)]
        steps = []
        for i in range(NBLK):
            roles = [("meta", 0, 16, 0)] + ([("prev", i, 128, NMETA + 128 * (i - 1))] if i >= 1 else []) + \
                    [("cur", i + 1, 128, NMETA + 128 * i)]
            n_i = 2 * len(roles)
            cnt = 0
            for kvh in range(2):
                for (role, kb, nk, kc0) in roles:
                    steps.append((i, kvh, role, kb, nk, kc0, cnt == 0, cnt == n_i - 1))
                    cnt += 1

        def swa_qk(n):
            i, kvh, role, kb, nk, kc0, st, last = steps[n]
            sidx = SB[n % 4]
            pS = ps[sidx]
            base = 64 * kvh
            kres = ("KaT", "m") if role == "meta" else ("KaT", kb - 1)
            def qk(e, w):
                ins = e.matmul(pS[0:nk, :].rearrange("p (g q) -> p g q", g=4), lhsT=Ka2[kvh][:, kc0:kc0 + nk],
                               rhs=QaT[:, :, 128 * i:128 * i + 128], start=True, stop=True)
                if w is not None:
                    ins._wait_ge(*w)
                return ins
            P.add("pe", qk, reads=[kres, ("QaT", i)], writes=[("ps", sidx)], attach=True)

        def swa_rest(n):
            i, kvh, role, kb, nk, kc0, st, last = steps[n]
            sidx = SB[n % 4]
            pS = ps[sidx]
            pj = n % NPT
            iO, iD = ODS[i % 2]
            pO, pD = ps[iO], ps[iD]
            for g in range(4):
                hd = 4 * kvh + g
                if role == "meta":
                    bcol = cf[0:nk, CF_BMETA + 8 * i + hd:CF_BMETA + 8 * i + hd + 1]
                elif role == "prev":
                    bcol = cf[0:nk, CF_BPREV + hd:CF_BPREV + hd + 1]
                else:
                    bcol = cf[0:nk, CF_BCUR + hd:CF_BCUR + hd + 1]
                P.add("act", lambda e, g=g, bcol=bcol: e.activation(
                    out=PT[pj][0:nk, 128 * g:128 * g + 128], in_=pS[0:nk, 128 * g:128 * g + 128],
                    func=AF.Exp, bias=bcol, scale=0.125), reads=[("ps", sidx), "cf"], writes=[("PT", pj)])
            if role != "meta":
                mo = CB_MPREV if role == "prev" else CB_MCUR
                P.add("dve", lambda e: e.tensor_tensor(
                    out=PT[pj][:, :], in0=PT[pj][:, :], in1=cb[:, mo:mo + 512], op=ALU.mult),
                    reads=[("PT", pj), "cb"], writes=[("PT", pj)])
            oo = CB_OLO if kvh == 0 else CB_OHI

            def pv(e, w):
                i0 = e.matmul(pO[:, :], lhsT=Va[kvh][0:nk, kb, :], rhs=PT[pj][0:nk, :], start=st, stop=last)
                if w is not None:
                    i0._wait_ge(*w)
                return e.matmul(pD[:, :], lhsT=cb[0:nk, oo:oo + 128], rhs=PT[pj][0:nk, :], start=st, stop=last)
            P.add("pe", pv, reads=[("PT", pj), ("Va", kvh), "cb"], writes=[("ps", iO), ("ps", iD)], attach=True)
            if defer:
                defer.pop(0)()
            if last:
                dj = 0
                defer.append(lambda: P.add("dve", lambda e: e.tensor_tensor(out=den[dj][:, :], in0=pD[:, :], in1=sinktab[:, :], op=ALU.add),
                                           reads=[("ps", iD), "sinktab"], writes=[("den", dj)]))
                defer.append(lambda: P.add("dve", lambda e: e.reciprocal(out=den[dj][:, :], in_=den[dj][:, :]),
                                           reads=[("den", dj)], writes=[("den", dj)]))
                defer.append(lambda: P.add("dve", lambda e: e.tensor_tensor(
                    out=OaT[:, :, 128 * i:128 * i + 128], in0=pO[:, :].rearrange("p (g q) -> p g q", g=4),
                    in1=den[dj][:, :].rearrange("p (g q) -> p g q", g=4), op=ALU.mult),
                    reads=[("ps", iO), ("den", dj)], writes=[("OaT", i // 4)]))
        defer = []
        for n in range(len(steps) + LOOK):
            if n < len(steps):
                swa_qk(n)
            if n - LOOK >= 0:
                swa_rest(n - LOOK)
        while defer:
            defer.pop(0)()
        P.barrier()
        if mstop <= 2:
            return

        wqb = AR(OFF_X, [128, 8, 512])
        wkb = AR(OFF_X + 8192, [128, 8, 512])
        wvb = AR(OFF_X + 16384, [128, 8, 512])
        QbT = AR(OFF_X + 24576, [128, SEQ])
        Kb2 = [AR(OFF_X + 28672 + 4128 * i, [128, TCOLS]) for i in range(2)]
        Vb = [AR(OFF_X + 36928 + 4352 * i, [128, 17, 128]) for i in range(2)]
        P.add("pool", lambda e: e.memset(Kb2[0][64:128, :], 0.0), writes=[("KbT", b) for b in ablocks])
        P.add("pool", lambda e: e.memset(Kb2[1][0:64, :], 0.0), writes=[("KbT", b) for b in ablocks])
        for c in range(4):
            for (wt, col0, nm) in ((wqb, 768, "wqb"), (wkb, 1280, "wkb"), (wvb, 1792, "wvb")):
                cast_dma(wt[:, :, 128 * c:128 * c + 128],
                         wi_d[:, col0 + 128 * c:col0 + 128 * c + 128].rearrange("(k p) n -> p k n", p=128), (nm, c), nm)
        for i in range(2):
            P.add("pool", lambda e, i=i: e.memset(Vb[i][:, :, :], 0.0), writes=[("Vb", i, bi) for bi in range(17)])
        for c in range(4):
            for t in range(4):
                tc0, tn, tb = tile_cols(t)
                a = nxt("psQ", 2)
                mm_group(ps[a][:, :], [(wqb[:, k, 128 * c:128 * c + 128], uT[:, k, tc0:tc0 + tn]) for k in range(8)],
                         reads=[("wqb", c)] + uT_res(tb), writes=[("ps", a)])
                evac(QbT[:, 512 * t:512 * t + 512], ps[a][:, :], [("ps", a)], [("QbT", t)], eng="dve")
            for t in ["m", 0, 1, 2, 3]:
                tc0, tn, tb = tile_cols(t)
                a = nxt("psQ", 2)
                mm_group(ps[a][:, 0:tn], [(wkb[:, k, 128 * c:128 * c + 128], uT[:, k, tc0:tc0 + tn]) for k in range(8)],
                         reads=[("wkb", c)] + uT_res(tb), writes=[("ps", a)])
                evac(Kb2[0][0:64, tc0:tc0 + tn], ps[a][0:64, 0:tn], [("ps", a)], [("KbT", b) for b in tb], eng="dve")
                evac(Kb2[1][64:128, tc0:tc0 + tn], ps[a][64:128, 0:tn], [("ps", a)], [("KbT", b) for b in tb], eng="dve")
            for bi, b in enumerate(ablocks):
                c0, nr = blk_cols(b)
                a = nxt("psQ", 2)
                mm_group(ps[a][0:nr, 0:128], [(uT[:, k, c0:c0 + nr], wvb[:, k, 128 * c:128 * c + 128]) for k in range(8)],
                         reads=[("wvb", c)] + uT_res([b]), writes=[("ps", a)])
                evac(Vb[0][0:nr, bi, 0:64], ps[a][0:nr, 0:64], [("ps", a)], [("Vb", 0, bi)], eng="dve")
                evac(Vb[1][0:nr, bi, 64:128], ps[a][0:nr, 64:128], [("ps", a)], [("Vb", 1, bi)], eng="dve")
            fsteps = []
            for j in range(4):
                kbs = [0] + [1 + r for r in range(4 * j + 4)]
                n_j = 2 * len(kbs)
                cnt = 0
                for kb in kbs:
                    for hh in range(2):
                        fsteps.append((j, kb, hh, cnt == 0, cnt == n_j - 1))
                        cnt += 1

            def fparams(n):
                j, kb, hh, st, last = fsteps[n]
                if kb == 0:
                    nk, kc0, c0, kres = 16, 0, 0, ("KbT", "m")
                else:
                    r = kb - 1
                    nk, kc0, kres = 128, NMETA + 128 * r, ("KbT", r)
                    c0 = 128 * (r - 4 * j) if r >= 4 * j else 0
                diag = kb >= 1 and (kb - 1) >= 4 * j
                return j, kb, hh, st, last, nk, kc0, c0, kres, diag

            def fox_qk(n, c=c):
                j, kb, hh, st, last, nk, kc0, c0, kres, diag = fparams(n)
                sidx = SB[n % 4]
                pS = ps[sidx]
                base = 64 * hh
                def qk(e, w):
                    ins = e.matmul(pS[0:nk, c0:512], lhsT=Kb2[hh][:, kc0:kc0 + nk],
                                   rhs=QbT[:, 512 * j + c0:512 * j + 512], start=True, stop=True)
                    if w is not None:
                        ins._wait_ge(*w)
                    return ins
                P.add("pe", qk, reads=[kres, ("QbT", j)], writes=[("ps", sidx)], attach=True)

            def fox_rest(n, c=c):
                j, kb, hh, st, last, nk, kc0, c0, kres, diag = fparams(n)
                sidx = SB[n % 4]
                pS = ps[sidx]
                pj = n % NPT
                iO, iD = ODS[(4 * c + j) % 2]
                pO, pD = ps[iO], ps[iD]
                hd = 2 * c + hh
                P.add("act", lambda e: e.activation(
                    out=PT[pj][0:nk, c0:512], in_=pS[0:nk, c0:512], func=AF.Exp,
                    bias=biasF[0:nk, kb, j, hd:hd + 1], scale=0.125),
                    reads=[("ps", sidx), "biasF"], writes=[("PT", pj)])
                if diag:
                    P.add("dve", lambda e: e.tensor_tensor(
                        out=PT[pj][:, c0:c0 + 128], in0=PT[pj][:, c0:c0 + 128],
                        in1=cb[:, CB_MCUR:CB_MCUR + 128], op=ALU.mult),
                        reads=[("PT", pj), "cb"], writes=[("PT", pj)])
                oo = CB_OLO if hh == 0 else CB_OHI

                def pv(e, w):
                    i0 = e.matmul(pO[:, c0:512], lhsT=Vb[hh][0:nk, kb, :], rhs=PT[pj][0:nk, c0:512], start=st, stop=last)
                    if w is not None:
                        i0._wait_ge(*w)
                    return e.matmul(pD[:, c0:512], lhsT=cb[0:nk, oo:oo + 128], rhs=PT[pj][0:nk, c0:512], start=st, stop=last)
                P.add("pe", pv, reads=[("PT", pj), ("Vb", hh, kb), "cb"], writes=[("ps", iO), ("ps", iD)], attach=True)
                if defer:
                    defer.pop(0)()
                if last:
                    dj = 0
                    defer.append(lambda: P.add("dve", lambda e: e.reciprocal(out=den[dj][:, :], in_=pD[:, :]),
                                               reads=[("ps", iD)], writes=[("den", dj)]))
                    defer.append(lambda: P.add("dve", lambda e: e.tensor_tensor(
                        out=ObT[:, c, 512 * j:512 * j + 512], in0=pO[:, :], in1=den[dj][:, :], op=ALU.mult),
                        reads=[("ps", iO), ("den", dj)], writes=[("ObT", j)]))
            for n in range(len(fsteps) + LOOK):
                if n < len(fsteps):
                    fox_qk(n)
                if n - LOOK >= 0:
                    fox_rest(n - LOOK)
            while defer:
                defer.pop(0)()
        P.barrier()
        if mstop <= 3:
            return

        mixT = AR(OFF_MIX, [128, 8, SEQ])
        wo_t = AR(OFF_PW, [128, 8, D])
        ovl = [[("pw", 0, "a", 0), ("pw", 0, "a", 1), ("pw", 0, "b"), ("pw", 0, "ga")],
               [("pw", 0, "gb"), ("pw", 1, "a", 0), ("pw", 1, "a", 1), ("pw", 1, "b")]]

        def issue_wo(hf):
            P.add("pool", lambda e: e.dma_start(
                out=wo_t[:, 4 * hf:4 * hf + 4, :], in_=wo_d[512 * hf:512 * hf + 512, :].rearrange("(k p) n -> p k n", p=128)),
                writes=[("wout", hf)] + ovl[hf], dma=f"wout{hf}")
        ptmp = [AR(OFF_PTMP + 2048 * i, [128, 512], F32) for i in range(2)]
        for st in range(4):
            s = nxt("pw", 2)
            o = OFF_PW + 12288 * s
            wa_t = AR(o, [128, 4, 256])
            wb_t = AR(o + 2048, [128, 4, 256])
            wga_t = AR(o + 4096, [128, 8, 256])
            wgb_t = AR(o + 8192, [128, 8, 256])
            c0 = 256 * st
            for hh in range(2):
                cast_dma(wa_t[64 * hh:64 * hh + 64, :, :],
                         wa_d[256 * hh:256 * hh + 256, c0:c0 + 256].rearrange("(g d) n -> d g n", d=64),
                         ("pw", s, "a", hh), f"pw{s}a{hh}")
            cast_dma(wb_t, wb_d[:, c0:c0 + 256].rearrange("(c p) n -> p c n", p=128), ("pw", s, "b"), f"pw{s}b")
            cast_dma(wga_t, wi_d[:, 2312 + c0:2312 + c0 + 256].rearrange("(k p) n -> p k n", p=128), ("pw", s, "ga"), f"pw{s}ga")
            cast_dma(wgb_t, wi_d[:, 3336 + c0:3336 + c0 + 256].rearrange("(k p) n -> p k n", p=128), ("pw", s, "gb"), f"pw{s}gb")
            if st == 3:
                issue_wo(0)
            for q in range(2):
                m = 2 * st + q
                for t in range(4):
                    tc0, tn, tb = tile_cols(t)
                    ia, ib, iga, igb = (nxt("psP", 8) for _ in range(4))
                    mm_group(ps[ia][:, :], [(wa_t[:, k, 128 * q:128 * q + 128], OaT[:, k, 512 * t:512 * t + 512]) for k in range(4)],
                             reads=[("pw", s, "a", 0), ("pw", s, "a", 1), ("OaT", t)], writes=[("ps", ia)])
                    mm_group(ps[ib][:, :], [(wb_t[:, k, 128 * q:128 * q + 128], ObT[:, k, 512 * t:512 * t + 512]) for k in range(4)],
                             reads=[("pw", s, "b"), ("ObT", t)], writes=[("ps", ib)])
                    mm_group(ps[iga][:, :], [(wga_t[:, k, 128 * q:128 * q + 128], uT[:, k, tc0:tc0 + 512]) for k in range(8)],
                             reads=[("pw", s, "ga")] + uT_res(tb), writes=[("ps", iga)])
                    mm_group(ps[igb][:, :], [(wgb_t[:, k, 128 * q:128 * q + 128], uT[:, k, tc0:tc0 + 512]) for k in range(8)],
                             reads=[("pw", s, "gb")] + uT_res(tb), writes=[("ps", igb)])
                    ja, jb = 0, 1
                    P.add("act", lambda e, ja=ja, iga=iga: e.activation(out=ptmp[ja][:, :], in_=ps[iga][:, :], func=AF.Sigmoid),
                          reads=[("ps", iga)], writes=[("ptmp", ja)])
                    P.add("act", lambda e, jb=jb, igb=igb: e.activation(out=ptmp[jb][:, :], in_=ps[igb][:, :], func=AF.Sigmoid),
                          reads=[("ps", igb)], writes=[("ptmp", jb)])
                    P.add("dve", lambda e, ja=ja, ia=ia: e.tensor_tensor(out=ptmp[ja][:, :], in0=ptmp[ja][:, :], in1=ps[ia][:, :], op=ALU.mult),
                          reads=[("ptmp", ja), ("ps", ia)], writes=[("ptmp", ja)])
                    P.add("dve", lambda e, jb=jb, ib=ib: e.tensor_tensor(out=ptmp[jb][:, :], in0=ptmp[jb][:, :], in1=ps[ib][:, :], op=ALU.mult),
                          reads=[("ptmp", jb), ("ps", ib)], writes=[("ptmp", jb)])
                    P.add("pool", lambda e, ja=ja, jb=jb, m=m, t=t: e.tensor_tensor(
                        out=mixT[:, m, 512 * t:512 * t + 512], in0=ptmp[ja][:, :], in1=ptmp[jb][:, :], op=ALU.add),
                        reads=[("ptmp", ja), ("ptmp", jb)], writes=[("mixT", t)])
        issue_wo(1)
        hook = flush = None
        if post_tail is not None:
            hook, flush = post_tail()
        for b in range(NBLK):
            for half in range(2):
                o = nxt("psP", 8)
                mm_group(ps[o][:, :], [(mixT[:, k, 128 * b:128 * b + 128], wo_t[:, k, 512 * half:512 * half + 512]) for k in range(8)],
                         reads=[("mixT", b // 4), ("wout", 0), ("wout", 1)], writes=[("ps", o)])
                dst = h[:, b, 512 * half:512 * half + 512]
                P.add("dve", lambda e, dst=dst, o=o: e.tensor_tensor(out=dst, in0=ps[o][:, :], in1=dst, op=ALU.add),
                      reads=[("ps", o), ("h", b)], writes=[("h", b)])
            if hook is not None:
                hook(b)
        if flush is not None:
            flush()

    def final_out(s, raw, hooked=False, after_block=None):
        ost = [AR(OFF_OST + 4096 * i, [128, D], F32) for i in range(2)]
        if raw:
            for b in range(NBLK):
                P.add("sp", lambda e, b=b: e.dma_start(out=out_d[s, 128 * b:128 * b + 128, :], in_=h[:, b, :]),
                      reads=[("h", b)], dma=f"out{b % 2}")
            return
        P.add("sp", lambda e: e.dma_start(out=gb[:], in_=n4_d.broadcast_to([128, D])), writes=["gb"], dma="gb")
        P.add("pool", lambda e: e.memset(ss[:], 0.0), writes=ALLSS)

        def hook(b):
            j = nxt("ost", 2)
            P.add("act", lambda e, b=b, j=j: e.activation(out=ost[j][:, :], in_=h[:, b, :], func=AF.Square,
                                                          accum_out=ss[:, b:b + 1]),
                  reads=[("h", b), ("ss", b), ("ost", j)], writes=[("ss", b), ("ost", j)], full=True)
            P.add("act", lambda e, b=b: e.activation(out=rs[:, b:b + 1], in_=ss[:, b:b + 1], func=AF.Sqrt,
                                                     bias=epsc[:, 0:1], scale=1.0 / D),
                  reads=[("ss", b), "epsc"], writes=[("rs", b)])
            P.add("dve", lambda e, b=b: e.reciprocal(out=rs[:, b:b + 1], in_=rs[:, b:b + 1]), reads=[("rs", b)], writes=[("rs", b)])
            P.add("dve", lambda e, b=b, j=j: e.scalar_tensor_tensor(
                out=ost[j][:, :], in0=h[:, b, :], scalar=rs[:, b:b + 1], in1=gb[:, :], op0=ALU.mult, op1=ALU.mult),
                reads=[("h", b), ("rs", b), "gb"], writes=[("ost", j)])
            P.add("sp", lambda e, b=b, j=j: e.dma_start(out=out_d[s, 128 * b:128 * b + 128, :], in_=ost[j][:, :]),
                  reads=[("ost", j)], dma=f"out{j}")
        if hooked:
            return hook, (lambda: None)
        for b in range(NBLK):
            hook(b)
            if after_block is not None:
                after_block(b)

    def load_x(s, b):
        P.add("sp", lambda e: e.dma_start(out=h[:, b, :], in_=x_d[s, 128 * b:128 * b + 128, :]),
              writes=[("h", b)], dma=f"x{b % 4}")

    for s in range(2):
        hm = AR(OFF_HM, [128, D], F32)
        if s == 0 or stage != 3:
            for b in range(NBLK):
                load_x(s, b)
        P.add("sp", lambda e: e.dma_start(out=hm[0:NMETA, :], in_=meta_d), writes=[("h", "m")], dma="xm")
        if stage == 3:
            ffn(w1i_d, w1o_d, n1_d, True, hm, "f1",
                tail=lambda: norm_to_uT(n2_d, ["m"] + list(range(NBLK)), hm, hooked=True,
                                        junk=AR(OFF_OST, [128, D]), junkres=("ost", 0)))
            P.barrier()
            mixer(hm, post_tail=lambda: norm_to_uT(n3_d, list(range(NBLK)), hm, hooked=True,
                                                    junk=AR(OFF_PTMP, [128, D]), junkres=("ptmp", 0)))
            P.barrier()
            ffn(w2i_d, w2o_d, n3_d, False, hm, "f2", do_norm=False)
            final_out(s, raw=False, after_block=((lambda b: load_x(1, b)) if s == 0 else None))
        else:
            ffn(w1i_d, w1o_d, n1_d, True, hm, "f1")
            if stage >= 2:
                norm_to_uT(n2_d, ["m"] + list(range(NBLK)), hm)
                P.barrier()
                mixer(hm, mstop=(stage - 10 if stage >= 10 else 9))
            P.barrier()
            final_out(s, raw=True)
    P.emit()
    es.close()
    return nc


_CACHE = {}


def kernel(x, meta_tokens, ffn1_norm, ffn1_w_in, ffn1_w_out, mix_norm, w_in, b_forget, attn_sinks,
           w_branch_a, w_branch_b, w_out, ffn2_norm, ffn2_w_in, ffn2_w_out, final_norm, _stage=3, _cores=8):
    f = lambda a: np.ascontiguousarray(np.asarray(a, dtype=np.float32))
    x = f(x)
    cf, cb = make_consts()
    shared = {
        "meta": f(meta_tokens), "n1": f(ffn1_norm).reshape(1, D), "n2": f(mix_norm).reshape(1, D),
        "n3": f(ffn2_norm).reshape(1, D), "n4": f(final_norm).reshape(1, D),
        "w1i": f(ffn1_w_in)[0], "w1o": f(ffn1_w_out)[0], "w2i": f(ffn2_w_in)[0], "w2o": f(ffn2_w_out)[0],
        "wi": f(w_in)[0], "bfg": f(b_forget).reshape(1, 8), "snk": f(attn_sinks).reshape(1, 8),
        "wa": f(w_branch_a)[0], "wb": f(w_branch_b)[0], "wo": f(w_out)[0], "cf": cf, "cb": cb,
    }
    if _stage not in _CACHE:
        _CACHE[_stage] = build(_stage)
    nc = _CACHE[_stage]
    in_maps = [dict(shared, x=x[2 * c:2 * c + 2]) for c in range(_cores)]
    res = run_bass_kernel_spmd(nc, in_maps, core_ids=list(range(_cores)))
    return np.concatenate([r["out"] for r in res.results], axis=0)
```

```python
import contextlib
import numpy as np
import ml_dtypes
import concourse.bass as bass
import concourse.mybir as mybir
from concourse.bass_utils import run_bass_kernel_spmd

F32 = mybir.dt.float32
BF16 = mybir.dt.bfloat16
AF = mybir.ActivationFunctionType
ALU = mybir.AluOpType

D = 1024
SEQ = 2048
NBLK = 16
NMETA = 16
TCOLS = SEQ + NMETA
DFF = 2816
NCH = 22
PASSES = [(0, 6), (6, 6), (12, 5), (17, 5)]
EPS = 1e-6
SLOPES = [2.0 ** (-(h + 1)) for h in range(8)]
ENGS = ("pe", "act", "dve", "pool", "sp")
ARENA_BYTES = 98304
DBG_SWA = 9

CF_TRI, CF_ONES, CF_IOTA, CF_BCUR, CF_BPREV, CF_BMETA, CF_N = 0, 128, 256, 384, 392, 400, 528
CB_ID, CB_MCUR, CB_MPREV, CB_OLO, CB_OHI, CB_N = 0, 128, 640, 1152, 1280, 1408


def make_consts():
    cf = np.zeros((128, CF_N), np.float32)
    p = np.arange(128)
    cf[:, CF_TRI:CF_TRI + 128] = (p[:, None] <= p[None, :]).astype(np.float32)
    cf[:, CF_ONES:CF_ONES + 128] = 1.0
    cf[:, CF_IOTA:CF_IOTA + 128] = p[None, :].astype(np.float32)
    sl = np.array(SLOPES, np.float32)
    cf[:, CF_BCUR:CF_BCUR + 8] = p[:, None] * sl[None, :]
    cf[:, CF_BPREV:CF_BPREV + 8] = (p[:, None] - 128.0) * sl[None, :]
    bm = np.zeros((128, 16, 8), np.float32)
    for i in range(16):
        bm[:, i, :] = (p[:, None] - 16.0 - 128.0 * i) * sl[None, :]
    cf[:, CF_BMETA:CF_BMETA + 128] = bm.reshape(128, 128)
    cb = np.zeros((128, CB_N), np.float32)
    cb[:, CB_ID:CB_ID + 128] = np.eye(128)
    mc = (p[:, None] <= p[None, :]).astype(np.float32)
    mp = (p[:, None] > p[None, :]).astype(np.float32)
    cb[:, CB_MCUR:CB_MCUR + 512] = np.tile(mc, (1, 4))
    cb[:, CB_MPREV:CB_MPREV + 512] = np.tile(mp, (1, 4))
    cb[:, CB_OLO:CB_OLO + 64] = 1.0
    cb[:, CB_OHI + 64:CB_OHI + 128] = 1.0
    return cf, cb.astype(ml_dtypes.bfloat16)


class Op:
    __slots__ = ("idx", "eng", "fn", "reads", "writes", "dma", "deps", "signal", "semval", "semname", "attach")

    def __init__(self, idx, eng, fn, reads, writes, dma):
        self.idx, self.eng, self.fn, self.reads, self.writes, self.dma = idx, eng, fn, reads, writes, dma
        self.deps = set()
        self.signal = False
        self.semval = 0
        self.semname = None
        self.attach = False


class Prog:
    def __init__(self, nc):
        self.nc = nc
        self.ops = []
        self.last_writer = {}
        self.readers = {}
        self.last_on = {}

    def add(self, eng, fn, reads=(), writes=(), dma=None, full=False, attach=False):
        if dma is not None:
            writes = tuple(writes) + (("__slot", dma),)
        op = Op(len(self.ops), eng, fn, tuple(reads), tuple(writes), dma)
        op.attach = attach
        deps = set()
        for r in op.reads:
            w = self.last_writer.get(r)
            if w is not None:
                deps.add(w)
        for r in op.writes:
            w = self.last_writer.get(r)
            if w is not None:
                deps.add(w)
            deps.update(self.readers.get(r, ()))
        for d in deps:
            dop = self.ops[d]
            if dop.dma is None and dop.eng == eng and dma is None:
                if eng == "pe":
                    continue
                if not full and not any(self.last_writer.get(r) == d for r in op.reads):
                    continue
            op.deps.add(d)
        for r in op.reads:
            self.readers.setdefault(r, []).append(op.idx)
        for r in op.writes:
            self.last_writer[r] = op.idx
            self.readers[r] = []
        self.ops.append(op)
        if dma is None:
            self.last_on[eng] = op.idx
        return op

    def barrier(self):
        lasts = dict(self.last_on)
        dmas = [idx for (k, idx) in self.last_writer.items() if isinstance(k, tuple) and k[0] == "__slot"]
        for e in ENGS:
            op = Op(len(self.ops), e, None, (), (), None)
            for e2, idx in lasts.items():
                if e2 != e:
                    op.deps.add(idx)
            op.deps.update(dmas)
            self.ops.append(op)

    def emit(self):
        nc, ops = self.nc, self.ops
        for op in ops:
            best = {}
            keep = set()
            for d in op.deps:
                dop = ops[d]
                if dop.dma is not None:
                    keep.add(d)
                elif d > best.get(dop.eng, -1):
                    best[dop.eng] = d
            keep.update(best.values())
            op.deps = keep
            for d in keep:
                ops[d].signal = True
        cnt = {}
        for op in ops:
            if op.dma is not None:
                op.signal = True
                op.semname = "d_" + op.dma
                cnt[op.semname] = cnt.get(op.semname, 0) + 16
                op.semval = cnt[op.semname]
            elif op.signal:
                op.semname = "e_" + op.eng
                cnt[op.semname] = cnt.get(op.semname, 0) + 1
                op.semval = cnt[op.semname]
        semnames = sorted(cnt)
        with contextlib.ExitStack() as es:
            sems = {n: es.enter_context(nc.semaphore(n)) for n in semnames}
            block = es.enter_context(nc.Block())
            by_eng = {e: [op for op in ops if op.eng == e] for e in ENGS}
            final = [(n, cnt[n]) for n in semnames if n.startswith("d_")]

            def run(engname, eng):
                known = {}
                for op in by_eng[engname]:
                    waits = {}
                    for d in op.deps:
                        dop = ops[d]
                        if dop.semval > waits.get(dop.semname, 0):
                            waits[dop.semname] = dop.semval
                    need = [(n, v) for n, v in sorted(waits.items()) if known.get(n, 0) < v]
                    emb = None
                    if op.attach and need:
                        emb = need.pop()
                    for n, v in need:
                        eng.wait_ge(sems[n], v)
                        known[n] = v
                    if op.fn is None:
                        continue
                    if op.attach:
                        if emb is not None:
                            known[emb[0]] = emb[1]
                        ins = op.fn(eng, None if emb is None else (sems[emb[0]], emb[1]))
                    else:
                        ins = op.fn(eng)
                    if op.signal:
                        ins.then_inc(sems[op.semname], 16 if op.dma is not None else 1)
                if engname == "sp":
                    for n, v in final:
                        eng.wait_ge(sems[n], v)

            @block.tensor
            def _(e):
                run("pe", e)

            @block.scalar
            def _(e):
                run("act", e)

            @block.vector
            def _(e):
                run("dve", e)

            @block.gpsimd
            def _(e):
                run("pool", e)

            @block.sync
            def _(e):
                run("sp", e)


def build(stage=3):
    nc = bass.Bass("TRN2", target_bir_lowering=False)
    dt_in = lambda n, s, d=F32: nc.dram_tensor(n, s, d, kind="ExternalInput").ap()
    x_d = dt_in("x", [2, SEQ, D])
    meta_d = dt_in("meta", [NMETA, D])
    n1_d, n2_d, n3_d, n4_d = (dt_in(n, [1, D]) for n in ("n1", "n2", "n3", "n4"))
    w1i_d, w1o_d = dt_in("w1i", [D, 2 * DFF]), dt_in("w1o", [DFF, D])
    w2i_d, w2o_d = dt_in("w2i", [D, 2 * DFF]), dt_in("w2o", [DFF, D])
    wi_d = dt_in("wi", [D, 4360])
    bf_d, sk_d = dt_in("bfg", [1, 8]), dt_in("snk", [1, 8])
    wa_d, wb_d, wo_d = dt_in("wa", [512, D]), dt_in("wb", [512, D]), dt_in("wo", [D, D])
    cf_d, cb_d = dt_in("cf", [128, CF_N]), dt_in("cb", [128, CB_N], BF16)
    out_d = nc.dram_tensor("out", [2, SEQ, D], F32, kind="ExternalOutput").ap()

    es = contextlib.ExitStack()
    sb = lambda n, s, d: es.enter_context(nc.sbuf_tensor(n, s, d))
    h = sb("h", [128, NBLK, D], F32)
    uT = sb("uT", [128, 8, TCOLS], BF16)
    cf = sb("cf_s", [128, CF_N], F32)
    cb = sb("cb_s", [128, CB_N], BF16)
    gb = sb("gb", [128, D], F32)
    ss = sb("ss", [128, 17], F32)
    rs = sb("rs", [128, 17], F32)
    epsc = sb("epsc", [128, 1], F32)
    bfg = sb("bfg_s", [128, 8], F32)
    snk = sb("snk_s", [128, 8], F32)
    sinktab = sb("sinktab", [128, 512], F32)
    arena = sb("arena", [128, ARENA_BYTES // 2], BF16)
    ps = [es.enter_context(nc.psum_tensor(f"ps{i}", [128, 512], F32)) for i in range(8)]
    pt = [ps[6 + i][:, 0:256].bitcast(BF16) for i in range(2)]

    def AR(off, shape, dtype=BF16):
        n = int(np.prod(shape[1:]))
        nb = n * (4 if dtype == F32 else 2)
        assert off % 4 == 0 and off + nb <= ARENA_BYTES, (off, nb)
        v = arena[:, off // 2:(off + nb) // 2]
        if dtype == F32:
            v = v.bitcast(F32)
        if len(shape) == 2:
            return v
        names = " ".join(f"a{i}" for i in range(len(shape) - 1))
        kw = {f"a{i}": shape[i + 1] for i in range(len(shape) - 2)}
        return v.rearrange(f"p ({names}) -> p {names}", **kw)

    P = Prog(nc)
    rot = {}

    def nxt(name, n):
        rot[name] = (rot.get(name, -1) + 1) % n
        return rot[name]

    P.add("sp", lambda e: e.dma_start(out=cf[:], in_=cf_d), writes=["cf"], dma="cf")
    P.add("sp", lambda e: e.dma_start(out=cb[:], in_=cb_d), writes=["cb"], dma="cb")
    P.add("sp", lambda e: e.dma_start(out=bfg[:], in_=bf_d.broadcast_to([128, 8])), writes=["bfg"], dma="bfg")
    P.add("sp", lambda e: e.dma_start(out=snk[:], in_=sk_d.broadcast_to([128, 8])), writes=["snk"], dma="snk")
    P.add("pool", lambda e: e.memset(epsc[:], EPS), writes=["epsc"])
    onec = sb("onec", [128, 1], F32)
    P.add("pool", lambda e: e.memset(onec[:], 1.0), writes=["onec"])
    ALLSS = ["ss"] + [("ss", b) for b in ["m"] + list(range(NBLK))]
    for g in range(4):
        for hh in range(2):
            hd = 4 * hh + g
            P.add("act", lambda e, g=g, hh=hh, hd=hd: e.activation(
                out=sinktab[64 * hh:64 * hh + 64, 128 * g:128 * g + 128],
                in_=cf[64 * hh:64 * hh + 64, CF_IOTA:CF_IOTA + 128], func=AF.Exp,
                bias=snk[64 * hh:64 * hh + 64, hd:hd + 1], scale=SLOPES[hd]),
                reads=["cf", "snk"], writes=["sinktab"])

    def tile_cols(t):
        if t == "m":
            return 0, NMETA, ["m"]
        return NMETA + 512 * t, 512, [4 * t + j for j in range(4)]

    def blk_cols(b):
        if b == "m":
            return 0, NMETA
        return NMETA + 128 * b, 128

    def cast_dma(dst, src, res, slot):
        P.add("pool", lambda e: e.dma_start(out=dst, in_=src), writes=[res], dma=slot)

    def mm_group(out, pairs, reads, writes):
        def fn(e):
            n = len(pairs)
            ins = None
            for i, (l, r) in enumerate(pairs):
                ins = e.matmul(out, lhsT=l, rhs=r, start=(i == 0), stop=(i == n - 1))
            return ins
        P.add("pe", fn, reads=reads, writes=writes)

    evac_flip = [0]

    def evac(out, in_, reads, writes, eng=None):
        if eng is None:
            evac_flip[0] ^= 1
            eng = "dve" if evac_flip[0] else "act"
        if eng == "act":
            P.add("act", lambda e: e.copy(out=out, in_=in_), reads=reads, writes=writes)
        else:
            P.add(eng, lambda e: e.tensor_copy(out=out, in_=in_), reads=reads, writes=writes)

    def norm_to_uT(gain_d, blocks, hm, hooked=False, junk=None, junkres="junk"):
        ut = [AR(OFF_UT + 2048 * i, [128, D]) for i in range(2)]
        P.add("sp", lambda e: e.dma_start(out=gb[:], in_=gain_d.broadcast_to([128, D])), writes=["gb"], dma="gb")
        P.add("pool", lambda e: e.memset(ss[:], 0.0), writes=ALLSS)
        jmap = {}

        def geom(b):
            c0, nr = blk_cols(b)
            src = hm[0:nr, :] if b == "m" else h[:, b, :]
            col = 16 if b == "m" else b
            return c0, nr, src, col

        def a1(b, j=None):
            c0, nr, src, col = geom(b)
            if junk is not None:
                out, ores = junk[0:nr, :], junkres
            else:
                out, ores = ut[j][0:nr, :], ("ut", j)
            P.add("act", lambda e: e.activation(out=out, in_=src, func=AF.Square, accum_out=ss[0:nr, col:col + 1]),
                  reads=[("h", b), ("ss", b)], writes=[("ss", b), ores], full=True)
            P.add("act", lambda e: e.activation(
                out=rs[0:nr, col:col + 1], in_=ss[0:nr, col:col + 1], func=AF.Sqrt, bias=epsc[0:nr, 0:1], scale=1.0 / D),
                reads=[("ss", b), "epsc"], writes=[("rs", b)])

        def a2(b, j):
            c0, nr, src, col = geom(b)
            jmap[b] = j
            P.add("dve", lambda e: e.reciprocal(out=rs[0:nr, col:col + 1], in_=rs[0:nr, col:col + 1]),
                  reads=[("rs", b)], writes=[("rs", b)])
            P.add("dve", lambda e: e.scalar_tensor_tensor(
                out=ut[j][0:nr, :], in0=src, scalar=rs[0:nr, col:col + 1], in1=gb[0:nr, :],
                op0=ALU.mult, op1=ALU.mult), reads=[("h", b), ("rs", b), "gb"], writes=[("ut", j)])

        def stage_a(b):
            j = nxt("ut", 2)
            a1(b, j)
            a2(b, j)

        def stage_b(b):
            c0, nr = blk_cols(b)
            j = jmap[b]
            for half in range(2):
                def fn(e, half=half):
                    ins = None
                    for c in range(4):
                        cc = 4 * half + c
                        ins = e.transpose(pt[half][:, 128 * c:128 * c + nr], ut[j][0:nr, 128 * cc:128 * cc + 128],
                                          cb[0:nr, CB_ID:CB_ID + nr])
                    return ins
                P.add("pe", fn, reads=[("ut", j), "cb"], writes=[("ps", 6 + half)])
                src_v = pt[half][:, :].rearrange("p (c n) -> p c n", c=4)[:, :, 0:nr]
                evac(uT[:, 4 * half:4 * half + 4, c0:c0 + nr], src_v, [("ps", 6 + half)], [("uT", b, half)],
                     eng=("act" if half == 0 else "dve"))
        if hooked:
            assert junk is not None
            seq = []

            def hook(b):
                seq.append(b)
                n = len(seq)
                if n >= 3:
                    stage_b(seq[n - 3])
                if n >= 2:
                    a2(seq[n - 2], nxt("ut", 2))
                a1(b)

            def flush():
                n = len(seq)
                if n >= 2:
                    stage_b(seq[n - 2])
                a2(seq[n - 1], nxt("ut", 2))
                stage_b(seq[n - 1])
            return hook, flush
        stage_a(blocks[0])
        for i, b in enumerate(blocks):
            if i + 1 < len(blocks):
                stage_a(blocks[i + 1])
            stage_b(b)

    def uT_res(blks):
        return [("uT", b, hf) for b in blks for hf in range(2)]

    def ffn(w_in_d, w_out_d, gain_d, with_meta, hm, tagp, do_norm=True, tail=None):
        blocks = (["m"] if with_meta else []) + list(range(NBLK))
        if do_norm:
            norm_to_uT(gain_d, blocks, hm)
        tiles = (["m"] if with_meta else []) + [0, 1, 2, 3]
        act = AR(OFF_ACT, [128, 6, TCOLS])
        wis = [AR(OFF_WI + 8192 * i, [128, 2, 8, 256]) for i in range(3)]
        wos = [AR(OFF_WO + 12288 * i, [128, 6, D]) for i in range(2)]
        tmp = [AR(OFF_TMP + 2048 * i, [128, 512], F32) for i in range(2)]
        for (m0, nch) in PASSES:
            wslot = nxt("wo", 2)
            wo_t = wos[wslot]
            ml = 0
            while ml < nch:
                ns = min(2, nch - ml)
                s = nxt("wi", 3)
                wi_t = wis[s]
                c0 = (m0 + ml) * 128
                cast_dma(wi_t[:, 0, :, 0:ns * 128], w_in_d[:, c0:c0 + ns * 128].rearrange("(k p) n -> p k n", p=128),
                         ("wi", s, 0), f"wi{s}g")
                cast_dma(wi_t[:, 1, :, 0:ns * 128],
                         w_in_d[:, DFF + c0:DFF + c0 + ns * 128].rearrange("(k p) n -> p k n", p=128),
                         ("wi", s, 1), f"wi{s}u")
                if ml == 0:
                    cast_dma(wo_t[:, 0:nch, :], w_out_d[m0 * 128:(m0 + nch) * 128, :].rearrange("(k p) n -> p k n", p=128),
                             ("wo", wslot), f"wo{wslot}")
                for q in range(ns):
                    mloc = ml + q
                    for t in tiles:
                        tc0, tn, tb = tile_cols(t)
                        a = nxt("psA", 2)
                        pA, pB = ps[2 * a], ps[2 * a + 1]
                        for which, pp in ((0, pA), (1, pB)):
                            mm_group(pp[:, 0:tn],
                                     [(wi_t[:, which, k, q * 128:(q + 1) * 128], uT[:, k, tc0:tc0 + tn]) for k in range(8)],
                                     reads=[("wi", s, which)] + uT_res(tb), writes=[("ps", 2 * a + which)])
                        j = nxt("tmp", 2)
                        P.add("act", lambda e, j=j, pA=pA, tn=tn: e.activation(out=tmp[j][:, 0:tn], in_=pA[:, 0:tn], func=AF.Silu),
                              reads=[("ps", 2 * a)], writes=[("tmp", j)])
                        P.add("dve", lambda e, j=j, pB=pB, tn=tn, mloc=mloc, tc0=tc0: e.tensor_tensor(
                            out=act[:, mloc, tc0:tc0 + tn], in0=tmp[j][:, 0:tn], in1=pB[:, 0:tn], op=ALU.mult),
                            reads=[("tmp", j), ("ps", 2 * a + 1)], writes=[("act", mloc, t)])
                ml += ns
            hook = flush = None
            if tail is not None and (m0, nch) == PASSES[-1]:
                hook, flush = tail()
            for b in blocks:
                bc0, nr = blk_cols(b)
                t = "m" if b == "m" else b // 4
                for half in range(2):
                    o = 4 + nxt("psO", 2)
                    mm_group(ps[o][0:nr, :],
                             [(act[:, k, bc0:bc0 + nr], wo_t[:, k, 512 * half:512 * half + 512]) for k in range(nch)],
                             reads=[("act", k, t) for k in range(nch)] + [("wo", wslot)], writes=[("ps", o)])
                    dst = hm[0:nr, 512 * half:512 * half + 512] if b == "m" else h[:, b, 512 * half:512 * half + 512]
                    P.add("dve", lambda e, dst=dst, o=o, nr=nr: e.scalar_tensor_tensor(
                        out=dst, in0=ps[o][0:nr, :], scalar=0.5, in1=dst, op0=ALU.mult, op1=ALU.add),
                        reads=[("ps", o), ("h", b)], writes=[("h", b)])
                if hook is not None:
                    hook(b)
            if flush is not None:
                flush()

    OFF_HM = 0
    OFF_UT = 4096
    OFF_ACT = 8192
    OFF_WI = OFF_ACT + 24768
    OFF_WO = OFF_WI + 24576
    OFF_TMP = OFF_WO + 24576
    OFF_OST = OFF_TMP + 4096
    assert OFF_OST + 8192 <= ARENA_BYTES
    OFF_OA = 8192
    OFF_OB = OFF_OA + 16384
    OFF_LF = OFF_OB + 16384
    OFF_BIASF = OFF_LF + 3264
    OFF_PT = OFF_BIASF + 2176
    OFF_DEN = OFF_PT + 4096
    OFF_X = OFF_DEN + 2048
    assert OFF_X + 45632 <= ARENA_BYTES, OFF_X
    OFF_PW = OFF_OB + 16384
    OFF_MIX = OFF_PW + 24576
    assert OFF_MIX + 32768 <= ARENA_BYTES
    OFF_PTMP = 0

    def mixer(hm, mstop=9, post_tail=None):
        ablocks = ["m"] + list(range(NBLK))
        NPT = 4
        PT = [AR(OFF_PT + 1024 * i, [128, 512]) for i in range(NPT)]
        den = [AR(OFF_DEN, [128, 512], F32)]
        OaT = AR(OFF_OA, [128, 4, SEQ])
        ObT = AR(OFF_OB, [128, 4, SEQ])
        lfv = [AR(OFF_LF + 544 * i, [128, 136], F32) for i in range(6)]
        xb, lt, tots, offv, cum = lfv[0], lfv[1], lfv[2], lfv[3], lfv[4]
        biasF = AR(OFF_BIASF, [128, 17, 4, 8], F32)
        wf = lfv[5][:, :].bitcast(BF16)[:, 0:64].rearrange("p (k n) -> p k n", k=8)
        cast_dma(wf, wi_d[:, 2304:2312].rearrange("(k p) n -> p k n", p=128), "wf", "wf")

        def f_fn(e):
            ins = None
            for bi, b in enumerate(ablocks):
                c0, nr = blk_cols(b)
                for k in range(8):
                    ins = e.matmul(ps[0][0:nr, 8 * bi:8 * bi + 8], lhsT=uT[:, k, c0:c0 + nr], rhs=wf[:, k, :],
                                   start=(k == 0), stop=(k == 7))
            return ins
        P.add("pe", f_fn, reads=["wf"] + uT_res(ablocks), writes=[("ps", 0)])
        P.add("pool", lambda e: e.memset(lt[:], 0.0), writes=["lt"])
        for (r0, r1, c0, c1) in ((0, 16, 0, 8), (0, 128, 8, 136)):
            bb = bfg[r0:r1, :] if c1 == 8 else bfg[r0:r1, :].unsqueeze(1).to_broadcast([r1 - r0, 16, 8])
            i0 = ps[0][r0:r1, c0:c1] if c1 == 8 else ps[0][r0:r1, c0:c1].rearrange("p (b h) -> p b h", h=8)
            o0 = xb[r0:r1, c0:c1] if c1 == 8 else xb[r0:r1, c0:c1].rearrange("p (b h) -> p b h", h=8)
            P.add("dve", lambda e, bb=bb, i0=i0, o0=o0: e.tensor_tensor(out=o0, in0=i0, in1=bb, op=ALU.add),
                  reads=[("ps", 0), "bfg"], writes=["xb"])
            P.add("act", lambda e, r0=r0, r1=r1, c0=c0, c1=c1: e.activation(
                out=xb[r0:r1, c0:c1], in_=xb[r0:r1, c0:c1], func=AF.Exp, scale=-1.0), reads=["xb"], writes=["xb"])
            P.add("act", lambda e, r0=r0, r1=r1, c0=c0, c1=c1: e.activation(
                out=lt[r0:r1, c0:c1], in_=xb[r0:r1, c0:c1], func=AF.Ln, bias=onec[r0:r1, 0:1]), reads=["xb", "lt", "onec"], writes=["lt"])
        P.add("pe", lambda e: e.matmul(ps[1][:, 0:136], lhsT=cf[:, CF_ONES:CF_ONES + 128], rhs=lt[:, :], start=True, stop=True),
              reads=["lt", "cf"], writes=[("ps", 1)])
        P.add("pe", lambda e: e.matmul(ps[2][:, 0:136], lhsT=cf[:, CF_TRI:CF_TRI + 128], rhs=lt[:, :], start=True, stop=True),
              reads=["lt", "cf"], writes=[("ps", 2)])
        P.add("dve", lambda e: e.tensor_copy(out=tots[:, :], in_=ps[1][:, 0:136]), reads=[("ps", 1)], writes=["tots"])
        P.add("pool", lambda e: e.memset(offv[:, :], 0.0), writes=["offv"])
        for b in range(1, 17):
            P.add("dve", lambda e, b=b: e.tensor_tensor(out=offv[:, 8 * b:8 * b + 8], in0=offv[:, 8 * b - 8:8 * b],
                                                        in1=tots[:, 8 * b - 8:8 * b], op=ALU.add),
                  reads=["offv", "tots"], writes=["offv"])
        P.add("dve", lambda e: e.tensor_tensor(out=cum[:, :], in0=ps[2][:, 0:136], in1=offv[:, :], op=ALU.add),
              reads=[("ps", 2), "offv"], writes=["cum"])
        cum3 = cum[:, :].rearrange("p (b h) -> p b h", h=8)
        for j in range(4):
            ref = 4 * j + 3
            for hd in range(8):
                P.add("dve", lambda e, j=j, hd=hd, ref=ref: e.tensor_scalar(
                    out=biasF[:, :, j, hd], in0=cum3[:, :, hd], scalar1=offv[:, 8 * ref + hd:8 * ref + hd + 1],
                    scalar2=None, op0=ALU.subtract), reads=["cum", "offv"], writes=["biasF"])
        if mstop <= 1:
            return

        wqa = AR(OFF_X, [128, 8, 4, 2, 64])
        wka = AR(OFF_X + 8192, [128, 8, 128])
        wva = AR(OFF_X + 10240, [128, 8, 128])
        QaT = AR(OFF_X + 12288, [128, 4, SEQ])
        Ka2 = [AR(OFF_X + 28672 + 4128 * i, [128, TCOLS]) for i in range(2)]
        Va = [AR(OFF_X + 36928 + 4352 * i, [128, 17, 128]) for i in range(2)]
        P.add("pool", lambda e: e.memset(Ka2[0][64:128, :], 0.0), writes=[("KaT", b) for b in ablocks])
        P.add("pool", lambda e: e.memset(Ka2[1][0:64, :], 0.0), writes=[("KaT", b) for b in ablocks])
        for hh in range(2):
            for g in range(4):
                cq = 256 * hh + 64 * g
                cast_dma(wqa[:, :, g, hh, :], wi_d[:, cq:cq + 64].rearrange("(k p) d -> p k d", p=128),
                         ("wqa", hh, g), f"wqa{hh}{g}")
        cast_dma(wka, wi_d[:, 512:640].rearrange("(k p) n -> p k n", p=128), "wka", "wka")
        cast_dma(wva, wi_d[:, 640:768].rearrange("(k p) n -> p k n", p=128), "wva", "wva")
        for i in range(2):
            P.add("pool", lambda e, i=i: e.memset(Va[i][:, :, :], 0.0), writes=[("Va", i)])
        for g in range(4 if DBG_SWA >= 0.2 else 0):
            for t in range(4):
                tc0, tn, tb = tile_cols(t)
                a = nxt("psQ", 2)
                mm_group(ps[a][:, :], [(wqa[:, k, g].rearrange("p a d -> p (a d)"), uT[:, k, tc0:tc0 + tn]) for k in range(8)],
                         reads=[("wqa", 0, g), ("wqa", 1, g)] + uT_res(tb), writes=[("ps", a)])
                evac(QaT[:, g, 512 * t:512 * t + 512], ps[a][:, :], [("ps", a)], [("QaT", 4 * t + j) for j in range(4)])
        for t in (["m", 0, 1, 2, 3] if DBG_SWA >= 0.3 else []):
            tc0, tn, tb = tile_cols(t)
            a = nxt("psQ", 2)
            mm_group(ps[a][:, 0:tn], [(wka[:, k, :], uT[:, k, tc0:tc0 + tn]) for k in range(8)],
                     reads=["wka"] + uT_res(tb), writes=[("ps", a)])
            evac(Ka2[0][0:64, tc0:tc0 + tn], ps[a][0:64, 0:tn], [("ps", a)], [("KaT", b) for b in tb], eng="dve")
            evac(Ka2[1][64:128, tc0:tc0 + tn], ps[a][64:128, 0:tn], [("ps", a)], [("KaT", b) for b in tb], eng="dve")
        for bi, b in enumerate(ablocks if DBG_SWA >= 0.4 else []):
            if DBG_SWA == 0.45 and b == "m":
                continue
            c0, nr = blk_cols(b)
            a = nxt("psQ", 2)
            mm_group(ps[a][0:nr, 0:128], [(uT[:, k, c0:c0 + nr], wva[:, k, :]) for k in range(8)],
                     reads=["wva"] + uT_res([b]), writes=[("ps", a)])
            evac(Va[0][0:nr, bi, 0:64], ps[a][0:nr, 0:64], [("ps", a)], [("Va", 0)], eng="dve")
            evac(Va[1][0:nr, bi, 64:128], ps[a][0:nr, 64:128], [("ps", a)], [("Va", 1)], eng="dve")
        SB = [0, 1, 2, 5]
        LOOK = 3
        ODS = [(3, 4), (6, 7)]
        steps = []
        for i in range(NBLK):
            roles = [("meta", 0, 16, 0)] + ([("prev", i, 128, NMETA + 128 * (i - 1))] if i >= 1 else []) + \
                    [("cur", i + 1, 128, NMETA + 128 * i)]
            n_i = 2 * len(roles)
            cnt = 0
            for kvh in range(2):
                for (role, kb, nk, kc0) in roles:
                    steps.append((i, kvh, role, kb, nk, kc0, cnt == 0, cnt == n_i - 1))
                    cnt += 1

        def swa_qk(n):
            i, kvh, role, kb, nk, kc0, st, last = steps[n]
            sidx = SB[n % 4]
            pS = ps[sidx]
            base = 64 * kvh
            kres = ("KaT", "m") if role == "meta" else ("KaT", kb - 1)
            def qk(e, w):
                ins = e.matmul(pS[0:nk, :].rearrange("p (g q) -> p g q", g=4), lhsT=Ka2[kvh][:, kc0:kc0 + nk],
                               rhs=QaT[:, :, 128 * i:128 * i + 128], start=True, stop=True)
                if w is not None:
                    ins._wait_ge(*w)
                return ins
            P.add("pe", qk, reads=[kres, ("QaT", i)], writes=[("ps", sidx)], attach=True)

        def swa_rest(n):
            i, kvh, role, kb, nk, kc0, st, last = steps[n]
            sidx = SB[n % 4]
            pS = ps[sidx]
            pj = n % NPT
            iO, iD = ODS[i % 2]
            pO, pD = ps[iO], ps[iD]
            for g in range(4):
                hd = 4 * kvh + g
                if role == "meta":
                    bcol = cf[0:nk, CF_BMETA + 8 * i + hd:CF_BMETA + 8 * i + hd + 1]
                elif role == "prev":
                    bcol = cf[0:nk, CF_BPREV + hd:CF_BPREV + hd + 1]
                else:
                    bcol = cf[0:nk, CF_BCUR + hd:CF_BCUR + hd + 1]
                P.add("act", lambda e, g=g, bcol=bcol: e.activation(
                    out=PT[pj][0:nk, 128 * g:128 * g + 128], in_=pS[0:nk, 128 * g:128 * g + 128],
                    func=AF.Exp, bias=bcol, scale=0.125), reads=[("ps", sidx), "cf"], writes=[("PT", pj)])
            if role != "meta":
                mo = CB_MPREV if role == "prev" else CB_MCUR
                P.add("dve", lambda e: e.tensor_tensor(
                    out=PT[pj][:, :], in0=PT[pj][:, :], in1=cb[:, mo:mo + 512], op=ALU.mult),
                    reads=[("PT", pj), "cb"], writes=[("PT", pj)])
            oo = CB_OLO if kvh == 0 else CB_OHI

            def pv(e, w):
                i0 = e.matmul(pO[:, :], lhsT=Va[kvh][0:nk, kb, :], rhs=PT[pj][0:nk, :], start=st, stop=last)
                if w is not None:
                    i0._wait_ge(*w)
                return e.matmul(pD[:, :], lhsT=cb[0:nk, oo:oo + 128], rhs=PT[pj][0:nk, :], start=st, stop=last)
            P.add("pe", pv, reads=[("PT", pj), ("Va", kvh), "cb"], writes=[("ps", iO), ("ps", iD)], attach=True)
            if defer:
                defer.pop(0)()
            if last:
                dj = 0
                defer.append(lambda: P.add("dve", lambda e: e.tensor_tensor(out=den[dj][:, :], in0=pD[:, :], in1=sinktab[:, :], op=ALU.add),
                                           reads=[("ps", iD), "sinktab"], writes=[("den", dj)]))
                defer.append(lambda: P.add("dve", lambda e: e.reciprocal(out=den[dj][:, :], in_=den[dj][:, :]),
                                           reads=[("den", dj)], writes=[("den", dj)]))
                defer.append(lambda: P.add("dve", lambda e: e.tensor_tensor(
                    out=OaT[:, :, 128 * i:128 * i + 128], in0=pO[:, :].rearrange("p (g q) -> p g q", g=4),
                    in1=den[dj][:, :].rearrange("p (g q) -> p g q", g=4), op=ALU.mult),
                    reads=[("ps", iO), ("den", dj)], writes=[("OaT", i // 4)]))
        defer = []
        for n in range(len(steps) + LOOK):
            if n < len(steps):
                swa_qk(n)
            if n - LOOK >= 0:
                swa_rest(n - LOOK)
        while defer:
            defer.pop(0)()
        P.barrier()
        if mstop <= 2:
            return

        wqb = AR(OFF_X, [128, 8, 512])
        wkb = AR(OFF_X + 8192, [128, 8, 512])
        wvb = AR(OFF_X + 16384, [128, 8, 512])
        QbT = AR(OFF_X + 24576, [128, SEQ])
        Kb2 = [AR(OFF_X + 28672 + 4128 * i, [128, TCOLS]) for i in range(2)]
        Vb = [AR(OFF_X + 36928 + 4352 * i, [128, 17, 128]) for i in range(2)]
        P.add("pool", lambda e: e.memset(Kb2[0][64:128, :], 0.0), writes=[("KbT", b) for b in ablocks])
        P.add("pool", lambda e: e.memset(Kb2[1][0:64, :], 0.0), writes=[("KbT", b) for b in ablocks])
        for c in range(4):
            for (wt, col0, nm) in ((wqb, 768, "wqb"), (wkb, 1280, "wkb"), (wvb, 1792, "wvb")):
                cast_dma(wt[:, :, 128 * c:128 * c + 128],
                         wi_d[:, col0 + 128 * c:col0 + 128 * c + 128].rearrange("(k p) n -> p k n", p=128), (nm, c), nm)
        for i in range(2):
            P.add("pool", lambda e, i=i: e.memset(Vb[i][:, :, :], 0.0), writes=[("Vb", i, bi) for bi in range(17)])
        for c in range(4):
            for t in range(4):
                tc0, tn, tb = tile_cols(t)
                a = nxt("psQ", 2)
                mm_group(ps[a][:, :], [(wqb[:, k, 128 * c:128 * c + 128], uT[:, k, tc0:tc0 + tn]) for k in range(8)],
                         reads=[("wqb", c)] + uT_res(tb), writes=[("ps", a)])
                evac(QbT[:, 512 * t:512 * t + 512], ps[a][:, :], [("ps", a)], [("QbT", t)], eng="dve")
            for t in ["m", 0, 1, 2, 3]:
                tc0, tn, tb = tile_cols(t)
                a = nxt("psQ", 2)
                mm_group(ps[a][:, 0:tn], [(wkb[:, k, 128 * c:128 * c + 128], uT[:, k, tc0:tc0 + tn]) for k in range(8)],
                         reads=[("wkb", c)] + uT_res(tb), writes=[("ps", a)])
                evac(Kb2[0][0:64, tc0:tc0 + tn], ps[a][0:64, 0:tn], [("ps", a)], [("KbT", b) for b in tb], eng="dve")
                evac(Kb2[1][64:128, tc0:tc0 + tn], ps[a][64:128, 0:tn], [("ps", a)], [("KbT", b) for b in tb], eng="dve")
            for bi, b in enumerate(ablocks):
                c0, nr = blk_cols(b)
                a = nxt("psQ", 2)
                mm_group(ps[a][0:nr, 0:128], [(uT[:, k, c0:c0 + nr], wvb[:, k, 128 * c:128 * c + 128]) for k in range(8)],
                         reads=[("wvb", c)] + uT_res([b]), writes=[("ps", a)])
                evac(Vb[0][0:nr, bi, 0:64], ps[a][0:nr, 0:64], [("ps", a)], [("Vb", 0, bi)], eng="dve")
                evac(Vb[1][0:nr, bi, 64:128], ps[a][0:nr, 64:128], [("ps", a)], [("Vb", 1, bi)], eng="dve")
            fsteps = []
            for j in range(4):
                kbs = [0] + [1 + r for r in range(4 * j + 4)]
                n_j = 2 * len(kbs)
                cnt = 0
                for kb in kbs:
                    for hh in range(2):
                        fsteps.append((j, kb, hh, cnt == 0, cnt == n_j - 1))
                        cnt += 1

            def fparams(n):
                j, kb, hh, st, last = fsteps[n]
                if kb == 0:
                    nk, kc0, c0, kres = 16, 0, 0, ("KbT", "m")
                else:
                    r = kb - 1
                    nk, kc0, kres = 128, NMETA + 128 * r, ("KbT", r)
                    c0 = 128 * (r - 4 * j) if r >= 4 * j else 0
                diag = kb >= 1 and (kb - 1) >= 4 * j
                return j, kb, hh, st, last, nk, kc0, c0, kres, diag

            def fox_qk(n, c=c):
                j, kb, hh, st, last, nk, kc0, c0, kres, diag = fparams(n)
                sidx = SB[n % 4]
                pS = ps[sidx]
                base = 64 * hh
                def qk(e, w):
                    ins = e.matmul(pS[0:nk, c0:512], lhsT=Kb2[hh][:, kc0:kc0 + nk],
                                   rhs=QbT[:, 512 * j + c0:512 * j + 512], start=True, stop=True)
                    if w is not None:
                        ins._wait_ge(*w)
                    return ins
                P.add("pe", qk, reads=[kres, ("QbT", j)], writes=[("ps", sidx)], attach=True)

            def fox_rest(n, c=c):
                j, kb, hh, st, last, nk, kc0, c0, kres, diag = fparams(n)
                sidx = SB[n % 4]
                pS = ps[sidx]
                pj = n % NPT
                iO, iD = ODS[(4 * c + j) % 2]
                pO, pD = ps[iO], ps[iD]
                hd = 2 * c + hh
                P.add("act", lambda e: e.activation(
                    out=PT[pj][0:nk, c0:512], in_=pS[0:nk, c0:512], func=AF.Exp,
                    bias=biasF[0:nk, kb, j, hd:hd + 1], scale=0.125),
                    reads=[("ps", sidx), "biasF"], writes=[("PT", pj)])
                if diag:
                    P.add("dve", lambda e: e.tensor_tensor(
                        out=PT[pj][:, c0:c0 + 128], in0=PT[pj][:, c0:c0 + 128],
                        in1=cb[:, CB_MCUR:CB_MCUR + 128], op=ALU.mult),
                        reads=[("PT", pj), "cb"], writes=[("PT", pj)])
                oo = CB_OLO if hh == 0 else CB_OHI

                def pv(e, w):
                    i0 = e.matmul(pO[:, c0:512], lhsT=Vb[hh][0:nk, kb, :], rhs=PT[pj][0:nk, c0:512], start=st, stop=last)
                    if w is not None:
                        i0._wait_ge(*w)
                    return e.matmul(pD[:, c0:512], lhsT=cb[0:nk, oo:oo + 128], rhs=PT[pj][0:nk, c0:512], start=st, stop=last)
                P.add("pe", pv, reads=[("PT", pj), ("Vb", hh, kb), "cb"], writes=[("ps", iO), ("ps", iD)], attach=True)
                if defer:
                    defer.pop(0)()
                if last:
                    dj = 0
                    defer.append(lambda: P.add("dve", lambda e: e.reciprocal(out=den[dj][:, :], in_=pD[:, :]),
                                               reads=[("ps", iD)], writes=[("den", dj)]))
                    defer.append(lambda: P.add("dve", lambda e: e.tensor_tensor(
                        out=ObT[:, c, 512 * j:512 * j + 512], in0=pO[:, :], in1=den[dj][:, :], op=ALU.mult),
                        reads=[("ps", iO), ("den", dj)], writes=[("ObT", j)]))
            for n in range(len(fsteps) + LOOK):
                if n < len(fsteps):
                    fox_qk(n)
                if n - LOOK >= 0:
                    fox_rest(n - LOOK)
            while defer:
                defer.pop(0)()
        P.barrier()
        if mstop <= 3:
            return

        mixT = AR(OFF_MIX, [128, 8, SEQ])
        wo_t = AR(OFF_PW, [128, 8, D])
        ovl = [[("pw", 0, "a", 0), ("pw", 0, "a", 1), ("pw", 0, "b"), ("pw", 0, "ga")],
               [("pw", 0, "gb"), ("pw", 1, "a", 0), ("pw", 1, "a", 1), ("pw", 1, "b")]]

        def issue_wo(hf):
            P.add("pool", lambda e: e.dma_start(
                out=wo_t[:, 4 * hf:4 * hf + 4, :], in_=wo_d[512 * hf:512 * hf + 512, :].rearrange("(k p) n -> p k n", p=128)),
                writes=[("wout", hf)] + ovl[hf], dma=f"wout{hf}")
        ptmp = [AR(OFF_PTMP + 2048 * i, [128, 512], F32) for i in range(2)]
        for st in range(4):
            s = nxt("pw", 2)
            o = OFF_PW + 12288 * s
            wa_t = AR(o, [128, 4, 256])
            wb_t = AR(o + 2048, [128, 4, 256])
            wga_t = AR(o + 4096, [128, 8, 256])
            wgb_t = AR(o + 8192, [128, 8, 256])
            c0 = 256 * st
            for hh in range(2):
                cast_dma(wa_t[64 * hh:64 * hh + 64, :, :],
                         wa_d[256 * hh:256 * hh + 256, c0:c0 + 256].rearrange("(g d) n -> d g n", d=64),
                         ("pw", s, "a", hh), f"pw{s}a{hh}")
            cast_dma(wb_t, wb_d[:, c0:c0 + 256].rearrange("(c p) n -> p c n", p=128), ("pw", s, "b"), f"pw{s}b")
            cast_dma(wga_t, wi_d[:, 2312 + c0:2312 + c0 + 256].rearrange("(k p) n -> p k n", p=128), ("pw", s, "ga"), f"pw{s}ga")
            cast_dma(wgb_t, wi_d[:, 3336 + c0:3336 + c0 + 256].rearrange("(k p) n -> p k n", p=128), ("pw", s, "gb"), f"pw{s}gb")
            if st == 3:
                issue_wo(0)
            for q in range(2):
                m = 2 * st + q
                for t in range(4):
                    tc0, tn, tb = tile_cols(t)
                    ia, ib, iga, igb = (nxt("psP", 8) for _ in range(4))
                    mm_group(ps[ia][:, :], [(wa_t[:, k, 128 * q:128 * q + 128], OaT[:, k, 512 * t:512 * t + 512]) for k in range(4)],
                             reads=[("pw", s, "a", 0), ("pw", s, "a", 1), ("OaT", t)], writes=[("ps", ia)])
                    mm_group(ps[ib][:, :], [(wb_t[:, k, 128 * q:128 * q + 128], ObT[:, k, 512 * t:512 * t + 512]) for k in range(4)],
                             reads=[("pw", s, "b"), ("ObT", t)], writes=[("ps", ib)])
                    mm_group(ps[iga][:, :], [(wga_t[:, k, 128 * q:128 * q + 128], uT[:, k, tc0:tc0 + 512]) for k in range(8)],
                             reads=[("pw", s, "ga")] + uT_res(tb), writes=[("ps", iga)])
                    mm_group(ps[igb][:, :], [(wgb_t[:, k, 128 * q:128 * q + 128], uT[:, k, tc0:tc0 + 512]) for k in range(8)],
                             reads=[("pw", s, "gb")] + uT_res(tb), writes=[("ps", igb)])
                    ja, jb = 0, 1
                    P.add("act", lambda e, ja=ja, iga=iga: e.activation(out=ptmp[ja][:, :], in_=ps[iga][:, :], func=AF.Sigmoid),
                          reads=[("ps", iga)], writes=[("ptmp", ja)])
                    P.add("act", lambda e, jb=jb, igb=igb: e.activation(out=ptmp[jb][:, :], in_=ps[igb][:, :], func=AF.Sigmoid),
                          reads=[("ps", igb)], writes=[("ptmp", jb)])
                    P.add("dve", lambda e, ja=ja, ia=ia: e.tensor_tensor(out=ptmp[ja][:, :], in0=ptmp[ja][:, :], in1=ps[ia][:, :], op=ALU.mult),
                          reads=[("ptmp", ja), ("ps", ia)], writes=[("ptmp", ja)])
                    P.add("dve", lambda e, jb=jb, ib=ib: e.tensor_tensor(out=ptmp[jb][:, :], in0=ptmp[jb][:, :], in1=ps[ib][:, :], op=ALU.mult),
                          reads=[("ptmp", jb), ("ps", ib)], writes=[("ptmp", jb)])
                    P.add("pool", lambda e, ja=ja, jb=jb, m=m, t=t: e.tensor_tensor(
                        out=mixT[:, m, 512 * t:512 * t + 512], in0=ptmp[ja][:, :], in1=ptmp[jb][:, :], op=ALU.add),
                        reads=[("ptmp", ja), ("ptmp", jb)], writes=[("mixT", t)])
        issue_wo(1)
        hook = flush = None
        if post_tail is not None:
            hook, flush = post_tail()
        for b in range(NBLK):
            for half in range(2):
                o = nxt("psP", 8)
                mm_group(ps[o][:, :], [(mixT[:, k, 128 * b:128 * b + 128], wo_t[:, k, 512 * half:512 * half + 512]) for k in range(8)],
                         reads=[("mixT", b // 4), ("wout", 0), ("wout", 1)], writes=[("ps", o)])
                dst = h[:, b, 512 * half:512 * half + 512]
                P.add("dve", lambda e, dst=dst, o=o: e.tensor_tensor(out=dst, in0=ps[o][:, :], in1=dst, op=ALU.add),
                      reads=[("ps", o), ("h", b)], writes=[("h", b)])
            if hook is not None:
                hook(b)
        if flush is not None:
            flush()

    def final_out(s, raw, hooked=False, after_block=None):
        ost = [AR(OFF_OST + 4096 * i, [128, D], F32) for i in range(2)]
        if raw:
            for b in range(NBLK):
                P.add("sp", lambda e, b=b: e.dma_start(out=out_d[s, 128 * b:128 * b + 128, :], in_=h[:, b, :]),
                      reads=[("h", b)], dma=f"out{b % 2}")
            return
        P.add("sp", lambda e: e.dma_start(out=gb[:], in_=n4_d.broadcast_to([128, D])), writes=["gb"], dma="gb")
        P.add("pool", lambda e: e.memset(ss[:], 0.0), writes=ALLSS)
        fjunk = AR(OFF_OST + 8192, [128, D])

        def a1(b):
            P.add("act", lambda e: e.activation(out=fjunk[:, :], in_=h[:, b, :], func=AF.Square, accum_out=ss[:, b:b + 1]),
                  reads=[("h", b), ("ss", b)], writes=[("ss", b), "fjunk"], full=True)
            P.add("act", lambda e: e.activation(out=rs[:, b:b + 1], in_=ss[:, b:b + 1], func=AF.Sqrt,
                                                bias=epsc[:, 0:1], scale=1.0 / D),
                  reads=[("ss", b), "epsc"], writes=[("rs", b)])

        def a2(b):
            j = nxt("ost", 2)
            P.add("dve", lambda e: e.reciprocal(out=rs[:, b:b + 1], in_=rs[:, b:b + 1]), reads=[("rs", b)], writes=[("rs", b)])
            P.add("dve", lambda e: e.scalar_tensor_tensor(
                out=ost[j][:, :], in0=h[:, b, :], scalar=rs[:, b:b + 1], in1=gb[:, :], op0=ALU.mult, op1=ALU.mult),
                reads=[("h", b), ("rs", b), "gb"], writes=[("ost", j)])
            P.add("sp", lambda e: e.dma_start(out=out_d[s, 128 * b:128 * b + 128, :], in_=ost[j][:, :]),
                  reads=[("ost", j)], dma=f"out{j}")
            if after_block is not None:
                after_block(b)
        if hooked:
            seq = []

            def hook(b):
                seq.append(b)
                if len(seq) >= 2:
                    a2(seq[-2])
                a1(b)

            def flush():
                a2(seq[-1])
            return hook, flush
        a1(0)
        for b in range(NBLK):
            if b + 1 < NBLK:
                a1(b + 1)
            a2(b)

    def load_x(s, b):
        P.add("sp", lambda e: e.dma_start(out=h[:, b, :], in_=x_d[s, 128 * b:128 * b + 128, :]),
              writes=[("h", b)], dma=f"x{b % 4}")

    for s in range(2):
        hm = AR(OFF_HM, [128, D], F32)
        if s == 0 or stage != 3:
            for b in range(NBLK):
                load_x(s, b)
        P.add("sp", lambda e: e.dma_start(out=hm[0:NMETA, :], in_=meta_d), writes=[("h", "m")], dma="xm")
        if stage == 3:
            ffn(w1i_d, w1o_d, n1_d, True, hm, "f1",
                tail=lambda: norm_to_uT(n2_d, ["m"] + list(range(NBLK)), hm, hooked=True,
                                        junk=AR(OFF_OST, [128, D]), junkres=("ost", 0)))
            P.barrier()
            mixer(hm, post_tail=lambda: norm_to_uT(n3_d, list(range(NBLK)), hm, hooked=True,
                                                    junk=AR(OFF_PTMP, [128, D]), junkres=("ptmp", 0)))
            P.barrier()
            ffn(w2i_d, w2o_d, n3_d, False, hm, "f2", do_norm=False,
                tail=lambda s=s: final_out(s, raw=False, hooked=True,
                                           after_block=((lambda b: load_x(1, b)) if s == 0 else None)))
        else:
            ffn(w1i_d, w1o_d, n1_d, True, hm, "f1")
            if stage >= 2:
                norm_to_uT(n2_d, ["m"] + list(range(NBLK)), hm)
                P.barrier()
                mixer(hm, mstop=(stage - 10 if stage >= 10 else 9))
            P.barrier()
            final_out(s, raw=True)
    P.emit()
    es.close()
    return nc


_CACHE = {}


def kernel(x, meta_tokens, ffn1_norm, ffn1_w_in, ffn1_w_out, mix_norm, w_in, b_forget, attn_sinks,
           w_branch_a, w_branch_b, w_out, ffn2_norm, ffn2_w_in, ffn2_w_out, final_norm, _stage=3, _cores=8):
    f = lambda a: np.ascontiguousarray(np.asarray(a, dtype=np.float32))
    x = f(x)
    cf, cb = make_consts()
    shared = {
        "meta": f(meta_tokens), "n1": f(ffn1_norm).reshape(1, D), "n2": f(mix_norm).reshape(1, D),
        "n3": f(ffn2_norm).reshape(1, D), "n4": f(final_norm).reshape(1, D),
        "w1i": f(ffn1_w_in)[0], "w1o": f(ffn1_w_out)[0], "w2i": f(ffn2_w_in)[0], "w2o": f(ffn2_w_out)[0],
        "wi": f(w_in)[0], "bfg": f(b_forget).reshape(1, 8), "snk": f(attn_sinks).reshape(1, 8),
        "wa": f(w_branch_a)[0], "wb": f(w_branch_b)[0], "wo": f(w_out)[0], "cf": cf, "cb": cb,
    }
    if _stage not in _CACHE:
        _CACHE[_stage] = build(_stage)
    nc = _CACHE[_stage]
    in_maps = [dict(shared, x=x[2 * c:2 * c + 2]) for c in range(_cores)]
    res = run_bass_kernel_spmd(nc, in_maps, core_ids=list(range(_cores)))
    return np.concatenate([r["out"] for r in res.results], axis=0)
```

```python
import contextlib
import numpy as np
import ml_dtypes
import concourse.bass as bass
import concourse.mybir as mybir
from concourse.bass_utils import run_bass_kernel_spmd

F32 = mybir.dt.float32
BF16 = mybir.dt.bfloat16
AF = mybir.ActivationFunctionType
ALU = mybir.AluOpType

D = 1024
SEQ = 2048
NBLK = 16
NMETA = 16
TCOLS = SEQ + NMETA
DFF = 2816
NCH = 22
PASSES = [(0, 6), (6, 6), (12, 5), (17, 5)]
EPS = 1e-6
SLOPES = [2.0 ** (-(h + 1)) for h in range(8)]
ENGS = ("pe", "act", "dve", "pool", "sp")
ARENA_BYTES = 98304
DBG_SWA = 9

CF_TRI, CF_ONES, CF_IOTA, CF_BCUR, CF_BPREV, CF_BMETA, CF_N = 0, 128, 256, 384, 392, 400, 528
CB_ID, CB_MCUR, CB_MPREV, CB_OLO, CB_OHI, CB_N = 0, 128, 640, 1152, 1280, 1408


def make_consts():
    cf = np.zeros((128, CF_N), np.float32)
    p = np.arange(128)
    cf[:, CF_TRI:CF_TRI + 128] = (p[:, None] <= p[None, :]).astype(np.float32)
    cf[:, CF_ONES:CF_ONES + 128] = 1.0
    cf[:, CF_IOTA:CF_IOTA + 128] = p[None, :].astype(np.float32)
    sl = np.array(SLOPES, np.float32)
    cf[:, CF_BCUR:CF_BCUR + 8] = p[:, None] * sl[None, :]
    cf[:, CF_BPREV:CF_BPREV + 8] = (p[:, None] - 128.0) * sl[None, :]
    bm = np.zeros((128, 16, 8), np.float32)
    for i in range(16):
        bm[:, i, :] = (p[:, None] - 16.0 - 128.0 * i) * sl[None, :]
    cf[:, CF_BMETA:CF_BMETA + 128] = bm.reshape(128, 128)
    cb = np.zeros((128, CB_N), np.float32)
    cb[:, CB_ID:CB_ID + 128] = np.eye(128)
    mc = (p[:, None] <= p[None, :]).astype(np.float32)
    mp = (p[:, None] > p[None, :]).astype(np.float32)
    cb[:, CB_MCUR:CB_MCUR + 512] = np.tile(mc, (1, 4))
    cb[:, CB_MPREV:CB_MPREV + 512] = np.tile(mp, (1, 4))
    cb[:, CB_OLO:CB_OLO + 64] = 1.0
    cb[:, CB_OHI + 64:CB_OHI + 128] = 1.0
    return cf, cb.astype(ml_dtypes.bfloat16)


class Op:
    __slots__ = ("idx", "eng", "fn", "reads", "writes", "dma", "deps", "signal", "semval", "semname", "attach")

    def __init__(self, idx, eng, fn, reads, writes, dma):
        self.idx, self.eng, self.fn, self.reads, self.writes, self.dma = idx, eng, fn, reads, writes, dma
        self.deps = set()
        self.signal = False
        self.semval = 0
        self.semname = None
        self.attach = False


class Prog:
    def __init__(self, nc):
        self.nc = nc
        self.ops = []
        self.last_writer = {}
        self.readers = {}
        self.last_on = {}

    def add(self, eng, fn, reads=(), writes=(), dma=None, full=False, attach=False):
        if dma is not None:
            writes = tuple(writes) + (("__slot", dma),)
        op = Op(len(self.ops), eng, fn, tuple(reads), tuple(writes), dma)
        op.attach = attach
        deps = set()
        for r in op.reads:
            w = self.last_writer.get(r)
            if w is not None:
                deps.add(w)
        for r in op.writes:
            w = self.last_writer.get(r)
            if w is not None:
                deps.add(w)
            deps.update(self.readers.get(r, ()))
        for d in deps:
            dop = self.ops[d]
            if dop.dma is None and dop.eng == eng and dma is None:
                if eng == "pe":
                    continue
                if not full and not any(self.last_writer.get(r) == d for r in op.reads):
                    continue
            op.deps.add(d)
        for r in op.reads:
            self.readers.setdefault(r, []).append(op.idx)
        for r in op.writes:
            self.last_writer[r] = op.idx
            self.readers[r] = []
        self.ops.append(op)
        if dma is None:
            self.last_on[eng] = op.idx
        return op

    def barrier(self):
        lasts = dict(self.last_on)
        dmas = [idx for (k, idx) in self.last_writer.items() if isinstance(k, tuple) and k[0] == "__slot"]
        for e in ENGS:
            op = Op(len(self.ops), e, None, (), (), None)
            for e2, idx in lasts.items():
                if e2 != e:
                    op.deps.add(idx)
            op.deps.update(dmas)
            self.ops.append(op)

    def emit(self):
        nc, ops = self.nc, self.ops
        for op in ops:
            best = {}
            keep = set()
            for d in op.deps:
                dop = ops[d]
                if dop.dma is not None:
                    keep.add(d)
                elif d > best.get(dop.eng, -1):
                    best[dop.eng] = d
            keep.update(best.values())
            op.deps = keep
            for d in keep:
                ops[d].signal = True
        cnt = {}
        for op in ops:
            if op.dma is not None:
                op.signal = True
                op.semname = "d_" + op.dma
                cnt[op.semname] = cnt.get(op.semname, 0) + 16
                op.semval = cnt[op.semname]
            elif op.signal:
                op.semname = "e_" + op.eng
                cnt[op.semname] = cnt.get(op.semname, 0) + 1
                op.semval = cnt[op.semname]
        semnames = sorted(cnt)
        with contextlib.ExitStack() as es:
            sems = {n: es.enter_context(nc.semaphore(n)) for n in semnames}
            block = es.enter_context(nc.Block())
            by_eng = {e: [op for op in ops if op.eng == e] for e in ENGS}
            final = [(n, cnt[n]) for n in semnames if n.startswith("d_")]

            def run(engname, eng):
                known = {}
                for op in by_eng[engname]:
                    waits = {}
                    for d in op.deps:
                        dop = ops[d]
                        if dop.semval > waits.get(dop.semname, 0):
                            waits[dop.semname] = dop.semval
                    need = [(n, v) for n, v in sorted(waits.items()) if known.get(n, 0) < v]
                    emb = None
                    if op.attach and need:
                        emb = need.pop()
                    for n, v in need:
                        eng.wait_ge(sems[n], v)
                        known[n] = v
                    if op.fn is None:
                        continue
                    if op.attach:
                        if emb is not None:
                            known[emb[0]] = emb[1]
                        ins = op.fn(eng, None if emb is None else (sems[emb[0]], emb[1]))
                    else:
                        ins = op.fn(eng)
                    if op.signal:
                        ins.then_inc(sems[op.semname], 16 if op.dma is not None else 1)
                if engname == "sp":
                    for n, v in final:
                        eng.wait_ge(sems[n], v)

            @block.tensor
            def _(e):
                run("pe", e)

            @block.scalar
            def _(e):
                run("act", e)

            @block.vector
            def _(e):
                run("dve", e)

            @block.gpsimd
            def _(e):
                run("pool", e)

            @block.sync
            def _(e):
                run("sp", e)


def build(stage=3):
    nc = bass.Bass("TRN2", target_bir_lowering=False)
    dt_in = lambda n, s, d=F32: nc.dram_tensor(n, s, d, kind="ExternalInput").ap()
    x_d = dt_in("x", [2, SEQ, D])
    meta_d = dt_in("meta", [NMETA, D])
    n1_d, n2_d, n3_d, n4_d = (dt_in(n, [1, D]) for n in ("n1", "n2", "n3", "n4"))
    w1i_d, w1o_d = dt_in("w1i", [D, 2 * DFF]), dt_in("w1o", [DFF, D])
    w2i_d, w2o_d = dt_in("w2i", [D, 2 * DFF]), dt_in("w2o", [DFF, D])
    wi_d = dt_in("wi", [D, 4360])
    bf_d, sk_d = dt_in("bfg", [1, 8]), dt_in("snk", [1, 8])
    wa_d, wb_d, wo_d = dt_in("wa", [512, D]), dt_in("wb", [512, D]), dt_in("wo", [D, D])
    cf_d, cb_d = dt_in("cf", [128, CF_N]), dt_in("cb", [128, CB_N], BF16)
    out_d = nc.dram_tensor("out", [2, SEQ, D], F32, kind="ExternalOutput").ap()

    es = contextlib.ExitStack()
    sb = lambda n, s, d: es.enter_context(nc.sbuf_tensor(n, s, d))
    h = sb("h", [128, NBLK, D], F32)
    uT = sb("uT", [128, 8, TCOLS], BF16)
    cf = sb("cf_s", [128, CF_N], F32)
    cb = sb("cb_s", [128, CB_N], BF16)
    gb = sb("gb", [128, D], F32)
    ss = sb("ss", [128, 17], F32)
    rs = sb("rs", [128, 17], F32)
    epsc = sb("epsc", [128, 1], F32)
    bfg = sb("bfg_s", [128, 8], F32)
    snk = sb("snk_s", [128, 8], F32)
    sinktab = sb("sinktab", [128, 512], F32)
    arena = sb("arena", [128, ARENA_BYTES // 2], BF16)
    ps = [es.enter_context(nc.psum_tensor(f"ps{i}", [128, 512], F32)) for i in range(8)]
    pt = [ps[6 + i][:, 0:256].bitcast(BF16) for i in range(2)]

    def AR(off, shape, dtype=BF16):
        n = int(np.prod(shape[1:]))
        nb = n * (4 if dtype == F32 else 2)
        assert off % 4 == 0 and off + nb <= ARENA_BYTES, (off, nb)
        v = arena[:, off // 2:(off + nb) // 2]
        if dtype == F32:
            v = v.bitcast(F32)
        if len(shape) == 2:
            return v
        names = " ".join(f"a{i}" for i in range(len(shape) - 1))
        kw = {f"a{i}": shape[i + 1] for i in range(len(shape) - 2)}
        return v.rearrange(f"p ({names}) -> p {names}", **kw)

    P = Prog(nc)
    rot = {}

    def nxt(name, n):
        rot[name] = (rot.get(name, -1) + 1) % n
        return rot[name]

    P.add("sp", lambda e: e.dma_start(out=cf[:], in_=cf_d), writes=["cf"], dma="cf")
    P.add("sp", lambda e: e.dma_start(out=cb[:], in_=cb_d), writes=["cb"], dma="cb")
    P.add("sp", lambda e: e.dma_start(out=bfg[:], in_=bf_d.broadcast_to([128, 8])), writes=["bfg"], dma="bfg")
    P.add("sp", lambda e: e.dma_start(out=snk[:], in_=sk_d.broadcast_to([128, 8])), writes=["snk"], dma="snk")
    P.add("pool", lambda e: e.memset(epsc[:], EPS), writes=["epsc"])
    onec = sb("onec", [128, 1], F32)
    P.add("pool", lambda e: e.memset(onec[:], 1.0), writes=["onec"])
    ALLSS = ["ss"] + [("ss", b) for b in ["m"] + list(range(NBLK))]
    for g in range(4):
        for hh in range(2):
            hd = 4 * hh + g
            P.add("act", lambda e, g=g, hh=hh, hd=hd: e.activation(
                out=sinktab[64 * hh:64 * hh + 64, 128 * g:128 * g + 128],
                in_=cf[64 * hh:64 * hh + 64, CF_IOTA:CF_IOTA + 128], func=AF.Exp,
                bias=snk[64 * hh:64 * hh + 64, hd:hd + 1], scale=SLOPES[hd]),
                reads=["cf", "snk"], writes=["sinktab"])

    def tile_cols(t):
        if t == "m":
            return 0, NMETA, ["m"]
        return NMETA + 512 * t, 512, [4 * t + j for j in range(4)]

    def blk_cols(b):
        if b == "m":
            return 0, NMETA
        return NMETA + 128 * b, 128

    def cast_dma(dst, src, res, slot):
        P.add("pool", lambda e: e.dma_start(out=dst, in_=src), writes=[res], dma=slot)

    def mm_group(out, pairs, reads, writes):
        def fn(e):
            n = len(pairs)
            ins = None
            for i, (l, r) in enumerate(pairs):
                ins = e.matmul(out, lhsT=l, rhs=r, start=(i == 0), stop=(i == n - 1))
            return ins
        P.add("pe", fn, reads=reads, writes=writes)

    evac_flip = [0]

    def evac(out, in_, reads, writes, eng=None):
        if eng is None:
            evac_flip[0] ^= 1
            eng = "dve" if evac_flip[0] else "act"
        if eng == "act":
            P.add("act", lambda e: e.copy(out=out, in_=in_), reads=reads, writes=writes)
        else:
            P.add(eng, lambda e: e.tensor_copy(out=out, in_=in_), reads=reads, writes=writes)

    def norm_to_uT(gain_d, blocks, hm, hooked=False, junk=None, junkres="junk"):
        ut = [AR(OFF_UT + 2048 * i, [128, D]) for i in range(2)]
        P.add("sp", lambda e: e.dma_start(out=gb[:], in_=gain_d.broadcast_to([128, D])), writes=["gb"], dma="gb")
        P.add("pool", lambda e: e.memset(ss[:], 0.0), writes=ALLSS)
        jmap = {}

        def geom(b):
            c0, nr = blk_cols(b)
            src = hm[0:nr, :] if b == "m" else h[:, b, :]
            col = 16 if b == "m" else b
            return c0, nr, src, col

        def a1(b, j=None):
            c0, nr, src, col = geom(b)
            if junk is not None:
                out, ores = junk[0:nr, :], junkres
            else:
                out, ores = ut[j][0:nr, :], ("ut", j)
            P.add("act", lambda e: e.activation(out=out, in_=src, func=AF.Square, accum_out=ss[0:nr, col:col + 1]),
                  reads=[("h", b), ("ss", b)], writes=[("ss", b), ores], full=True)
            P.add("act", lambda e: e.activation(
                out=rs[0:nr, col:col + 1], in_=ss[0:nr, col:col + 1], func=AF.Sqrt, bias=epsc[0:nr, 0:1], scale=1.0 / D),
                reads=[("ss", b), "epsc"], writes=[("rs", b)])

        def a2(b, j):
            c0, nr, src, col = geom(b)
            jmap[b] = j
            P.add("dve", lambda e: e.reciprocal(out=rs[0:nr, col:col + 1], in_=rs[0:nr, col:col + 1]),
                  reads=[("rs", b)], writes=[("rs", b)])
            P.add("dve", lambda e: e.scalar_tensor_tensor(
                out=ut[j][0:nr, :], in0=src, scalar=rs[0:nr, col:col + 1], in1=gb[0:nr, :],
                op0=ALU.mult, op1=ALU.mult), reads=[("h", b), ("rs", b), "gb"], writes=[("ut", j)])

        def stage_a(b):
            j = nxt("ut", 2)
            a1(b, j)
            a2(b, j)

        def stage_b(b):
            c0, nr = blk_cols(b)
            j = jmap[b]
            for half in range(2):
                def fn(e, half=half):
                    ins = None
                    for c in range(4):
                        cc = 4 * half + c
                        ins = e.transpose(pt[half][:, 128 * c:128 * c + nr], ut[j][0:nr, 128 * cc:128 * cc + 128],
                                          cb[0:nr, CB_ID:CB_ID + nr])
                    return ins
                P.add("pe", fn, reads=[("ut", j), "cb"], writes=[("ps", 6 + half)])
                src_v = pt[half][:, :].rearrange("p (c n) -> p c n", c=4)[:, :, 0:nr]
                evac(uT[:, 4 * half:4 * half + 4, c0:c0 + nr], src_v, [("ps", 6 + half)], [("uT", b, half)],
                     eng=("act" if half == 0 else "dve"))
        if hooked:
            assert junk is not None
            seq = []

            def hook(b):
                seq.append(b)
                n = len(seq)
                if n >= 3:
                    stage_b(seq[n - 3])
                if n >= 2:
                    a2(seq[n - 2], nxt("ut", 2))
                a1(b)

            def flush():
                n = len(seq)
                if n >= 2:
                    stage_b(seq[n - 2])
                a2(seq[n - 1], nxt("ut", 2))
                stage_b(seq[n - 1])
            return hook, flush
        stage_a(blocks[0])
        for i, b in enumerate(blocks):
            if i + 1 < len(blocks):
                stage_a(blocks[i + 1])
            stage_b(b)

    def uT_res(blks):
        return [("uT", b, hf) for b in blks for hf in range(2)]

    def ffn(w_in_d, w_out_d, gain_d, with_meta, hm, tagp, do_norm=True, tail=None):
        blocks = (["m"] if with_meta else []) + list(range(NBLK))
        if do_norm:
            norm_to_uT(gain_d, blocks, hm)
        tiles = (["m"] if with_meta else []) + [0, 1, 2, 3]
        act = AR(OFF_ACT, [128, 6, TCOLS])
        wis = [AR(OFF_WI + 8192 * i, [128, 2, 8, 256]) for i in range(3)]
        wos = [AR(OFF_WO + 12288 * i, [128, 6, D]) for i in range(2)]
        tmp = [AR(OFF_TMP + 2048 * i, [128, 512], F32) for i in range(2)]
        for (m0, nch) in PASSES:
            wslot = nxt("wo", 2)
            wo_t = wos[wslot]
            ml = 0
            while ml < nch:
                ns = min(2, nch - ml)
                s = nxt("wi", 3)
                wi_t = wis[s]
                c0 = (m0 + ml) * 128
                cast_dma(wi_t[:, 0, :, 0:ns * 128], w_in_d[:, c0:c0 + ns * 128].rearrange("(k p) n -> p k n", p=128),
                         ("wi", s, 0), f"wi{s}g")
                cast_dma(wi_t[:, 1, :, 0:ns * 128],
                         w_in_d[:, DFF + c0:DFF + c0 + ns * 128].rearrange("(k p) n -> p k n", p=128),
                         ("wi", s, 1), f"wi{s}u")
                if ml == 0:
                    cast_dma(wo_t[:, 0:nch, :], w_out_d[m0 * 128:(m0 + nch) * 128, :].rearrange("(k p) n -> p k n", p=128),
                             ("wo", wslot), f"wo{wslot}")
                for q in range(ns):
                    mloc = ml + q
                    for t in tiles:
                        tc0, tn, tb = tile_cols(t)
                        a = nxt("psA", 2)
                        pA, pB = ps[2 * a], ps[2 * a + 1]
                        for which, pp in ((0, pA), (1, pB)):
                            mm_group(pp[:, 0:tn],
                                     [(wi_t[:, which, k, q * 128:(q + 1) * 128], uT[:, k, tc0:tc0 + tn]) for k in range(8)],
                                     reads=[("wi", s, which)] + uT_res(tb), writes=[("ps", 2 * a + which)])
                        j = nxt("tmp", 2)
                        P.add("act", lambda e, j=j, pA=pA, tn=tn: e.activation(out=tmp[j][:, 0:tn], in_=pA[:, 0:tn], func=AF.Silu),
                              reads=[("ps", 2 * a)], writes=[("tmp", j)])
                        P.add("dve", lambda e, j=j, pB=pB, tn=tn, mloc=mloc, tc0=tc0: e.tensor_tensor(
                            out=act[:, mloc, tc0:tc0 + tn], in0=tmp[j][:, 0:tn], in1=pB[:, 0:tn], op=ALU.mult),
                            reads=[("tmp", j), ("ps", 2 * a + 1)], writes=[("act", mloc, t)])
                ml += ns
            hook = flush = None
            if tail is not None and (m0, nch) == PASSES[-1]:
                hook, flush = tail()
            for b in blocks:
                bc0, nr = blk_cols(b)
                t = "m" if b == "m" else b // 4
                for half in range(2):
                    o = 4 + nxt("psO", 2)
                    mm_group(ps[o][0:nr, :],
                             [(act[:, k, bc0:bc0 + nr], wo_t[:, k, 512 * half:512 * half + 512]) for k in range(nch)],
                             reads=[("act", k, t) for k in range(nch)] + [("wo", wslot)], writes=[("ps", o)])
                    dst = hm[0:nr, 512 * half:512 * half + 512] if b == "m" else h[:, b, 512 * half:512 * half + 512]
                    P.add("dve", lambda e, dst=dst, o=o, nr=nr: e.scalar_tensor_tensor(
                        out=dst, in0=ps[o][0:nr, :], scalar=0.5, in1=dst, op0=ALU.mult, op1=ALU.add),
                        reads=[("ps", o), ("h", b)], writes=[("h", b)])
                if hook is not None:
                    hook(b)
            if flush is not None:
                flush()

    OFF_HM = 0
    OFF_UT = 4096
    OFF_ACT = 8192
    OFF_WI = OFF_ACT + 24768
    OFF_WO = OFF_WI + 24576
    OFF_TMP = OFF_WO + 24576
    OFF_OST = OFF_TMP + 4096
    assert OFF_OST + 8192 <= ARENA_BYTES
    OFF_OA = 8192
    OFF_OB = OFF_OA + 16384
    OFF_LF = OFF_OB + 16384
    OFF_BIASF = OFF_LF + 3264
    OFF_PT = OFF_BIASF + 2176
    OFF_DEN = OFF_PT + 4096
    OFF_X = OFF_DEN + 2048
    assert OFF_X + 45632 <= ARENA_BYTES, OFF_X
    OFF_PW = OFF_OB + 16384
    OFF_MIX = OFF_PW + 24576
    assert OFF_MIX + 32768 <= ARENA_BYTES
    OFF_PTMP = 0

    def mixer(hm, mstop=9, post_tail=None):
        ablocks = ["m"] + list(range(NBLK))
        NPT = 4
        PT = [AR(OFF_PT + 1024 * i, [128, 512]) for i in range(NPT)]
        den = [AR(OFF_DEN, [128, 512], F32)]
        OaT = AR(OFF_OA, [128, 4, SEQ])
        ObT = AR(OFF_OB, [128, 4, SEQ])
        lfv = [AR(OFF_LF + 544 * i, [128, 136], F32) for i in range(6)]
        xb, lt, tots, offv, cum = lfv[0], lfv[1], lfv[2], lfv[3], lfv[4]
        biasF = AR(OFF_BIASF, [128, 17, 4, 8], F32)
        wf = lfv[5][:, :].bitcast(BF16)[:, 0:64].rearrange("p (k n) -> p k n", k=8)
        cast_dma(wf, wi_d[:, 2304:2312].rearrange("(k p) n -> p k n", p=128), "wf", "wf")

        def f_fn(e):
            ins = None
            for bi, b in enumerate(ablocks):
                c0, nr = blk_cols(b)
                for k in range(8):
                    ins = e.matmul(ps[0][0:nr, 8 * bi:8 * bi + 8], lhsT=uT[:, k, c0:c0 + nr], rhs=wf[:, k, :],
                                   start=(k == 0), stop=(k == 7))
            return ins
        P.add("pe", f_fn, reads=["wf"] + uT_res(ablocks), writes=[("ps", 0)])
        P.add("pool", lambda e: e.memset(lt[:], 0.0), writes=["lt"])
        for (r0, r1, c0, c1) in ((0, 16, 0, 8), (0, 128, 8, 136)):
            bb = bfg[r0:r1, :] if c1 == 8 else bfg[r0:r1, :].unsqueeze(1).to_broadcast([r1 - r0, 16, 8])
            i0 = ps[0][r0:r1, c0:c1] if c1 == 8 else ps[0][r0:r1, c0:c1].rearrange("p (b h) -> p b h", h=8)
            o0 = xb[r0:r1, c0:c1] if c1 == 8 else xb[r0:r1, c0:c1].rearrange("p (b h) -> p b h", h=8)
            P.add("dve", lambda e, bb=bb, i0=i0, o0=o0: e.tensor_tensor(out=o0, in0=i0, in1=bb, op=ALU.add),
                  reads=[("ps", 0), "bfg"], writes=["xb"])
            P.add("act", lambda e, r0=r0, r1=r1, c0=c0, c1=c1: e.activation(
                out=xb[r0:r1, c0:c1], in_=xb[r0:r1, c0:c1], func=AF.Exp, scale=-1.0), reads=["xb"], writes=["xb"])
            P.add("act", lambda e, r0=r0, r1=r1, c0=c0, c1=c1: e.activation(
                out=lt[r0:r1, c0:c1], in_=xb[r0:r1, c0:c1], func=AF.Ln, bias=onec[r0:r1, 0:1]), reads=["xb", "lt", "onec"], writes=["lt"])
        P.add("pe", lambda e: e.matmul(ps[1][:, 0:136], lhsT=cf[:, CF_ONES:CF_ONES + 128], rhs=lt[:, :], start=True, stop=True),
              reads=["lt", "cf"], writes=[("ps", 1)])
        P.add("pe", lambda e: e.matmul(ps[2][:, 0:136], lhsT=cf[:, CF_TRI:CF_TRI + 128], rhs=lt[:, :], start=True, stop=True),
              reads=["lt", "cf"], writes=[("ps", 2)])
        P.add("dve", lambda e: e.tensor_copy(out=tots[:, :], in_=ps[1][:, 0:136]), reads=[("ps", 1)], writes=["tots"])
        P.add("pool", lambda e: e.memset(offv[:, :], 0.0), writes=["offv"])
        for b in range(1, 17):
            P.add("dve", lambda e, b=b: e.tensor_tensor(out=offv[:, 8 * b:8 * b + 8], in0=offv[:, 8 * b - 8:8 * b],
                                                        in1=tots[:, 8 * b - 8:8 * b], op=ALU.add),
                  reads=["offv", "tots"], writes=["offv"])
        P.add("dve", lambda e: e.tensor_tensor(out=cum[:, :], in0=ps[2][:, 0:136], in1=offv[:, :], op=ALU.add),
              reads=[("ps", 2), "offv"], writes=["cum"])
        cum3 = cum[:, :].rearrange("p (b h) -> p b h", h=8)
        for j in range(4):
            ref = 4 * j + 3
            for hd in range(8):
                P.add("dve", lambda e, j=j, hd=hd, ref=ref: e.tensor_scalar(
                    out=biasF[:, :, j, hd], in0=cum3[:, :, hd], scalar1=offv[:, 8 * ref + hd:8 * ref + hd + 1],
                    scalar2=None, op0=ALU.subtract), reads=["cum", "offv"], writes=["biasF"])
        if mstop <= 1:
            return

        wqa = AR(OFF_X, [128, 8, 4, 2, 64])
        wka = AR(OFF_X + 8192, [128, 8, 128])
        wva = AR(OFF_X + 10240, [128, 8, 128])
        QaT = AR(OFF_X + 12288, [128, 4, SEQ])
        Ka2 = [AR(OFF_X + 28672 + 4128 * i, [128, TCOLS]) for i in range(2)]
        Va = [AR(OFF_X + 36928 + 4352 * i, [128, 17, 128]) for i in range(2)]
        P.add("pool", lambda e: e.memset(Ka2[0][64:128, :], 0.0), writes=[("KaT", b) for b in ablocks])
        P.add("pool", lambda e: e.memset(Ka2[1][0:64, :], 0.0), writes=[("KaT", b) for b in ablocks])
        for hh in range(2):
            for g in range(4):
                cq = 256 * hh + 64 * g
                cast_dma(wqa[:, :, g, hh, :], wi_d[:, cq:cq + 64].rearrange("(k p) d -> p k d", p=128),
                         ("wqa", hh, g), f"wqa{hh}{g}")
        cast_dma(wka, wi_d[:, 512:640].rearrange("(k p) n -> p k n", p=128), "wka", "wka")
        cast_dma(wva, wi_d[:, 640:768].rearrange("(k p) n -> p k n", p=128), "wva", "wva")
        for i in range(2):
            P.add("pool", lambda e, i=i: e.memset(Va[i][:, :, :], 0.0), writes=[("Va", i)])
        for g in range(4 if DBG_SWA >= 0.2 else 0):
            for t in range(4):
                tc0, tn, tb = tile_cols(t)
                a = nxt("psQ", 2)
                mm_group(ps[a][:, :], [(wqa[:, k, g].rearrange("p a d -> p (a d)"), uT[:, k, tc0:tc0 + tn]) for k in range(8)],
                         reads=[("wqa", 0, g), ("wqa", 1, g)] + uT_res(tb), writes=[("ps", a)])
                evac(QaT[:, g, 512 * t:512 * t + 512], ps[a][:, :], [("ps", a)], [("QaT", 4 * t + j) for j in range(4)])
        for t in (["m", 0, 1, 2, 3] if DBG_SWA >= 0.3 else []):
            tc0, tn, tb = tile_cols(t)
            a = nxt("psQ", 2)
            mm_group(ps[a][:, 0:tn], [(wka[:, k, :], uT[:, k, tc0:tc0 + tn]) for k in range(8)],
                     reads=["wka"] + uT_res(tb), writes=[("ps", a)])
            evac(Ka2[0][0:64, tc0:tc0 + tn], ps[a][0:64, 0:tn], [("ps", a)], [("KaT", b) for b in tb], eng="dve")
            evac(Ka2[1][64:128, tc0:tc0 + tn], ps[a][64:128, 0:tn], [("ps", a)], [("KaT", b) for b in tb], eng="dve")
        for bi, b in enumerate(ablocks if DBG_SWA >= 0.4 else []):
            if DBG_SWA == 0.45 and b == "m":
                continue
            c0, nr = blk_cols(b)
            a = nxt("psQ", 2)
            mm_group(ps[a][0:nr, 0:128], [(uT[:, k, c0:c0 + nr], wva[:, k, :]) for k in range(8)],
                     reads=["wva"] + uT_res([b]), writes=[("ps", a)])
            evac(Va[0][0:nr, bi, 0:64], ps[a][0:nr, 0:64], [("ps", a)], [("Va", 0)], eng="dve")
            evac(Va[1][0:nr, bi, 64:128], ps[a][0:nr, 64:128], [("ps", a)], [("Va", 1)], eng="dve")
        SB = [0, 1, 2, 5]
        LOOK = 3
        ODS = [(3, 4), (6, 7)]
        steps = []
        for i in range(NBLK):
            roles = [("meta", 0, 16, 0)] + ([("prev", i, 128, NMETA + 128 * (i - 1))] if i >= 1 else []) + \
                    [("cur", i + 1, 128, NMETA + 128 * i)]
            n_i = 2 * len(roles)
            cnt = 0
            for kvh in range(2):
                for (role, kb, nk, kc0) in roles:
                    steps.append((i, kvh, role, kb, nk, kc0, cnt == 0, cnt == n_i - 1))
                    cnt += 1

        def swa_qk(n):
            i, kvh, role, kb, nk, kc0, st, last = steps[n]
            sidx = SB[n % 4]
            pS = ps[sidx]
            base = 64 * kvh
            kres = ("KaT", "m") if role == "meta" else ("KaT", kb - 1)
            def qk(e, w):
                ins = e.matmul(pS[0:nk, :].rearrange("p (g q) -> p g q", g=4), lhsT=Ka2[kvh][:, kc0:kc0 + nk],
                               rhs=QaT[:, :, 128 * i:128 * i + 128], start=True, stop=True)
                if w is not None:
                    ins._wait_ge(*w)
                return ins
            P.add("pe", qk, reads=[kres, ("QaT", i)], writes=[("ps", sidx)], attach=True)

        def swa_rest(n):
            i, kvh, role, kb, nk, kc0, st, last = steps[n]
            sidx = SB[n % 4]
            pS = ps[sidx]
            pj = n % NPT
            iO, iD = ODS[i % 2]
            pO, pD = ps[iO], ps[iD]
            for g in range(4):
                hd = 4 * kvh + g
                if role == "meta":
                    bcol = cf[0:nk, CF_BMETA + 8 * i + hd:CF_BMETA + 8 * i + hd + 1]
                elif role == "prev":
                    bcol = cf[0:nk, CF_BPREV + hd:CF_BPREV + hd + 1]
                else:
                    bcol = cf[0:nk, CF_BCUR + hd:CF_BCUR + hd + 1]
                P.add("act", lambda e, g=g, bcol=bcol: e.activation(
                    out=PT[pj][0:nk, 128 * g:128 * g + 128], in_=pS[0:nk, 128 * g:128 * g + 128],
                    func=AF.Exp, bias=bcol, scale=0.125), reads=[("ps", sidx), "cf"], writes=[("PT", pj)])
            if role != "meta":
                mo = CB_MPREV if role == "prev" else CB_MCUR
                P.add("dve", lambda e: e.tensor_tensor(
                    out=PT[pj][:, :], in0=PT[pj][:, :], in1=cb[:, mo:mo + 512], op=ALU.mult),
                    reads=[("PT", pj), "cb"], writes=[("PT", pj)])
            oo = CB_OLO if kvh == 0 else CB_OHI

            def pv(e, w):
                i0 = e.matmul(pO[:, :], lhsT=Va[kvh][0:nk, kb, :], rhs=PT[pj][0:nk, :], start=st, stop=last)
                if w is not None:
                    i0._wait_ge(*w)
                return e.matmul(pD[:, :], lhsT=cb[0:nk, oo:oo + 128], rhs=PT[pj][0:nk, :], start=st, stop=last)
            P.add("pe", pv, reads=[("PT", pj), ("Va", kvh), "cb"], writes=[("ps", iO), ("ps", iD)], attach=True)
            if defer:
                defer.pop(0)()
            if last:
                dj = 0
                defer.append(lambda: P.add("dve", lambda e: e.tensor_tensor(out=den[dj][:, :], in0=pD[:, :], in1=sinktab[:, :], op=ALU.add),
                                           reads=[("ps", iD), "sinktab"], writes=[("den", dj)]))
                defer.append(lambda: P.add("dve", lambda e: e.reciprocal(out=den[dj][:, :], in_=den[dj][:, :]),
                                           reads=[("den", dj)], writes=[("den", dj)]))
                defer.append(lambda: P.add("dve", lambda e: e.tensor_tensor(
                    out=OaT[:, :, 128 * i:128 * i + 128], in0=pO[:, :].rearrange("p (g q) -> p g q", g=4),
                    in1=den[dj][:, :].rearrange("p (g q) -> p g q", g=4), op=ALU.mult),
                    reads=[("ps", iO), ("den", dj)], writes=[("OaT", i // 4)]))
        defer = []
        for n in range(len(steps) + LOOK):
            if n < len(steps):
                swa_qk(n)
            if n - LOOK >= 0:
                swa_rest(n - LOOK)
        while defer:
            defer.pop(0)()
        P.barrier()
        if mstop <= 2:
            return

        wqb = AR(OFF_X, [128, 8, 512])
        wkb = AR(OFF_X + 8192, [128, 8, 512])
        wvb = AR(OFF_X + 16384, [128, 8, 512])
        QbT = AR(OFF_X + 24576, [128, SEQ])
        Kb2 = [AR(OFF_X + 28672 + 4128 * i, [128, TCOLS]) for i in range(2)]
        Vb = [AR(OFF_X + 36928 + 4352 * i, [128, 17, 128]) for i in range(2)]
        P.add("pool", lambda e: e.memset(Kb2[0][64:128, :], 0.0), writes=[("KbT", b) for b in ablocks])
        P.add("pool", lambda e: e.memset(Kb2[1][0:64, :], 0.0), writes=[("KbT", b) for b in ablocks])
        for c in range(4):
            for (wt, col0, nm) in ((wqb, 768, "wqb"), (wkb, 1280, "wkb"), (wvb, 1792, "wvb")):
                cast_dma(wt[:, :, 128 * c:128 * c + 128],
                         wi_d[:, col0 + 128 * c:col0 + 128 * c + 128].rearrange("(k p) n -> p k n", p=128), (nm, c), nm)
        for i in range(2):
            P.add("pool", lambda e, i=i: e.memset(Vb[i][:, :, :], 0.0), writes=[("Vb", i, bi) for bi in range(17)])
        for c in range(4):
            for t in range(4):
                tc0, tn, tb = tile_cols(t)
                a = nxt("psQ", 2)
                mm_group(ps[a][:, :], [(wqb[:, k, 128 * c:128 * c + 128], uT[:, k, tc0:tc0 + tn]) for k in range(8)],
                         reads=[("wqb", c)] + uT_res(tb), writes=[("ps", a)])
                evac(QbT[:, 512 * t:512 * t + 512], ps[a][:, :], [("ps", a)], [("QbT", t)], eng="dve")
            for t in ["m", 0, 1, 2, 3]:
                tc0, tn, tb = tile_cols(t)
                a = nxt("psQ", 2)
                mm_group(ps[a][:, 0:tn], [(wkb[:, k, 128 * c:128 * c + 128], uT[:, k, tc0:tc0 + tn]) for k in range(8)],
                         reads=[("wkb", c)] + uT_res(tb), writes=[("ps", a)])
                evac(Kb2[0][0:64, tc0:tc0 + tn], ps[a][0:64, 0:tn], [("ps", a)], [("KbT", b) for b in tb], eng="dve")
                evac(Kb2[1][64:128, tc0:tc0 + tn], ps[a][64:128, 0:tn], [("ps", a)], [("KbT", b) for b in tb], eng="dve")
            for bi, b in enumerate(ablocks):
                c0, nr = blk_cols(b)
                a = nxt("psQ", 2)
                mm_group(ps[a][0:nr, 0:128], [(uT[:, k, c0:c0 + nr], wvb[:, k, 128 * c:128 * c + 128]) for k in range(8)],
                         reads=[("wvb", c)] + uT_res([b]), writes=[("ps", a)])
                evac(Vb[0][0:nr, bi, 0:64], ps[a][0:nr, 0:64], [("ps", a)], [("Vb", 0, bi)], eng="dve")
                evac(Vb[1][0:nr, bi, 64:128], ps[a][0:nr, 64:128], [("ps", a)], [("Vb", 1, bi)], eng="dve")
            fsteps = []
            for j in range(4):
                kbs = [0] + [1 + r for r in range(4 * j + 4)]
                n_j = 2 * len(kbs)
                cnt = 0
                for kb in kbs:
                    for hh in range(2):
                        fsteps.append((j, kb, hh, cnt == 0, cnt == n_j - 1))
                        cnt += 1

            def fparams(n):
                j, kb, hh, st, last = fsteps[n]
                if kb == 0:
                    nk, kc0, c0, kres = 16, 0, 0, ("KbT", "m")
                else:
                    r = kb - 1
                    nk, kc0, kres = 128, NMETA + 128 * r, ("KbT", r)
                    c0 = 128 * (r - 4 * j) if r >= 4 * j else 0
                diag = kb >= 1 and (kb - 1) >= 4 * j
                return j, kb, hh, st, last, nk, kc0, c0, kres, diag

            def fox_qk(n, c=c):
                j, kb, hh, st, last, nk, kc0, c0, kres, diag = fparams(n)
                sidx = SB[n % 4]
                pS = ps[sidx]
                base = 64 * hh
                def qk(e, w):
                    ins = e.matmul(pS[0:nk, c0:512], lhsT=Kb2[hh][:, kc0:kc0 + nk],
                                   rhs=QbT[:, 512 * j + c0:512 * j + 512], start=True, stop=True)
                    if w is not None:
                        ins._wait_ge(*w)
                    return ins
                P.add("pe", qk, reads=[kres, ("QbT", j)], writes=[("ps", sidx)], attach=True)

            def fox_rest(n, c=c):
                j, kb, hh, st, last, nk, kc0, c0, kres, diag = fparams(n)
                sidx = SB[n % 4]
                pS = ps[sidx]
                pj = n % NPT
                iO, iD = ODS[(4 * c + j) % 2]
                pO, pD = ps[iO], ps[iD]
                hd = 2 * c + hh
                P.add("act", lambda e: e.activation(
                    out=PT[pj][0:nk, c0:512], in_=pS[0:nk, c0:512], func=AF.Exp,
                    bias=biasF[0:nk, kb, j, hd:hd + 1], scale=0.125),
                    reads=[("ps", sidx), "biasF"], writes=[("PT", pj)])
                if diag:
                    P.add("dve", lambda e: e.tensor_tensor(
                        out=PT[pj][:, c0:c0 + 128], in0=PT[pj][:, c0:c0 + 128],
                        in1=cb[:, CB_MCUR:CB_MCUR + 128], op=ALU.mult),
                        reads=[("PT", pj), "cb"], writes=[("PT", pj)])
                oo = CB_OLO if hh == 0 else CB_OHI

                def pv(e, w):
                    i0 = e.matmul(pO[:, c0:512], lhsT=Vb[hh][0:nk, kb, :], rhs=PT[pj][0:nk, c0:512], start=st, stop=last)
                    if w is not None:
                        i0._wait_ge(*w)
                    return e.matmul(pD[:, c0:512], lhsT=cb[0:nk, oo:oo + 128], rhs=PT[pj][0:nk, c0:512], start=st, stop=last)
                P.add("pe", pv, reads=[("PT", pj), ("Vb", hh, kb), "cb"], writes=[("ps", iO), ("ps", iD)], attach=True)
                if defer:
                    defer.pop(0)()
                if last:
                    dj = 0
                    defer.append(lambda: P.add("dve", lambda e: e.reciprocal(out=den[dj][:, :], in_=pD[:, :]),
                                               reads=[("ps", iD)], writes=[("den", dj)]))
                    defer.append(lambda: P.add("dve", lambda e: e.tensor_tensor(
                        out=ObT[:, c, 512 * j:512 * j + 512], in0=pO[:, :], in1=den[dj][:, :], op=ALU.mult),
                        reads=[("ps", iO), ("den", dj)], writes=[("ObT", j)]))
            for n in range(len(fsteps) + LOOK):
                if n < len(fsteps):
                    fox_qk(n)
                if n - LOOK >= 0:
                    fox_rest(n - LOOK)
            while defer:
                defer.pop(0)()
        P.barrier()
        if mstop <= 3:
            return

        mixT = AR(OFF_MIX, [128, 8, SEQ])
        wo_t = AR(OFF_PW, [128, 8, D])
        ovl = [[("pw", 0, "a", 0), ("pw", 0, "a", 1), ("pw", 0, "b"), ("pw", 0, "ga")],
               [("pw", 0, "gb"), ("pw", 1, "a", 0), ("pw", 1, "a", 1), ("pw", 1, "b")]]

        def issue_wo(hf):
            P.add("pool", lambda e: e.dma_start(
                out=wo_t[:, 4 * hf:4 * hf + 4, :], in_=wo_d[512 * hf:512 * hf + 512, :].rearrange("(k p) n -> p k n", p=128)),
                writes=[("wout", hf)] + ovl[hf], dma=f"wout{hf}")
        ptmp = [AR(OFF_PTMP + 2048 * i, [128, 512], F32) for i in range(2)]
        def pw_tiles(st):
            sl = st % 2
            o = OFF_PW + 12288 * sl
            return sl, AR(o, [128, 4, 256]), AR(o + 2048, [128, 4, 256]), AR(o + 4096, [128, 8, 256]), AR(o + 8192, [128, 8, 256])

        def pw_issue(st):
            sl, wa_t, wb_t, wga_t, wgb_t = pw_tiles(st)
            c0 = 256 * st
            for hh in range(2):
                cast_dma(wa_t[64 * hh:64 * hh + 64, :, :],
                         wa_d[256 * hh:256 * hh + 256, c0:c0 + 256].rearrange("(g d) n -> d g n", d=64),
                         ("pw", sl, "a", hh), f"pw{sl}a{hh}")
            cast_dma(wb_t, wb_d[:, c0:c0 + 256].rearrange("(c p) n -> p c n", p=128), ("pw", sl, "b"), f"pw{sl}b")
            cast_dma(wga_t, wi_d[:, 2312 + c0:2312 + c0 + 256].rearrange("(k p) n -> p k n", p=128), ("pw", sl, "ga"), f"pw{sl}ga")
            cast_dma(wgb_t, wi_d[:, 3336 + c0:3336 + c0 + 256].rearrange("(k p) n -> p k n", p=128), ("pw", sl, "gb"), f"pw{sl}gb")
        pw_issue(0)
        for st in range(4):
            s, wa_t, wb_t, wga_t, wgb_t = pw_tiles(st)
            if st + 1 < 4:
                pw_issue(st + 1)
            if st == 3:
                issue_wo(0)
            for q in range(2):
                m = 2 * st + q
                for t in range(4):
                    tc0, tn, tb = tile_cols(t)
                    ia, ib, iga, igb = (nxt("psP", 8) for _ in range(4))
                    mm_group(ps[ia][:, :], [(wa_t[:, k, 128 * q:128 * q + 128], OaT[:, k, 512 * t:512 * t + 512]) for k in range(4)],
                             reads=[("pw", s, "a", 0), ("pw", s, "a", 1), ("OaT", t)], writes=[("ps", ia)])
                    mm_group(ps[ib][:, :], [(wb_t[:, k, 128 * q:128 * q + 128], ObT[:, k, 512 * t:512 * t + 512]) for k in range(4)],
                             reads=[("pw", s, "b"), ("ObT", t)], writes=[("ps", ib)])
                    mm_group(ps[iga][:, :], [(wga_t[:, k, 128 * q:128 * q + 128], uT[:, k, tc0:tc0 + 512]) for k in range(8)],
                             reads=[("pw", s, "ga")] + uT_res(tb), writes=[("ps", iga)])
                    mm_group(ps[igb][:, :], [(wgb_t[:, k, 128 * q:128 * q + 128], uT[:, k, tc0:tc0 + 512]) for k in range(8)],
                             reads=[("pw", s, "gb")] + uT_res(tb), writes=[("ps", igb)])
                    ja, jb = 0, 1
                    P.add("act", lambda e, ja=ja, iga=iga: e.activation(out=ptmp[ja][:, :], in_=ps[iga][:, :], func=AF.Sigmoid),
                          reads=[("ps", iga)], writes=[("ptmp", ja)])
                    P.add("act", lambda e, jb=jb, igb=igb: e.activation(out=ptmp[jb][:, :], in_=ps[igb][:, :], func=AF.Sigmoid),
                          reads=[("ps", igb)], writes=[("ptmp", jb)])
                    P.add("dve", lambda e, ja=ja, ia=ia: e.tensor_tensor(out=ptmp[ja][:, :], in0=ptmp[ja][:, :], in1=ps[ia][:, :], op=ALU.mult),
                          reads=[("ptmp", ja), ("ps", ia)], writes=[("ptmp", ja)])
                    P.add("dve", lambda e, jb=jb, ib=ib: e.tensor_tensor(out=ptmp[jb][:, :], in0=ptmp[jb][:, :], in1=ps[ib][:, :], op=ALU.mult),
                          reads=[("ptmp", jb), ("ps", ib)], writes=[("ptmp", jb)])
                    P.add("pool", lambda e, ja=ja, jb=jb, m=m, t=t: e.tensor_tensor(
                        out=mixT[:, m, 512 * t:512 * t + 512], in0=ptmp[ja][:, :], in1=ptmp[jb][:, :], op=ALU.add),
                        reads=[("ptmp", ja), ("ptmp", jb)], writes=[("mixT", t)])
        issue_wo(1)
        hook = flush = None
        if post_tail is not None:
            hook, flush = post_tail()
        for b in range(NBLK):
            for half in range(2):
                o = nxt("psP", 8)
                mm_group(ps[o][:, :], [(mixT[:, k, 128 * b:128 * b + 128], wo_t[:, k, 512 * half:512 * half + 512]) for k in range(8)],
                         reads=[("mixT", b // 4), ("wout", 0), ("wout", 1)], writes=[("ps", o)])
                dst = h[:, b, 512 * half:512 * half + 512]
                P.add("dve", lambda e, dst=dst, o=o: e.tensor_tensor(out=dst, in0=ps[o][:, :], in1=dst, op=ALU.add),
                      reads=[("ps", o), ("h", b)], writes=[("h", b)])
            if hook is not None:
                hook(b)
        if flush is not None:
            flush()

    def final_out(s, raw, hooked=False, after_block=None):
        ost = [AR(OFF_OST + 4096 * i, [128, D], F32) for i in range(2)]
        if raw:
            for b in range(NBLK):
                P.add("sp", lambda e, b=b: e.dma_start(out=out_d[s, 128 * b:128 * b + 128, :], in_=h[:, b, :]),
                      reads=[("h", b)], dma=f"out{b % 2}")
            return
        P.add("sp", lambda e: e.dma_start(out=gb[:], in_=n4_d.broadcast_to([128, D])), writes=["gb"], dma="gb")
        P.add("pool", lambda e: e.memset(ss[:], 0.0), writes=ALLSS)
        fjunk = AR(OFF_OST + 8192, [128, D])

        def a1(b):
            P.add("act", lambda e: e.activation(out=fjunk[:, :], in_=h[:, b, :], func=AF.Square, accum_out=ss[:, b:b + 1]),
                  reads=[("h", b), ("ss", b)], writes=[("ss", b), "fjunk"], full=True)
            P.add("act", lambda e: e.activation(out=rs[:, b:b + 1], in_=ss[:, b:b + 1], func=AF.Sqrt,
                                                bias=epsc[:, 0:1], scale=1.0 / D),
                  reads=[("ss", b), "epsc"], writes=[("rs", b)])

        def a2(b):
            j = nxt("ost", 2)
            P.add("dve", lambda e: e.reciprocal(out=rs[:, b:b + 1], in_=rs[:, b:b + 1]), reads=[("rs", b)], writes=[("rs", b)])
            P.add("dve", lambda e: e.scalar_tensor_tensor(
                out=ost[j][:, :], in0=h[:, b, :], scalar=rs[:, b:b + 1], in1=gb[:, :], op0=ALU.mult, op1=ALU.mult),
                reads=[("h", b), ("rs", b), "gb"], writes=[("ost", j)])
            P.add("sp", lambda e: e.dma_start(out=out_d[s, 128 * b:128 * b + 128, :], in_=ost[j][:, :]),
                  reads=[("ost", j)], dma=f"out{j}")
            if after_block is not None:
                after_block(b)
        if hooked:
            seq = []

            def hook(b):
                seq.append(b)
                if len(seq) >= 2:
                    a2(seq[-2])
                a1(b)

            def flush():
                a2(seq[-1])
            return hook, flush
        a1(0)
        for b in range(NBLK):
            if b + 1 < NBLK:
                a1(b + 1)
            a2(b)

    def load_x(s, b):
        P.add("sp", lambda e: e.dma_start(out=h[:, b, :], in_=x_d[s, 128 * b:128 * b + 128, :]),
              writes=[("h", b)], dma=f"x{b % 4}")

    for s in range(2):
        hm = AR(OFF_HM, [128, D], F32)
        if s == 0 or stage != 3:
            for b in range(NBLK):
                load_x(s, b)
        P.add("sp", lambda e: e.dma_start(out=hm[0:NMETA, :], in_=meta_d), writes=[("h", "m")], dma="xm")
        if stage == 3:
            ffn(w1i_d, w1o_d, n1_d, True, hm, "f1",
                tail=lambda: norm_to_uT(n2_d, ["m"] + list(range(NBLK)), hm, hooked=True,
                                        junk=AR(OFF_OST, [128, D]), junkres=("ost", 0)))
            P.barrier()
            mixer(hm, post_tail=lambda: norm_to_uT(n3_d, list(range(NBLK)), hm, hooked=True,
                                                    junk=AR(OFF_PTMP, [128, D]), junkres=("ptmp", 0)))
            P.barrier()
            ffn(w2i_d, w2o_d, n3_d, False, hm, "f2", do_norm=False,
                tail=lambda s=s: final_out(s, raw=False, hooked=True,
                                           after_block=((lambda b: load_x(1, b)) if s == 0 else None)))
        else:
            ffn(w1i_d, w1o_d, n1_d, True, hm, "f1")
            if stage >= 2:
                norm_to_uT(n2_d, ["m"] + list(range(NBLK)), hm)
                P.barrier()
                mixer(hm, mstop=(stage - 10 if stage >= 10 else 9))
            P.barrier()
            final_out(s, raw=True)
    P.emit()
    es.close()
    return nc


_CACHE = {}


def kernel(x, meta_tokens, ffn1_norm, ffn1_w_in, ffn1_w_out, mix_norm, w_in, b_forget, attn_sinks,
           w_branch_a, w_branch_b, w_out, ffn2_norm, ffn2_w_in, ffn2_w_out, final_norm, _stage=3, _cores=8):
    f = lambda a: np.ascontiguousarray(np.asarray(a, dtype=np.float32))
    x = f(x)
    cf, cb = make_consts()
    shared = {
        "meta": f(meta_tokens), "n1": f(ffn1_norm).reshape(1, D), "n2": f(mix_norm).reshape(1, D),
        "n3": f(ffn2_norm).reshape(1, D), "n4": f(final_norm).reshape(1, D),
        "w1i": f(ffn1_w_in)[0], "w1o": f(ffn1_w_out)[0], "w2i": f(ffn2_w_in)[0], "w2o": f(ffn2_w_out)[0],
        "wi": f(w_in)[0], "bfg": f(b_forget).reshape(1, 8), "snk": f(attn_sinks).reshape(1, 8),
        "wa": f(w_branch_a)[0], "wb": f(w_branch_b)[0], "wo": f(w_out)[0], "cf": cf, "cb": cb,
    }
    if _stage not in _CACHE:
        _CACHE[_stage] = build(_stage)
    nc = _CACHE[_stage]
    in_maps = [dict(shared, x=x[2 * c:2 * c + 2]) for c in range(_cores)]
    res = run_bass_kernel_spmd(nc, in_maps, core_ids=list(range(_cores)))
    return np.concatenate([r["out"] for r in res.results], axis=0)
```

```python
import contextlib
import numpy as np
import ml_dtypes
import concourse.bass as bass
import concourse.mybir as mybir
from concourse.bass_utils import run_bass_kernel_spmd

F32 = mybir.dt.float32
BF16 = mybir.dt.bfloat16
AF = mybir.ActivationFunctionType
ALU = mybir.AluOpType

D = 1024
SEQ = 2048
NBLK = 16
NMETA = 16
TCOLS = SEQ + NMETA
DFF = 2816
NCH = 22
PASSES = [(0, 6), (6, 6), (12, 5), (17, 5)]
EPS = 1e-6
SLOPES = [2.0 ** (-(h + 1)) for h in range(8)]
ENGS = ("pe", "act", "dve", "pool", "sp")
ARENA_BYTES = 98304
DBG_SWA = 9

CF_TRI, CF_ONES, CF_IOTA, CF_BCUR, CF_BPREV, CF_BMETA, CF_N = 0, 128, 256, 384, 392, 400, 528
CB_ID, CB_MCUR, CB_MPREV, CB_OLO, CB_OHI, CB_N = 0, 128, 640, 1152, 1280, 1408


def make_consts():
    cf = np.zeros((128, CF_N), np.float32)
    p = np.arange(128)
    cf[:, CF_TRI:CF_TRI + 128] = (p[:, None] <= p[None, :]).astype(np.float32)
    cf[:, CF_ONES:CF_ONES + 128] = 1.0
    cf[:, CF_IOTA:CF_IOTA + 128] = p[None, :].astype(np.float32)
    sl = np.array(SLOPES, np.float32)
    cf[:, CF_BCUR:CF_BCUR + 8] = p[:, None] * sl[None, :]
    cf[:, CF_BPREV:CF_BPREV + 8] = (p[:, None] - 128.0) * sl[None, :]
    bm = np.zeros((128, 16, 8), np.float32)
    for i in range(16):
        bm[:, i, :] = (p[:, None] - 16.0 - 128.0 * i) * sl[None, :]
    cf[:, CF_BMETA:CF_BMETA + 128] = bm.reshape(128, 128)
    cb = np.zeros((128, CB_N), np.float32)
    cb[:, CB_ID:CB_ID + 128] = np.eye(128)
    mc = (p[:, None] <= p[None, :]).astype(np.float32)
    mp = (p[:, None] > p[None, :]).astype(np.float32)
    cb[:, CB_MCUR:CB_MCUR + 512] = np.tile(mc, (1, 4))
    cb[:, CB_MPREV:CB_MPREV + 512] = np.tile(mp, (1, 4))
    cb[:, CB_OLO:CB_OLO + 64] = 1.0
    cb[:, CB_OHI + 64:CB_OHI + 128] = 1.0
    return cf, cb.astype(ml_dtypes.bfloat16)


class Op:
    __slots__ = ("idx", "eng", "fn", "reads", "writes", "dma", "deps", "signal", "semval", "semname", "attach")

    def __init__(self, idx, eng, fn, reads, writes, dma):
        self.idx, self.eng, self.fn, self.reads, self.writes, self.dma = idx, eng, fn, reads, writes, dma
        self.deps = set()
        self.signal = False
        self.semval = 0
        self.semname = None
        self.attach = False


class Prog:
    def __init__(self, nc):
        self.nc = nc
        self.ops = []
        self.last_writer = {}
        self.readers = {}
        self.last_on = {}

    def add(self, eng, fn, reads=(), writes=(), dma=None, full=False, attach=False):
        if dma is not None:
            writes = tuple(writes) + (("__slot", dma),)
        op = Op(len(self.ops), eng, fn, tuple(reads), tuple(writes), dma)
        op.attach = attach
        deps = set()
        for r in op.reads:
            w = self.last_writer.get(r)
            if w is not None:
                deps.add(w)
        for r in op.writes:
            w = self.last_writer.get(r)
            if w is not None:
                deps.add(w)
            deps.update(self.readers.get(r, ()))
        for d in deps:
            dop = self.ops[d]
            if dop.dma is None and dop.eng == eng and dma is None:
                if eng == "pe":
                    continue
                if not full and not any(self.last_writer.get(r) == d for r in op.reads):
                    continue
            op.deps.add(d)
        for r in op.reads:
            self.readers.setdefault(r, []).append(op.idx)
        for r in op.writes:
            self.last_writer[r] = op.idx
            self.readers[r] = []
        self.ops.append(op)
        if dma is None:
            self.last_on[eng] = op.idx
        return op

    def barrier(self):
        lasts = dict(self.last_on)
        dmas = [idx for (k, idx) in self.last_writer.items() if isinstance(k, tuple) and k[0] == "__slot"]
        for e in ENGS:
            op = Op(len(self.ops), e, None, (), (), None)
            for e2, idx in lasts.items():
                if e2 != e:
                    op.deps.add(idx)
            op.deps.update(dmas)
            self.ops.append(op)

    def emit(self):
        nc, ops = self.nc, self.ops
        for op in ops:
            best = {}
            keep = set()
            for d in op.deps:
                dop = ops[d]
                if dop.dma is not None:
                    keep.add(d)
                elif d > best.get(dop.eng, -1):
                    best[dop.eng] = d
            keep.update(best.values())
            op.deps = keep
            for d in keep:
                ops[d].signal = True
        cnt = {}
        for op in ops:
            if op.dma is not None:
                op.signal = True
                op.semname = "d_" + op.dma
                cnt[op.semname] = cnt.get(op.semname, 0) + 16
                op.semval = cnt[op.semname]
            elif op.signal:
                op.semname = "e_" + op.eng
                cnt[op.semname] = cnt.get(op.semname, 0) + 1
                op.semval = cnt[op.semname]
        semnames = sorted(cnt)
        with contextlib.ExitStack() as es:
            sems = {n: es.enter_context(nc.semaphore(n)) for n in semnames}
            block = es.enter_context(nc.Block())
            by_eng = {e: [op for op in ops if op.eng == e] for e in ENGS}
            final = [(n, cnt[n]) for n in semnames if n.startswith("d_")]

            def run(engname, eng):
                known = {}
                for op in by_eng[engname]:
                    waits = {}
                    for d in op.deps:
                        dop = ops[d]
                        if dop.semval > waits.get(dop.semname, 0):
                            waits[dop.semname] = dop.semval
                    need = [(n, v) for n, v in sorted(waits.items()) if known.get(n, 0) < v]
                    emb = None
                    if op.attach and need:
                        emb = need.pop()
                    for n, v in need:
                        eng.wait_ge(sems[n], v)
                        known[n] = v
                    if op.fn is None:
                        continue
                    if op.attach:
                        if emb is not None:
                            known[emb[0]] = emb[1]
                        ins = op.fn(eng, None if emb is None else (sems[emb[0]], emb[1]))
                    else:
                        ins = op.fn(eng)
                    if op.signal:
                        ins.then_inc(sems[op.semname], 16 if op.dma is not None else 1)
                if engname == "sp":
                    for n, v in final:
                        eng.wait_ge(sems[n], v)

            @block.tensor
            def _(e):
                run("pe", e)

            @block.scalar
            def _(e):
                run("act", e)

            @block.vector
            def _(e):
                run("dve", e)

            @block.gpsimd
            def _(e):
                run("pool", e)

            @block.sync
            def _(e):
                run("sp", e)


def build(stage=3):
    nc = bass.Bass("TRN2", target_bir_lowering=False)
    dt_in = lambda n, s, d=F32: nc.dram_tensor(n, s, d, kind="ExternalInput").ap()
    x_d = dt_in("x", [2, SEQ, D])
    meta_d = dt_in("meta", [NMETA, D])
    n1_d, n2_d, n3_d, n4_d = (dt_in(n, [1, D]) for n in ("n1", "n2", "n3", "n4"))
    w1i_d, w1o_d = dt_in("w1i", [D, 2 * DFF]), dt_in("w1o", [DFF, D])
    w2i_d, w2o_d = dt_in("w2i", [D, 2 * DFF]), dt_in("w2o", [DFF, D])
    wi_d = dt_in("wi", [D, 4360])
    bf_d, sk_d = dt_in("bfg", [1, 8]), dt_in("snk", [1, 8])
    wa_d, wb_d, wo_d = dt_in("wa", [512, D]), dt_in("wb", [512, D]), dt_in("wo", [D, D])
    cf_d, cb_d = dt_in("cf", [128, CF_N]), dt_in("cb", [128, CB_N], BF16)
    out_d = nc.dram_tensor("out", [2, SEQ, D], F32, kind="ExternalOutput").ap()

    es = contextlib.ExitStack()
    sb = lambda n, s, d: es.enter_context(nc.sbuf_tensor(n, s, d))
    h = sb("h", [128, NBLK, D], F32)
    uT = sb("uT", [128, 8, TCOLS], BF16)
    cf = sb("cf_s", [128, CF_N], F32)
    cb = sb("cb_s", [128, CB_N], BF16)
    gb = sb("gb", [128, D], F32)
    ss = sb("ss", [128, 17], F32)
    rs = sb("rs", [128, 17], F32)
    epsc = sb("epsc", [128, 1], F32)
    bfg = sb("bfg_s", [128, 8], F32)
    snk = sb("snk_s", [128, 8], F32)
    sinktab = sb("sinktab", [128, 512], F32)
    arena = sb("arena", [128, ARENA_BYTES // 2], BF16)
    ps = [es.enter_context(nc.psum_tensor(f"ps{i}", [128, 512], F32)) for i in range(8)]
    pt = [ps[6 + i][:, 0:256].bitcast(BF16) for i in range(2)]

    def AR(off, shape, dtype=BF16):
        n = int(np.prod(shape[1:]))
        nb = n * (4 if dtype == F32 else 2)
        assert off % 4 == 0 and off + nb <= ARENA_BYTES, (off, nb)
        v = arena[:, off // 2:(off + nb) // 2]
        if dtype == F32:
            v = v.bitcast(F32)
        if len(shape) == 2:
            return v
        names = " ".join(f"a{i}" for i in range(len(shape) - 1))
        kw = {f"a{i}": shape[i + 1] for i in range(len(shape) - 2)}
        return v.rearrange(f"p ({names}) -> p {names}", **kw)

    P = Prog(nc)
    rot = {}

    def nxt(name, n):
        rot[name] = (rot.get(name, -1) + 1) % n
        return rot[name]

    P.add("sp", lambda e: e.dma_start(out=cf[:], in_=cf_d), writes=["cf"], dma="cf")
    P.add("sp", lambda e: e.dma_start(out=cb[:], in_=cb_d), writes=["cb"], dma="cb")
    P.add("sp", lambda e: e.dma_start(out=bfg[:], in_=bf_d.broadcast_to([128, 8])), writes=["bfg"], dma="bfg")
    P.add("sp", lambda e: e.dma_start(out=snk[:], in_=sk_d.broadcast_to([128, 8])), writes=["snk"], dma="snk")
    P.add("pool", lambda e: e.memset(epsc[:], EPS), writes=["epsc"])
    onec = sb("onec", [128, 1], F32)
    P.add("pool", lambda e: e.memset(onec[:], 1.0), writes=["onec"])
    ALLSS = ["ss"] + [("ss", b) for b in ["m"] + list(range(NBLK))]
    for g in range(4):
        for hh in range(2):
            hd = 4 * hh + g
            P.add("act", lambda e, g=g, hh=hh, hd=hd: e.activation(
                out=sinktab[64 * hh:64 * hh + 64, 128 * g:128 * g + 128],
                in_=cf[64 * hh:64 * hh + 64, CF_IOTA:CF_IOTA + 128], func=AF.Exp,
                bias=snk[64 * hh:64 * hh + 64, hd:hd + 1], scale=SLOPES[hd]),
                reads=["cf", "snk"], writes=["sinktab"])

    def tile_cols(t):
        if t == "m":
            return 0, NMETA, ["m"]
        return NMETA + 512 * t, 512, [4 * t + j for j in range(4)]

    def blk_cols(b):
        if b == "m":
            return 0, NMETA
        return NMETA + 128 * b, 128

    def cast_dma(dst, src, res, slot):
        P.add("pool", lambda e: e.dma_start(out=dst, in_=src), writes=[res], dma=slot)

    def mm_group(out, pairs, reads, writes):
        def fn(e):
            n = len(pairs)
            ins = None
            for i, (l, r) in enumerate(pairs):
                ins = e.matmul(out, lhsT=l, rhs=r, start=(i == 0), stop=(i == n - 1))
            return ins
        P.add("pe", fn, reads=reads, writes=writes)

    evac_flip = [0]

    def evac(out, in_, reads, writes, eng=None):
        if eng is None:
            evac_flip[0] ^= 1
            eng = "dve" if evac_flip[0] else "act"
        if eng == "act":
            P.add("act", lambda e: e.copy(out=out, in_=in_), reads=reads, writes=writes)
        else:
            P.add(eng, lambda e: e.tensor_copy(out=out, in_=in_), reads=reads, writes=writes)

    def norm_to_uT(gain_d, blocks, hm, hooked=False, junk=None, junkres="junk"):
        ut = [AR(OFF_UT + 2048 * i, [128, D]) for i in range(2)]
        P.add("sp", lambda e: e.dma_start(out=gb[:], in_=gain_d.broadcast_to([128, D])), writes=["gb"], dma="gb")
        P.add("pool", lambda e: e.memset(ss[:], 0.0), writes=ALLSS)
        jmap = {}

        def geom(b):
            c0, nr = blk_cols(b)
            src = hm[0:nr, :] if b == "m" else h[:, b, :]
            col = 16 if b == "m" else b
            return c0, nr, src, col

        def a1(b, j=None):
            c0, nr, src, col = geom(b)
            if junk is not None:
                out, ores = junk[0:nr, :], junkres
            else:
                out, ores = ut[j][0:nr, :], ("ut", j)
            P.add("act", lambda e: e.activation(out=out, in_=src, func=AF.Square, accum_out=ss[0:nr, col:col + 1]),
                  reads=[("h", b), ("ss", b)], writes=[("ss", b), ores], full=True)
            P.add("act", lambda e: e.activation(
                out=rs[0:nr, col:col + 1], in_=ss[0:nr, col:col + 1], func=AF.Sqrt, bias=epsc[0:nr, 0:1], scale=1.0 / D),
                reads=[("ss", b), "epsc"], writes=[("rs", b)])

        def a2(b, j):
            c0, nr, src, col = geom(b)
            jmap[b] = j
            P.add("dve", lambda e: e.reciprocal(out=rs[0:nr, col:col + 1], in_=rs[0:nr, col:col + 1]),
                  reads=[("rs", b)], writes=[("rs", b)])
            P.add("dve", lambda e: e.scalar_tensor_tensor(
                out=ut[j][0:nr, :], in0=src, scalar=rs[0:nr, col:col + 1], in1=gb[0:nr, :],
                op0=ALU.mult, op1=ALU.mult), reads=[("h", b), ("rs", b), "gb"], writes=[("ut", j)])

        def stage_a(b):
            j = nxt("ut", 2)
            a1(b, j)
            a2(b, j)

        def stage_b(b):
            c0, nr = blk_cols(b)
            j = jmap[b]
            for half in range(2):
                def fn(e, half=half):
                    ins = None
                    for c in range(4):
                        cc = 4 * half + c
                        ins = e.transpose(pt[half][:, 128 * c:128 * c + nr], ut[j][0:nr, 128 * cc:128 * cc + 128],
                                          cb[0:nr, CB_ID:CB_ID + nr])
                    return ins
                P.add("pe", fn, reads=[("ut", j), "cb"], writes=[("ps", 6 + half)])
                src_v = pt[half][:, :].rearrange("p (c n) -> p c n", c=4)[:, :, 0:nr]
                evac(uT[:, 4 * half:4 * half + 4, c0:c0 + nr], src_v, [("ps", 6 + half)], [("uT", b, half)],
                     eng=("act" if half == 0 else "dve"))
        if hooked:
            assert junk is not None
            seq = []

            def hook(b):
                seq.append(b)
                n = len(seq)
                if n >= 3:
                    stage_b(seq[n - 3])
                if n >= 2:
                    a2(seq[n - 2], nxt("ut", 2))
                a1(b)

            def flush():
                n = len(seq)
                if n >= 2:
                    stage_b(seq[n - 2])
                a2(seq[n - 1], nxt("ut", 2))
                stage_b(seq[n - 1])
            return hook, flush
        stage_a(blocks[0])
        for i, b in enumerate(blocks):
            if i + 1 < len(blocks):
                stage_a(blocks[i + 1])
            stage_b(b)

    def uT_res(blks):
        return [("uT", b, hf) for b in blks for hf in range(2)]

    def ffn(w_in_d, w_out_d, gain_d, with_meta, hm, tagp, do_norm=True, tail=None):
        blocks = (["m"] if with_meta else []) + list(range(NBLK))
        if do_norm:
            norm_to_uT(gain_d, blocks, hm)
        tiles = (["m"] if with_meta else []) + [0, 1, 2, 3]
        act = AR(OFF_ACT, [128, 6, TCOLS])
        wis = [AR(OFF_WI + 8192 * i, [128, 2, 8, 256]) for i in range(3)]
        wos = [AR(OFF_WO + 12288 * i, [128, 6, D]) for i in range(2)]
        tmp = [AR(OFF_TMP + 2048 * i, [128, 512], F32) for i in range(2)]
        for (m0, nch) in PASSES:
            wslot = nxt("wo", 2)
            wo_t = wos[wslot]
            ml = 0
            while ml < nch:
                ns = min(2, nch - ml)
                s = nxt("wi", 3)
                wi_t = wis[s]
                c0 = (m0 + ml) * 128
                cast_dma(wi_t[:, 0, :, 0:ns * 128], w_in_d[:, c0:c0 + ns * 128].rearrange("(k p) n -> p k n", p=128),
                         ("wi", s, 0), f"wi{s}g")
                cast_dma(wi_t[:, 1, :, 0:ns * 128],
                         w_in_d[:, DFF + c0:DFF + c0 + ns * 128].rearrange("(k p) n -> p k n", p=128),
                         ("wi", s, 1), f"wi{s}u")
                if ml == 0:
                    cast_dma(wo_t[:, 0:nch, :], w_out_d[m0 * 128:(m0 + nch) * 128, :].rearrange("(k p) n -> p k n", p=128),
                             ("wo", wslot), f"wo{wslot}")
                for q in range(ns):
                    mloc = ml + q
                    for t in tiles:
                        tc0, tn, tb = tile_cols(t)
                        a = nxt("psA", 2)
                        pA, pB = ps[2 * a], ps[2 * a + 1]
                        for which, pp in ((0, pA), (1, pB)):
                            mm_group(pp[:, 0:tn],
                                     [(wi_t[:, which, k, q * 128:(q + 1) * 128], uT[:, k, tc0:tc0 + tn]) for k in range(8)],
                                     reads=[("wi", s, which)] + uT_res(tb), writes=[("ps", 2 * a + which)])
                        j = nxt("tmp", 2)
                        P.add("act", lambda e, j=j, pA=pA, tn=tn: e.activation(out=tmp[j][:, 0:tn], in_=pA[:, 0:tn], func=AF.Silu),
                              reads=[("ps", 2 * a)], writes=[("tmp", j)])
                        P.add("dve", lambda e, j=j, pB=pB, tn=tn, mloc=mloc, tc0=tc0: e.tensor_tensor(
                            out=act[:, mloc, tc0:tc0 + tn], in0=tmp[j][:, 0:tn], in1=pB[:, 0:tn], op=ALU.mult),
                            reads=[("tmp", j), ("ps", 2 * a + 1)], writes=[("act", mloc, t)])
                ml += ns
            hook = flush = None
            if tail is not None and (m0, nch) == PASSES[-1]:
                hook, flush = tail()
            for b in blocks:
                bc0, nr = blk_cols(b)
                t = "m" if b == "m" else b // 4
                for half in range(2):
                    o = 4 + nxt("psO", 2)
                    mm_group(ps[o][0:nr, :],
                             [(act[:, k, bc0:bc0 + nr], wo_t[:, k, 512 * half:512 * half + 512]) for k in range(nch)],
                             reads=[("act", k, t) for k in range(nch)] + [("wo", wslot)], writes=[("ps", o)])
                    dst = hm[0:nr, 512 * half:512 * half + 512] if b == "m" else h[:, b, 512 * half:512 * half + 512]
                    P.add("dve", lambda e, dst=dst, o=o, nr=nr: e.scalar_tensor_tensor(
                        out=dst, in0=ps[o][0:nr, :], scalar=0.5, in1=dst, op0=ALU.mult, op1=ALU.add),
                        reads=[("ps", o), ("h", b)], writes=[("h", b)])
                if hook is not None:
                    hook(b)
            if flush is not None:
                flush()

    OFF_HM = 0
    OFF_UT = 4096
    OFF_ACT = 8192
    OFF_WI = OFF_ACT + 24768
    OFF_WO = OFF_WI + 24576
    OFF_TMP = OFF_WO + 24576
    OFF_OST = OFF_TMP + 4096
    assert OFF_OST + 8192 <= ARENA_BYTES
    OFF_OA = 8192
    OFF_OB = OFF_OA + 16384
    OFF_LF = OFF_OB + 16384
    OFF_BIASF = OFF_LF + 3264
    OFF_PT = OFF_BIASF + 2176
    OFF_DEN = OFF_PT + 4096
    OFF_X = OFF_DEN + 2048
    assert OFF_X + 45632 <= ARENA_BYTES, OFF_X
    OFF_PW = OFF_OB + 16384
    OFF_MIX = OFF_PW + 24576
    assert OFF_MIX + 32768 <= ARENA_BYTES
    OFF_PTMP = 0

    def mixer(hm, mstop=9, post_tail=None):
        ablocks = ["m"] + list(range(NBLK))
        NPT = 4
        PT = [AR(OFF_PT + 1024 * i, [128, 512]) for i in range(NPT)]
        den = [AR(OFF_DEN, [128, 512], F32)]
        OaT = AR(OFF_OA, [128, 4, SEQ])
        ObT = AR(OFF_OB, [128, 4, SEQ])
        lfv = [AR(OFF_LF + 544 * i, [128, 136], F32) for i in range(6)]
        xb, lt, tots, offv, cum = lfv[0], lfv[1], lfv[2], lfv[3], lfv[4]
        biasF = AR(OFF_BIASF, [128, 17, 4, 8], F32)
        wf = lfv[5][:, :].bitcast(BF16)[:, 0:64].rearrange("p (k n) -> p k n", k=8)
        cast_dma(wf, wi_d[:, 2304:2312].rearrange("(k p) n -> p k n", p=128), "wf", "wf")

        def f_fn(e):
            ins = None
            for bi, b in enumerate(ablocks):
                c0, nr = blk_cols(b)
                for k in range(8):
                    ins = e.matmul(ps[0][0:nr, 8 * bi:8 * bi + 8], lhsT=uT[:, k, c0:c0 + nr], rhs=wf[:, k, :],
                                   start=(k == 0), stop=(k == 7))
            return ins
        P.add("pe", f_fn, reads=["wf"] + uT_res(ablocks), writes=[("ps", 0)])
        P.add("pool", lambda e: e.memset(lt[:], 0.0), writes=["lt"])
        for (r0, r1, c0, c1) in ((0, 16, 0, 8), (0, 128, 8, 136)):
            bb = bfg[r0:r1, :] if c1 == 8 else bfg[r0:r1, :].unsqueeze(1).to_broadcast([r1 - r0, 16, 8])
            i0 = ps[0][r0:r1, c0:c1] if c1 == 8 else ps[0][r0:r1, c0:c1].rearrange("p (b h) -> p b h", h=8)
            o0 = xb[r0:r1, c0:c1] if c1 == 8 else xb[r0:r1, c0:c1].rearrange("p (b h) -> p b h", h=8)
            P.add("dve", lambda e, bb=bb, i0=i0, o0=o0: e.tensor_tensor(out=o0, in0=i0, in1=bb, op=ALU.add),
                  reads=[("ps", 0), "bfg"], writes=["xb"])
            P.add("act", lambda e, r0=r0, r1=r1, c0=c0, c1=c1: e.activation(
                out=xb[r0:r1, c0:c1], in_=xb[r0:r1, c0:c1], func=AF.Exp, scale=-1.0), reads=["xb"], writes=["xb"])
            P.add("act", lambda e, r0=r0, r1=r1, c0=c0, c1=c1: e.activation(
                out=lt[r0:r1, c0:c1], in_=xb[r0:r1, c0:c1], func=AF.Ln, bias=onec[r0:r1, 0:1]), reads=["xb", "lt", "onec"], writes=["lt"])
        P.add("pe", lambda e: e.matmul(ps[1][:, 0:136], lhsT=cf[:, CF_ONES:CF_ONES + 128], rhs=lt[:, :], start=True, stop=True),
              reads=["lt", "cf"], writes=[("ps", 1)])
        P.add("pe", lambda e: e.matmul(ps[2][:, 0:136], lhsT=cf[:, CF_TRI:CF_TRI + 128], rhs=lt[:, :], start=True, stop=True),
              reads=["lt", "cf"], writes=[("ps", 2)])
        P.add("dve", lambda e: e.tensor_copy(out=tots[:, :], in_=ps[1][:, 0:136]), reads=[("ps", 1)], writes=["tots"])
        P.add("pool", lambda e: e.memset(offv[:, :], 0.0), writes=["offv"])
        for b in range(1, 17):
            P.add("dve", lambda e, b=b: e.tensor_tensor(out=offv[:, 8 * b:8 * b + 8], in0=offv[:, 8 * b - 8:8 * b],
                                                        in1=tots[:, 8 * b - 8:8 * b], op=ALU.add),
                  reads=["offv", "tots"], writes=["offv"])
        P.add("dve", lambda e: e.tensor_tensor(out=cum[:, :], in0=ps[2][:, 0:136], in1=offv[:, :], op=ALU.add),
              reads=[("ps", 2), "offv"], writes=["cum"])
        cum3 = cum[:, :].rearrange("p (b h) -> p b h", h=8)
        for j in range(4):
            ref = 4 * j + 3
            for hd in range(8):
                P.add("dve", lambda e, j=j, hd=hd, ref=ref: e.tensor_scalar(
                    out=biasF[:, :, j, hd], in0=cum3[:, :, hd], scalar1=offv[:, 8 * ref + hd:8 * ref + hd + 1],
                    scalar2=None, op0=ALU.subtract), reads=["cum", "offv"], writes=["biasF"])
        if mstop <= 1:
            return

        wqa = AR(OFF_X, [128, 8, 4, 2, 64])
        wka = AR(OFF_X + 8192, [128, 8, 128])
        wva = AR(OFF_X + 10240, [128, 8, 128])
        QaT = AR(OFF_X + 12288, [128, 4, SEQ])
        Ka2 = [AR(OFF_X + 28672 + 4128 * i, [128, TCOLS]) for i in range(2)]
        Va = [AR(OFF_X + 36928 + 4352 * i, [128, 17, 128]) for i in range(2)]
        P.add("pool", lambda e: e.memset(Ka2[0][64:128, :], 0.0), writes=[("KaT", b) for b in ablocks])
        P.add("pool", lambda e: e.memset(Ka2[1][0:64, :], 0.0), writes=[("KaT", b) for b in ablocks])
        for hh in range(2):
            for g in range(4):
                cq = 256 * hh + 64 * g
                cast_dma(wqa[:, :, g, hh, :], wi_d[:, cq:cq + 64].rearrange("(k p) d -> p k d", p=128),
                         ("wqa", hh, g), f"wqa{hh}{g}")
        cast_dma(wka, wi_d[:, 512:640].rearrange("(k p) n -> p k n", p=128), "wka", "wka")
        cast_dma(wva, wi_d[:, 640:768].rearrange("(k p) n -> p k n", p=128), "wva", "wva")
        for i in range(2):
            P.add("pool", lambda e, i=i: e.memset(Va[i][:, :, :], 0.0), writes=[("Va", i)])
        for g in range(4 if DBG_SWA >= 0.2 else 0):
            for t in range(4):
                tc0, tn, tb = tile_cols(t)
                a = nxt("psQ", 2)
                mm_group(ps[a][:, :], [(wqa[:, k, g].rearrange("p a d -> p (a d)"), uT[:, k, tc0:tc0 + tn]) for k in range(8)],
                         reads=[("wqa", 0, g), ("wqa", 1, g)] + uT_res(tb), writes=[("ps", a)])
                evac(QaT[:, g, 512 * t:512 * t + 512], ps[a][:, :], [("ps", a)], [("QaT", 4 * t + j) for j in range(4)])
        for t in (["m", 0, 1, 2, 3] if DBG_SWA >= 0.3 else []):
            tc0, tn, tb = tile_cols(t)
            a = nxt("psQ", 2)
            mm_group(ps[a][:, 0:tn], [(wka[:, k, :], uT[:, k, tc0:tc0 + tn]) for k in range(8)],
                     reads=["wka"] + uT_res(tb), writes=[("ps", a)])
            evac(Ka2[0][0:64, tc0:tc0 + tn], ps[a][0:64, 0:tn], [("ps", a)], [("KaT", b) for b in tb], eng="dve")
            evac(Ka2[1][64:128, tc0:tc0 + tn], ps[a][64:128, 0:tn], [("ps", a)], [("KaT", b) for b in tb], eng="dve")
        for bi, b in enumerate(ablocks if DBG_SWA >= 0.4 else []):
            if DBG_SWA == 0.45 and b == "m":
                continue
            c0, nr = blk_cols(b)
            a = nxt("psQ", 2)
            mm_group(ps[a][0:nr, 0:128], [(uT[:, k, c0:c0 + nr], wva[:, k, :]) for k in range(8)],
                     reads=["wva"] + uT_res([b]), writes=[("ps", a)])
            evac(Va[0][0:nr, bi, 0:64], ps[a][0:nr, 0:64], [("ps", a)], [("Va", 0)], eng="dve")
            evac(Va[1][0:nr, bi, 64:128], ps[a][0:nr, 64:128], [("ps", a)], [("Va", 1)], eng="dve")
        SB = [0, 1, 2, 5]
        LOOK = 3
        ODS = [(3, 4), (6, 7)]
        steps = []
        for i in range(NBLK):
            roles = [("meta", 0, 16, 0)] + ([("prev", i, 128, NMETA + 128 * (i - 1))] if i >= 1 else []) + \
                    [("cur", i + 1, 128, NMETA + 128 * i)]
            n_i = 2 * len(roles)
            cnt = 0
            for kvh in range(2):
                for (role, kb, nk, kc0) in roles:
                    steps.append((i, kvh, role, kb, nk, kc0, cnt == 0, cnt == n_i - 1))
                    cnt += 1

        def swa_qk(n):
            i, kvh, role, kb, nk, kc0, st, last = steps[n]
            sidx = SB[n % 4]
            pS = ps[sidx]
            base = 64 * kvh
            kres = ("KaT", "m") if role == "meta" else ("KaT", kb - 1)
            def qk(e, w):
                ins = e.matmul(pS[0:nk, :].rearrange("p (g q) -> p g q", g=4), lhsT=Ka2[kvh][:, kc0:kc0 + nk],
                               rhs=QaT[:, :, 128 * i:128 * i + 128], start=True, stop=True)
                if w is not None:
                    ins._wait_ge(*w)
                return ins
            P.add("pe", qk, reads=[kres, ("QaT", i)], writes=[("ps", sidx)], attach=True)

        def swa_rest(n):
            i, kvh, role, kb, nk, kc0, st, last = steps[n]
            sidx = SB[n % 4]
            pS = ps[sidx]
            pj = n % NPT
            iO, iD = ODS[i % 2]
            pO, pD = ps[iO], ps[iD]
            for g in range(4):
                hd = 4 * kvh + g
                if role == "meta":
                    bcol = cf[0:nk, CF_BMETA + 8 * i + hd:CF_BMETA + 8 * i + hd + 1]
                elif role == "prev":
                    bcol = cf[0:nk, CF_BPREV + hd:CF_BPREV + hd + 1]
                else:
                    bcol = cf[0:nk, CF_BCUR + hd:CF_BCUR + hd + 1]
                P.add("act", lambda e, g=g, bcol=bcol: e.activation(
                    out=PT[pj][0:nk, 128 * g:128 * g + 128], in_=pS[0:nk, 128 * g:128 * g + 128],
                    func=AF.Exp, bias=bcol, scale=0.125), reads=[("ps", sidx), "cf"], writes=[("PT", pj)])
            if role != "meta":
                mo = CB_MPREV if role == "prev" else CB_MCUR
                P.add("pool", lambda e: e.tensor_tensor(
                    out=PT[pj][:, :], in0=PT[pj][:, :], in1=cb[:, mo:mo + 512], op=ALU.mult),
                    reads=[("PT", pj), "cb"], writes=[("PT", pj)])
            oo = CB_OLO if kvh == 0 else CB_OHI

            def pv(e, w):
                i0 = e.matmul(pO[:, :], lhsT=Va[kvh][0:nk, kb, :], rhs=PT[pj][0:nk, :], start=st, stop=last)
                if w is not None:
                    i0._wait_ge(*w)
                return e.matmul(pD[:, :], lhsT=cb[0:nk, oo:oo + 128], rhs=PT[pj][0:nk, :], start=st, stop=last)
            P.add("pe", pv, reads=[("PT", pj), ("Va", kvh), "cb"], writes=[("ps", iO), ("ps", iD)], attach=True)
            if defer:
                defer.pop(0)()
            if last:
                dj = 0
                defer.append(lambda: P.add("dve", lambda e: e.tensor_tensor(out=den[dj][:, :], in0=pD[:, :], in1=sinktab[:, :], op=ALU.add),
                                           reads=[("ps", iD), "sinktab"], writes=[("den", dj)]))
                defer.append(lambda: P.add("dve", lambda e: e.reciprocal(out=den[dj][:, :], in_=den[dj][:, :]),
                                           reads=[("den", dj)], writes=[("den", dj)]))
                defer.append(lambda: P.add("dve", lambda e: e.tensor_tensor(
                    out=OaT[:, :, 128 * i:128 * i + 128], in0=pO[:, :].rearrange("p (g q) -> p g q", g=4),
                    in1=den[dj][:, :].rearrange("p (g q) -> p g q", g=4), op=ALU.mult),
                    reads=[("ps", iO), ("den", dj)], writes=[("OaT", i // 4)]))
        defer = []
        for n in range(len(steps) + LOOK):
            if n < len(steps):
                swa_qk(n)
            if n - LOOK >= 0:
                swa_rest(n - LOOK)
        while defer:
            defer.pop(0)()
        P.barrier()
        if mstop <= 2:
            return

        wqb = AR(OFF_X, [128, 8, 512])
        wkb = AR(OFF_X + 8192, [128, 8, 512])
        wvb = AR(OFF_X + 16384, [128, 8, 512])
        QbT = AR(OFF_X + 24576, [128, SEQ])
        Kb2 = [AR(OFF_X + 28672 + 4128 * i, [128, TCOLS]) for i in range(2)]
        Vb = [AR(OFF_X + 36928 + 4352 * i, [128, 17, 128]) for i in range(2)]
        P.add("pool", lambda e: e.memset(Kb2[0][64:128, :], 0.0), writes=[("KbT", b) for b in ablocks])
        P.add("pool", lambda e: e.memset(Kb2[1][0:64, :], 0.0), writes=[("KbT", b) for b in ablocks])
        for c in range(4):
            for (wt, col0, nm) in ((wqb, 768, "wqb"), (wkb, 1280, "wkb"), (wvb, 1792, "wvb")):
                cast_dma(wt[:, :, 128 * c:128 * c + 128],
                         wi_d[:, col0 + 128 * c:col0 + 128 * c + 128].rearrange("(k p) n -> p k n", p=128), (nm, c), nm)
        for i in range(2):
            P.add("pool", lambda e, i=i: e.memset(Vb[i][:, :, :], 0.0), writes=[("Vb", i, bi) for bi in range(17)])
        for c in range(4):
            for t in range(4):
                tc0, tn, tb = tile_cols(t)
                a = nxt("psQ", 2)
                mm_group(ps[a][:, :], [(wqb[:, k, 128 * c:128 * c + 128], uT[:, k, tc0:tc0 + tn]) for k in range(8)],
                         reads=[("wqb", c)] + uT_res(tb), writes=[("ps", a)])
                evac(QbT[:, 512 * t:512 * t + 512], ps[a][:, :], [("ps", a)], [("QbT", t)], eng="dve")
            for t in ["m", 0, 1, 2, 3]:
                tc0, tn, tb = tile_cols(t)
                a = nxt("psQ", 2)
                mm_group(ps[a][:, 0:tn], [(wkb[:, k, 128 * c:128 * c + 128], uT[:, k, tc0:tc0 + tn]) for k in range(8)],
                         reads=[("wkb", c)] + uT_res(tb), writes=[("ps", a)])
                evac(Kb2[0][0:64, tc0:tc0 + tn], ps[a][0:64, 0:tn], [("ps", a)], [("KbT", b) for b in tb], eng="dve")
                evac(Kb2[1][64:128, tc0:tc0 + tn], ps[a][64:128, 0:tn], [("ps", a)], [("KbT", b) for b in tb], eng="dve")
            for bi, b in enumerate(ablocks):
                c0, nr = blk_cols(b)
                a = nxt("psQ", 2)
                mm_group(ps[a][0:nr, 0:128], [(uT[:, k, c0:c0 + nr], wvb[:, k, 128 * c:128 * c + 128]) for k in range(8)],
                         reads=[("wvb", c)] + uT_res([b]), writes=[("ps", a)])
                evac(Vb[0][0:nr, bi, 0:64], ps[a][0:nr, 0:64], [("ps", a)], [("Vb", 0, bi)], eng="dve")
                evac(Vb[1][0:nr, bi, 64:128], ps[a][0:nr, 64:128], [("ps", a)], [("Vb", 1, bi)], eng="dve")
            fsteps = []
            for j in range(4):
                kbs = [0] + [1 + r for r in range(4 * j + 4)]
                n_j = 2 * len(kbs)
                cnt = 0
                for kb in kbs:
                    for hh in range(2):
                        fsteps.append((j, kb, hh, cnt == 0, cnt == n_j - 1))
                        cnt += 1

            def fparams(n):
                j, kb, hh, st, last = fsteps[n]
                if kb == 0:
                    nk, kc0, c0, kres = 16, 0, 0, ("KbT", "m")
                else:
                    r = kb - 1
                    nk, kc0, kres = 128, NMETA + 128 * r, ("KbT", r)
                    c0 = 128 * (r - 4 * j) if r >= 4 * j else 0
                diag = kb >= 1 and (kb - 1) >= 4 * j
                return j, kb, hh, st, last, nk, kc0, c0, kres, diag

            def fox_qk(n, c=c):
                j, kb, hh, st, last, nk, kc0, c0, kres, diag = fparams(n)
                sidx = SB[n % 4]
                pS = ps[sidx]
                base = 64 * hh
                def qk(e, w):
                    ins = e.matmul(pS[0:nk, c0:512], lhsT=Kb2[hh][:, kc0:kc0 + nk],
                                   rhs=QbT[:, 512 * j + c0:512 * j + 512], start=True, stop=True)
                    if w is not None:
                        ins._wait_ge(*w)
                    return ins
                P.add("pe", qk, reads=[kres, ("QbT", j)], writes=[("ps", sidx)], attach=True)

            def fox_rest(n, c=c):
                j, kb, hh, st, last, nk, kc0, c0, kres, diag = fparams(n)
                sidx = SB[n % 4]
                pS = ps[sidx]
                pj = n % NPT
                iO, iD = ODS[(4 * c + j) % 2]
                pO, pD = ps[iO], ps[iD]
                hd = 2 * c + hh
                P.add("act", lambda e: e.activation(
                    out=PT[pj][0:nk, c0:512], in_=pS[0:nk, c0:512], func=AF.Exp,
                    bias=biasF[0:nk, kb, j, hd:hd + 1], scale=0.125),
                    reads=[("ps", sidx), "biasF"], writes=[("PT", pj)])
                if diag:
                    P.add("dve", lambda e: e.tensor_tensor(
                        out=PT[pj][:, c0:c0 + 128], in0=PT[pj][:, c0:c0 + 128],
                        in1=cb[:, CB_MCUR:CB_MCUR + 128], op=ALU.mult),
                        reads=[("PT", pj), "cb"], writes=[("PT", pj)])
                oo = CB_OLO if hh == 0 else CB_OHI

                def pv(e, w):
                    i0 = e.matmul(pO[:, c0:512], lhsT=Vb[hh][0:nk, kb, :], rhs=PT[pj][0:nk, c0:512], start=st, stop=last)
                    if w is not None:
                        i0._wait_ge(*w)
                    return e.matmul(pD[:, c0:512], lhsT=cb[0:nk, oo:oo + 128], rhs=PT[pj][0:nk, c0:512], start=st, stop=last)
                P.add("pe", pv, reads=[("PT", pj), ("Vb", hh, kb), "cb"], writes=[("ps", iO), ("ps", iD)], attach=True)
                if defer:
                    defer.pop(0)()
                if last:
                    dj = 0
                    defer.append(lambda: P.add("dve", lambda e: e.reciprocal(out=den[dj][:, :], in_=pD[:, :]),
                                               reads=[("ps", iD)], writes=[("den", dj)]))
                    defer.append(lambda: P.add("dve", lambda e: e.tensor_tensor(
                        out=ObT[:, c, 512 * j:512 * j + 512], in0=pO[:, :], in1=den[dj][:, :], op=ALU.mult),
                        reads=[("ps", iO), ("den", dj)], writes=[("ObT", j)]))
            for n in range(len(fsteps) + LOOK):
                if n < len(fsteps):
                    fox_qk(n)
                if n - LOOK >= 0:
                    fox_rest(n - LOOK)
            while defer:
                defer.pop(0)()
        P.barrier()
        if mstop <= 3:
            return

        mixT = AR(OFF_MIX, [128, 8, SEQ])
        wo_t = AR(OFF_PW, [128, 8, D])
        ovl = [[("pw", 0, "a", 0), ("pw", 0, "a", 1), ("pw", 0, "b"), ("pw", 0, "ga")],
               [("pw", 0, "gb"), ("pw", 1, "a", 0), ("pw", 1, "a", 1), ("pw", 1, "b")]]

        def issue_wo(hf):
            P.add("pool", lambda e: e.dma_start(
                out=wo_t[:, 4 * hf:4 * hf + 4, :], in_=wo_d[512 * hf:512 * hf + 512, :].rearrange("(k p) n -> p k n", p=128)),
                writes=[("wout", hf)] + ovl[hf], dma=f"wout{hf}")
        ptmp = [AR(OFF_PTMP + 2048 * i, [128, 512], F32) for i in range(2)]
        def pw_tiles(st):
            sl = st % 2
            o = OFF_PW + 12288 * sl
            return sl, AR(o, [128, 4, 256]), AR(o + 2048, [128, 4, 256]), AR(o + 4096, [128, 8, 256]), AR(o + 8192, [128, 8, 256])

        def pw_issue(st):
            sl, wa_t, wb_t, wga_t, wgb_t = pw_tiles(st)
            c0 = 256 * st
            for hh in range(2):
                cast_dma(wa_t[64 * hh:64 * hh + 64, :, :],
                         wa_d[256 * hh:256 * hh + 256, c0:c0 + 256].rearrange("(g d) n -> d g n", d=64),
                         ("pw", sl, "a", hh), f"pw{sl}a{hh}")
            cast_dma(wb_t, wb_d[:, c0:c0 + 256].rearrange("(c p) n -> p c n", p=128), ("pw", sl, "b"), f"pw{sl}b")
            cast_dma(wga_t, wi_d[:, 2312 + c0:2312 + c0 + 256].rearrange("(k p) n -> p k n", p=128), ("pw", sl, "ga"), f"pw{sl}ga")
            cast_dma(wgb_t, wi_d[:, 3336 + c0:3336 + c0 + 256].rearrange("(k p) n -> p k n", p=128), ("pw", sl, "gb"), f"pw{sl}gb")
        pw_issue(0)
        for st in range(4):
            s, wa_t, wb_t, wga_t, wgb_t = pw_tiles(st)
            if st + 1 < 4:
                pw_issue(st + 1)
            if st == 3:
                issue_wo(0)
            for q in range(2):
                m = 2 * st + q
                for t in range(4):
                    tc0, tn, tb = tile_cols(t)
                    ia, ib, iga, igb = (nxt("psP", 8) for _ in range(4))
                    mm_group(ps[ia][:, :], [(wa_t[:, k, 128 * q:128 * q + 128], OaT[:, k, 512 * t:512 * t + 512]) for k in range(4)],
                             reads=[("pw", s, "a", 0), ("pw", s, "a", 1), ("OaT", t)], writes=[("ps", ia)])
                    mm_group(ps[ib][:, :], [(wb_t[:, k, 128 * q:128 * q + 128], ObT[:, k, 512 * t:512 * t + 512]) for k in range(4)],
                             reads=[("pw", s, "b"), ("ObT", t)], writes=[("ps", ib)])
                    mm_group(ps[iga][:, :], [(wga_t[:, k, 128 * q:128 * q + 128], uT[:, k, tc0:tc0 + 512]) for k in range(8)],
                             reads=[("pw", s, "ga")] + uT_res(tb), writes=[("ps", iga)])
                    mm_group(ps[igb][:, :], [(wgb_t[:, k, 128 * q:128 * q + 128], uT[:, k, tc0:tc0 + 512]) for k in range(8)],
                             reads=[("pw", s, "gb")] + uT_res(tb), writes=[("ps", igb)])
                    ja, jb = 0, 1
                    P.add("act", lambda e, ja=ja, iga=iga: e.activation(out=ptmp[ja][:, :], in_=ps[iga][:, :], func=AF.Sigmoid),
                          reads=[("ps", iga)], writes=[("ptmp", ja)])
                    P.add("act", lambda e, jb=jb, igb=igb: e.activation(out=ptmp[jb][:, :], in_=ps[igb][:, :], func=AF.Sigmoid),
                          reads=[("ps", igb)], writes=[("ptmp", jb)])
                    P.add("dve", lambda e, ja=ja, ia=ia: e.tensor_tensor(out=ptmp[ja][:, :], in0=ptmp[ja][:, :], in1=ps[ia][:, :], op=ALU.mult),
                          reads=[("ptmp", ja), ("ps", ia)], writes=[("ptmp", ja)])
                    P.add("dve", lambda e, jb=jb, ib=ib: e.tensor_tensor(out=ptmp[jb][:, :], in0=ptmp[jb][:, :], in1=ps[ib][:, :], op=ALU.mult),
                          reads=[("ptmp", jb), ("ps", ib)], writes=[("ptmp", jb)])
                    P.add("pool", lambda e, ja=ja, jb=jb, m=m, t=t: e.tensor_tensor(
                        out=mixT[:, m, 512 * t:512 * t + 512], in0=ptmp[ja][:, :], in1=ptmp[jb][:, :], op=ALU.add),
                        reads=[("ptmp", ja), ("ptmp", jb)], writes=[("mixT", t)])
        issue_wo(1)
        hook = flush = None
        if post_tail is not None:
            hook, flush = post_tail()
        for b in range(NBLK):
            for half in range(2):
                o = nxt("psP", 8)
                mm_group(ps[o][:, :], [(mixT[:, k, 128 * b:128 * b + 128], wo_t[:, k, 512 * half:512 * half + 512]) for k in range(8)],
                         reads=[("mixT", b // 4), ("wout", 0), ("wout", 1)], writes=[("ps", o)])
                dst = h[:, b, 512 * half:512 * half + 512]
                P.add("dve", lambda e, dst=dst, o=o: e.tensor_tensor(out=dst, in0=ps[o][:, :], in1=dst, op=ALU.add),
                      reads=[("ps", o), ("h", b)], writes=[("h", b)])
            if hook is not None:
                hook(b)
        if flush is not None:
            flush()

    def final_out(s, raw, hooked=False, after_block=None):
        ost = [AR(OFF_OST + 4096 * i, [128, D], F32) for i in range(2)]
        if raw:
            for b in range(NBLK):
                P.add("sp", lambda e, b=b: e.dma_start(out=out_d[s, 128 * b:128 * b + 128, :], in_=h[:, b, :]),
                      reads=[("h", b)], dma=f"out{b % 2}")
            return
        P.add("sp", lambda e: e.dma_start(out=gb[:], in_=n4_d.broadcast_to([128, D])), writes=["gb"], dma="gb")
        P.add("pool", lambda e: e.memset(ss[:], 0.0), writes=ALLSS)
        fjunk = AR(OFF_OST + 8192, [128, D])

        def a1(b):
            P.add("act", lambda e: e.activation(out=fjunk[:, :], in_=h[:, b, :], func=AF.Square, accum_out=ss[:, b:b + 1]),
                  reads=[("h", b), ("ss", b)], writes=[("ss", b), "fjunk"], full=True)
            P.add("act", lambda e: e.activation(out=rs[:, b:b + 1], in_=ss[:, b:b + 1], func=AF.Sqrt,
                                                bias=epsc[:, 0:1], scale=1.0 / D),
                  reads=[("ss", b), "epsc"], writes=[("rs", b)])

        def a2(b):
            j = nxt("ost", 2)
            P.add("dve", lambda e: e.reciprocal(out=rs[:, b:b + 1], in_=rs[:, b:b + 1]), reads=[("rs", b)], writes=[("rs", b)])
            P.add("dve", lambda e: e.scalar_tensor_tensor(
                out=ost[j][:, :], in0=h[:, b, :], scalar=rs[:, b:b + 1], in1=gb[:, :], op0=ALU.mult, op1=ALU.mult),
                reads=[("h", b), ("rs", b), "gb"], writes=[("ost", j)])
            P.add("sp", lambda e: e.dma_start(out=out_d[s, 128 * b:128 * b + 128, :], in_=ost[j][:, :]),
                  reads=[("ost", j)], dma=f"out{j}")
            if after_block is not None:
                after_block(b)
        if hooked:
            seq = []

            def hook(b):
                seq.append(b)
                if len(seq) >= 2:
                    a2(seq[-2])
                a1(b)

            def flush():
                a2(seq[-1])
            return hook, flush
        a1(0)
        for b in range(NBLK):
            if b + 1 < NBLK:
                a1(b + 1)
            a2(b)

    def load_x(s, b):
        P.add("sp", lambda e: e.dma_start(out=h[:, b, :], in_=x_d[s, 128 * b:128 * b + 128, :]),
              writes=[("h", b)], dma=f"x{b % 4}")

    for s in range(2):
        hm = AR(OFF_HM, [128, D], F32)
        if s == 0 or stage != 3:
            for b in range(NBLK):
                load_x(s, b)
        P.add("sp", lambda e: e.dma_start(out=hm[0:NMETA, :], in_=meta_d), writes=[("h", "m")], dma="xm")
        if stage == 3:
            ffn(w1i_d, w1o_d, n1_d, True, hm, "f1",
                tail=lambda: norm_to_uT(n2_d, ["m"] + list(range(NBLK)), hm, hooked=True,
                                        junk=AR(OFF_OST, [128, D]), junkres=("ost", 0)))
            P.barrier()
            mixer(hm, post_tail=lambda: norm_to_uT(n3_d, list(range(NBLK)), hm, hooked=True,
                                                    junk=AR(OFF_PTMP, [128, D]), junkres=("ptmp", 0)))
            P.barrier()
            ffn(w2i_d, w2o_d, n3_d, False, hm, "f2", do_norm=False,
                tail=lambda s=s: final_out(s, raw=False, hooked=True,
                                           after_block=((lambda b: load_x(1, b)) if s == 0 else None)))
        else:
            ffn(w1i_d, w1o_d, n1_d, True, hm, "f1")
            if stage >= 2:
                norm_to_uT(n2_d, ["m"] + list(range(NBLK)), hm)
                P.barrier()
                mixer(hm, mstop=(stage - 10 if stage >= 10 else 9))
            P.barrier()
            final_out(s, raw=True)
    P.emit()
    es.close()
    return nc


_CACHE = {}


def kernel(x, meta_tokens, ffn1_norm, ffn1_w_in, ffn1_w_out, mix_norm, w_in, b_forget, attn_sinks,
           w_branch_a, w_branch_b, w_out, ffn2_norm, ffn2_w_in, ffn2_w_out, final_norm, _stage=3, _cores=8):
    f = lambda a: np.ascontiguousarray(np.asarray(a, dtype=np.float32))
    x = f(x)
    cf, cb = make_consts()
    shared = {
        "meta": f(meta_tokens), "n1": f(ffn1_norm).reshape(1, D), "n2": f(mix_norm).reshape(1, D),
        "n3": f(ffn2_norm).reshape(1, D), "n4": f(final_norm).reshape(1, D),
        "w1i": f(ffn1_w_in)[0], "w1o": f(ffn1_w_out)[0], "w2i": f(ffn2_w_in)[0], "w2o": f(ffn2_w_out)[0],
        "wi": f(w_in)[0], "bfg": f(b_forget).reshape(1, 8), "snk": f(attn_sinks).reshape(1, 8),
        "wa": f(w_branch_a)[0], "wb": f(w_branch_b)[0], "wo": f(w_out)[0], "cf": cf, "cb": cb,
    }
    if _stage not in _CACHE:
        _CACHE[_stage] = build(_stage)
    nc = _CACHE[_stage]
    in_maps = [dict(shared, x=x[2 * c:2 * c + 2]) for c in range(_cores)]
    res = run_bass_kernel_spmd(nc, in_maps, core_ids=list(range(_cores)))
    return np.concatenate([r["out"] for r in res.results], axis=0)
```

```python
import contextlib
import numpy as np
import ml_dtypes
import concourse.bass as bass
import concourse.mybir as mybir
from concourse.bass_utils import run_bass_kernel_spmd

F32 = mybir.dt.float32
BF16 = mybir.dt.bfloat16
AF = mybir.ActivationFunctionType
ALU = mybir.AluOpType

D = 1024
SEQ = 2048
NBLK = 16
NMETA = 16
TCOLS = SEQ + NMETA
DFF = 2816
NCH = 22
PASSES = [(0, 6), (6, 6), (12, 5), (17, 5)]
EPS = 1e-6
SLOPES = [2.0 ** (-(h + 1)) for h in range(8)]
ENGS = ("pe", "act", "dve", "pool", "sp")
ARENA_BYTES = 98304
DBG_SWA = 9

CF_TRI, CF_ONES, CF_IOTA, CF_BCUR, CF_BPREV, CF_BMETA, CF_N = 0, 128, 256, 384, 392, 400, 528
CB_ID, CB_MCUR, CB_MPREV, CB_OLO, CB_OHI, CB_N = 0, 128, 640, 1152, 1280, 1408


def make_consts():
    cf = np.zeros((128, CF_N), np.float32)
    p = np.arange(128)
    cf[:, CF_TRI:CF_TRI + 128] = (p[:, None] <= p[None, :]).astype(np.float32)
    cf[:, CF_ONES:CF_ONES + 128] = 1.0
    cf[:, CF_IOTA:CF_IOTA + 128] = p[None, :].astype(np.float32)
    sl = np.array(SLOPES, np.float32)
    cf[:, CF_BCUR:CF_BCUR + 8] = p[:, None] * sl[None, :]
    cf[:, CF_BPREV:CF_BPREV + 8] = (p[:, None] - 128.0) * sl[None, :]
    bm = np.zeros((128, 16, 8), np.float32)
    for i in range(16):
        bm[:, i, :] = (p[:, None] - 16.0 - 128.0 * i) * sl[None, :]
    cf[:, CF_BMETA:CF_BMETA + 128] = bm.reshape(128, 128)
    cb = np.zeros((128, CB_N), np.float32)
    cb[:, CB_ID:CB_ID + 128] = np.eye(128)
    mc = (p[:, None] <= p[None, :]).astype(np.float32)
    mp = (p[:, None] > p[None, :]).astype(np.float32)
    cb[:, CB_MCUR:CB_MCUR + 512] = np.tile(mc, (1, 4))
    cb[:, CB_MPREV:CB_MPREV + 512] = np.tile(mp, (1, 4))
    cb[:, CB_OLO:CB_OLO + 64] = 1.0
    cb[:, CB_OHI + 64:CB_OHI + 128] = 1.0
    return cf, cb.astype(ml_dtypes.bfloat16)


class Op:
    __slots__ = ("idx", "eng", "fn", "reads", "writes", "dma", "deps", "signal", "semval", "semname", "attach")

    def __init__(self, idx, eng, fn, reads, writes, dma):
        self.idx, self.eng, self.fn, self.reads, self.writes, self.dma = idx, eng, fn, reads, writes, dma
        self.deps = set()
        self.signal = False
        self.semval = 0
        self.semname = None
        self.attach = False


class Prog:
    def __init__(self, nc):
        self.nc = nc
        self.ops = []
        self.last_writer = {}
        self.readers = {}
        self.last_on = {}

    def add(self, eng, fn, reads=(), writes=(), dma=None, full=False, attach=False):
        if dma is not None:
            writes = tuple(writes) + (("__slot", dma),)
        op = Op(len(self.ops), eng, fn, tuple(reads), tuple(writes), dma)
        op.attach = attach
        deps = set()
        for r in op.reads:
            w = self.last_writer.get(r)
            if w is not None:
                deps.add(w)
        for r in op.writes:
            w = self.last_writer.get(r)
            if w is not None:
                deps.add(w)
            deps.update(self.readers.get(r, ()))
        for d in deps:
            dop = self.ops[d]
            if dop.dma is None and dop.eng == eng and dma is None:
                if eng == "pe":
                    continue
                if not full and not any(self.last_writer.get(r) == d for r in op.reads):
                    continue
            op.deps.add(d)
        for r in op.reads:
            self.readers.setdefault(r, []).append(op.idx)
        for r in op.writes:
            self.last_writer[r] = op.idx
            self.readers[r] = []
        self.ops.append(op)
        if dma is None:
            self.last_on[eng] = op.idx
        return op

    def barrier(self):
        lasts = dict(self.last_on)
        dmas = [idx for (k, idx) in self.last_writer.items() if isinstance(k, tuple) and k[0] == "__slot"]
        for e in ENGS:
            op = Op(len(self.ops), e, None, (), (), None)
            for e2, idx in lasts.items():
                if e2 != e:
                    op.deps.add(idx)
            op.deps.update(dmas)
            self.ops.append(op)

    def emit(self):
        nc, ops = self.nc, self.ops
        for op in ops:
            best = {}
            keep = set()
            for d in op.deps:
                dop = ops[d]
                if dop.dma is not None:
                    keep.add(d)
                elif d > best.get(dop.eng, -1):
                    best[dop.eng] = d
            keep.update(best.values())
            op.deps = keep
            for d in keep:
                ops[d].signal = True
        cnt = {}
        for op in ops:
            if op.dma is not None:
                op.signal = True
                op.semname = "d_" + op.dma
                cnt[op.semname] = cnt.get(op.semname, 0) + 16
                op.semval = cnt[op.semname]
            elif op.signal:
                op.semname = "e_" + op.eng
                cnt[op.semname] = cnt.get(op.semname, 0) + 1
                op.semval = cnt[op.semname]
        semnames = sorted(cnt)
        with contextlib.ExitStack() as es:
            sems = {n: es.enter_context(nc.semaphore(n)) for n in semnames}
            block = es.enter_context(nc.Block())
            by_eng = {e: [op for op in ops if op.eng == e] for e in ENGS}
            final = [(n, cnt[n]) for n in semnames if n.startswith("d_")]

            def run(engname, eng):
                known = {}
                for op in by_eng[engname]:
                    waits = {}
                    for d in op.deps:
                        dop = ops[d]
                        if dop.semval > waits.get(dop.semname, 0):
                            waits[dop.semname] = dop.semval
                    need = [(n, v) for n, v in sorted(waits.items()) if known.get(n, 0) < v]
                    emb = None
                    if op.attach and need:
                        emb = need.pop()
                    for n, v in need:
                        eng.wait_ge(sems[n], v)
                        known[n] = v
                    if op.fn is None:
                        continue
                    if op.attach:
                        if emb is not None:
                            known[emb[0]] = emb[1]
                        ins = op.fn(eng, None if emb is None else (sems[emb[0]], emb[1]))
                    else:
                        ins = op.fn(eng)
                    if op.signal:
                        ins.then_inc(sems[op.semname], 16 if op.dma is not None else 1)
                if engname == "sp":
                    for n, v in final:
                        eng.wait_ge(sems[n], v)

            @block.tensor
            def _(e):
                run("pe", e)

            @block.scalar
            def _(e):
                run("act", e)

            @block.vector
            def _(e):
                run("dve", e)

            @block.gpsimd
            def _(e):
                run("pool", e)

            @block.sync
            def _(e):
                run("sp", e)


def build(stage=3):
    nc = bass.Bass("TRN2", target_bir_lowering=False)
    dt_in = lambda n, s, d=F32: nc.dram_tensor(n, s, d, kind="ExternalInput").ap()
    x_d = dt_in("x", [2, SEQ, D])
    meta_d = dt_in("meta", [NMETA, D])
    n1_d, n2_d, n3_d, n4_d = (dt_in(n, [1, D]) for n in ("n1", "n2", "n3", "n4"))
    w1i_d, w1o_d = dt_in("w1i", [D, 2 * DFF]), dt_in("w1o", [DFF, D])
    w2i_d, w2o_d = dt_in("w2i", [D, 2 * DFF]), dt_in("w2o", [DFF, D])
    wi_d = dt_in("wi", [D, 4360])
    bf_d, sk_d = dt_in("bfg", [1, 8]), dt_in("snk", [1, 8])
    wa_d, wb_d, wo_d = dt_in("wa", [512, D]), dt_in("wb", [512, D]), dt_in("wo", [D, D])
    cf_d, cb_d = dt_in("cf", [128, CF_N]), dt_in("cb", [128, CB_N], BF16)
    out_d = nc.dram_tensor("out", [2, SEQ, D], F32, kind="ExternalOutput").ap()

    es = contextlib.ExitStack()
    sb = lambda n, s, d: es.enter_context(nc.sbuf_tensor(n, s, d))
    h = sb("h", [128, NBLK, D], F32)
    uT = sb("uT", [128, 8, TCOLS], BF16)
    cf = sb("cf_s", [128, CF_N], F32)
    cb = sb("cb_s", [128, CB_N], BF16)
    gb = sb("gb", [128, D], F32)
    ss = sb("ss", [128, 17], F32)
    rs = sb("rs", [128, 17], F32)
    epsc = sb("epsc", [128, 1], F32)
    bfg = sb("bfg_s", [128, 8], F32)
    snk = sb("snk_s", [128, 8], F32)
    sinktab = sb("sinktab", [128, 512], F32)
    arena = sb("arena", [128, ARENA_BYTES // 2], BF16)
    ps = [es.enter_context(nc.psum_tensor(f"ps{i}", [128, 512], F32)) for i in range(8)]
    pt = [ps[6 + i][:, 0:256].bitcast(BF16) for i in range(2)]

    def AR(off, shape, dtype=BF16):
        n = int(np.prod(shape[1:]))
        nb = n * (4 if dtype == F32 else 2)
        assert off % 4 == 0 and off + nb <= ARENA_BYTES, (off, nb)
        v = arena[:, off // 2:(off + nb) // 2]
        if dtype == F32:
            v = v.bitcast(F32)
        if len(shape) == 2:
            return v
        names = " ".join(f"a{i}" for i in range(len(shape) - 1))
        kw = {f"a{i}": shape[i + 1] for i in range(len(shape) - 2)}
        return v.rearrange(f"p ({names}) -> p {names}", **kw)

    P = Prog(nc)
    rot = {}

    def nxt(name, n):
        rot[name] = (rot.get(name, -1) + 1) % n
        return rot[name]

    P.add("sp", lambda e: e.dma_start(out=cf[:], in_=cf_d), writes=["cf"], dma="cf")
    P.add("sp", lambda e: e.dma_start(out=cb[:], in_=cb_d), writes=["cb"], dma="cb")
    P.add("sp", lambda e: e.dma_start(out=bfg[:], in_=bf_d.broadcast_to([128, 8])), writes=["bfg"], dma="bfg")
    P.add("sp", lambda e: e.dma_start(out=snk[:], in_=sk_d.broadcast_to([128, 8])), writes=["snk"], dma="snk")
    P.add("pool", lambda e: e.memset(epsc[:], EPS), writes=["epsc"])
    onec = sb("onec", [128, 1], F32)
    P.add("pool", lambda e: e.memset(onec[:], 1.0), writes=["onec"])
    ALLSS = ["ss"] + [("ss", b) for b in ["m"] + list(range(NBLK))]
    for g in range(4):
        for hh in range(2):
            hd = 4 * hh + g
            P.add("act", lambda e, g=g, hh=hh, hd=hd: e.activation(
                out=sinktab[64 * hh:64 * hh + 64, 128 * g:128 * g + 128],
                in_=cf[64 * hh:64 * hh + 64, CF_IOTA:CF_IOTA + 128], func=AF.Exp,
                bias=snk[64 * hh:64 * hh + 64, hd:hd + 1], scale=SLOPES[hd]),
                reads=["cf", "snk"], writes=["sinktab"])

    def tile_cols(t):
        if t == "m":
            return 0, NMETA, ["m"]
        return NMETA + 512 * t, 512, [4 * t + j for j in range(4)]

    def blk_cols(b):
        if b == "m":
            return 0, NMETA
        return NMETA + 128 * b, 128

    def cast_dma(dst, src, res, slot):
        P.add("pool", lambda e: e.dma_start(out=dst, in_=src), writes=[res], dma=slot)

    def mm_group(out, pairs, reads, writes):
        def fn(e):
            n = len(pairs)
            ins = None
            for i, (l, r) in enumerate(pairs):
                ins = e.matmul(out, lhsT=l, rhs=r, start=(i == 0), stop=(i == n - 1))
            return ins
        P.add("pe", fn, reads=reads, writes=writes)

    evac_flip = [0]

    def evac(out, in_, reads, writes, eng=None):
        if eng is None:
            evac_flip[0] ^= 1
            eng = "dve" if evac_flip[0] else "act"
        if eng == "act":
            P.add("act", lambda e: e.copy(out=out, in_=in_), reads=reads, writes=writes)
        else:
            P.add(eng, lambda e: e.tensor_copy(out=out, in_=in_), reads=reads, writes=writes)

    def norm_to_uT(gain_d, blocks, hm, hooked=False, junk=None, junkres="junk"):
        ut = [AR(OFF_UT + 2048 * i, [128, D]) for i in range(2)]
        P.add("sp", lambda e: e.dma_start(out=gb[:], in_=gain_d.broadcast_to([128, D])), writes=["gb"], dma="gb")
        P.add("pool", lambda e: e.memset(ss[:], 0.0), writes=ALLSS)
        jmap = {}

        def geom(b):
            c0, nr = blk_cols(b)
            src = hm[0:nr, :] if b == "m" else h[:, b, :]
            col = 16 if b == "m" else b
            return c0, nr, src, col

        def a1(b, j=None):
            c0, nr, src, col = geom(b)
            if junk is not None:
                out, ores = junk[0:nr, :], junkres
            else:
                out, ores = ut[j][0:nr, :], ("ut", j)
            P.add("act", lambda e: e.activation(out=out, in_=src, func=AF.Square, accum_out=ss[0:nr, col:col + 1]),
                  reads=[("h", b), ("ss", b)], writes=[("ss", b), ores], full=True)
            P.add("act", lambda e: e.activation(
                out=rs[0:nr, col:col + 1], in_=ss[0:nr, col:col + 1], func=AF.Sqrt, bias=epsc[0:nr, 0:1], scale=1.0 / D),
                reads=[("ss", b), "epsc"], writes=[("rs", b)])

        def a2(b, j):
            c0, nr, src, col = geom(b)
            jmap[b] = j
            P.add("dve", lambda e: e.reciprocal(out=rs[0:nr, col:col + 1], in_=rs[0:nr, col:col + 1]),
                  reads=[("rs", b)], writes=[("rs", b)])
            P.add("dve", lambda e: e.scalar_tensor_tensor(
                out=ut[j][0:nr, :], in0=src, scalar=rs[0:nr, col:col + 1], in1=gb[0:nr, :],
                op0=ALU.mult, op1=ALU.mult), reads=[("h", b), ("rs", b), "gb"], writes=[("ut", j)])

        def stage_a(b):
            j = nxt("ut", 2)
            a1(b, j)
            a2(b, j)

        def stage_b(b):
            c0, nr = blk_cols(b)
            j = jmap[b]
            for half in range(2):
                def fn(e, half=half):
                    ins = None
                    for c in range(4):
                        cc = 4 * half + c
                        ins = e.transpose(pt[half][:, 128 * c:128 * c + nr], ut[j][0:nr, 128 * cc:128 * cc + 128],
                                          cb[0:nr, CB_ID:CB_ID + nr])
                    return ins
                P.add("pe", fn, reads=[("ut", j), "cb"], writes=[("ps", 6 + half)])
                src_v = pt[half][:, :].rearrange("p (c n) -> p c n", c=4)[:, :, 0:nr]
                evac(uT[:, 4 * half:4 * half + 4, c0:c0 + nr], src_v, [("ps", 6 + half)], [("uT", b, half)],
                     eng=("act" if half == 0 else "dve"))
        if hooked:
            assert junk is not None
            seq = []

            def hook(b):
                seq.append(b)
                n = len(seq)
                if n >= 3:
                    stage_b(seq[n - 3])
                if n >= 2:
                    a2(seq[n - 2], nxt("ut", 2))
                a1(b)

            def flush():
                n = len(seq)
                if n >= 2:
                    stage_b(seq[n - 2])
                a2(seq[n - 1], nxt("ut", 2))
                stage_b(seq[n - 1])
            return hook, flush
        stage_a(blocks[0])
        for i, b in enumerate(blocks):
            if i + 1 < len(blocks):
                stage_a(blocks[i + 1])
            stage_b(b)

    def uT_res(blks):
        return [("uT", b, hf) for b in blks for hf in range(2)]

    def ffn(w_in_d, w_out_d, gain_d, with_meta, hm, tagp, do_norm=True, tail=None):
        blocks = (["m"] if with_meta else []) + list(range(NBLK))
        if do_norm:
            norm_to_uT(gain_d, blocks, hm)
        tiles = (["m"] if with_meta else []) + [0, 1, 2, 3]
        act = AR(OFF_ACT, [128, 6, TCOLS])
        wis = [AR(OFF_WI + 8192 * i, [128, 2, 8, 256]) for i in range(3)]
        wos = [AR(OFF_WO + 12288 * i, [128, 6, D]) for i in range(2)]
        tmp = [AR(OFF_TMP + 2048 * i, [128, 512], F32) for i in range(2)]
        for (m0, nch) in PASSES:
            wslot = nxt("wo", 2)
            wo_t = wos[wslot]
            ml = 0
            while ml < nch:
                ns = min(2, nch - ml)
                s = nxt("wi", 3)
                wi_t = wis[s]
                c0 = (m0 + ml) * 128
                cast_dma(wi_t[:, 0, :, 0:ns * 128], w_in_d[:, c0:c0 + ns * 128].rearrange("(k p) n -> p k n", p=128),
                         ("wi", s, 0), f"wi{s}g")
                cast_dma(wi_t[:, 1, :, 0:ns * 128],
                         w_in_d[:, DFF + c0:DFF + c0 + ns * 128].rearrange("(k p) n -> p k n", p=128),
                         ("wi", s, 1), f"wi{s}u")
                if ml == 0:
                    cast_dma(wo_t[:, 0:nch, :], w_out_d[m0 * 128:(m0 + nch) * 128, :].rearrange("(k p) n -> p k n", p=128),
                             ("wo", wslot), f"wo{wslot}")
                for q in range(ns):
                    mloc = ml + q
                    for t in tiles:
                        tc0, tn, tb = tile_cols(t)
                        a = nxt("psA", 2)
                        pA, pB = ps[2 * a], ps[2 * a + 1]
                        for which, pp in ((0, pA), (1, pB)):
                            mm_group(pp[:, 0:tn],
                                     [(wi_t[:, which, k, q * 128:(q + 1) * 128], uT[:, k, tc0:tc0 + tn]) for k in range(8)],
                                     reads=[("wi", s, which)] + uT_res(tb), writes=[("ps", 2 * a + which)])
                        j = nxt("tmp", 2)
                        P.add("act", lambda e, j=j, pA=pA, tn=tn: e.activation(out=tmp[j][:, 0:tn], in_=pA[:, 0:tn], func=AF.Silu),
                              reads=[("ps", 2 * a)], writes=[("tmp", j)])
                        P.add("dve", lambda e, j=j, pB=pB, tn=tn, mloc=mloc, tc0=tc0: e.tensor_tensor(
                            out=act[:, mloc, tc0:tc0 + tn], in0=tmp[j][:, 0:tn], in1=pB[:, 0:tn], op=ALU.mult),
                            reads=[("tmp", j), ("ps", 2 * a + 1)], writes=[("act", mloc, t)])
                ml += ns
            hook = flush = None
            if tail is not None and (m0, nch) == PASSES[-1]:
                hook, flush = tail()
            for b in blocks:
                bc0, nr = blk_cols(b)
                t = "m" if b == "m" else b // 4
                for half in range(2):
                    o = 4 + nxt("psO", 2)
                    mm_group(ps[o][0:nr, :],
                             [(act[:, k, bc0:bc0 + nr], wo_t[:, k, 512 * half:512 * half + 512]) for k in range(nch)],
                             reads=[("act", k, t) for k in range(nch)] + [("wo", wslot)], writes=[("ps", o)])
                    dst = hm[0:nr, 512 * half:512 * half + 512] if b == "m" else h[:, b, 512 * half:512 * half + 512]
                    P.add("dve", lambda e, dst=dst, o=o, nr=nr: e.scalar_tensor_tensor(
                        out=dst, in0=ps[o][0:nr, :], scalar=0.5, in1=dst, op0=ALU.mult, op1=ALU.add),
                        reads=[("ps", o), ("h", b)], writes=[("h", b)])
                if hook is not None:
                    hook(b)
            if flush is not None:
                flush()

    OFF_HM = 0
    OFF_UT = 4096
    OFF_ACT = 8192
    OFF_WI = OFF_ACT + 24768
    OFF_WO = OFF_WI + 24576
    OFF_TMP = OFF_WO + 24576
    OFF_OST = OFF_TMP + 4096
    assert OFF_OST + 8192 <= ARENA_BYTES
    OFF_OA = 8192
    OFF_OB = OFF_OA + 16384
    OFF_LF = OFF_OB + 16384
    OFF_BIASF = OFF_LF + 3264
    OFF_PT = OFF_BIASF + 2176
    OFF_DEN = OFF_PT + 4096
    OFF_X = OFF_DEN + 2048
    assert OFF_X + 45632 <= ARENA_BYTES, OFF_X
    OFF_PW = OFF_OB + 16384
    OFF_MIX = OFF_PW + 24576
    assert OFF_MIX + 32768 <= ARENA_BYTES
    OFF_PTMP = 0

    def mixer(hm, mstop=9, post_tail=None):
        ablocks = ["m"] + list(range(NBLK))
        NPT = 4
        PT = [AR(OFF_PT + 1024 * i, [128, 512]) for i in range(NPT)]
        den = [AR(OFF_DEN, [128, 512], F32)]
        OaT = AR(OFF_OA, [128, 4, SEQ])
        ObT = AR(OFF_OB, [128, 4, SEQ])
        lfv = [AR(OFF_LF + 544 * i, [128, 136], F32) for i in range(6)]
        xb, lt, tots, offv, cum = lfv[0], lfv[1], lfv[2], lfv[3], lfv[4]
        biasF = AR(OFF_BIASF, [128, 17, 4, 8], F32)
        wf = lfv[5][:, :].bitcast(BF16)[:, 0:64].rearrange("p (k n) -> p k n", k=8)
        cast_dma(wf, wi_d[:, 2304:2312].rearrange("(k p) n -> p k n", p=128), "wf", "wf")

        def f_fn(e):
            ins = None
            for bi, b in enumerate(ablocks):
                c0, nr = blk_cols(b)
                for k in range(8):
                    ins = e.matmul(ps[0][0:nr, 8 * bi:8 * bi + 8], lhsT=uT[:, k, c0:c0 + nr], rhs=wf[:, k, :],
                                   start=(k == 0), stop=(k == 7))
            return ins
        P.add("pe", f_fn, reads=["wf"] + uT_res(ablocks), writes=[("ps", 0)])
        P.add("pool", lambda e: e.memset(lt[:], 0.0), writes=["lt"])
        for (r0, r1, c0, c1) in ((0, 16, 0, 8), (0, 128, 8, 136)):
            bb = bfg[r0:r1, :] if c1 == 8 else bfg[r0:r1, :].unsqueeze(1).to_broadcast([r1 - r0, 16, 8])
            i0 = ps[0][r0:r1, c0:c1] if c1 == 8 else ps[0][r0:r1, c0:c1].rearrange("p (b h) -> p b h", h=8)
            o0 = xb[r0:r1, c0:c1] if c1 == 8 else xb[r0:r1, c0:c1].rearrange("p (b h) -> p b h", h=8)
            P.add("dve", lambda e, bb=bb, i0=i0, o0=o0: e.tensor_tensor(out=o0, in0=i0, in1=bb, op=ALU.add),
                  reads=[("ps", 0), "bfg"], writes=["xb"])
            P.add("act", lambda e, r0=r0, r1=r1, c0=c0, c1=c1: e.activation(
                out=xb[r0:r1, c0:c1], in_=xb[r0:r1, c0:c1], func=AF.Exp, scale=-1.0), reads=["xb"], writes=["xb"])
            P.add("act", lambda e, r0=r0, r1=r1, c0=c0, c1=c1: e.activation(
                out=lt[r0:r1, c0:c1], in_=xb[r0:r1, c0:c1], func=AF.Ln, bias=onec[r0:r1, 0:1]), reads=["xb", "lt", "onec"], writes=["lt"])
        P.add("pe", lambda e: e.matmul(ps[1][:, 0:136], lhsT=cf[:, CF_ONES:CF_ONES + 128], rhs=lt[:, :], start=True, stop=True),
              reads=["lt", "cf"], writes=[("ps", 1)])
        P.add("pe", lambda e: e.matmul(ps[2][:, 0:136], lhsT=cf[:, CF_TRI:CF_TRI + 128], rhs=lt[:, :], start=True, stop=True),
              reads=["lt", "cf"], writes=[("ps", 2)])
        P.add("dve", lambda e: e.tensor_copy(out=tots[:, :], in_=ps[1][:, 0:136]), reads=[("ps", 1)], writes=["tots"])
        P.add("pool", lambda e: e.memset(offv[:, :], 0.0), writes=["offv"])
        for b in range(1, 17):
            P.add("dve", lambda e, b=b: e.tensor_tensor(out=offv[:, 8 * b:8 * b + 8], in0=offv[:, 8 * b - 8:8 * b],
                                                        in1=tots[:, 8 * b - 8:8 * b], op=ALU.add),
                  reads=["offv", "tots"], writes=["offv"])
        P.add("dve", lambda e: e.tensor_tensor(out=cum[:, :], in0=ps[2][:, 0:136], in1=offv[:, :], op=ALU.add),
              reads=[("ps", 2), "offv"], writes=["cum"])
        cum3 = cum[:, :].rearrange("p (b h) -> p b h", h=8)
        for j in range(4):
            ref = 4 * j + 3
            for hd in range(8):
                P.add("dve", lambda e, j=j, hd=hd, ref=ref: e.tensor_scalar(
                    out=biasF[:, :, j, hd], in0=cum3[:, :, hd], scalar1=offv[:, 8 * ref + hd:8 * ref + hd + 1],
                    scalar2=None, op0=ALU.subtract), reads=["cum", "offv"], writes=["biasF"])
        if mstop <= 1:
            return

        wqa = AR(OFF_X, [128, 8, 4, 2, 64])
        wka = AR(OFF_X + 8192, [128, 8, 128])
        wva = AR(OFF_X + 10240, [128, 8, 128])
        QaT = AR(OFF_X + 12288, [128, 4, SEQ])
        Ka2 = [AR(OFF_X + 28672 + 4128 * i, [128, TCOLS]) for i in range(2)]
        Va = [AR(OFF_X + 36928 + 4352 * i, [128, 17, 128]) for i in range(2)]
        P.add("pool", lambda e: e.memset(Ka2[0][64:128, :], 0.0), writes=[("KaT", b) for b in ablocks])
        P.add("pool", lambda e: e.memset(Ka2[1][0:64, :], 0.0), writes=[("KaT", b) for b in ablocks])
        for hh in range(2):
            for g in range(4):
                cq = 256 * hh + 64 * g
                cast_dma(wqa[:, :, g, hh, :], wi_d[:, cq:cq + 64].rearrange("(k p) d -> p k d", p=128),
                         ("wqa", hh, g), f"wqa{hh}{g}")
        cast_dma(wka, wi_d[:, 512:640].rearrange("(k p) n -> p k n", p=128), "wka", "wka")
        cast_dma(wva, wi_d[:, 640:768].rearrange("(k p) n -> p k n", p=128), "wva", "wva")
        for i in range(2):
            P.add("pool", lambda e, i=i: e.memset(Va[i][:, :, :], 0.0), writes=[("Va", i)])
        for g in range(4 if DBG_SWA >= 0.2 else 0):
            for t in range(4):
                tc0, tn, tb = tile_cols(t)
                a = nxt("psQ", 2)
                mm_group(ps[a][:, :], [(wqa[:, k, g].rearrange("p a d -> p (a d)"), uT[:, k, tc0:tc0 + tn]) for k in range(8)],
                         reads=[("wqa", 0, g), ("wqa", 1, g)] + uT_res(tb), writes=[("ps", a)])
                evac(QaT[:, g, 512 * t:512 * t + 512], ps[a][:, :], [("ps", a)], [("QaT", 4 * t + j) for j in range(4)])
        for t in (["m", 0, 1, 2, 3] if DBG_SWA >= 0.3 else []):
            tc0, tn, tb = tile_cols(t)
            a = nxt("psQ", 2)
            mm_group(ps[a][:, 0:tn], [(wka[:, k, :], uT[:, k, tc0:tc0 + tn]) for k in range(8)],
                     reads=["wka"] + uT_res(tb), writes=[("ps", a)])
            evac(Ka2[0][0:64, tc0:tc0 + tn], ps[a][0:64, 0:tn], [("ps", a)], [("KaT", b) for b in tb], eng="dve")
            evac(Ka2[1][64:128, tc0:tc0 + tn], ps[a][64:128, 0:tn], [("ps", a)], [("KaT", b) for b in tb], eng="dve")
        for bi, b in enumerate(ablocks if DBG_SWA >= 0.4 else []):
            if DBG_SWA == 0.45 and b == "m":
                continue
            c0, nr = blk_cols(b)
            a = nxt("psQ", 2)
            mm_group(ps[a][0:nr, 0:128], [(uT[:, k, c0:c0 + nr], wva[:, k, :]) for k in range(8)],
                     reads=["wva"] + uT_res([b]), writes=[("ps", a)])
            evac(Va[0][0:nr, bi, 0:64], ps[a][0:nr, 0:64], [("ps", a)], [("Va", 0)], eng="dve")
            evac(Va[1][0:nr, bi, 64:128], ps[a][0:nr, 64:128], [("ps", a)], [("Va", 1)], eng="dve")
        SB = [0, 1, 2, 5]
        LOOK = 3
        ODS = [(3, 4), (6, 7)]
        steps = []
        for i in range(NBLK):
            roles = [("meta", 0, 16, 0)] + ([("prev", i, 128, NMETA + 128 * (i - 1))] if i >= 1 else []) + \
                    [("cur", i + 1, 128, NMETA + 128 * i)]
            n_i = 2 * len(roles)
            cnt = 0
            for kvh in range(2):
                for (role, kb, nk, kc0) in roles:
                    steps.append((i, kvh, role, kb, nk, kc0, cnt == 0, cnt == n_i - 1))
                    cnt += 1

        def swa_qk(n):
            i, kvh, role, kb, nk, kc0, st, last = steps[n]
            sidx = SB[n % 4]
            pS = ps[sidx]
            base = 64 * kvh
            kres = ("KaT", "m") if role == "meta" else ("KaT", kb - 1)
            def qk(e, w):
                ins = e.matmul(pS[0:nk, :].rearrange("p (g q) -> p g q", g=4), lhsT=Ka2[kvh][:, kc0:kc0 + nk],
                               rhs=QaT[:, :, 128 * i:128 * i + 128], start=True, stop=True)
                if w is not None:
                    ins._wait_ge(*w)
                return ins
            P.add("pe", qk, reads=[kres, ("QaT", i)], writes=[("ps", sidx)], attach=True)

        def swa_rest(n):
            i, kvh, role, kb, nk, kc0, st, last = steps[n]
            sidx = SB[n % 4]
            pS = ps[sidx]
            pj = n % NPT
            iO, iD = ODS[i % 2]
            pO, pD = ps[iO], ps[iD]
            for g in range(4):
                hd = 4 * kvh + g
                if role == "meta":
                    bcol = cf[0:nk, CF_BMETA + 8 * i + hd:CF_BMETA + 8 * i + hd + 1]
                elif role == "prev":
                    bcol = cf[0:nk, CF_BPREV + hd:CF_BPREV + hd + 1]
                else:
                    bcol = cf[0:nk, CF_BCUR + hd:CF_BCUR + hd + 1]
                P.add("act", lambda e, g=g, bcol=bcol: e.activation(
                    out=PT[pj][0:nk, 128 * g:128 * g + 128], in_=pS[0:nk, 128 * g:128 * g + 128],
                    func=AF.Exp, bias=bcol, scale=0.125), reads=[("ps", sidx), "cf"], writes=[("PT", pj)])
            if role != "meta":
                mo = CB_MPREV if role == "prev" else CB_MCUR
                P.add("dve", lambda e: e.tensor_tensor(
                    out=PT[pj][:, :], in0=PT[pj][:, :], in1=cb[:, mo:mo + 512], op=ALU.mult),
                    reads=[("PT", pj), "cb"], writes=[("PT", pj)])
            oo = CB_OLO if kvh == 0 else CB_OHI

            def pv(e, w):
                i0 = e.matmul(pO[:, :], lhsT=Va[kvh][0:nk, kb, :], rhs=PT[pj][0:nk, :], start=st, stop=last)
                if w is not None:
                    i0._wait_ge(*w)
                return e.matmul(pD[:, :], lhsT=cb[0:nk, oo:oo + 128], rhs=PT[pj][0:nk, :], start=st, stop=last)
            P.add("pe", pv, reads=[("PT", pj), ("Va", kvh), "cb"], writes=[("ps", iO), ("ps", iD)], attach=True)
            if defer:
                defer.pop(0)()
            if last:
                dj = 0
                defer.append(lambda: P.add("dve", lambda e: e.tensor_tensor(out=den[dj][:, :], in0=pD[:, :], in1=sinktab[:, :], op=ALU.add),
                                           reads=[("ps", iD), "sinktab"], writes=[("den", dj)]))
                for hf in range(2):
                    defer.append(lambda hf=hf: P.add("dve", lambda e: e.reciprocal(
                        out=den[dj][:, 256 * hf:256 * hf + 256], in_=den[dj][:, 256 * hf:256 * hf + 256]),
                        reads=[("den", dj)], writes=[("den", dj)]))
                defer.append(lambda: P.add("dve", lambda e: e.tensor_tensor(
                    out=OaT[:, :, 128 * i:128 * i + 128], in0=pO[:, :].rearrange("p (g q) -> p g q", g=4),
                    in1=den[dj][:, :].rearrange("p (g q) -> p g q", g=4), op=ALU.mult),
                    reads=[("ps", iO), ("den", dj)], writes=[("OaT", i // 4)]))
        defer = []
        for n in range(len(steps) + LOOK):
            if n < len(steps):
                swa_qk(n)
            if n - LOOK >= 0:
                swa_rest(n - LOOK)
        while defer:
            defer.pop(0)()
        P.barrier()
        if mstop <= 2:
            return

        wqb = AR(OFF_X, [128, 8, 512])
        wkb = AR(OFF_X + 8192, [128, 8, 512])
        wvb = AR(OFF_X + 16384, [128, 8, 512])
        QbT = AR(OFF_X + 24576, [128, SEQ])
        Kb2 = [AR(OFF_X + 28672 + 4128 * i, [128, TCOLS]) for i in range(2)]
        Vb = [AR(OFF_X + 36928 + 4352 * i, [128, 17, 128]) for i in range(2)]
        P.add("pool", lambda e: e.memset(Kb2[0][64:128, :], 0.0), writes=[("KbT", b) for b in ablocks])
        P.add("pool", lambda e: e.memset(Kb2[1][0:64, :], 0.0), writes=[("KbT", b) for b in ablocks])
        for c in range(4):
            for (wt, col0, nm) in ((wqb, 768, "wqb"), (wkb, 1280, "wkb"), (wvb, 1792, "wvb")):
                cast_dma(wt[:, :, 128 * c:128 * c + 128],
                         wi_d[:, col0 + 128 * c:col0 + 128 * c + 128].rearrange("(k p) n -> p k n", p=128), (nm, c), nm)
        for i in range(2):
            P.add("pool", lambda e, i=i: e.memset(Vb[i][:, :, :], 0.0), writes=[("Vb", i, bi) for bi in range(17)])
        for c in range(4):
            for t in range(4):
                tc0, tn, tb = tile_cols(t)
                a = nxt("psQ", 2)
                mm_group(ps[a][:, :], [(wqb[:, k, 128 * c:128 * c + 128], uT[:, k, tc0:tc0 + tn]) for k in range(8)],
                         reads=[("wqb", c)] + uT_res(tb), writes=[("ps", a)])
                evac(QbT[:, 512 * t:512 * t + 512], ps[a][:, :], [("ps", a)], [("QbT", t)], eng="dve")
            for t in ["m", 0, 1, 2, 3]:
                tc0, tn, tb = tile_cols(t)
                a = nxt("psQ", 2)
                mm_group(ps[a][:, 0:tn], [(wkb[:, k, 128 * c:128 * c + 128], uT[:, k, tc0:tc0 + tn]) for k in range(8)],
                         reads=[("wkb", c)] + uT_res(tb), writes=[("ps", a)])
                evac(Kb2[0][0:64, tc0:tc0 + tn], ps[a][0:64, 0:tn], [("ps", a)], [("KbT", b) for b in tb], eng="dve")
                evac(Kb2[1][64:128, tc0:tc0 + tn], ps[a][64:128, 0:tn], [("ps", a)], [("KbT", b) for b in tb], eng="dve")
            for bi, b in enumerate(ablocks):
                c0, nr = blk_cols(b)
                a = nxt("psQ", 2)
                mm_group(ps[a][0:nr, 0:128], [(uT[:, k, c0:c0 + nr], wvb[:, k, 128 * c:128 * c + 128]) for k in range(8)],
                         reads=[("wvb", c)] + uT_res([b]), writes=[("ps", a)])
                evac(Vb[0][0:nr, bi, 0:64], ps[a][0:nr, 0:64], [("ps", a)], [("Vb", 0, bi)], eng="dve")
                evac(Vb[1][0:nr, bi, 64:128], ps[a][0:nr, 64:128], [("ps", a)], [("Vb", 1, bi)], eng="dve")
            fsteps = []
            for j in range(4):
                kbs = [0] + [1 + r for r in range(4 * j + 4)]
                n_j = 2 * len(kbs)
                cnt = 0
                for kb in kbs:
                    for hh in range(2):
                        fsteps.append((j, kb, hh, cnt == 0, cnt == n_j - 1))
                        cnt += 1

            def fparams(n):
                j, kb, hh, st, last = fsteps[n]
                if kb == 0:
                    nk, kc0, c0, kres = 16, 0, 0, ("KbT", "m")
                else:
                    r = kb - 1
                    nk, kc0, kres = 128, NMETA + 128 * r, ("KbT", r)
                    c0 = 128 * (r - 4 * j) if r >= 4 * j else 0
                diag = kb >= 1 and (kb - 1) >= 4 * j
                return j, kb, hh, st, last, nk, kc0, c0, kres, diag

            def fox_qk(n, c=c):
                j, kb, hh, st, last, nk, kc0, c0, kres, diag = fparams(n)
                sidx = SB[n % 4]
                pS = ps[sidx]
                base = 64 * hh
                def qk(e, w):
                    ins = e.matmul(pS[0:nk, c0:512], lhsT=Kb2[hh][:, kc0:kc0 + nk],
                                   rhs=QbT[:, 512 * j + c0:512 * j + 512], start=True, stop=True)
                    if w is not None:
                        ins._wait_ge(*w)
                    return ins
                P.add("pe", qk, reads=[kres, ("QbT", j)], writes=[("ps", sidx)], attach=True)

            def fox_rest(n, c=c):
                j, kb, hh, st, last, nk, kc0, c0, kres, diag = fparams(n)
                sidx = SB[n % 4]
                pS = ps[sidx]
                pj = n % NPT
                iO, iD = ODS[(4 * c + j) % 2]
                pO, pD = ps[iO], ps[iD]
                hd = 2 * c + hh
                P.add("act", lambda e: e.activation(
                    out=PT[pj][0:nk, c0:512], in_=pS[0:nk, c0:512], func=AF.Exp,
                    bias=biasF[0:nk, kb, j, hd:hd + 1], scale=0.125),
                    reads=[("ps", sidx), "biasF"], writes=[("PT", pj)])
                if diag:
                    P.add("dve", lambda e: e.tensor_tensor(
                        out=PT[pj][:, c0:c0 + 128], in0=PT[pj][:, c0:c0 + 128],
                        in1=cb[:, CB_MCUR:CB_MCUR + 128], op=ALU.mult),
                        reads=[("PT", pj), "cb"], writes=[("PT", pj)])
                oo = CB_OLO if hh == 0 else CB_OHI

                def pv(e, w):
                    i0 = e.matmul(pO[:, c0:512], lhsT=Vb[hh][0:nk, kb, :], rhs=PT[pj][0:nk, c0:512], start=st, stop=last)
                    if w is not None:
                        i0._wait_ge(*w)
                    return e.matmul(pD[:, c0:512], lhsT=cb[0:nk, oo:oo + 128], rhs=PT[pj][0:nk, c0:512], start=st, stop=last)
                P.add("pe", pv, reads=[("PT", pj), ("Vb", hh, kb), "cb"], writes=[("ps", iO), ("ps", iD)], attach=True)
                if defer:
                    defer.pop(0)()
                if last:
                    dj = 0
                    for hf in range(2):
                        defer.append(lambda hf=hf: P.add("dve", lambda e: e.reciprocal(
                            out=den[dj][:, 256 * hf:256 * hf + 256], in_=pD[:, 256 * hf:256 * hf + 256]),
                            reads=[("ps", iD)], writes=[("den", dj)]))
                    defer.append(lambda: P.add("dve", lambda e: e.tensor_tensor(
                        out=ObT[:, c, 512 * j:512 * j + 512], in0=pO[:, :], in1=den[dj][:, :], op=ALU.mult),
                        reads=[("ps", iO), ("den", dj)], writes=[("ObT", j)]))
            for n in range(len(fsteps) + LOOK):
                if n < len(fsteps):
                    fox_qk(n)
                if n - LOOK >= 0:
                    fox_rest(n - LOOK)
            while defer:
                defer.pop(0)()
        P.barrier()
        if mstop <= 3:
            return

        mixT = AR(OFF_MIX, [128, 8, SEQ])
        wo_t = AR(OFF_PW, [128, 8, D])
        ovl = [[("pw", 0, "a", 0), ("pw", 0, "a", 1), ("pw", 0, "b"), ("pw", 0, "ga")],
               [("pw", 0, "gb"), ("pw", 1, "a", 0), ("pw", 1, "a", 1), ("pw", 1, "b")]]

        def issue_wo(hf):
            P.add("pool", lambda e: e.dma_start(
                out=wo_t[:, 4 * hf:4 * hf + 4, :], in_=wo_d[512 * hf:512 * hf + 512, :].rearrange("(k p) n -> p k n", p=128)),
                writes=[("wout", hf)] + ovl[hf], dma=f"wout{hf}")
        ptmp = [AR(OFF_PTMP + 2048 * i, [128, 512], F32) for i in range(2)]
        def pw_tiles(st):
            sl = st % 2
            o = OFF_PW + 12288 * sl
            return sl, AR(o, [128, 4, 256]), AR(o + 2048, [128, 4, 256]), AR(o + 4096, [128, 8, 256]), AR(o + 8192, [128, 8, 256])

        def pw_issue(st):
            sl, wa_t, wb_t, wga_t, wgb_t = pw_tiles(st)
            c0 = 256 * st
            for hh in range(2):
                cast_dma(wa_t[64 * hh:64 * hh + 64, :, :],
                         wa_d[256 * hh:256 * hh + 256, c0:c0 + 256].rearrange("(g d) n -> d g n", d=64),
                         ("pw", sl, "a", hh), f"pw{sl}a{hh}")
            cast_dma(wb_t, wb_d[:, c0:c0 + 256].rearrange("(c p) n -> p c n", p=128), ("pw", sl, "b"), f"pw{sl}b")
            cast_dma(wga_t, wi_d[:, 2312 + c0:2312 + c0 + 256].rearrange("(k p) n -> p k n", p=128), ("pw", sl, "ga"), f"pw{sl}ga")
            cast_dma(wgb_t, wi_d[:, 3336 + c0:3336 + c0 + 256].rearrange("(k p) n -> p k n", p=128), ("pw", sl, "gb"), f"pw{sl}gb")
        pw_issue(0)
        for st in range(4):
            s, wa_t, wb_t, wga_t, wgb_t = pw_tiles(st)
            if st + 1 < 4:
                pw_issue(st + 1)
            if st == 3:
                issue_wo(0)
            for q in range(2):
                m = 2 * st + q
                for t in range(4):
                    tc0, tn, tb = tile_cols(t)
                    ia, ib, iga, igb = (nxt("psP", 8) for _ in range(4))
                    mm_group(ps[ia][:, :], [(wa_t[:, k, 128 * q:128 * q + 128], OaT[:, k, 512 * t:512 * t + 512]) for k in range(4)],
                             reads=[("pw", s, "a", 0), ("pw", s, "a", 1), ("OaT", t)], writes=[("ps", ia)])
                    mm_group(ps[ib][:, :], [(wb_t[:, k, 128 * q:128 * q + 128], ObT[:, k, 512 * t:512 * t + 512]) for k in range(4)],
                             reads=[("pw", s, "b"), ("ObT", t)], writes=[("ps", ib)])
                    mm_group(ps[iga][:, :], [(wga_t[:, k, 128 * q:128 * q + 128], uT[:, k, tc0:tc0 + 512]) for k in range(8)],
                             reads=[("pw", s, "ga")] + uT_res(tb), writes=[("ps", iga)])
                    mm_group(ps[igb][:, :], [(wgb_t[:, k, 128 * q:128 * q + 128], uT[:, k, tc0:tc0 + 512]) for k in range(8)],
                             reads=[("pw", s, "gb")] + uT_res(tb), writes=[("ps", igb)])
                    ja, jb = 0, 1
                    P.add("act", lambda e, ja=ja, iga=iga: e.activation(out=ptmp[ja][:, :], in_=ps[iga][:, :], func=AF.Sigmoid),
                          reads=[("ps", iga)], writes=[("ptmp", ja)])
                    P.add("act", lambda e, jb=jb, igb=igb: e.activation(out=ptmp[jb][:, :], in_=ps[igb][:, :], func=AF.Sigmoid),
                          reads=[("ps", igb)], writes=[("ptmp", jb)])
                    P.add("dve", lambda e, ja=ja, ia=ia: e.tensor_tensor(out=ptmp[ja][:, :], in0=ptmp[ja][:, :], in1=ps[ia][:, :], op=ALU.mult),
                          reads=[("ptmp", ja), ("ps", ia)], writes=[("ptmp", ja)])
                    P.add("dve", lambda e, jb=jb, ib=ib: e.tensor_tensor(out=ptmp[jb][:, :], in0=ptmp[jb][:, :], in1=ps[ib][:, :], op=ALU.mult),
                          reads=[("ptmp", jb), ("ps", ib)], writes=[("ptmp", jb)])
                    P.add("pool", lambda e, ja=ja, jb=jb, m=m, t=t: e.tensor_tensor(
                        out=mixT[:, m, 512 * t:512 * t + 512], in0=ptmp[ja][:, :], in1=ptmp[jb][:, :], op=ALU.add),
                        reads=[("ptmp", ja), ("ptmp", jb)], writes=[("mixT", t)])
        issue_wo(1)
        hook = flush = None
        if post_tail is not None:
            hook, flush = post_tail()
        for b in range(NBLK):
            for half in range(2):
                o = nxt("psP", 8)
                mm_group(ps[o][:, :], [(mixT[:, k, 128 * b:128 * b + 128], wo_t[:, k, 512 * half:512 * half + 512]) for k in range(8)],
                         reads=[("mixT", b // 4), ("wout", 0), ("wout", 1)], writes=[("ps", o)])
                dst = h[:, b, 512 * half:512 * half + 512]
                P.add("dve", lambda e, dst=dst, o=o: e.tensor_tensor(out=dst, in0=ps[o][:, :], in1=dst, op=ALU.add),
                      reads=[("ps", o), ("h", b)], writes=[("h", b)])
            if hook is not None:
                hook(b)
        if flush is not None:
            flush()

    def final_out(s, raw, hooked=False, after_block=None):
        ost = [AR(OFF_OST + 4096 * i, [128, D], F32) for i in range(2)]
        if raw:
            for b in range(NBLK):
                P.add("sp", lambda e, b=b: e.dma_start(out=out_d[s, 128 * b:128 * b + 128, :], in_=h[:, b, :]),
                      reads=[("h", b)], dma=f"out{b % 2}")
            return
        P.add("sp", lambda e: e.dma_start(out=gb[:], in_=n4_d.broadcast_to([128, D])), writes=["gb"], dma="gb")
        P.add("pool", lambda e: e.memset(ss[:], 0.0), writes=ALLSS)
        fjunk = AR(OFF_OST + 8192, [128, D])

        def a1(b):
            P.add("act", lambda e: e.activation(out=fjunk[:, :], in_=h[:, b, :], func=AF.Square, accum_out=ss[:, b:b + 1]),
                  reads=[("h", b), ("ss", b)], writes=[("ss", b), "fjunk"], full=True)
            P.add("act", lambda e: e.activation(out=rs[:, b:b + 1], in_=ss[:, b:b + 1], func=AF.Sqrt,
                                                bias=epsc[:, 0:1], scale=1.0 / D),
                  reads=[("ss", b), "epsc"], writes=[("rs", b)])

        def a2(b):
            j = nxt("ost", 2)
            P.add("dve", lambda e: e.reciprocal(out=rs[:, b:b + 1], in_=rs[:, b:b + 1]), reads=[("rs", b)], writes=[("rs", b)])
            P.add("dve", lambda e: e.scalar_tensor_tensor(
                out=ost[j][:, :], in0=h[:, b, :], scalar=rs[:, b:b + 1], in1=gb[:, :], op0=ALU.mult, op1=ALU.mult),
                reads=[("h", b), ("rs", b), "gb"], writes=[("ost", j)])
            P.add("sp", lambda e: e.dma_start(out=out_d[s, 128 * b:128 * b + 128, :], in_=ost[j][:, :]),
                  reads=[("ost", j)], dma=f"out{j}")
            if after_block is not None:
                after_block(b)
        if hooked:
            seq = []

            def hook(b):
                seq.append(b)
                if len(seq) >= 2:
                    a2(seq[-2])
                a1(b)

            def flush():
                a2(seq[-1])
            return hook, flush
        a1(0)
        for b in range(NBLK):
            if b + 1 < NBLK:
                a1(b + 1)
            a2(b)

    def load_x(s, b):
        P.add("sp", lambda e: e.dma_start(out=h[:, b, :], in_=x_d[s, 128 * b:128 * b + 128, :]),
              writes=[("h", b)], dma=f"x{b % 4}")

    for s in range(2):
        hm = AR(OFF_HM, [128, D], F32)
        if s == 0 or stage != 3:
            for b in range(NBLK):
                load_x(s, b)
        P.add("sp", lambda e: e.dma_start(out=hm[0:NMETA, :], in_=meta_d), writes=[("h", "m")], dma="xm")
        if stage == 3:
            ffn(w1i_d, w1o_d, n1_d, True, hm, "f1",
                tail=lambda: norm_to_uT(n2_d, ["m"] + list(range(NBLK)), hm, hooked=True,
                                        junk=AR(OFF_OST, [128, D]), junkres=("ost", 0)))
            P.barrier()
            mixer(hm, post_tail=lambda: norm_to_uT(n3_d, list(range(NBLK)), hm, hooked=True,
                                                    junk=AR(OFF_PTMP, [128, D]), junkres=("ptmp", 0)))
            P.barrier()
            ffn(w2i_d, w2o_d, n3_d, False, hm, "f2", do_norm=False,
                tail=lambda s=s: final_out(s, raw=False, hooked=True,
                                           after_block=((lambda b: load_x(1, b)) if s == 0 else None)))
        else:
            ffn(w1i_d, w1o_d, n1_d, True, hm, "f1")
            if stage >= 2:
                norm_to_uT(n2_d, ["m"] + list(range(NBLK)), hm)
                P.barrier()
                mixer(hm, mstop=(stage - 10 if stage >= 10 else 9))
            P.barrier()
            final_out(s, raw=True)
    P.emit()
    es.close()
    return nc


_CACHE = {}


def kernel(x, meta_tokens, ffn1_norm, ffn1_w_in, ffn1_w_out, mix_norm, w_in, b_forget, attn_sinks,
           w_branch_a, w_branch_b, w_out, ffn2_norm, ffn2_w_in, ffn2_w_out, final_norm, _stage=3, _cores=8):
    f = lambda a: np.ascontiguousarray(np.asarray(a, dtype=np.float32))
    x = f(x)
    cf, cb = make_consts()
    shared = {
        "meta": f(meta_tokens), "n1": f(ffn1_norm).reshape(1, D), "n2": f(mix_norm).reshape(1, D),
        "n3": f(ffn2_norm).reshape(1, D), "n4": f(final_norm).reshape(1, D),
        "w1i": f(ffn1_w_in)[0], "w1o": f(ffn1_w_out)[0], "w2i": f(ffn2_w_in)[0], "w2o": f(ffn2_w_out)[0],
        "wi": f(w_in)[0], "bfg": f(b_forget).reshape(1, 8), "snk": f(attn_sinks).reshape(1, 8),
        "wa": f(w_branch_a)[0], "wb": f(w_branch_b)[0], "wo": f(w_out)[0], "cf": cf, "cb": cb,
    }
    if _stage not in _CACHE:
        _CACHE[_stage] = build(_stage)
    nc = _CACHE[_stage]
    in_maps = [dict(shared, x=x[2 * c:2 * c + 2]) for c in range(_cores)]
    res = run_bass_kernel_spmd(nc, in_maps, core_ids=list(range(_cores)))
    return np.concatenate([r["out"] for r in res.results], axis=0)
```

```python
import contextlib
import numpy as np
import ml_dtypes
import concourse.bass as bass
import concourse.mybir as mybir
from concourse.bass_utils import run_bass_kernel_spmd

F32 = mybir.dt.float32
BF16 = mybir.dt.bfloat16
AF = mybir.ActivationFunctionType
ALU = mybir.AluOpType

D = 1024
SEQ = 2048
NBLK = 16
NMETA = 16
TCOLS = SEQ + NMETA
DFF = 2816
NCH = 22
PASSES = [(0, 6), (6, 6), (12, 5), (17, 5)]
EPS = 1e-6
SLOPES = [2.0 ** (-(h + 1)) for h in range(8)]
ENGS = ("pe", "act", "dve", "pool", "sp")
ARENA_BYTES = 98304
DBG_SWA = 9

CF_TRI, CF_ONES, CF_IOTA, CF_BCUR, CF_BPREV, CF_BMETA, CF_N = 0, 128, 256, 384, 392, 400, 528
CB_ID, CB_MCUR, CB_MPREV, CB_OLO, CB_OHI, CB_N = 0, 128, 640, 1152, 1280, 1408


def make_consts():
    cf = np.zeros((128, CF_N), np.float32)
    p = np.arange(128)
    cf[:, CF_TRI:CF_TRI + 128] = (p[:, None] <= p[None, :]).astype(np.float32)
    cf[:, CF_ONES:CF_ONES + 128] = 1.0
    cf[:, CF_IOTA:CF_IOTA + 128] = p[None, :].astype(np.float32)
    sl = np.array(SLOPES, np.float32)
    cf[:, CF_BCUR:CF_BCUR + 8] = p[:, None] * sl[None, :]
    cf[:, CF_BPREV:CF_BPREV + 8] = (p[:, None] - 128.0) * sl[None, :]
    bm = np.zeros((128, 16, 8), np.float32)
    for i in range(16):
        bm[:, i, :] = (p[:, None] - 16.0 - 128.0 * i) * sl[None, :]
    cf[:, CF_BMETA:CF_BMETA + 128] = bm.reshape(128, 128)
    cb = np.zeros((128, CB_N), np.float32)
    cb[:, CB_ID:CB_ID + 128] = np.eye(128)
    mc = (p[:, None] <= p[None, :]).astype(np.float32)
    mp = (p[:, None] > p[None, :]).astype(np.float32)
    cb[:, CB_MCUR:CB_MCUR + 512] = np.tile(mc, (1, 4))
    cb[:, CB_MPREV:CB_MPREV + 512] = np.tile(mp, (1, 4))
    cb[:, CB_OLO:CB_OLO + 64] = 1.0
    cb[:, CB_OHI + 64:CB_OHI + 128] = 1.0
    return cf, cb.astype(ml_dtypes.bfloat16)


class Op:
    __slots__ = ("idx", "eng", "fn", "reads", "writes", "dma", "deps", "signal", "semval", "semname", "attach")

    def __init__(self, idx, eng, fn, reads, writes, dma):
        self.idx, self.eng, self.fn, self.reads, self.writes, self.dma = idx, eng, fn, reads, writes, dma
        self.deps = set()
        self.signal = False
        self.semval = 0
        self.semname = None
        self.attach = False


class Prog:
    def __init__(self, nc):
        self.nc = nc
        self.ops = []
        self.last_writer = {}
        self.readers = {}
        self.last_on = {}

    def add(self, eng, fn, reads=(), writes=(), dma=None, full=False, attach=False):
        if dma is not None:
            writes = tuple(writes) + (("__slot", dma),)
        op = Op(len(self.ops), eng, fn, tuple(reads), tuple(writes), dma)
        op.attach = attach
        deps = set()
        for r in op.reads:
            w = self.last_writer.get(r)
            if w is not None:
                deps.add(w)
        for r in op.writes:
            w = self.last_writer.get(r)
            if w is not None:
                deps.add(w)
            deps.update(self.readers.get(r, ()))
        for d in deps:
            dop = self.ops[d]
            if dop.dma is None and dop.eng == eng and dma is None:
                if eng == "pe":
                    continue
                if not full and not any(self.last_writer.get(r) == d for r in op.reads):
                    continue
            op.deps.add(d)
        for r in op.reads:
            self.readers.setdefault(r, []).append(op.idx)
        for r in op.writes:
            self.last_writer[r] = op.idx
            self.readers[r] = []
        self.ops.append(op)
        if dma is None:
            self.last_on[eng] = op.idx
        return op

    def barrier(self):
        lasts = dict(self.last_on)
        dmas = [idx for (k, idx) in self.last_writer.items() if isinstance(k, tuple) and k[0] == "__slot"]
        for e in ENGS:
            op = Op(len(self.ops), e, None, (), (), None)
            for e2, idx in lasts.items():
                if e2 != e:
                    op.deps.add(idx)
            op.deps.update(dmas)
            self.ops.append(op)

    def emit(self):
        nc, ops = self.nc, self.ops
        for op in ops:
            best = {}
            keep = set()
            for d in op.deps:
                dop = ops[d]
                if dop.dma is not None:
                    keep.add(d)
                elif d > best.get(dop.eng, -1):
                    best[dop.eng] = d
            keep.update(best.values())
            op.deps = keep
            for d in keep:
                ops[d].signal = True
        cnt = {}
        for op in ops:
            if op.dma is not None:
                op.signal = True
                op.semname = "d_" + op.dma
                cnt[op.semname] = cnt.get(op.semname, 0) + 16
                op.semval = cnt[op.semname]
            elif op.signal:
                op.semname = "e_" + op.eng
                cnt[op.semname] = cnt.get(op.semname, 0) + 1
                op.semval = cnt[op.semname]
        semnames = sorted(cnt)
        with contextlib.ExitStack() as es:
            sems = {n: es.enter_context(nc.semaphore(n)) for n in semnames}
            block = es.enter_context(nc.Block())
            by_eng = {e: [op for op in ops if op.eng == e] for e in ENGS}
            final = [(n, cnt[n]) for n in semnames if n.startswith("d_")]

            def run(engname, eng):
                known = {}
                for op in by_eng[engname]:
                    waits = {}
                    for d in op.deps:
                        dop = ops[d]
                        if dop.semval > waits.get(dop.semname, 0):
                            waits[dop.semname] = dop.semval
                    need = [(n, v) for n, v in sorted(waits.items()) if known.get(n, 0) < v]
                    emb = None
                    if op.attach and need:
                        emb = need.pop()
                    for n, v in need:
                        eng.wait_ge(sems[n], v)
                        known[n] = v
                    if op.fn is None:
                        continue
                    if op.attach:
                        if emb is not None:
                            known[emb[0]] = emb[1]
                        ins = op.fn(eng, None if emb is None else (sems[emb[0]], emb[1]))
                    else:
                        ins = op.fn(eng)
                    if op.signal:
                        ins.then_inc(sems[op.semname], 16 if op.dma is not None else 1)
                if engname == "sp":
                    for n, v in final:
                        eng.wait_ge(sems[n], v)

            @block.tensor
            def _(e):
                run("pe", e)

            @block.scalar
            def _(e):
                run("act", e)

            @block.vector
            def _(e):
                run("dve", e)

            @block.gpsimd
            def _(e):
                run("pool", e)

            @block.sync
            def _(e):
                run("sp", e)


def build(stage=3):
    nc = bass.Bass("TRN2", target_bir_lowering=False)
    dt_in = lambda n, s, d=F32: nc.dram_tensor(n, s, d, kind="ExternalInput").ap()
    x_d = dt_in("x", [2, SEQ, D])
    meta_d = dt_in("meta", [NMETA, D])
    n1_d, n2_d, n3_d, n4_d = (dt_in(n, [1, D]) for n in ("n1", "n2", "n3", "n4"))
    w1i_d, w1o_d = dt_in("w1i", [D, 2 * DFF]), dt_in("w1o", [DFF, D])
    w2i_d, w2o_d = dt_in("w2i", [D, 2 * DFF]), dt_in("w2o", [DFF, D])
    wi_d = dt_in("wi", [D, 4360])
    bf_d, sk_d = dt_in("bfg", [1, 8]), dt_in("snk", [1, 8])
    wa_d, wb_d, wo_d = dt_in("wa", [512, D]), dt_in("wb", [512, D]), dt_in("wo", [D, D])
    cf_d, cb_d = dt_in("cf", [128, CF_N]), dt_in("cb", [128, CB_N], BF16)
    out_d = nc.dram_tensor("out", [2, SEQ, D], F32, kind="ExternalOutput").ap()

    es = contextlib.ExitStack()
    sb = lambda n, s, d: es.enter_context(nc.sbuf_tensor(n, s, d))
    h = sb("h", [128, NBLK, D], F32)
    uT = sb("uT", [128, 8, TCOLS], BF16)
    cf = sb("cf_s", [128, CF_N], F32)
    cb = sb("cb_s", [128, CB_N], BF16)
    gb = sb("gb", [128, D], F32)
    ss = sb("ss", [128, 17], F32)
    rs = sb("rs", [128, 17], F32)
    epsc = sb("epsc", [128, 1], F32)
    bfg = sb("bfg_s", [128, 8], F32)
    snk = sb("snk_s", [128, 8], F32)
    sinktab = sb("sinktab", [128, 512], F32)
    arena = sb("arena", [128, ARENA_BYTES // 2], BF16)
    ps = [es.enter_context(nc.psum_tensor(f"ps{i}", [128, 512], F32)) for i in range(8)]
    pt = [ps[6 + i][:, 0:256].bitcast(BF16) for i in range(2)]

    def AR(off, shape, dtype=BF16):
        n = int(np.prod(shape[1:]))
        nb = n * (4 if dtype == F32 else 2)
        assert off % 4 == 0 and off + nb <= ARENA_BYTES, (off, nb)
        v = arena[:, off // 2:(off + nb) // 2]
        if dtype == F32:
            v = v.bitcast(F32)
        if len(shape) == 2:
            return v
        names = " ".join(f"a{i}" for i in range(len(shape) - 1))
        kw = {f"a{i}": shape[i + 1] for i in range(len(shape) - 2)}
        return v.rearrange(f"p ({names}) -> p {names}", **kw)

    P = Prog(nc)
    rot = {}

    def nxt(name, n):
        rot[name] = (rot.get(name, -1) + 1) % n
        return rot[name]

    P.add("sp", lambda e: e.dma_start(out=cf[:], in_=cf_d), writes=["cf"], dma="cf")
    P.add("sp", lambda e: e.dma_start(out=cb[:], in_=cb_d), writes=["cb"], dma="cb")
    P.add("sp", lambda e: e.dma_start(out=bfg[:], in_=bf_d.broadcast_to([128, 8])), writes=["bfg"], dma="bfg")
    P.add("sp", lambda e: e.dma_start(out=snk[:], in_=sk_d.broadcast_to([128, 8])), writes=["snk"], dma="snk")
    P.add("pool", lambda e: e.memset(epsc[:], EPS), writes=["epsc"])
    onec = sb("onec", [128, 1], F32)
    P.add("pool", lambda e: e.memset(onec[:], 1.0), writes=["onec"])
    ALLSS = ["ss"] + [("ss", b) for b in ["m"] + list(range(NBLK))]
    for g in range(4):
        for hh in range(2):
            hd = 4 * hh + g
            P.add("act", lambda e, g=g, hh=hh, hd=hd: e.activation(
                out=sinktab[64 * hh:64 * hh + 64, 128 * g:128 * g + 128],
                in_=cf[64 * hh:64 * hh + 64, CF_IOTA:CF_IOTA + 128], func=AF.Exp,
                bias=snk[64 * hh:64 * hh + 64, hd:hd + 1], scale=SLOPES[hd]),
                reads=["cf", "snk"], writes=["sinktab"])

    def tile_cols(t):
        if t == "m":
            return 0, NMETA, ["m"]
        return NMETA + 512 * t, 512, [4 * t + j for j in range(4)]

    def blk_cols(b):
        if b == "m":
            return 0, NMETA
        return NMETA + 128 * b, 128

    def cast_dma(dst, src, res, slot):
        P.add("pool", lambda e: e.dma_start(out=dst, in_=src), writes=[res], dma=slot)

    def mm_group(out, pairs, reads, writes):
        def fn(e):
            n = len(pairs)
            ins = None
            for i, (l, r) in enumerate(pairs):
                ins = e.matmul(out, lhsT=l, rhs=r, start=(i == 0), stop=(i == n - 1))
            return ins
        P.add("pe", fn, reads=reads, writes=writes)

    evac_flip = [0]

    def evac(out, in_, reads, writes, eng=None):
        if eng is None:
            evac_flip[0] ^= 1
            eng = "dve" if evac_flip[0] else "act"
        if eng == "act":
            P.add("act", lambda e: e.copy(out=out, in_=in_), reads=reads, writes=writes)
        else:
            P.add(eng, lambda e: e.tensor_copy(out=out, in_=in_), reads=reads, writes=writes)

    def norm_to_uT(gain_d, blocks, hm, hooked=False, junk=None, junkres="junk"):
        ut = [AR(OFF_UT + 2048 * i, [128, D]) for i in range(2)]
        P.add("sp", lambda e: e.dma_start(out=gb[:], in_=gain_d.broadcast_to([128, D])), writes=["gb"], dma="gb")
        P.add("pool", lambda e: e.memset(ss[:], 0.0), writes=ALLSS)
        jmap = {}

        def geom(b):
            c0, nr = blk_cols(b)
            src = hm[0:nr, :] if b == "m" else h[:, b, :]
            col = 16 if b == "m" else b
            return c0, nr, src, col

        def a1(b, j=None):
            c0, nr, src, col = geom(b)
            if junk is not None:
                out, ores = junk[0:nr, :], junkres
            else:
                out, ores = ut[j][0:nr, :], ("ut", j)
            P.add("act", lambda e: e.activation(out=out, in_=src, func=AF.Square, accum_out=ss[0:nr, col:col + 1]),
                  reads=[("h", b), ("ss", b)], writes=[("ss", b), ores], full=True)
            P.add("act", lambda e: e.activation(
                out=rs[0:nr, col:col + 1], in_=ss[0:nr, col:col + 1], func=AF.Sqrt, bias=epsc[0:nr, 0:1], scale=1.0 / D),
                reads=[("ss", b), "epsc"], writes=[("rs", b)])

        def a2(b, j):
            c0, nr, src, col = geom(b)
            jmap[b] = j
            P.add("dve", lambda e: e.reciprocal(out=rs[0:nr, col:col + 1], in_=rs[0:nr, col:col + 1]),
                  reads=[("rs", b)], writes=[("rs", b)])
            P.add("dve", lambda e: e.scalar_tensor_tensor(
                out=ut[j][0:nr, :], in0=src, scalar=rs[0:nr, col:col + 1], in1=gb[0:nr, :],
                op0=ALU.mult, op1=ALU.mult), reads=[("h", b), ("rs", b), "gb"], writes=[("ut", j)])

        def stage_a(b):
            j = nxt("ut", 2)
            a1(b, j)
            a2(b, j)

        def stage_b(b):
            c0, nr = blk_cols(b)
            j = jmap[b]
            for half in range(2):
                def fn(e, half=half):
                    ins = None
                    for c in range(4):
                        cc = 4 * half + c
                        ins = e.transpose(pt[half][:, 128 * c:128 * c + nr], ut[j][0:nr, 128 * cc:128 * cc + 128],
                                          cb[0:nr, CB_ID:CB_ID + nr])
                    return ins
                P.add("pe", fn, reads=[("ut", j), "cb"], writes=[("ps", 6 + half)])
                src_v = pt[half][:, :].rearrange("p (c n) -> p c n", c=4)[:, :, 0:nr]
                evac(uT[:, 4 * half:4 * half + 4, c0:c0 + nr], src_v, [("ps", 6 + half)], [("uT", b, half)],
                     eng=("act" if half == 0 else "dve"))
        if hooked:
            assert junk is not None
            seq = []

            def hook(b):
                seq.append(b)
                n = len(seq)
                if n >= 3:
                    stage_b(seq[n - 3])
                if n >= 2:
                    a2(seq[n - 2], nxt("ut", 2))
                a1(b)

            def flush():
                n = len(seq)
                if n >= 2:
                    stage_b(seq[n - 2])
                a2(seq[n - 1], nxt("ut", 2))
                stage_b(seq[n - 1])
            return hook, flush
        stage_a(blocks[0])
        for i, b in enumerate(blocks):
            if i + 1 < len(blocks):
                stage_a(blocks[i + 1])
            stage_b(b)

    def uT_res(blks):
        return [("uT", b, hf) for b in blks for hf in range(2)]

    def ffn(w_in_d, w_out_d, gain_d, with_meta, hm, tagp, do_norm=True, tail=None):
        blocks = (["m"] if with_meta else []) + list(range(NBLK))
        if do_norm:
            norm_to_uT(gain_d, blocks, hm)
        tiles = (["m"] if with_meta else []) + [0, 1, 2, 3]
        act = AR(OFF_ACT, [128, 6, TCOLS])
        wis = [AR(OFF_WI + 8192 * i, [128, 2, 8, 256]) for i in range(3)]
        wos = [AR(OFF_WO + 12288 * i, [128, 6, D]) for i in range(2)]
        tmp = [AR(OFF_TMP + 2048 * i, [128, 512], F32) for i in range(2)]
        for (m0, nch) in PASSES:
            wslot = nxt("wo", 2)
            wo_t = wos[wslot]
            ml = 0
            while ml < nch:
                ns = min(2, nch - ml)
                s = nxt("wi", 3)
                wi_t = wis[s]
                c0 = (m0 + ml) * 128
                cast_dma(wi_t[:, 0, :, 0:ns * 128], w_in_d[:, c0:c0 + ns * 128].rearrange("(k p) n -> p k n", p=128),
                         ("wi", s, 0), f"wi{s}g")
                cast_dma(wi_t[:, 1, :, 0:ns * 128],
                         w_in_d[:, DFF + c0:DFF + c0 + ns * 128].rearrange("(k p) n -> p k n", p=128),
                         ("wi", s, 1), f"wi{s}u")
                if ml == 0:
                    cast_dma(wo_t[:, 0:nch, :], w_out_d[m0 * 128:(m0 + nch) * 128, :].rearrange("(k p) n -> p k n", p=128),
                             ("wo", wslot), f"wo{wslot}")
                for q in range(ns):
                    mloc = ml + q
                    for t in tiles:
                        tc0, tn, tb = tile_cols(t)
                        a = nxt("psA", 2)
                        pA, pB = ps[2 * a], ps[2 * a + 1]
                        for which, pp in ((0, pA), (1, pB)):
                            mm_group(pp[:, 0:tn],
                                     [(wi_t[:, which, k, q * 128:(q + 1) * 128], uT[:, k, tc0:tc0 + tn]) for k in range(8)],
                                     reads=[("wi", s, which)] + uT_res(tb), writes=[("ps", 2 * a + which)])
                        j = nxt("tmp", 2)
                        P.add("act", lambda e, j=j, pA=pA, tn=tn: e.activation(out=tmp[j][:, 0:tn], in_=pA[:, 0:tn], func=AF.Silu),
                              reads=[("ps", 2 * a)], writes=[("tmp", j)])
                        P.add("dve", lambda e, j=j, pB=pB, tn=tn, mloc=mloc, tc0=tc0: e.tensor_tensor(
                            out=act[:, mloc, tc0:tc0 + tn], in0=tmp[j][:, 0:tn], in1=pB[:, 0:tn], op=ALU.mult),
                            reads=[("tmp", j), ("ps", 2 * a + 1)], writes=[("act", mloc, t)])
                ml += ns
            hook = flush = None
            if tail is not None and (m0, nch) == PASSES[-1]:
                hook, flush = tail()
            for b in blocks:
                bc0, nr = blk_cols(b)
                t = "m" if b == "m" else b // 4
                for half in range(2):
                    o = 4 + nxt("psO", 2)
                    mm_group(ps[o][0:nr, :],
                             [(act[:, k, bc0:bc0 + nr], wo_t[:, k, 512 * half:512 * half + 512]) for k in range(nch)],
                             reads=[("act", k, t) for k in range(nch)] + [("wo", wslot)], writes=[("ps", o)])
                    dst = hm[0:nr, 512 * half:512 * half + 512] if b == "m" else h[:, b, 512 * half:512 * half + 512]
                    P.add("dve", lambda e, dst=dst, o=o, nr=nr: e.scalar_tensor_tensor(
                        out=dst, in0=ps[o][0:nr, :], scalar=0.5, in1=dst, op0=ALU.mult, op1=ALU.add),
                        reads=[("ps", o), ("h", b)], writes=[("h", b)])
                if hook is not None:
                    hook(b)
            if flush is not None:
                flush()

    OFF_HM = 0
    OFF_UT = 4096
    OFF_ACT = 8192
    OFF_WI = OFF_ACT + 24768
    OFF_WO = OFF_WI + 24576
    OFF_TMP = OFF_WO + 24576
    OFF_OST = OFF_TMP + 4096
    assert OFF_OST + 8192 <= ARENA_BYTES
    OFF_OA = 8192
    OFF_OB = OFF_OA + 16384
    OFF_LF = OFF_OB + 16384
    OFF_BIASF = OFF_LF + 3264
    OFF_PT = OFF_BIASF + 2176
    OFF_DEN = OFF_PT + 4096
    OFF_X = OFF_DEN + 2048
    assert OFF_X + 45632 <= ARENA_BYTES, OFF_X
    OFF_PW = OFF_OB + 16384
    OFF_MIX = OFF_PW + 24576
    assert OFF_MIX + 32768 <= ARENA_BYTES
    OFF_PTMP = 0

    def mixer(hm, mstop=9, post_tail=None):
        ablocks = ["m"] + list(range(NBLK))
        NPT = 4
        PT = [AR(OFF_PT + 1024 * i, [128, 512]) for i in range(NPT)]
        den = [AR(OFF_DEN, [128, 512], F32)]
        OaT = AR(OFF_OA, [128, 4, SEQ])
        ObT = AR(OFF_OB, [128, 4, SEQ])
        lfv = [AR(OFF_LF + 544 * i, [128, 136], F32) for i in range(6)]
        xb, lt, tots, offv, cum = lfv[0], lfv[1], lfv[2], lfv[3], lfv[4]
        biasF = AR(OFF_BIASF, [128, 17, 4, 8], F32)
        wf = lfv[5][:, :].bitcast(BF16)[:, 0:64].rearrange("p (k n) -> p k n", k=8)
        cast_dma(wf, wi_d[:, 2304:2312].rearrange("(k p) n -> p k n", p=128), "wf", "wf")

        def f_fn(e):
            ins = None
            for bi, b in enumerate(ablocks):
                c0, nr = blk_cols(b)
                for k in range(8):
                    ins = e.matmul(ps[0][0:nr, 8 * bi:8 * bi + 8], lhsT=uT[:, k, c0:c0 + nr], rhs=wf[:, k, :],
                                   start=(k == 0), stop=(k == 7))
            return ins
        P.add("pe", f_fn, reads=["wf"] + uT_res(ablocks), writes=[("ps", 0)])
        P.add("pool", lambda e: e.memset(lt[:], 0.0), writes=["lt"])
        for (r0, r1, c0, c1) in ((0, 16, 0, 8), (0, 128, 8, 136)):
            bb = bfg[r0:r1, :] if c1 == 8 else bfg[r0:r1, :].unsqueeze(1).to_broadcast([r1 - r0, 16, 8])
            i0 = ps[0][r0:r1, c0:c1] if c1 == 8 else ps[0][r0:r1, c0:c1].rearrange("p (b h) -> p b h", h=8)
            o0 = xb[r0:r1, c0:c1] if c1 == 8 else xb[r0:r1, c0:c1].rearrange("p (b h) -> p b h", h=8)
            P.add("dve", lambda e, bb=bb, i0=i0, o0=o0: e.tensor_tensor(out=o0, in0=i0, in1=bb, op=ALU.add),
                  reads=[("ps", 0), "bfg"], writes=["xb"])
            P.add("act", lambda e, r0=r0, r1=r1, c0=c0, c1=c1: e.activation(
                out=xb[r0:r1, c0:c1], in_=xb[r0:r1, c0:c1], func=AF.Exp, scale=-1.0), reads=["xb"], writes=["xb"])
            P.add("act", lambda e, r0=r0, r1=r1, c0=c0, c1=c1: e.activation(
                out=lt[r0:r1, c0:c1], in_=xb[r0:r1, c0:c1], func=AF.Ln, bias=onec[r0:r1, 0:1]), reads=["xb", "lt", "onec"], writes=["lt"])
        P.add("pe", lambda e: e.matmul(ps[1][:, 0:136], lhsT=cf[:, CF_ONES:CF_ONES + 128], rhs=lt[:, :], start=True, stop=True),
              reads=["lt", "cf"], writes=[("ps", 1)])
        P.add("pe", lambda e: e.matmul(ps[2][:, 0:136], lhsT=cf[:, CF_TRI:CF_TRI + 128], rhs=lt[:, :], start=True, stop=True),
              reads=["lt", "cf"], writes=[("ps", 2)])
        P.add("dve", lambda e: e.tensor_copy(out=tots[:, :], in_=ps[1][:, 0:136]), reads=[("ps", 1)], writes=["tots"])
        P.add("pool", lambda e: e.memset(offv[:, :], 0.0), writes=["offv"])
        for b in range(1, 17):
            P.add("dve", lambda e, b=b: e.tensor_tensor(out=offv[:, 8 * b:8 * b + 8], in0=offv[:, 8 * b - 8:8 * b],
                                                        in1=tots[:, 8 * b - 8:8 * b], op=ALU.add),
                  reads=["offv", "tots"], writes=["offv"])
        P.add("dve", lambda e: e.tensor_tensor(out=cum[:, :], in0=ps[2][:, 0:136], in1=offv[:, :], op=ALU.add),
              reads=[("ps", 2), "offv"], writes=["cum"])
        cum3 = cum[:, :].rearrange("p (b h) -> p b h", h=8)
        for j in range(4):
            ref = 4 * j + 3
            for hd in range(8):
                P.add("dve", lambda e, j=j, hd=hd, ref=ref: e.tensor_scalar(
                    out=biasF[:, :, j, hd], in0=cum3[:, :, hd], scalar1=offv[:, 8 * ref + hd:8 * ref + hd + 1],
                    scalar2=None, op0=ALU.subtract), reads=["cum", "offv"], writes=["biasF"])
        if mstop <= 1:
            return

        wqa = AR(OFF_X, [128, 8, 4, 2, 64])
        wka = AR(OFF_X + 8192, [128, 8, 128])
        wva = AR(OFF_X + 10240, [128, 8, 128])
        QaT = AR(OFF_X + 12288, [128, 4, SEQ])
        Ka2 = [AR(OFF_X + 28672 + 4128 * i, [128, TCOLS]) for i in range(2)]
        Va = [AR(OFF_X + 36928 + 4352 * i, [128, 17, 128]) for i in range(2)]
        P.add("pool", lambda e: e.memset(Ka2[0][64:128, :], 0.0), writes=[("KaT", b) for b in ablocks])
        P.add("pool", lambda e: e.memset(Ka2[1][0:64, :], 0.0), writes=[("KaT", b) for b in ablocks])
        for hh in range(2):
            for g in range(4):
                cq = 256 * hh + 64 * g
                cast_dma(wqa[:, :, g, hh, :], wi_d[:, cq:cq + 64].rearrange("(k p) d -> p k d", p=128),
                         ("wqa", hh, g), f"wqa{hh}{g}")
        cast_dma(wka, wi_d[:, 512:640].rearrange("(k p) n -> p k n", p=128), "wka", "wka")
        cast_dma(wva, wi_d[:, 640:768].rearrange("(k p) n -> p k n", p=128), "wva", "wva")
        for i in range(2):
            P.add("pool", lambda e, i=i: e.memset(Va[i][:, :, :], 0.0), writes=[("Va", i)])
        for g in range(4 if DBG_SWA >= 0.2 else 0):
            for t in range(4):
                tc0, tn, tb = tile_cols(t)
                a = nxt("psQ", 2)
                mm_group(ps[a][:, :], [(wqa[:, k, g].rearrange("p a d -> p (a d)"), uT[:, k, tc0:tc0 + tn]) for k in range(8)],
                         reads=[("wqa", 0, g), ("wqa", 1, g)] + uT_res(tb), writes=[("ps", a)])
                evac(QaT[:, g, 512 * t:512 * t + 512], ps[a][:, :], [("ps", a)], [("QaT", 4 * t + j) for j in range(4)])
        for t in (["m", 0, 1, 2, 3] if DBG_SWA >= 0.3 else []):
            tc0, tn, tb = tile_cols(t)
            a = nxt("psQ", 2)
            mm_group(ps[a][:, 0:tn], [(wka[:, k, :], uT[:, k, tc0:tc0 + tn]) for k in range(8)],
                     reads=["wka"] + uT_res(tb), writes=[("ps", a)])
            evac(Ka2[0][0:64, tc0:tc0 + tn], ps[a][0:64, 0:tn], [("ps", a)], [("KaT", b) for b in tb], eng="dve")
            evac(Ka2[1][64:128, tc0:tc0 + tn], ps[a][64:128, 0:tn], [("ps", a)], [("KaT", b) for b in tb], eng="dve")
        for bi, b in enumerate(ablocks if DBG_SWA >= 0.4 else []):
            if DBG_SWA == 0.45 and b == "m":
                continue
            c0, nr = blk_cols(b)
            a = nxt("psQ", 2)
            mm_group(ps[a][0:nr, 0:128], [(uT[:, k, c0:c0 + nr], wva[:, k, :]) for k in range(8)],
                     reads=["wva"] + uT_res([b]), writes=[("ps", a)])
            evac(Va[0][0:nr, bi, 0:64], ps[a][0:nr, 0:64], [("ps", a)], [("Va", 0)], eng="dve")
            evac(Va[1][0:nr, bi, 64:128], ps[a][0:nr, 64:128], [("ps", a)], [("Va", 1)], eng="dve")
        SB = [0, 1, 2, 5]
        LOOK = 3
        ODS = [(3, 4), (6, 7)]
        steps = []
        for i in range(NBLK):
            roles = [("meta", 0, 16, 0)] + ([("prev", i, 128, NMETA + 128 * (i - 1))] if i >= 1 else []) + \
                    [("cur", i + 1, 128, NMETA + 128 * i)]
            n_i = 2 * len(roles)
            cnt = 0
            for kvh in range(2):
                for (role, kb, nk, kc0) in roles:
                    steps.append((i, kvh, role, kb, nk, kc0, cnt == 0, cnt == n_i - 1))
                    cnt += 1

        def swa_qk(n):
            i, kvh, role, kb, nk, kc0, st, last = steps[n]
            sidx = SB[n % 4]
            pS = ps[sidx]
            base = 64 * kvh
            kres = ("KaT", "m") if role == "meta" else ("KaT", kb - 1)
            def qk(e, w):
                ins = e.matmul(pS[0:nk, :].rearrange("p (g q) -> p g q", g=4), lhsT=Ka2[kvh][:, kc0:kc0 + nk],
                               rhs=QaT[:, :, 128 * i:128 * i + 128], start=True, stop=True)
                if w is not None:
                    ins._wait_ge(*w)
                return ins
            P.add("pe", qk, reads=[kres, ("QaT", i)], writes=[("ps", sidx)], attach=True)

        def swa_rest(n):
            i, kvh, role, kb, nk, kc0, st, last = steps[n]
            sidx = SB[n % 4]
            pS = ps[sidx]
            pj = n % NPT
            iO, iD = ODS[i % 2]
            pO, pD = ps[iO], ps[iD]
            for g in range(4):
                hd = 4 * kvh + g
                if role == "meta":
                    bcol = cf[0:nk, CF_BMETA + 8 * i + hd:CF_BMETA + 8 * i + hd + 1]
                elif role == "prev":
                    bcol = cf[0:nk, CF_BPREV + hd:CF_BPREV + hd + 1]
                else:
                    bcol = cf[0:nk, CF_BCUR + hd:CF_BCUR + hd + 1]
                P.add("act", lambda e, g=g, bcol=bcol: e.activation(
                    out=PT[pj][0:nk, 128 * g:128 * g + 128], in_=pS[0:nk, 128 * g:128 * g + 128],
                    func=AF.Exp, bias=bcol, scale=0.125), reads=[("ps", sidx), "cf"], writes=[("PT", pj)])
            if role != "meta":
                mo = CB_MPREV if role == "prev" else CB_MCUR
                P.add("dve", lambda e: e.tensor_tensor(
                    out=PT[pj][:, :], in0=PT[pj][:, :], in1=cb[:, mo:mo + 512], op=ALU.mult),
                    reads=[("PT", pj), "cb"], writes=[("PT", pj)])
            oo = CB_OLO if kvh == 0 else CB_OHI

            def pv(e, w):
                i0 = e.matmul(pO[:, :], lhsT=Va[kvh][0:nk, kb, :], rhs=PT[pj][0:nk, :], start=st, stop=last)
                if w is not None:
                    i0._wait_ge(*w)
                return e.matmul(pD[:, :], lhsT=cb[0:nk, oo:oo + 128], rhs=PT[pj][0:nk, :], start=st, stop=last)
            P.add("pe", pv, reads=[("PT", pj), ("Va", kvh), "cb"], writes=[("ps", iO), ("ps", iD)], attach=True)
            if defer:
                defer.pop(0)()
            if last:
                dj = 0
                defer.append(lambda: P.add("dve", lambda e: e.tensor_tensor(out=den[dj][:, :], in0=pD[:, :], in1=sinktab[:, :], op=ALU.add),
                                           reads=[("ps", iD), "sinktab"], writes=[("den", dj)]))
                for hf in range(4):
                    defer.append(lambda hf=hf: P.add("dve", lambda e: e.reciprocal(
                        out=den[dj][:, 128 * hf:128 * hf + 128], in_=den[dj][:, 128 * hf:128 * hf + 128]),
                        reads=[("den", dj)], writes=[("den", dj)]))
                defer.append(lambda: P.add("dve", lambda e: e.tensor_tensor(
                    out=OaT[:, :, 128 * i:128 * i + 128], in0=pO[:, :].rearrange("p (g q) -> p g q", g=4),
                    in1=den[dj][:, :].rearrange("p (g q) -> p g q", g=4), op=ALU.mult),
                    reads=[("ps", iO), ("den", dj)], writes=[("OaT", i // 4)]))
        defer = []
        for n in range(len(steps) + LOOK):
            if n < len(steps):
                swa_qk(n)
            if n - LOOK >= 0:
                swa_rest(n - LOOK)
        while defer:
            defer.pop(0)()
        P.barrier()
        if mstop <= 2:
            return

        wqb = AR(OFF_X, [128, 8, 512])
        wkb = AR(OFF_X + 8192, [128, 8, 512])
        wvb = AR(OFF_X + 16384, [128, 8, 512])
        QbT = AR(OFF_X + 24576, [128, SEQ])
        Kb2 = [AR(OFF_X + 28672 + 4128 * i, [128, TCOLS]) for i in range(2)]
        Vb = [AR(OFF_X + 36928 + 4352 * i, [128, 17, 128]) for i in range(2)]
        P.add("pool", lambda e: e.memset(Kb2[0][64:128, :], 0.0), writes=[("KbT", b) for b in ablocks])
        P.add("pool", lambda e: e.memset(Kb2[1][0:64, :], 0.0), writes=[("KbT", b) for b in ablocks])
        for c in range(4):
            for (wt, col0, nm) in ((wqb, 768, "wqb"), (wkb, 1280, "wkb"), (wvb, 1792, "wvb")):
                cast_dma(wt[:, :, 128 * c:128 * c + 128],
                         wi_d[:, col0 + 128 * c:col0 + 128 * c + 128].rearrange("(k p) n -> p k n", p=128), (nm, c), nm)
        for i in range(2):
            P.add("pool", lambda e, i=i: e.memset(Vb[i][:, :, :], 0.0), writes=[("Vb", i, bi) for bi in range(17)])
        for c in range(4):
            for t in range(4):
                tc0, tn, tb = tile_cols(t)
                a = nxt("psQ", 2)
                mm_group(ps[a][:, :], [(wqb[:, k, 128 * c:128 * c + 128], uT[:, k, tc0:tc0 + tn]) for k in range(8)],
                         reads=[("wqb", c)] + uT_res(tb), writes=[("ps", a)])
                evac(QbT[:, 512 * t:512 * t + 512], ps[a][:, :], [("ps", a)], [("QbT", t)], eng="dve")
            for t in ["m", 0, 1, 2, 3]:
                tc0, tn, tb = tile_cols(t)
                a = nxt("psQ", 2)
                mm_group(ps[a][:, 0:tn], [(wkb[:, k, 128 * c:128 * c + 128], uT[:, k, tc0:tc0 + tn]) for k in range(8)],
                         reads=[("wkb", c)] + uT_res(tb), writes=[("ps", a)])
                evac(Kb2[0][0:64, tc0:tc0 + tn], ps[a][0:64, 0:tn], [("ps", a)], [("KbT", b) for b in tb], eng="dve")
                evac(Kb2[1][64:128, tc0:tc0 + tn], ps[a][64:128, 0:tn], [("ps", a)], [("KbT", b) for b in tb], eng="dve")
            for bi, b in enumerate(ablocks):
                c0, nr = blk_cols(b)
                a = nxt("psQ", 2)
                mm_group(ps[a][0:nr, 0:128], [(uT[:, k, c0:c0 + nr], wvb[:, k, 128 * c:128 * c + 128]) for k in range(8)],
                         reads=[("wvb", c)] + uT_res([b]), writes=[("ps", a)])
                evac(Vb[0][0:nr, bi, 0:64], ps[a][0:nr, 0:64], [("ps", a)], [("Vb", 0, bi)], eng="dve")
                evac(Vb[1][0:nr, bi, 64:128], ps[a][0:nr, 64:128], [("ps", a)], [("Vb", 1, bi)], eng="dve")
            fsteps = []
            for j in range(4):
                kbs = [0] + [1 + r for r in range(4 * j + 4)]
                n_j = 2 * len(kbs)
                cnt = 0
                for kb in kbs:
                    for hh in range(2):
                        fsteps.append((j, kb, hh, cnt == 0, cnt == n_j - 1))
                        cnt += 1

            def fparams(n):
                j, kb, hh, st, last = fsteps[n]
                if kb == 0:
                    nk, kc0, c0, kres = 16, 0, 0, ("KbT", "m")
                else:
                    r = kb - 1
                    nk, kc0, kres = 128, NMETA + 128 * r, ("KbT", r)
                    c0 = 128 * (r - 4 * j) if r >= 4 * j else 0
                diag = kb >= 1 and (kb - 1) >= 4 * j
                return j, kb, hh, st, last, nk, kc0, c0, kres, diag

            def fox_qk(n, c=c):
                j, kb, hh, st, last, nk, kc0, c0, kres, diag = fparams(n)
                sidx = SB[n % 4]
                pS = ps[sidx]
                base = 64 * hh
                def qk(e, w):
                    ins = e.matmul(pS[0:nk, c0:512], lhsT=Kb2[hh][:, kc0:kc0 + nk],
                                   rhs=QbT[:, 512 * j + c0:512 * j + 512], start=True, stop=True)
                    if w is not None:
                        ins._wait_ge(*w)
                    return ins
                P.add("pe", qk, reads=[kres, ("QbT", j)], writes=[("ps", sidx)], attach=True)

            def fox_rest(n, c=c):
                j, kb, hh, st, last, nk, kc0, c0, kres, diag = fparams(n)
                sidx = SB[n % 4]
                pS = ps[sidx]
                pj = n % NPT
                iO, iD = ODS[(4 * c + j) % 2]
                pO, pD = ps[iO], ps[iD]
                hd = 2 * c + hh
                P.add("act", lambda e: e.activation(
                    out=PT[pj][0:nk, c0:512], in_=pS[0:nk, c0:512], func=AF.Exp,
                    bias=biasF[0:nk, kb, j, hd:hd + 1], scale=0.125),
                    reads=[("ps", sidx), "biasF"], writes=[("PT", pj)])
                if diag:
                    P.add("dve", lambda e: e.tensor_tensor(
                        out=PT[pj][:, c0:c0 + 128], in0=PT[pj][:, c0:c0 + 128],
                        in1=cb[:, CB_MCUR:CB_MCUR + 128], op=ALU.mult),
                        reads=[("PT", pj), "cb"], writes=[("PT", pj)])
                oo = CB_OLO if hh == 0 else CB_OHI

                def pv(e, w):
                    i0 = e.matmul(pO[:, c0:512], lhsT=Vb[hh][0:nk, kb, :], rhs=PT[pj][0:nk, c0:512], start=st, stop=last)
                    if w is not None:
                        i0._wait_ge(*w)
                    return e.matmul(pD[:, c0:512], lhsT=cb[0:nk, oo:oo + 128], rhs=PT[pj][0:nk, c0:512], start=st, stop=last)
                P.add("pe", pv, reads=[("PT", pj), ("Vb", hh, kb), "cb"], writes=[("ps", iO), ("ps", iD)], attach=True)
                if defer:
                    defer.pop(0)()
                if last:
                    dj = 0
                    for hf in range(2):
                        defer.append(lambda hf=hf: P.add("dve", lambda e: e.reciprocal(
                            out=den[dj][:, 256 * hf:256 * hf + 256], in_=pD[:, 256 * hf:256 * hf + 256]),
                            reads=[("ps", iD)], writes=[("den", dj)]))
                    defer.append(lambda: P.add("dve", lambda e: e.tensor_tensor(
                        out=ObT[:, c, 512 * j:512 * j + 512], in0=pO[:, :], in1=den[dj][:, :], op=ALU.mult),
                        reads=[("ps", iO), ("den", dj)], writes=[("ObT", j)]))
            for n in range(len(fsteps) + LOOK):
                if n < len(fsteps):
                    fox_qk(n)
                if n - LOOK >= 0:
                    fox_rest(n - LOOK)
            while defer:
                defer.pop(0)()
        P.barrier()
        if mstop <= 3:
            return

        mixT = AR(OFF_MIX, [128, 8, SEQ])
        wo_t = AR(OFF_PW, [128, 8, D])
        ovl = [[("pw", 0, "a", 0), ("pw", 0, "a", 1), ("pw", 0, "b"), ("pw", 0, "ga")],
               [("pw", 0, "gb"), ("pw", 1, "a", 0), ("pw", 1, "a", 1), ("pw", 1, "b")]]

        def issue_wo(hf):
            P.add("pool", lambda e: e.dma_start(
                out=wo_t[:, 4 * hf:4 * hf + 4, :], in_=wo_d[512 * hf:512 * hf + 512, :].rearrange("(k p) n -> p k n", p=128)),
                writes=[("wout", hf)] + ovl[hf], dma=f"wout{hf}")
        ptmp = [AR(OFF_PTMP + 2048 * i, [128, 512], F32) for i in range(2)]
        def pw_tiles(st):
            sl = st % 2
            o = OFF_PW + 12288 * sl
            return sl, AR(o, [128, 4, 256]), AR(o + 2048, [128, 4, 256]), AR(o + 4096, [128, 8, 256]), AR(o + 8192, [128, 8, 256])

        def pw_issue(st):
            sl, wa_t, wb_t, wga_t, wgb_t = pw_tiles(st)
            c0 = 256 * st
            for hh in range(2):
                cast_dma(wa_t[64 * hh:64 * hh + 64, :, :],
                         wa_d[256 * hh:256 * hh + 256, c0:c0 + 256].rearrange("(g d) n -> d g n", d=64),
                         ("pw", sl, "a", hh), f"pw{sl}a{hh}")
            cast_dma(wb_t, wb_d[:, c0:c0 + 256].rearrange("(c p) n -> p c n", p=128), ("pw", sl, "b"), f"pw{sl}b")
            cast_dma(wga_t, wi_d[:, 2312 + c0:2312 + c0 + 256].rearrange("(k p) n -> p k n", p=128), ("pw", sl, "ga"), f"pw{sl}ga")
            cast_dma(wgb_t, wi_d[:, 3336 + c0:3336 + c0 + 256].rearrange("(k p) n -> p k n", p=128), ("pw", sl, "gb"), f"pw{sl}gb")
        pw_issue(0)
        for st in range(4):
            s, wa_t, wb_t, wga_t, wgb_t = pw_tiles(st)
            if st + 1 < 4:
                pw_issue(st + 1)
            if st == 3:
                issue_wo(0)
            for q in range(2):
                m = 2 * st + q
                for t in range(4):
                    tc0, tn, tb = tile_cols(t)
                    ia, ib, iga, igb = (nxt("psP", 8) for _ in range(4))
                    mm_group(ps[ia][:, :], [(wa_t[:, k, 128 * q:128 * q + 128], OaT[:, k, 512 * t:512 * t + 512]) for k in range(4)],
                             reads=[("pw", s, "a", 0), ("pw", s, "a", 1), ("OaT", t)], writes=[("ps", ia)])
                    mm_group(ps[ib][:, :], [(wb_t[:, k, 128 * q:128 * q + 128], ObT[:, k, 512 * t:512 * t + 512]) for k in range(4)],
                             reads=[("pw", s, "b"), ("ObT", t)], writes=[("ps", ib)])
                    mm_group(ps[iga][:, :], [(wga_t[:, k, 128 * q:128 * q + 128], uT[:, k, tc0:tc0 + 512]) for k in range(8)],
                             reads=[("pw", s, "ga")] + uT_res(tb), writes=[("ps", iga)])
                    mm_group(ps[igb][:, :], [(wgb_t[:, k, 128 * q:128 * q + 128], uT[:, k, tc0:tc0 + 512]) for k in range(8)],
                             reads=[("pw", s, "gb")] + uT_res(tb), writes=[("ps", igb)])
                    ja, jb = 0, 1
                    P.add("act", lambda e, ja=ja, iga=iga: e.activation(out=ptmp[ja][:, :], in_=ps[iga][:, :], func=AF.Sigmoid),
                          reads=[("ps", iga)], writes=[("ptmp", ja)])
                    P.add("act", lambda e, jb=jb, igb=igb: e.activation(out=ptmp[jb][:, :], in_=ps[igb][:, :], func=AF.Sigmoid),
                          reads=[("ps", igb)], writes=[("ptmp", jb)])
                    P.add("dve", lambda e, ja=ja, ia=ia: e.tensor_tensor(out=ptmp[ja][:, :], in0=ptmp[ja][:, :], in1=ps[ia][:, :], op=ALU.mult),
                          reads=[("ptmp", ja), ("ps", ia)], writes=[("ptmp", ja)])
                    P.add("dve", lambda e, jb=jb, ib=ib: e.tensor_tensor(out=ptmp[jb][:, :], in0=ptmp[jb][:, :], in1=ps[ib][:, :], op=ALU.mult),
                          reads=[("ptmp", jb), ("ps", ib)], writes=[("ptmp", jb)])
                    P.add("pool", lambda e, ja=ja, jb=jb, m=m, t=t: e.tensor_tensor(
                        out=mixT[:, m, 512 * t:512 * t + 512], in0=ptmp[ja][:, :], in1=ptmp[jb][:, :], op=ALU.add),
                        reads=[("ptmp", ja), ("ptmp", jb)], writes=[("mixT", t)])
        issue_wo(1)
        hook = flush = None
        if post_tail is not None:
            hook, flush = post_tail()
        for b in range(NBLK):
            for half in range(2):
                o = nxt("psP", 8)
                mm_group(ps[o][:, :], [(mixT[:, k, 128 * b:128 * b + 128], wo_t[:, k, 512 * half:512 * half + 512]) for k in range(8)],
                         reads=[("mixT", b // 4), ("wout", 0), ("wout", 1)], writes=[("ps", o)])
                dst = h[:, b, 512 * half:512 * half + 512]
                P.add("dve", lambda e, dst=dst, o=o: e.tensor_tensor(out=dst, in0=ps[o][:, :], in1=dst, op=ALU.add),
                      reads=[("ps", o), ("h", b)], writes=[("h", b)])
            if hook is not None:
                hook(b)
        if flush is not None:
            flush()

    def final_out(s, raw, hooked=False, after_block=None):
        ost = [AR(OFF_OST + 4096 * i, [128, D], F32) for i in range(2)]
        if raw:
            for b in range(NBLK):
                P.add("sp", lambda e, b=b: e.dma_start(out=out_d[s, 128 * b:128 * b + 128, :], in_=h[:, b, :]),
                      reads=[("h", b)], dma=f"out{b % 2}")
            return
        P.add("sp", lambda e: e.dma_start(out=gb[:], in_=n4_d.broadcast_to([128, D])), writes=["gb"], dma="gb")
        P.add("pool", lambda e: e.memset(ss[:], 0.0), writes=ALLSS)
        fjunk = AR(OFF_OST + 8192, [128, D])

        def a1(b):
            P.add("act", lambda e: e.activation(out=fjunk[:, :], in_=h[:, b, :], func=AF.Square, accum_out=ss[:, b:b + 1]),
                  reads=[("h", b), ("ss", b)], writes=[("ss", b), "fjunk"], full=True)
            P.add("act", lambda e: e.activation(out=rs[:, b:b + 1], in_=ss[:, b:b + 1], func=AF.Sqrt,
                                                bias=epsc[:, 0:1], scale=1.0 / D),
                  reads=[("ss", b), "epsc"], writes=[("rs", b)])

        def a2(b):
            j = nxt("ost", 2)
            P.add("dve", lambda e: e.reciprocal(out=rs[:, b:b + 1], in_=rs[:, b:b + 1]), reads=[("rs", b)], writes=[("rs", b)])
            P.add("dve", lambda e: e.scalar_tensor_tensor(
                out=ost[j][:, :], in0=h[:, b, :], scalar=rs[:, b:b + 1], in1=gb[:, :], op0=ALU.mult, op1=ALU.mult),
                reads=[("h", b), ("rs", b), "gb"], writes=[("ost", j)])
            P.add("sp", lambda e: e.dma_start(out=out_d[s, 128 * b:128 * b + 128, :], in_=ost[j][:, :]),
                  reads=[("ost", j)], dma=f"out{j}")
            if after_block is not None:
                after_block(b)
        if hooked:
            seq = []

            def hook(b):
                seq.append(b)
                if len(seq) >= 2:
                    a2(seq[-2])
                a1(b)

            def flush():
                a2(seq[-1])
            return hook, flush
        a1(0)
        for b in range(NBLK):
            if b + 1 < NBLK:
                a1(b + 1)
            a2(b)

    def load_x(s, b):
        P.add("sp", lambda e: e.dma_start(out=h[:, b, :], in_=x_d[s, 128 * b:128 * b + 128, :]),
              writes=[("h", b)], dma=f"x{b % 4}")

    for s in range(2):
        hm = AR(OFF_HM, [128, D], F32)
        if s == 0 or stage != 3:
            for b in range(NBLK):
                load_x(s, b)
        P.add("sp", lambda e: e.dma_start(out=hm[0:NMETA, :], in_=meta_d), writes=[("h", "m")], dma="xm")
        if stage == 3:
            ffn(w1i_d, w1o_d, n1_d, True, hm, "f1",
                tail=lambda: norm_to_uT(n2_d, ["m"] + list(range(NBLK)), hm, hooked=True,
                                        junk=AR(OFF_OST, [128, D]), junkres=("ost", 0)))
            P.barrier()
            mixer(hm, post_tail=lambda: norm_to_uT(n3_d, list(range(NBLK)), hm, hooked=True,
                                                    junk=AR(OFF_PTMP, [128, D]), junkres=("ptmp", 0)))
            P.barrier()
            ffn(w2i_d, w2o_d, n3_d, False, hm, "f2", do_norm=False,
                tail=lambda s=s: final_out(s, raw=False, hooked=True,
                                           after_block=((lambda b: load_x(1, b)) if s == 0 else None)))
        else:
            ffn(w1i_d, w1o_d, n1_d, True, hm, "f1")
            if stage >= 2:
                norm_to_uT(n2_d, ["m"] + list(range(NBLK)), hm)
                P.barrier()
                mixer(hm, mstop=(stage - 10 if stage >= 10 else 9))
            P.barrier()
            final_out(s, raw=True)
    P.emit()
    es.close()
    return nc


_CACHE = {}


def kernel(x, meta_tokens, ffn1_norm, ffn1_w_in, ffn1_w_out, mix_norm, w_in, b_forget, attn_sinks,
           w_branch_a, w_branch_b, w_out, ffn2_norm, ffn2_w_in, ffn2_w_out, final_norm, _stage=3, _cores=8):
    f = lambda a: np.ascontiguousarray(np.asarray(a, dtype=np.float32))
    x = f(x)
    cf, cb = make_consts()
    shared = {
        "meta": f(meta_tokens), "n1": f(ffn1_norm).reshape(1, D), "n2": f(mix_norm).reshape(1, D),
        "n3": f(ffn2_norm).reshape(1, D), "n4": f(final_norm).reshape(1, D),
        "w1i": f(ffn1_w_in)[0], "w1o": f(ffn1_w_out)[0], "w2i": f(ffn2_w_in)[0], "w2o": f(ffn2_w_out)[0],
        "wi": f(w_in)[0], "bfg": f(b_forget).reshape(1, 8), "snk": f(attn_sinks).reshape(1, 8),
        "wa": f(w_branch_a)[0], "wb": f(w_branch_b)[0], "wo": f(w_out)[0], "cf": cf, "cb": cb,
    }
    if _stage not in _CACHE:
        _CACHE[_stage] = build(_stage)
    nc = _CACHE[_stage]
    in_maps = [dict(shared, x=x[2 * c:2 * c + 2]) for c in range(_cores)]
    res = run_bass_kernel_spmd(nc, in_maps, core_ids=list(range(_cores)))
    return np.concatenate([r["out"] for r in res.results], axis=0)
```

```python
import contextlib
import numpy as np
import ml_dtypes
import concourse.bass as bass
import concourse.mybir as mybir
from concourse.bass_utils import run_bass_kernel_spmd

F32 = mybir.dt.float32
BF16 = mybir.dt.bfloat16
AF = mybir.ActivationFunctionType
ALU = mybir.AluOpType

D = 1024
SEQ = 2048
NBLK = 16
NMETA = 16
TCOLS = SEQ + NMETA
DFF = 2816
NCH = 22
PASSES = [(0, 6), (6, 6), (12, 5), (17, 5)]
EPS = 1e-6
SLOPES = [2.0 ** (-(h + 1)) for h in range(8)]
ENGS = ("pe", "act", "dve", "pool", "sp")
ARENA_BYTES = 98304
DBG_SWA = 9

CF_TRI, CF_ONES, CF_IOTA, CF_BCUR, CF_BPREV, CF_BMETA, CF_N = 0, 128, 256, 384, 392, 400, 528
CB_ID, CB_MCUR, CB_MPREV, CB_OLO, CB_OHI, CB_N = 0, 128, 640, 1152, 1280, 1408


def make_consts():
    cf = np.zeros((128, CF_N), np.float32)
    p = np.arange(128)
    cf[:, CF_TRI:CF_TRI + 128] = (p[:, None] <= p[None, :]).astype(np.float32)
    cf[:, CF_ONES:CF_ONES + 128] = 1.0
    cf[:, CF_IOTA:CF_IOTA + 128] = p[None, :].astype(np.float32)
    sl = np.array(SLOPES, np.float32)
    cf[:, CF_BCUR:CF_BCUR + 8] = p[:, None] * sl[None, :]
    cf[:, CF_BPREV:CF_BPREV + 8] = (p[:, None] - 128.0) * sl[None, :]
    bm = np.zeros((128, 16, 8), np.float32)
    for i in range(16):
        bm[:, i, :] = (p[:, None] - 16.0 - 128.0 * i) * sl[None, :]
    cf[:, CF_BMETA:CF_BMETA + 128] = bm.reshape(128, 128)
    cb = np.zeros((128, CB_N), np.float32)
    cb[:, CB_ID:CB_ID + 128] = np.eye(128)
    mc = (p[:, None] <= p[None, :]).astype(np.float32)
    mp = (p[:, None] > p[None, :]).astype(np.float32)
    cb[:, CB_MCUR:CB_MCUR + 512] = np.tile(mc, (1, 4))
    cb[:, CB_MPREV:CB_MPREV + 512] = np.tile(mp, (1, 4))
    cb[:, CB_OLO:CB_OLO + 64] = 1.0
    cb[:, CB_OHI + 64:CB_OHI + 128] = 1.0
    return cf, cb.astype(ml_dtypes.bfloat16)


class Op:
    __slots__ = ("idx", "eng", "fn", "reads", "writes", "dma", "deps", "signal", "semval", "semname", "attach")

    def __init__(self, idx, eng, fn, reads, writes, dma):
        self.idx, self.eng, self.fn, self.reads, self.writes, self.dma = idx, eng, fn, reads, writes, dma
        self.deps = set()
        self.signal = False
        self.semval = 0
        self.semname = None
        self.attach = False


class Prog:
    def __init__(self, nc):
        self.nc = nc
        self.ops = []
        self.last_writer = {}
        self.readers = {}
        self.last_on = {}

    def add(self, eng, fn, reads=(), writes=(), dma=None, full=False, attach=False):
        if dma is not None:
            writes = tuple(writes) + (("__slot", dma),)
        op = Op(len(self.ops), eng, fn, tuple(reads), tuple(writes), dma)
        op.attach = attach
        deps = set()
        for r in op.reads:
            w = self.last_writer.get(r)
            if w is not None:
                deps.add(w)
        for r in op.writes:
            w = self.last_writer.get(r)
            if w is not None:
                deps.add(w)
            deps.update(self.readers.get(r, ()))
        for d in deps:
            dop = self.ops[d]
            if dop.dma is None and dop.eng == eng and dma is None:
                if eng == "pe":
                    continue
                if not full and not any(self.last_writer.get(r) == d for r in op.reads):
                    continue
            op.deps.add(d)
        for r in op.reads:
            self.readers.setdefault(r, []).append(op.idx)
        for r in op.writes:
            self.last_writer[r] = op.idx
            self.readers[r] = []
        self.ops.append(op)
        if dma is None:
            self.last_on[eng] = op.idx
        return op

    def barrier(self):
        lasts = dict(self.last_on)
        dmas = [idx for (k, idx) in self.last_writer.items() if isinstance(k, tuple) and k[0] == "__slot"]
        for e in ENGS:
            op = Op(len(self.ops), e, None, (), (), None)
            for e2, idx in lasts.items():
                if e2 != e:
                    op.deps.add(idx)
            op.deps.update(dmas)
            self.ops.append(op)

    def emit(self):
        nc, ops = self.nc, self.ops
        for op in ops:
            best = {}
            keep = set()
            for d in op.deps:
                dop = ops[d]
                if dop.dma is not None:
                    keep.add(d)
                elif d > best.get(dop.eng, -1):
                    best[dop.eng] = d
            keep.update(best.values())
            op.deps = keep
            for d in keep:
                ops[d].signal = True
        cnt = {}
        for op in ops:
            if op.dma is not None:
                op.signal = True
                op.semname = "d_" + op.dma
                cnt[op.semname] = cnt.get(op.semname, 0) + 16
                op.semval = cnt[op.semname]
            elif op.signal:
                op.semname = "e_" + op.eng
                cnt[op.semname] = cnt.get(op.semname, 0) + 1
                op.semval = cnt[op.semname]
        semnames = sorted(cnt)
        with contextlib.ExitStack() as es:
            sems = {n: es.enter_context(nc.semaphore(n)) for n in semnames}
            block = es.enter_context(nc.Block())
            by_eng = {e: [op for op in ops if op.eng == e] for e in ENGS}
            final = [(n, cnt[n]) for n in semnames if n.startswith("d_")]

            def run(engname, eng):
                known = {}
                for op in by_eng[engname]:
                    waits = {}
                    for d in op.deps:
                        dop = ops[d]
                        if dop.semval > waits.get(dop.semname, 0):
                            waits[dop.semname] = dop.semval
                    need = [(n, v) for n, v in sorted(waits.items()) if known.get(n, 0) < v]
                    emb = None
                    if op.attach and need:
                        emb = need.pop()
                    for n, v in need:
                        eng.wait_ge(sems[n], v)
                        known[n] = v
                    if op.fn is None:
                        continue
                    if op.attach:
                        if emb is not None:
                            known[emb[0]] = emb[1]
                        ins = op.fn(eng, None if emb is None else (sems[emb[0]], emb[1]))
                    else:
                        ins = op.fn(eng)
                    if op.signal:
                        ins.then_inc(sems[op.semname], 16 if op.dma is not None else 1)
                if engname == "sp":
                    for n, v in final:
                        eng.wait_ge(sems[n], v)

            @block.tensor
            def _(e):
                run("pe", e)

            @block.scalar
            def _(e):
                run("act", e)

            @block.vector
            def _(e):
                run("dve", e)

            @block.gpsimd
            def _(e):
                run("pool", e)

            @block.sync
            def _(e):
                run("sp", e)


def build(stage=3):
    nc = bass.Bass("TRN2", target_bir_lowering=False)
    dt_in = lambda n, s, d=F32: nc.dram_tensor(n, s, d, kind="ExternalInput").ap()
    x_d = dt_in("x", [2, SEQ, D])
    meta_d = dt_in("meta", [NMETA, D])
    n1_d, n2_d, n3_d, n4_d = (dt_in(n, [1, D]) for n in ("n1", "n2", "n3", "n4"))
    w1i_d, w1o_d = dt_in("w1i", [D, 2 * DFF]), dt_in("w1o", [DFF, D])
    w2i_d, w2o_d = dt_in("w2i", [D, 2 * DFF]), dt_in("w2o", [DFF, D])
    wi_d = dt_in("wi", [D, 4360])
    bf_d, sk_d = dt_in("bfg", [1, 8]), dt_in("snk", [1, 8])
    wa_d, wb_d, wo_d = dt_in("wa", [512, D]), dt_in("wb", [512, D]), dt_in("wo", [D, D])
    cf_d, cb_d = dt_in("cf", [128, CF_N]), dt_in("cb", [128, CB_N], BF16)
    out_d = nc.dram_tensor("out", [2, SEQ, D], F32, kind="ExternalOutput").ap()

    es = contextlib.ExitStack()
    sb = lambda n, s, d: es.enter_context(nc.sbuf_tensor(n, s, d))
    h = sb("h", [128, NBLK, D], F32)
    uT = sb("uT", [128, 8, TCOLS], BF16)
    cf = sb("cf_s", [128, CF_N], F32)
    cb = sb("cb_s", [128, CB_N], BF16)
    gb = sb("gb", [128, D], F32)
    ss = sb("ss", [128, 17], F32)
    rs = sb("rs", [128, 17], F32)
    epsc = sb("epsc", [128, 1], F32)
    bfg = sb("bfg_s", [128, 8], F32)
    snk = sb("snk_s", [128, 8], F32)
    sinktab = sb("sinktab", [128, 512], F32)
    arena = sb("arena", [128, ARENA_BYTES // 2], BF16)
    ps = [es.enter_context(nc.psum_tensor(f"ps{i}", [128, 512], F32)) for i in range(8)]
    pt = [ps[6 + i][:, 0:256].bitcast(BF16) for i in range(2)]

    def AR(off, shape, dtype=BF16):
        n = int(np.prod(shape[1:]))
        nb = n * (4 if dtype == F32 else 2)
        assert off % 4 == 0 and off + nb <= ARENA_BYTES, (off, nb)
        v = arena[:, off // 2:(off + nb) // 2]
        if dtype == F32:
            v = v.bitcast(F32)
        if len(shape) == 2:
            return v
        names = " ".join(f"a{i}" for i in range(len(shape) - 1))
        kw = {f"a{i}": shape[i + 1] for i in range(len(shape) - 2)}
        return v.rearrange(f"p ({names}) -> p {names}", **kw)

    P = Prog(nc)
    rot = {}

    def nxt(name, n):
        rot[name] = (rot.get(name, -1) + 1) % n
        return rot[name]

    P.add("sp", lambda e: e.dma_start(out=cf[:], in_=cf_d), writes=["cf"], dma="cf")
    P.add("sp", lambda e: e.dma_start(out=cb[:], in_=cb_d), writes=["cb"], dma="cb")
    P.add("sp", lambda e: e.dma_start(out=bfg[:], in_=bf_d.broadcast_to([128, 8])), writes=["bfg"], dma="bfg")
    P.add("sp", lambda e: e.dma_start(out=snk[:], in_=sk_d.broadcast_to([128, 8])), writes=["snk"], dma="snk")
    P.add("pool", lambda e: e.memset(epsc[:], EPS), writes=["epsc"])
    onec = sb("onec", [128, 1], F32)
    P.add("pool", lambda e: e.memset(onec[:], 1.0), writes=["onec"])
    ALLSS = ["ss"] + [("ss", b) for b in ["m"] + list(range(NBLK))]
    for g in range(4):
        for hh in range(2):
            hd = 4 * hh + g
            P.add("act", lambda e, g=g, hh=hh, hd=hd: e.activation(
                out=sinktab[64 * hh:64 * hh + 64, 128 * g:128 * g + 128],
                in_=cf[64 * hh:64 * hh + 64, CF_IOTA:CF_IOTA + 128], func=AF.Exp,
                bias=snk[64 * hh:64 * hh + 64, hd:hd + 1], scale=SLOPES[hd]),
                reads=["cf", "snk"], writes=["sinktab"])

    def tile_cols(t):
        if t == "m":
            return 0, NMETA, ["m"]
        return NMETA + 512 * t, 512, [4 * t + j for j in range(4)]

    def blk_cols(b):
        if b == "m":
            return 0, NMETA
        return NMETA + 128 * b, 128

    def cast_dma(dst, src, res, slot):
        P.add("pool", lambda e: e.dma_start(out=dst, in_=src), writes=[res], dma=slot)

    def mm_group(out, pairs, reads, writes):
        def fn(e):
            n = len(pairs)
            ins = None
            for i, (l, r) in enumerate(pairs):
                ins = e.matmul(out, lhsT=l, rhs=r, start=(i == 0), stop=(i == n - 1))
            return ins
        P.add("pe", fn, reads=reads, writes=writes)

    evac_flip = [0]

    def evac(out, in_, reads, writes, eng=None):
        if eng is None:
            evac_flip[0] ^= 1
            eng = "dve" if evac_flip[0] else "act"
        if eng == "act":
            P.add("act", lambda e: e.copy(out=out, in_=in_), reads=reads, writes=writes)
        else:
            P.add(eng, lambda e: e.tensor_copy(out=out, in_=in_), reads=reads, writes=writes)

    def norm_to_uT(gain_d, blocks, hm, hooked=False, junk=None, junkres="junk"):
        ut = [AR(OFF_UT + 2048 * i, [128, D]) for i in range(2)]
        P.add("sp", lambda e: e.dma_start(out=gb[:], in_=gain_d.broadcast_to([128, D])), writes=["gb"], dma="gb")
        P.add("pool", lambda e: e.memset(ss[:], 0.0), writes=ALLSS)
        jmap = {}

        def geom(b):
            c0, nr = blk_cols(b)
            src = hm[0:nr, :] if b == "m" else h[:, b, :]
            col = 16 if b == "m" else b
            return c0, nr, src, col

        def a1(b, j=None):
            c0, nr, src, col = geom(b)
            if junk is not None:
                out, ores = junk[0:nr, :], junkres
            else:
                out, ores = ut[j][0:nr, :], ("ut", j)
            P.add("act", lambda e: e.activation(out=out, in_=src, func=AF.Square, accum_out=ss[0:nr, col:col + 1]),
                  reads=[("h", b), ("ss", b)], writes=[("ss", b), ores], full=True)
            P.add("act", lambda e: e.activation(
                out=rs[0:nr, col:col + 1], in_=ss[0:nr, col:col + 1], func=AF.Sqrt, bias=epsc[0:nr, 0:1], scale=1.0 / D),
                reads=[("ss", b), "epsc"], writes=[("rs", b)])

        def a2(b, j):
            c0, nr, src, col = geom(b)
            jmap[b] = j
            P.add("dve", lambda e: e.reciprocal(out=rs[0:nr, col:col + 1], in_=rs[0:nr, col:col + 1]),
                  reads=[("rs", b)], writes=[("rs", b)])
            P.add("dve", lambda e: e.scalar_tensor_tensor(
                out=ut[j][0:nr, :], in0=src, scalar=rs[0:nr, col:col + 1], in1=gb[0:nr, :],
                op0=ALU.mult, op1=ALU.mult), reads=[("h", b), ("rs", b), "gb"], writes=[("ut", j)])

        def stage_a(b):
            j = nxt("ut", 2)
            a1(b, j)
            a2(b, j)

        def stage_b(b):
            c0, nr = blk_cols(b)
            j = jmap[b]
            for half in range(2):
                def fn(e, half=half):
                    ins = None
                    for c in range(4):
                        cc = 4 * half + c
                        ins = e.transpose(pt[half][:, 128 * c:128 * c + nr], ut[j][0:nr, 128 * cc:128 * cc + 128],
                                          cb[0:nr, CB_ID:CB_ID + nr])
                    return ins
                P.add("pe", fn, reads=[("ut", j), "cb"], writes=[("ps", 6 + half)])
                src_v = pt[half][:, :].rearrange("p (c n) -> p c n", c=4)[:, :, 0:nr]
                evac(uT[:, 4 * half:4 * half + 4, c0:c0 + nr], src_v, [("ps", 6 + half)], [("uT", b, half)],
                     eng=("act" if half == 0 else "dve"))
        if hooked:
            assert junk is not None
            seq = []

            def hook(b):
                seq.append(b)
                n = len(seq)
                if n >= 3:
                    stage_b(seq[n - 3])
                if n >= 2:
                    a2(seq[n - 2], nxt("ut", 2))
                a1(b)

            def flush():
                n = len(seq)
                if n >= 2:
                    stage_b(seq[n - 2])
                a2(seq[n - 1], nxt("ut", 2))
                stage_b(seq[n - 1])
            return hook, flush
        stage_a(blocks[0])
        for i, b in enumerate(blocks):
            if i + 1 < len(blocks):
                stage_a(blocks[i + 1])
            stage_b(b)

    def uT_res(blks):
        return [("uT", b, hf) for b in blks for hf in range(2)]

    def ffn(w_in_d, w_out_d, gain_d, with_meta, hm, tagp, do_norm=True, tail=None):
        blocks = (["m"] if with_meta else []) + list(range(NBLK))
        if do_norm:
            norm_to_uT(gain_d, blocks, hm)
        tiles = (["m"] if with_meta else []) + [0, 1, 2, 3]
        act = AR(OFF_ACT, [128, 6, TCOLS])
        wis = [AR(OFF_WI + 8192 * i, [128, 2, 8, 256]) for i in range(3)]
        wos = [AR(OFF_WO + 12288 * i, [128, 6, D]) for i in range(2)]
        tmp = [AR(OFF_TMP + 2048 * i, [128, 512], F32) for i in range(2)]
        for (m0, nch) in PASSES:
            wslot = nxt("wo", 2)
            wo_t = wos[wslot]
            ml = 0
            while ml < nch:
                ns = min(2, nch - ml)
                s = nxt("wi", 3)
                wi_t = wis[s]
                c0 = (m0 + ml) * 128
                cast_dma(wi_t[:, 0, :, 0:ns * 128], w_in_d[:, c0:c0 + ns * 128].rearrange("(k p) n -> p k n", p=128),
                         ("wi", s, 0), f"wi{s}g")
                cast_dma(wi_t[:, 1, :, 0:ns * 128],
                         w_in_d[:, DFF + c0:DFF + c0 + ns * 128].rearrange("(k p) n -> p k n", p=128),
                         ("wi", s, 1), f"wi{s}u")
                if ml == 0:
                    cast_dma(wo_t[:, 0:nch, :], w_out_d[m0 * 128:(m0 + nch) * 128, :].rearrange("(k p) n -> p k n", p=128),
                             ("wo", wslot), f"wo{wslot}")
                for q in range(ns):
                    mloc = ml + q
                    for t in tiles:
                        tc0, tn, tb = tile_cols(t)
                        a = nxt("psA", 2)
                        pA, pB = ps[2 * a], ps[2 * a + 1]
                        for which, pp in ((0, pA), (1, pB)):
                            mm_group(pp[:, 0:tn],
                                     [(wi_t[:, which, k, q * 128:(q + 1) * 128], uT[:, k, tc0:tc0 + tn]) for k in range(8)],
                                     reads=[("wi", s, which)] + uT_res(tb), writes=[("ps", 2 * a + which)])
                        j = nxt("tmp", 2)
                        P.add("act", lambda e, j=j, pA=pA, tn=tn: e.activation(out=tmp[j][:, 0:tn], in_=pA[:, 0:tn], func=AF.Silu),
                              reads=[("ps", 2 * a)], writes=[("tmp", j)])
                        P.add("dve", lambda e, j=j, pB=pB, tn=tn, mloc=mloc, tc0=tc0: e.tensor_tensor(
                            out=act[:, mloc, tc0:tc0 + tn], in0=tmp[j][:, 0:tn], in1=pB[:, 0:tn], op=ALU.mult),
                            reads=[("tmp", j), ("ps", 2 * a + 1)], writes=[("act", mloc, t)])
                ml += ns
            hook = flush = None
            if tail is not None and (m0, nch) == PASSES[-1]:
                hook, flush = tail()
            for b in blocks:
                bc0, nr = blk_cols(b)
                t = "m" if b == "m" else b // 4
                for half in range(2):
                    o = 4 + nxt("psO", 2)
                    mm_group(ps[o][0:nr, :],
                             [(act[:, k, bc0:bc0 + nr], wo_t[:, k, 512 * half:512 * half + 512]) for k in range(nch)],
                             reads=[("act", k, t) for k in range(nch)] + [("wo", wslot)], writes=[("ps", o)])
                    dst = hm[0:nr, 512 * half:512 * half + 512] if b == "m" else h[:, b, 512 * half:512 * half + 512]
                    P.add("dve", lambda e, dst=dst, o=o, nr=nr: e.scalar_tensor_tensor(
                        out=dst, in0=ps[o][0:nr, :], scalar=0.5, in1=dst, op0=ALU.mult, op1=ALU.add),
                        reads=[("ps", o), ("h", b)], writes=[("h", b)])
                if hook is not None:
                    hook(b)
            if flush is not None:
                flush()

    OFF_HM = 0
    OFF_UT = 4096
    OFF_ACT = 8192
    OFF_WI = OFF_ACT + 24768
    OFF_WO = OFF_WI + 24576
    OFF_TMP = OFF_WO + 24576
    OFF_OST = OFF_TMP + 4096
    assert OFF_OST + 8192 <= ARENA_BYTES
    OFF_OA = 8192
    OFF_OB = OFF_OA + 16384
    OFF_LF = OFF_OB + 16384
    OFF_BIASF = OFF_LF + 3264
    OFF_PT = OFF_BIASF + 2176
    OFF_DEN = OFF_PT + 4096
    OFF_X = OFF_DEN + 2048
    assert OFF_X + 45632 <= ARENA_BYTES, OFF_X
    OFF_PW = OFF_OB + 16384
    OFF_MIX = OFF_PW + 24576
    assert OFF_MIX + 32768 <= ARENA_BYTES
    OFF_PTMP = 0

    def mixer(hm, mstop=9, post_tail=None):
        ablocks = ["m"] + list(range(NBLK))
        NPT = 4
        PT = [AR(OFF_PT + 1024 * i, [128, 512]) for i in range(NPT)]
        den = [AR(OFF_DEN, [128, 512], F32)]
        OaT = AR(OFF_OA, [128, 4, SEQ])
        ObT = AR(OFF_OB, [128, 4, SEQ])
        lfv = [AR(OFF_LF + 544 * i, [128, 136], F32) for i in range(6)]
        xb, lt, tots, offv, cum = lfv[0], lfv[1], lfv[2], lfv[3], lfv[4]
        biasF = AR(OFF_BIASF, [128, 17, 4, 8], F32)
        wf = lfv[5][:, :].bitcast(BF16)[:, 0:64].rearrange("p (k n) -> p k n", k=8)
        cast_dma(wf, wi_d[:, 2304:2312].rearrange("(k p) n -> p k n", p=128), "wf", "wf")

        def f_fn(e):
            ins = None
            for bi, b in enumerate(ablocks):
                c0, nr = blk_cols(b)
                for k in range(8):
                    ins = e.matmul(ps[0][0:nr, 8 * bi:8 * bi + 8], lhsT=uT[:, k, c0:c0 + nr], rhs=wf[:, k, :],
                                   start=(k == 0), stop=(k == 7))
            return ins
        P.add("pe", f_fn, reads=["wf"] + uT_res(ablocks), writes=[("ps", 0)])
        P.add("pool", lambda e: e.memset(lt[:], 0.0), writes=["lt"])
        for (r0, r1, c0, c1) in ((0, 16, 0, 8), (0, 128, 8, 136)):
            bb = bfg[r0:r1, :] if c1 == 8 else bfg[r0:r1, :].unsqueeze(1).to_broadcast([r1 - r0, 16, 8])
            i0 = ps[0][r0:r1, c0:c1] if c1 == 8 else ps[0][r0:r1, c0:c1].rearrange("p (b h) -> p b h", h=8)
            o0 = xb[r0:r1, c0:c1] if c1 == 8 else xb[r0:r1, c0:c1].rearrange("p (b h) -> p b h", h=8)
            P.add("dve", lambda e, bb=bb, i0=i0, o0=o0: e.tensor_tensor(out=o0, in0=i0, in1=bb, op=ALU.add),
                  reads=[("ps", 0), "bfg"], writes=["xb"])
            P.add("act", lambda e, r0=r0, r1=r1, c0=c0, c1=c1: e.activation(
                out=xb[r0:r1, c0:c1], in_=xb[r0:r1, c0:c1], func=AF.Exp, scale=-1.0), reads=["xb"], writes=["xb"])
            P.add("act", lambda e, r0=r0, r1=r1, c0=c0, c1=c1: e.activation(
                out=lt[r0:r1, c0:c1], in_=xb[r0:r1, c0:c1], func=AF.Ln, bias=onec[r0:r1, 0:1]), reads=["xb", "lt", "onec"], writes=["lt"])
        P.add("pe", lambda e: e.matmul(ps[1][:, 0:136], lhsT=cf[:, CF_ONES:CF_ONES + 128], rhs=lt[:, :], start=True, stop=True),
              reads=["lt", "cf"], writes=[("ps", 1)])
        P.add("pe", lambda e: e.matmul(ps[2][:, 0:136], lhsT=cf[:, CF_TRI:CF_TRI + 128], rhs=lt[:, :], start=True, stop=True),
              reads=["lt", "cf"], writes=[("ps", 2)])
        P.add("dve", lambda e: e.tensor_copy(out=tots[:, :], in_=ps[1][:, 0:136]), reads=[("ps", 1)], writes=["tots"])
        P.add("pool", lambda e: e.memset(offv[:, :], 0.0), writes=["offv"])
        for b in range(1, 17):
            P.add("dve", lambda e, b=b: e.tensor_tensor(out=offv[:, 8 * b:8 * b + 8], in0=offv[:, 8 * b - 8:8 * b],
                                                        in1=tots[:, 8 * b - 8:8 * b], op=ALU.add),
                  reads=["offv", "tots"], writes=["offv"])
        P.add("dve", lambda e: e.tensor_tensor(out=cum[:, :], in0=ps[2][:, 0:136], in1=offv[:, :], op=ALU.add),
              reads=[("ps", 2), "offv"], writes=["cum"])
        cum3 = cum[:, :].rearrange("p (b h) -> p b h", h=8)
        for j in range(4):
            ref = 4 * j + 3
            for hd in range(8):
                P.add("dve", lambda e, j=j, hd=hd, ref=ref: e.tensor_scalar(
                    out=biasF[:, :, j, hd], in0=cum3[:, :, hd], scalar1=offv[:, 8 * ref + hd:8 * ref + hd + 1],
                    scalar2=None, op0=ALU.subtract), reads=["cum", "offv"], writes=["biasF"])
        if mstop <= 1:
            return

        wqa = AR(OFF_X, [128, 8, 4, 2, 64])
        wka = AR(OFF_X + 8192, [128, 8, 128])
        wva = AR(OFF_X + 10240, [128, 8, 128])
        QaT = AR(OFF_X + 12288, [128, 4, SEQ])
        Ka2 = [AR(OFF_X + 28672 + 4128 * i, [128, TCOLS]) for i in range(2)]
        Va = [AR(OFF_X + 36928 + 4352 * i, [128, 17, 128]) for i in range(2)]
        P.add("pool", lambda e: e.memset(Ka2[0][64:128, :], 0.0), writes=[("KaT", b) for b in ablocks])
        P.add("pool", lambda e: e.memset(Ka2[1][0:64, :], 0.0), writes=[("KaT", b) for b in ablocks])
        for hh in range(2):
            for g in range(4):
                cq = 256 * hh + 64 * g
                cast_dma(wqa[:, :, g, hh, :], wi_d[:, cq:cq + 64].rearrange("(k p) d -> p k d", p=128),
                         ("wqa", hh, g), f"wqa{hh}{g}")
        cast_dma(wka, wi_d[:, 512:640].rearrange("(k p) n -> p k n", p=128), "wka", "wka")
        cast_dma(wva, wi_d[:, 640:768].rearrange("(k p) n -> p k n", p=128), "wva", "wva")
        for i in range(2):
            P.add("pool", lambda e, i=i: e.memset(Va[i][:, :, :], 0.0), writes=[("Va", i)])
        for g in range(4 if DBG_SWA >= 0.2 else 0):
            for t in range(4):
                tc0, tn, tb = tile_cols(t)
                a = nxt("psQ", 2)
                mm_group(ps[a][:, :], [(wqa[:, k, g].rearrange("p a d -> p (a d)"), uT[:, k, tc0:tc0 + tn]) for k in range(8)],
                         reads=[("wqa", 0, g), ("wqa", 1, g)] + uT_res(tb), writes=[("ps", a)])
                evac(QaT[:, g, 512 * t:512 * t + 512], ps[a][:, :], [("ps", a)], [("QaT", 4 * t + j) for j in range(4)])
        for t in (["m", 0, 1, 2, 3] if DBG_SWA >= 0.3 else []):
            tc0, tn, tb = tile_cols(t)
            a = nxt("psQ", 2)
            mm_group(ps[a][:, 0:tn], [(wka[:, k, :], uT[:, k, tc0:tc0 + tn]) for k in range(8)],
                     reads=["wka"] + uT_res(tb), writes=[("ps", a)])
            evac(Ka2[0][0:64, tc0:tc0 + tn], ps[a][0:64, 0:tn], [("ps", a)], [("KaT", b) for b in tb], eng="dve")
            evac(Ka2[1][64:128, tc0:tc0 + tn], ps[a][64:128, 0:tn], [("ps", a)], [("KaT", b) for b in tb], eng="dve")
        for bi, b in enumerate(ablocks if DBG_SWA >= 0.4 else []):
            if DBG_SWA == 0.45 and b == "m":
                continue
            c0, nr = blk_cols(b)
            a = nxt("psQ", 2)
            mm_group(ps[a][0:nr, 0:128], [(uT[:, k, c0:c0 + nr], wva[:, k, :]) for k in range(8)],
                     reads=["wva"] + uT_res([b]), writes=[("ps", a)])
            evac(Va[0][0:nr, bi, 0:64], ps[a][0:nr, 0:64], [("ps", a)], [("Va", 0)], eng="dve")
            evac(Va[1][0:nr, bi, 64:128], ps[a][0:nr, 64:128], [("ps", a)], [("Va", 1)], eng="dve")
        SB = [0, 1, 2, 5]
        LOOK = 3
        ODS = [(3, 4), (6, 7)]
        steps = []
        for i in range(NBLK):
            roles = [("meta", 0, 16, 0)] + ([("prev", i, 128, NMETA + 128 * (i - 1))] if i >= 1 else []) + \
                    [("cur", i + 1, 128, NMETA + 128 * i)]
            n_i = 2 * len(roles)
            cnt = 0
            for kvh in range(2):
                for (role, kb, nk, kc0) in roles:
                    steps.append((i, kvh, role, kb, nk, kc0, cnt == 0, cnt == n_i - 1))
                    cnt += 1

        def swa_qk(n):
            i, kvh, role, kb, nk, kc0, st, last = steps[n]
            sidx = SB[n % 4]
            pS = ps[sidx]
            base = 64 * kvh
            kres = ("KaT", "m") if role == "meta" else ("KaT", kb - 1)
            def qk(e, w):
                ins = e.matmul(pS[0:nk, :].rearrange("p (g q) -> p g q", g=4), lhsT=Ka2[kvh][:, kc0:kc0 + nk],
                               rhs=QaT[:, :, 128 * i:128 * i + 128], start=True, stop=True)
                if w is not None:
                    ins._wait_ge(*w)
                return ins
            P.add("pe", qk, reads=[kres, ("QaT", i)], writes=[("ps", sidx)], attach=True)

        def swa_rest(n):
            i, kvh, role, kb, nk, kc0, st, last = steps[n]
            sidx = SB[n % 4]
            pS = ps[sidx]
            pj = n % NPT
            iO, iD = ODS[i % 2]
            pO, pD = ps[iO], ps[iD]
            for g in range(4):
                hd = 4 * kvh + g
                if role == "meta":
                    bcol = cf[0:nk, CF_BMETA + 8 * i + hd:CF_BMETA + 8 * i + hd + 1]
                elif role == "prev":
                    bcol = cf[0:nk, CF_BPREV + hd:CF_BPREV + hd + 1]
                else:
                    bcol = cf[0:nk, CF_BCUR + hd:CF_BCUR + hd + 1]
                P.add("act", lambda e, g=g, bcol=bcol: e.activation(
                    out=PT[pj][0:nk, 128 * g:128 * g + 128], in_=pS[0:nk, 128 * g:128 * g + 128],
                    func=AF.Exp, bias=bcol, scale=0.125), reads=[("ps", sidx), "cf"], writes=[("PT", pj)])
            if role != "meta":
                mo = CB_MPREV if role == "prev" else CB_MCUR
                P.add("dve", lambda e: e.tensor_tensor(
                    out=PT[pj][:, :], in0=PT[pj][:, :], in1=cb[:, mo:mo + 512], op=ALU.mult),
                    reads=[("PT", pj), "cb"], writes=[("PT", pj)])
            oo = CB_OLO if kvh == 0 else CB_OHI

            def pv(e, w):
                i0 = e.matmul(pO[:, :], lhsT=Va[kvh][0:nk, kb, :], rhs=PT[pj][0:nk, :], start=st, stop=last)
                if w is not None:
                    i0._wait_ge(*w)
                return e.matmul(pD[:, :], lhsT=cb[0:nk, oo:oo + 128], rhs=PT[pj][0:nk, :], start=st, stop=last)
            P.add("pe", pv, reads=[("PT", pj), ("Va", kvh), "cb"], writes=[("ps", iO), ("ps", iD)], attach=True)
            if defer:
                defer.pop(0)()
            if last:
                dj = 0
                defer.append(lambda: P.add("dve", lambda e: e.tensor_tensor(out=den[dj][:, :], in0=pD[:, :], in1=sinktab[:, :], op=ALU.add),
                                           reads=[("ps", iD), "sinktab"], writes=[("den", dj)], full=True))
                for hf in range(4):
                    defer.append(lambda hf=hf: P.add("dve", lambda e: e.reciprocal(
                        out=den[dj][:, 128 * hf:128 * hf + 128], in_=den[dj][:, 128 * hf:128 * hf + 128]),
                        reads=[("den", dj)], writes=[("den", dj)]))
                defer.append(lambda: P.add("dve", lambda e: e.tensor_tensor(
                    out=OaT[:, :, 128 * i:128 * i + 128], in0=pO[:, :].rearrange("p (g q) -> p g q", g=4),
                    in1=den[dj][:, :].rearrange("p (g q) -> p g q", g=4), op=ALU.mult),
                    reads=[("ps", iO), ("den", dj)], writes=[("OaT", i // 4)]))
        defer = []
        for n in range(len(steps) + LOOK):
            if n < len(steps):
                swa_qk(n)
            if n - LOOK >= 0:
                swa_rest(n - LOOK)
        while defer:
            defer.pop(0)()
        P.barrier()
        if mstop <= 2:
            return

        wqb = AR(OFF_X, [128, 8, 512])
        wkb = AR(OFF_X + 8192, [128, 8, 512])
        wvb = AR(OFF_X + 16384, [128, 8, 512])
        QbT = AR(OFF_X + 24576, [128, SEQ])
        Kb2 = [AR(OFF_X + 28672 + 4128 * i, [128, TCOLS]) for i in range(2)]
        Vb = [AR(OFF_X + 36928 + 4352 * i, [128, 17, 128]) for i in range(2)]
        P.add("pool", lambda e: e.memset(Kb2[0][64:128, :], 0.0), writes=[("KbT", b) for b in ablocks])
        P.add("pool", lambda e: e.memset(Kb2[1][0:64, :], 0.0), writes=[("KbT", b) for b in ablocks])
        for c in range(4):
            for (wt, col0, nm) in ((wqb, 768, "wqb"), (wkb, 1280, "wkb"), (wvb, 1792, "wvb")):
                cast_dma(wt[:, :, 128 * c:128 * c + 128],
                         wi_d[:, col0 + 128 * c:col0 + 128 * c + 128].rearrange("(k p) n -> p k n", p=128), (nm, c), nm)
        for i in range(2):
            P.add("pool", lambda e, i=i: e.memset(Vb[i][:, :, :], 0.0), writes=[("Vb", i, bi) for bi in range(17)])
        for c in range(4):
            for t in range(4):
                tc0, tn, tb = tile_cols(t)
                a = nxt("psQ", 2)
                mm_group(ps[a][:, :], [(wqb[:, k, 128 * c:128 * c + 128], uT[:, k, tc0:tc0 + tn]) for k in range(8)],
                         reads=[("wqb", c)] + uT_res(tb), writes=[("ps", a)])
                evac(QbT[:, 512 * t:512 * t + 512], ps[a][:, :], [("ps", a)], [("QbT", t)], eng="dve")
            for t in ["m", 0, 1, 2, 3]:
                tc0, tn, tb = tile_cols(t)
                a = nxt("psQ", 2)
                mm_group(ps[a][:, 0:tn], [(wkb[:, k, 128 * c:128 * c + 128], uT[:, k, tc0:tc0 + tn]) for k in range(8)],
                         reads=[("wkb", c)] + uT_res(tb), writes=[("ps", a)])
                evac(Kb2[0][0:64, tc0:tc0 + tn], ps[a][0:64, 0:tn], [("ps", a)], [("KbT", b) for b in tb], eng="dve")
                evac(Kb2[1][64:128, tc0:tc0 + tn], ps[a][64:128, 0:tn], [("ps", a)], [("KbT", b) for b in tb], eng="dve")
            for bi, b in enumerate(ablocks):
                c0, nr = blk_cols(b)
                a = nxt("psQ", 2)
                mm_group(ps[a][0:nr, 0:128], [(uT[:, k, c0:c0 + nr], wvb[:, k, 128 * c:128 * c + 128]) for k in range(8)],
                         reads=[("wvb", c)] + uT_res([b]), writes=[("ps", a)])
                evac(Vb[0][0:nr, bi, 0:64], ps[a][0:nr, 0:64], [("ps", a)], [("Vb", 0, bi)], eng="dve")
                evac(Vb[1][0:nr, bi, 64:128], ps[a][0:nr, 64:128], [("ps", a)], [("Vb", 1, bi)], eng="dve")
            fsteps = []
            for j in range(4):
                kbs = [0] + [1 + r for r in range(4 * j + 4)]
                n_j = 2 * len(kbs)
                cnt = 0
                for kb in kbs:
                    for hh in range(2):
                        fsteps.append((j, kb, hh, cnt == 0, cnt == n_j - 1))
                        cnt += 1

            def fparams(n):
                j, kb, hh, st, last = fsteps[n]
                if kb == 0:
                    nk, kc0, c0, kres = 16, 0, 0, ("KbT", "m")
                else:
                    r = kb - 1
                    nk, kc0, kres = 128, NMETA + 128 * r, ("KbT", r)
                    c0 = 128 * (r - 4 * j) if r >= 4 * j else 0
                diag = kb >= 1 and (kb - 1) >= 4 * j
                return j, kb, hh, st, last, nk, kc0, c0, kres, diag

            def fox_qk(n, c=c):
                j, kb, hh, st, last, nk, kc0, c0, kres, diag = fparams(n)
                sidx = SB[n % 4]
                pS = ps[sidx]
                base = 64 * hh
                def qk(e, w):
                    ins = e.matmul(pS[0:nk, c0:512], lhsT=Kb2[hh][:, kc0:kc0 + nk],
                                   rhs=QbT[:, 512 * j + c0:512 * j + 512], start=True, stop=True)
                    if w is not None:
                        ins._wait_ge(*w)
                    return ins
                P.add("pe", qk, reads=[kres, ("QbT", j)], writes=[("ps", sidx)], attach=True)

            def fox_rest(n, c=c):
                j, kb, hh, st, last, nk, kc0, c0, kres, diag = fparams(n)
                sidx = SB[n % 4]
                pS = ps[sidx]
                pj = n % NPT
                iO, iD = ODS[(4 * c + j) % 2]
                pO, pD = ps[iO], ps[iD]
                hd = 2 * c + hh
                P.add("act", lambda e: e.activation(
                    out=PT[pj][0:nk, c0:512], in_=pS[0:nk, c0:512], func=AF.Exp,
                    bias=biasF[0:nk, kb, j, hd:hd + 1], scale=0.125),
                    reads=[("ps", sidx), "biasF"], writes=[("PT", pj)])
                if diag:
                    P.add("dve", lambda e: e.tensor_tensor(
                        out=PT[pj][:, c0:c0 + 128], in0=PT[pj][:, c0:c0 + 128],
                        in1=cb[:, CB_MCUR:CB_MCUR + 128], op=ALU.mult),
                        reads=[("PT", pj), "cb"], writes=[("PT", pj)])
                oo = CB_OLO if hh == 0 else CB_OHI

                def pv(e, w):
                    i0 = e.matmul(pO[:, c0:512], lhsT=Vb[hh][0:nk, kb, :], rhs=PT[pj][0:nk, c0:512], start=st, stop=last)
                    if w is not None:
                        i0._wait_ge(*w)
                    return e.matmul(pD[:, c0:512], lhsT=cb[0:nk, oo:oo + 128], rhs=PT[pj][0:nk, c0:512], start=st, stop=last)
                P.add("pe", pv, reads=[("PT", pj), ("Vb", hh, kb), "cb"], writes=[("ps", iO), ("ps", iD)], attach=True)
                if defer:
                    defer.pop(0)()
                if last:
                    dj = 0
                    for hf in range(2):
                        defer.append(lambda hf=hf: P.add("dve", lambda e: e.reciprocal(
                            out=den[dj][:, 256 * hf:256 * hf + 256], in_=pD[:, 256 * hf:256 * hf + 256]),
                            reads=[("ps", iD)], writes=[("den", dj)], full=True))
                    defer.append(lambda: P.add("dve", lambda e: e.tensor_tensor(
                        out=ObT[:, c, 512 * j:512 * j + 512], in0=pO[:, :], in1=den[dj][:, :], op=ALU.mult),
                        reads=[("ps", iO), ("den", dj)], writes=[("ObT", j)]))
            for n in range(len(fsteps) + LOOK):
                if n < len(fsteps):
                    fox_qk(n)
                if n - LOOK >= 0:
                    fox_rest(n - LOOK)
            while defer:
                defer.pop(0)()
        P.barrier()
        if mstop <= 3:
            return

        mixT = AR(OFF_MIX, [128, 8, SEQ])
        wo_t = AR(OFF_PW, [128, 8, D])
        ovl = [[("pw", 0, "a", 0), ("pw", 0, "a", 1), ("pw", 0, "b"), ("pw", 0, "ga")],
               [("pw", 0, "gb"), ("pw", 1, "a", 0), ("pw", 1, "a", 1), ("pw", 1, "b")]]

        def issue_wo(hf):
            P.add("pool", lambda e: e.dma_start(
                out=wo_t[:, 4 * hf:4 * hf + 4, :], in_=wo_d[512 * hf:512 * hf + 512, :].rearrange("(k p) n -> p k n", p=128)),
                writes=[("wout", hf)] + ovl[hf], dma=f"wout{hf}")
        ptmp = [AR(OFF_PTMP + 2048 * i, [128, 512], F32) for i in range(2)]
        def pw_tiles(st):
            sl = st % 2
            o = OFF_PW + 12288 * sl
            return sl, AR(o, [128, 4, 256]), AR(o + 2048, [128, 4, 256]), AR(o + 4096, [128, 8, 256]), AR(o + 8192, [128, 8, 256])

        def pw_issue(st):
            sl, wa_t, wb_t, wga_t, wgb_t = pw_tiles(st)
            c0 = 256 * st
            for hh in range(2):
                cast_dma(wa_t[64 * hh:64 * hh + 64, :, :],
                         wa_d[256 * hh:256 * hh + 256, c0:c0 + 256].rearrange("(g d) n -> d g n", d=64),
                         ("pw", sl, "a", hh), f"pw{sl}a{hh}")
            cast_dma(wb_t, wb_d[:, c0:c0 + 256].rearrange("(c p) n -> p c n", p=128), ("pw", sl, "b"), f"pw{sl}b")
            cast_dma(wga_t, wi_d[:, 2312 + c0:2312 + c0 + 256].rearrange("(k p) n -> p k n", p=128), ("pw", sl, "ga"), f"pw{sl}ga")
            cast_dma(wgb_t, wi_d[:, 3336 + c0:3336 + c0 + 256].rearrange("(k p) n -> p k n", p=128), ("pw", sl, "gb"), f"pw{sl}gb")
        pw_issue(0)
        for st in range(4):
            s, wa_t, wb_t, wga_t, wgb_t = pw_tiles(st)
            if st + 1 < 4:
                pw_issue(st + 1)
            if st == 3:
                issue_wo(0)
            for q in range(2):
                m = 2 * st + q
                for t in range(4):
                    tc0, tn, tb = tile_cols(t)
                    ia, ib, iga, igb = (nxt("psP", 8) for _ in range(4))
                    mm_group(ps[ia][:, :], [(wa_t[:, k, 128 * q:128 * q + 128], OaT[:, k, 512 * t:512 * t + 512]) for k in range(4)],
                             reads=[("pw", s, "a", 0), ("pw", s, "a", 1), ("OaT", t)], writes=[("ps", ia)])
                    mm_group(ps[ib][:, :], [(wb_t[:, k, 128 * q:128 * q + 128], ObT[:, k, 512 * t:512 * t + 512]) for k in range(4)],
                             reads=[("pw", s, "b"), ("ObT", t)], writes=[("ps", ib)])
                    mm_group(ps[iga][:, :], [(wga_t[:, k, 128 * q:128 * q + 128], uT[:, k, tc0:tc0 + 512]) for k in range(8)],
                             reads=[("pw", s, "ga")] + uT_res(tb), writes=[("ps", iga)])
                    mm_group(ps[igb][:, :], [(wgb_t[:, k, 128 * q:128 * q + 128], uT[:, k, tc0:tc0 + 512]) for k in range(8)],
                             reads=[("pw", s, "gb")] + uT_res(tb), writes=[("ps", igb)])
                    ja, jb = 0, 1
                    P.add("act", lambda e, ja=ja, iga=iga: e.activation(out=ptmp[ja][:, :], in_=ps[iga][:, :], func=AF.Sigmoid),
                          reads=[("ps", iga)], writes=[("ptmp", ja)])
                    P.add("act", lambda e, jb=jb, igb=igb: e.activation(out=ptmp[jb][:, :], in_=ps[igb][:, :], func=AF.Sigmoid),
                          reads=[("ps", igb)], writes=[("ptmp", jb)])
                    P.add("dve", lambda e, ja=ja, ia=ia: e.tensor_tensor(out=ptmp[ja][:, :], in0=ptmp[ja][:, :], in1=ps[ia][:, :], op=ALU.mult),
                          reads=[("ptmp", ja), ("ps", ia)], writes=[("ptmp", ja)])
                    P.add("dve", lambda e, jb=jb, ib=ib: e.tensor_tensor(out=ptmp[jb][:, :], in0=ptmp[jb][:, :], in1=ps[ib][:, :], op=ALU.mult),
                          reads=[("ptmp", jb), ("ps", ib)], writes=[("ptmp", jb)])
                    P.add("pool", lambda e, ja=ja, jb=jb, m=m, t=t: e.tensor_tensor(
                        out=mixT[:, m, 512 * t:512 * t + 512], in0=ptmp[ja][:, :], in1=ptmp[jb][:, :], op=ALU.add),
                        reads=[("ptmp", ja), ("ptmp", jb)], writes=[("mixT", t)])
        issue_wo(1)
        hook = flush = None
        if post_tail is not None:
            hook, flush = post_tail()
        for b in range(NBLK):
            for half in range(2):
                o = nxt("psP", 8)
                mm_group(ps[o][:, :], [(mixT[:, k, 128 * b:128 * b + 128], wo_t[:, k, 512 * half:512 * half + 512]) for k in range(8)],
                         reads=[("mixT", b // 4), ("wout", 0), ("wout", 1)], writes=[("ps", o)])
                dst = h[:, b, 512 * half:512 * half + 512]
                P.add("dve", lambda e, dst=dst, o=o: e.tensor_tensor(out=dst, in0=ps[o][:, :], in1=dst, op=ALU.add),
                      reads=[("ps", o), ("h", b)], writes=[("h", b)])
            if hook is not None:
                hook(b)
        if flush is not None:
            flush()

    def final_out(s, raw, hooked=False, after_block=None):
        ost = [AR(OFF_OST + 4096 * i, [128, D], F32) for i in range(2)]
        if raw:
            for b in range(NBLK):
                P.add("sp", lambda e, b=b: e.dma_start(out=out_d[s, 128 * b:128 * b + 128, :], in_=h[:, b, :]),
                      reads=[("h", b)], dma=f"out{b % 2}")
            return
        P.add("sp", lambda e: e.dma_start(out=gb[:], in_=n4_d.broadcast_to([128, D])), writes=["gb"], dma="gb")
        P.add("pool", lambda e: e.memset(ss[:], 0.0), writes=ALLSS)
        fjunk = AR(OFF_OST + 8192, [128, D])

        def a1(b):
            P.add("act", lambda e: e.activation(out=fjunk[:, :], in_=h[:, b, :], func=AF.Square, accum_out=ss[:, b:b + 1]),
                  reads=[("h", b), ("ss", b)], writes=[("ss", b), "fjunk"], full=True)
            P.add("act", lambda e: e.activation(out=rs[:, b:b + 1], in_=ss[:, b:b + 1], func=AF.Sqrt,
                                                bias=epsc[:, 0:1], scale=1.0 / D),
                  reads=[("ss", b), "epsc"], writes=[("rs", b)])

        def a2(b):
            j = nxt("ost", 2)
            P.add("dve", lambda e: e.reciprocal(out=rs[:, b:b + 1], in_=rs[:, b:b + 1]), reads=[("rs", b)], writes=[("rs", b)])
            P.add("dve", lambda e: e.scalar_tensor_tensor(
                out=ost[j][:, :], in0=h[:, b, :], scalar=rs[:, b:b + 1], in1=gb[:, :], op0=ALU.mult, op1=ALU.mult),
                reads=[("h", b), ("rs", b), "gb"], writes=[("ost", j)])
            P.add("sp", lambda e: e.dma_start(out=out_d[s, 128 * b:128 * b + 128, :], in_=ost[j][:, :]),
                  reads=[("ost", j)], dma=f"out{j}")
            if after_block is not None:
                after_block(b)
        if hooked:
            seq = []

            def hook(b):
                seq.append(b)
                if len(seq) >= 2:
                    a2(seq[-2])
                a1(b)

            def flush():
                a2(seq[-1])
            return hook, flush
        a1(0)
        for b in range(NBLK):
            if b + 1 < NBLK:
                a1(b + 1)
            a2(b)

    def load_x(s, b):
        P.add("sp", lambda e: e.dma_start(out=h[:, b, :], in_=x_d[s, 128 * b:128 * b + 128, :]),
              writes=[("h", b)], dma=f"x{b % 4}")

    for s in range(2):
        hm = AR(OFF_HM, [128, D], F32)
        if s == 0 or stage != 3:
            for b in range(NBLK):
                load_x(s, b)
        P.add("sp", lambda e: e.dma_start(out=hm[0:NMETA, :], in_=meta_d), writes=[("h", "m")], dma="xm")
        if stage == 3:
            ffn(w1i_d, w1o_d, n1_d, True, hm, "f1",
                tail=lambda: norm_to_uT(n2_d, ["m"] + list(range(NBLK)), hm, hooked=True,
                                        junk=AR(OFF_OST, [128, D]), junkres=("ost", 0)))
            P.barrier()
            mixer(hm, post_tail=lambda: norm_to_uT(n3_d, list(range(NBLK)), hm, hooked=True,
                                                    junk=AR(OFF_PTMP, [128, D]), junkres=("ptmp", 0)))
            P.barrier()
            ffn(w2i_d, w2o_d, n3_d, False, hm, "f2", do_norm=False,
                tail=lambda s=s: final_out(s, raw=False, hooked=True,
                                           after_block=((lambda b: load_x(1, b)) if s == 0 else None)))
        else:
            ffn(w1i_d, w1o_d, n1_d, True, hm, "f1")
            if stage >= 2:
                norm_to_uT(n2_d, ["m"] + list(range(NBLK)), hm)
                P.barrier()
                mixer(hm, mstop=(stage - 10 if stage >= 10 else 9))
            P.barrier()
            final_out(s, raw=True)
    P.emit()
    es.close()
    return nc


_CACHE = {}


def kernel(x, meta_tokens, ffn1_norm, ffn1_w_in, ffn1_w_out, mix_norm, w_in, b_forget, attn_sinks,
           w_branch_a, w_branch_b, w_out, ffn2_norm, ffn2_w_in, ffn2_w_out, final_norm, _stage=3, _cores=8):
    f = lambda a: np.ascontiguousarray(np.asarray(a, dtype=np.float32))
    x = f(x)
    cf, cb = make_consts()
    shared = {
        "meta": f(meta_tokens), "n1": f(ffn1_norm).reshape(1, D), "n2": f(mix_norm).reshape(1, D),
        "n3": f(ffn2_norm).reshape(1, D), "n4": f(final_norm).reshape(1, D),
        "w1i": f(ffn1_w_in)[0], "w1o": f(ffn1_w_out)[0], "w2i": f(ffn2_w_in)[0], "w2o": f(ffn2_w_out)[0],
        "wi": f(w_in)[0], "bfg": f(b_forget).reshape(1, 8), "snk": f(attn_sinks).reshape(1, 8),
        "wa": f(w_branch_a)[0], "wb": f(w_branch_b)[0], "wo": f(w_out)[0], "cf": cf, "cb": cb,
    }
    if _stage not in _CACHE:
        _CACHE[_stage] = build(_stage)
    nc = _CACHE[_stage]
    in_maps = [dict(shared, x=x[2 * c:2 * c + 2]) for c in range(_cores)]
    res = run_bass_kernel_spmd(nc, in_maps, core_ids=list(range(_cores)))
    return np.concatenate([r["out"] for r in res.results], axis=0)
```
